# Optimizing a Trainium2 kernel written in Bass

```python
import jax, jax.numpy as jnp
from jax import lax
import numpy as np

D_MODEL = 4096
BATCH = 1
SEQ = 8192
DEPTH = 1

CTX_LEN = 256
GRID_W = 64
MIX_WIDTH = D_MODEL
A_WIDTH = MIX_WIDTH // 2
A_GROUPS = 16
A_GROUP_DIM = A_WIDTH // A_GROUPS
CHUNK = 128
B_WIDTH = MIX_WIDTH - A_WIDTH
HEAD_DIM = 128
N_HEADS = B_WIDTH // HEAD_DIM
N_KV_HEADS = 4
KV_GROUP = N_HEADS // N_KV_HEADS
KV_WIDTH = N_KV_HEADS * HEAD_DIM
Q_BLOCK = 128
ROPE_THETA = 10000.0
D_FF = 11008
CONV_W = 3
EPS = 1e-6
N_MOD = 6
IN_COLS = 2 * A_WIDTH + B_WIDTH + 2 * KV_WIDTH
SPLITS = (2 * A_WIDTH, 2 * A_WIDTH + B_WIDTH, 2 * A_WIDTH + B_WIDTH + KV_WIDTH)

kernel_name = 'hybrid_gmlp_gqa_dit_block'


def rms_norm(x, g):
    xf = x.astype(jnp.float32)
    y = xf * lax.rsqrt(jnp.mean(xf * xf, axis=-1, keepdims=True) + EPS)
    return (y * g.astype(jnp.float32)).astype(x.dtype)


def layer_norm(x, g, b):
    xf = x.astype(jnp.float32)
    mu = jnp.mean(xf, axis=-1, keepdims=True)
    var = jnp.mean(jnp.square(xf - mu), axis=-1, keepdims=True)
    y = (xf - mu) * lax.rsqrt(var + EPS)
    return (y * g.astype(jnp.float32) + b.astype(jnp.float32)).astype(x.dtype)


def modulate(xn, shift, scale):
    return xn * (1 + scale) + shift


def adaln(cond, w_ada, b_ada, n):
    m = jax.nn.silu(cond) @ w_ada[:, :n * D_MODEL] + b_ada[:n * D_MODEL]
    return jnp.split(m, n, axis=-1)


def split_heads(t, n):
    return t.reshape(t.shape[0], t.shape[1], n, HEAD_DIM)


def rope_axis(x, pos):
    half = x.shape[-1] // 2
    freqs = ROPE_THETA ** (-jnp.arange(half, dtype=jnp.float32) / half)
    ang = pos.astype(jnp.float32)[:, None] * freqs[None, :]
    cos = jnp.cos(ang)[:, None, :]
    sin = jnp.sin(ang)[:, None, :]
    xf = x.astype(jnp.float32)
    x1, x2 = xf[..., :half], xf[..., half:]
    return jnp.concatenate([x1 * cos - x2 * sin, x2 * cos + x1 * sin], axis=-1).astype(x.dtype)


def axial_rope(x, row, col):
    h = x.shape[-1] // 2
    return jnp.concatenate([rope_axis(x[..., :h], row), rope_axis(x[..., h:], col)], axis=-1)


def chunk_mlp(z, ln_g, ln_b, w_s, b_s):
    bsz, L, _ = z.shape
    z = jax.nn.gelu(z, approximate=False)
    u, v = jnp.split(z, 2, axis=-1)
    v = layer_norm(v, ln_g, ln_b).reshape(bsz, L // CHUNK, CHUNK, A_GROUPS, A_GROUP_DIM)
    mixed = jnp.einsum('gpq,bnqgc->bnpgc', w_s, v) + b_s.T[:, :, None]
    return u * mixed.reshape(bsz, L, A_WIDTH)


def attend(qi, k, v):
    s = jnp.einsum('bqkgd,bskd->bkgqs', qi, k).astype(jnp.float32) * (HEAD_DIM ** -0.5)
    p = jax.nn.softmax(s, axis=-1).astype(v.dtype)
    return jnp.einsum('bkgqs,bskd->bqkgd', p, v)


def latent_attention(q, k_all, v_all):
    bsz, S = q.shape[:2]
    qb = q.reshape(bsz, S // Q_BLOCK, Q_BLOCK, N_KV_HEADS, KV_GROUP, HEAD_DIM).transpose(1, 0, 2, 3, 4, 5)
    o = lax.map(lambda qi: attend(qi, k_all, v_all), qb)
    return o.transpose(1, 0, 2, 3, 4, 5).reshape(bsz, S, B_WIDTH)


def merge_mixers(out_a, out_b, g_out_a, g_out_b, w_out):
    return jnp.concatenate([rms_norm(out_a, g_out_a), rms_norm(out_b, g_out_b)], axis=-1) @ w_out


def depthwise_conv(h, w, b):
    L = h.shape[1]
    pad = CONV_W // 2
    hp = jnp.pad(h, ((0, 0), (pad, pad), (0, 0)))
    return sum(hp[:, j:j + L] * w[j] for j in range(CONV_W)) + b


def ffn_sublayer(x, g_pre, g_post, shift, scale, gate, w_up, conv_w, conv_b, w_down):
    h = modulate(rms_norm(x, g_pre), shift, scale)
    a = depthwise_conv(h @ w_up, conv_w, conv_b)
    gt, up = jnp.split(a, 2, axis=-1)
    f = (jax.nn.silu(gt) * up) @ w_down
    return x + gate * rms_norm(f, g_post)


def setup_inputs(seed: int = 0) -> dict:
    key = jax.random.key(seed)
    ks = jax.random.split(key, 28)
    f32 = jnp.float32

    def nrm(k, shape, scale):
        return jax.random.normal(k, shape, f32) * scale

    def gain(k, shape):
        return 1.0 + 0.05 * jax.random.normal(k, shape, f32)

    L = DEPTH
    return {
        'x': nrm(ks[0], (BATCH, SEQ, D_MODEL), 1.0),
        'c': nrm(ks[1], (BATCH, D_MODEL), 1.0),
        'ctx': nrm(ks[2], (BATCH, CTX_LEN, D_MODEL), 1.0),
        'c_ctx': nrm(ks[3], (D_MODEL,), 1.0),
        'w_ada': nrm(ks[4], (L, D_MODEL, N_MOD * D_MODEL), 0.5 * D_MODEL ** -0.5),
        'b_ada': nrm(ks[5], (L, N_MOD * D_MODEL), 0.02),
        'g_pre_mix': gain(ks[6], (L, D_MODEL)),
        'g_post_mix': gain(ks[7], (L, D_MODEL)),
        'g_pre_ffn': gain(ks[8], (L, D_MODEL)),
        'g_post_ffn': gain(ks[9], (L, D_MODEL)),
        'w_in': nrm(ks[10], (L, D_MODEL, IN_COLS), D_MODEL ** -0.5),
        'ln_v_g': gain(ks[11], (L, A_WIDTH)),
        'ln_v_b': nrm(ks[12], (L, A_WIDTH), 0.02),
        'w_s': nrm(ks[13], (L, A_GROUPS, CHUNK, CHUNK), CHUNK ** -0.5),
        'b_s': gain(ks[14], (L, A_GROUPS, CHUNK)),
        'g_q': gain(ks[15], (L, HEAD_DIM)),
        'g_k': gain(ks[16], (L, HEAD_DIM)),
        'g_out_a': gain(ks[17], (L, A_WIDTH)),
        'g_out_b': gain(ks[18], (L, B_WIDTH)),
        'w_out': nrm(ks[19], (L, MIX_WIDTH, D_MODEL), MIX_WIDTH ** -0.5),
        'w_up': nrm(ks[20], (L, D_MODEL, 2 * D_FF), D_MODEL ** -0.5),
        'conv_w': nrm(ks[21], (L, CONV_W, 2 * D_FF), CONV_W ** -0.5),
        'conv_b': nrm(ks[22], (L, 2 * D_FF), 0.02),
        'w_down': nrm(ks[23], (L, D_FF, D_MODEL), D_FF ** -0.5),
    }


def reference(x, c, ctx, c_ctx, w_ada, b_ada, g_pre_mix, g_post_mix, g_pre_ffn, g_post_ffn,
              w_in, ln_v_g, ln_v_b, w_s, b_s, g_q, g_k, g_out_a, g_out_b, w_out,
              w_up, conv_w, conv_b, w_down):
    bsz, S, _ = x.shape
    C = ctx.shape[1]
    ROWS = S // GRID_W
    row = jnp.broadcast_to(jnp.arange(ROWS, dtype=jnp.int32)[:, None], (ROWS, GRID_W)).reshape(-1)
    col = jnp.broadcast_to(jnp.arange(GRID_W, dtype=jnp.int32)[None, :], (ROWS, GRID_W)).reshape(-1)

    for l in range(DEPTH):
        last = l == DEPTH - 1
        sh_a, sc_a, gt_a, sh_f, sc_f, gt_f = [m[:, None, :] for m in adaln(c, w_ada[l], b_ada[l], N_MOD)]
        cmods = adaln(c_ctx, w_ada[l], b_ada[l], 2 if last else N_MOD)

        h = modulate(rms_norm(x, g_pre_mix[l]), sh_a, sc_a)
        hc = modulate(rms_norm(ctx, g_pre_mix[l]), cmods[0], cmods[1])

        za, q, k, v = jnp.split(h @ w_in[l], SPLITS, axis=-1)
        q = axial_rope(rms_norm(split_heads(q, N_HEADS), g_q[l]), row, col)
        k = axial_rope(rms_norm(split_heads(k, N_KV_HEADS), g_k[l]), row, col)
        v = split_heads(v, N_KV_HEADS)
        if last:
            kc, vc = jnp.split(hc @ w_in[l][:, SPLITS[1]:], 2, axis=-1)
        else:
            zac, qc, kc, vc = jnp.split(hc @ w_in[l], SPLITS, axis=-1)
        kc = rms_norm(split_heads(kc, N_KV_HEADS), g_k[l])
        vc = split_heads(vc, N_KV_HEADS)
        k_all = jnp.concatenate([kc, k], axis=1)
        v_all = jnp.concatenate([vc, v], axis=1)

        out_a = chunk_mlp(za, ln_v_g[l], ln_v_b[l], w_s[l], b_s[l])
        out_b = latent_attention(q, k_all, v_all)
        o = merge_mixers(out_a, out_b, g_out_a[l], g_out_b[l], w_out[l])
        x_mix = x + gt_a * rms_norm(o, g_post_mix[l])

        if not last:
            out_ac = chunk_mlp(zac, ln_v_g[l], ln_v_b[l], w_s[l], b_s[l])
            qc = rms_norm(split_heads(qc, N_HEADS), g_q[l])
            out_bc = attend(qc.reshape(bsz, C, N_KV_HEADS, KV_GROUP, HEAD_DIM), kc, vc).reshape(bsz, C, B_WIDTH)
            oc = merge_mixers(out_ac, out_bc, g_out_a[l], g_out_b[l], w_out[l])
            ctx_mix = ctx + cmods[2] * rms_norm(oc, g_post_mix[l])
            ctx = ffn_sublayer(ctx_mix, g_pre_ffn[l], g_post_ffn[l], cmods[3], cmods[4], cmods[5],
                               w_up[l], conv_w[l], conv_b[l], w_down[l])

        x = ffn_sublayer(x_mix, g_pre_ffn[l], g_post_ffn[l], sh_f, sc_f, gt_f,
                         w_up[l], conv_w[l], conv_b[l], w_down[l])
    return x
```

```python
import math
import numpy as np
import concourse.bass as bass
import concourse.mybir as mybir
from concourse.bass_utils import run_bass_kernel_spmd
from contextlib import ExitStack

F32 = mybir.dt.float32
BF16 = mybir.dt.bfloat16
AF = mybir.ActivationFunctionType
ALU = mybir.AluOpType
AX = mybir.AxisListType

ENGS = ("pe", "act", "dve", "pool", "sp")
D = 4096
S = 8192
NCORE = 8
TOK = S // NCORE
DFF = 11008
NFC = DFF // 128
EPS = 1e-6
NT = 66
NKEY = NT * 128
NOWN = 10
NG = 5
VW = 130


class Op:
    __slots__ = ("eng", "fn", "reads", "writes", "dma", "deps", "idx", "need_inc", "seq", "dma_cnt")

    def __init__(self, eng, fn, reads, writes, dma):
        self.eng, self.fn, self.reads, self.writes, self.dma = eng, fn, reads, writes, dma
        self.deps = set()
        self.need_inc = False
        self.seq = 0
        self.dma_cnt = 0


class Prog:
    def __init__(self, nc):
        self.nc = nc
        self.ops = []
        self.last_w = {}
        self.readers = {}
        self.last_eng = {}
        self.dma_since = []
        self.pending_barrier = {}

    def op(self, eng, fn, reads=(), writes=(), dma=None):
        o = Op(eng, fn, tuple(reads), tuple(writes), dma)
        o.idx = len(self.ops)
        for r in o.reads:
            w = self.last_w.get(r)
            if w is not None:
                o.deps.add(w)
        for w_ in o.writes:
            w = self.last_w.get(w_)
            if w is not None:
                o.deps.add(w)
            for rd in self.readers.get(w_, ()):
                o.deps.add(rd)
        for r in o.reads:
            self.readers.setdefault(r, []).append(o.idx)
        for w_ in o.writes:
            self.last_w[w_] = o.idx
            self.readers[w_] = []
        if eng in self.pending_barrier:
            o.deps |= self.pending_barrier.pop(eng)
        o.deps.discard(o.idx)
        self.ops.append(o)
        if dma is None:
            self.last_cmp = getattr(self, "last_cmp", {})
            self.last_cmp[eng] = o.idx
        self.last_eng[eng] = o.idx
        if dma is not None:
            self.dma_since.append(o.idx)
        return o

    def barrier(self):
        b = set(self.last_eng.values()) | set(self.dma_since)
        for e in ENGS:
            self.pending_barrier[e] = set(b) | self.pending_barrier.get(e, set())
        self.dma_since = []

    def emit(self, final_waits=()):
        nc = self.nc
        ops = self.ops
        for o in ops:
            for d in o.deps:
                p = ops[d]
                if p.dma is not None:
                    continue
                if p.eng == "pe" and o.eng == "pe":
                    continue
                p.need_inc = True
        if final_waits is None:
            for e, i_ in getattr(self, "last_cmp", {}).items():
                ops[i_].need_inc = True
            for o in ops:
                if o.dma is None and o.eng != "pe" and False:
                    o.need_inc = True
        cnt = {e: 0 for e in ENGS}
        dcnt = {}
        for o in ops:
            if o.dma is not None:
                dcnt[o.dma] = dcnt.get(o.dma, 0) + 1
                o.dma_cnt = dcnt[o.dma]
            elif o.need_inc:
                cnt[o.eng] += 1
                o.seq = cnt[o.eng]
        with ExitStack() as es:
            esem = {e: es.enter_context(nc.semaphore("s_" + e)) for e in ENGS}
            dsem = {k: es.enter_context(nc.semaphore("d_%d" % i)) for i, k in enumerate(dcnt)}
            block = es.enter_context(nc.Block())
            per_eng = {e: [o for o in ops if o.eng == e] for e in ENGS}

            def body(ename):
                def run(engine):
                    waited = {}
                    for o in per_eng[ename]:
                        need = {}
                        for d in o.deps:
                            p = ops[d]
                            if p.dma is not None:
                                key = ("d", p.dma)
                                val = 16 * p.dma_cnt
                                sem = dsem[p.dma]
                            else:
                                if p.eng == "pe" and ename == "pe":
                                    continue
                                key = ("e", p.eng)
                                val = p.seq
                                sem = esem[p.eng]
                            if val > need.get(key, (0, None))[0]:
                                need[key] = (val, sem)
                        for key, (val, sem) in need.items():
                            if waited.get(key, 0) >= val:
                                continue
                            engine.wait_ge(sem, val)
                            waited[key] = val
                        ins = o.fn(engine)
                        if o.dma is not None:
                            ins.then_inc(dsem[o.dma], 16)
                        elif o.need_inc:
                            ins.then_inc(esem[ename], 1)
                    if ename == "sp" and final_waits is None:
                        for e2 in ENGS:
                            if cnt[e2] > 0:
                                engine.wait_ge(esem[e2], cnt[e2])
                        for k2 in dcnt:
                            engine.wait_ge(dsem[k2], 16 * dcnt[k2])
                    elif ename == "sp":
                        for i in final_waits:
                            p = ops[i]
                            engine.wait_ge(dsem[p.dma], 16 * dcnt[p.dma])
                return run

            block.tensor(body("pe"))
            block.scalar(body("act"))
            block.vector(body("dve"))
            block.gpsimd(body("pool"))
            block.sync(body("sp"))


class SB:
    BASE = 16512
    LIMIT = 229376 - 64

    def __init__(self, nc):
        self.nc = nc
        self.off = SB.BASE
        self.n = 0

    def alloc(self, shape, dtype):
        nbytes = int(np.prod(shape[1:])) * (2 if dtype == BF16 else 4)
        nbytes = (nbytes + 63) // 64 * 64
        assert self.off + nbytes <= SB.LIMIT, ("SBUF overflow", self.off, nbytes)
        self.n += 1
        t = self.nc.alloc_sbuf_tensor_at("sb%d" % self.n, list(shape), dtype, offset=self.off)
        self.off += nbytes
        return t.ap()

    def mark(self):
        return self.off

    def release(self, m):
        self.off = m


def pipeline(n, stages, delays, order=None):
    maxd = max(delays)
    order = list(reversed(range(len(stages)))) if order is None else order
    for i in range(n + maxd):
        for s_ in order:
            u = i - delays[s_]
            if 0 <= u < n:
                stages[s_](u)


def build_program(stop=None, debug=False):
    nc = bass.Bass("TRN2", target_bir_lowering=False)

    def din(name, shape):
        return nc.dram_tensor(name, list(shape), F32, kind="ExternalInput").ap()

    xr = din("xr", [S, D])
    ctx = din("ctx", [256, D])
    pos_d = din("pos", [128, NT * 2])
    jidx_d = din("jidx", [128, 32])
    cvec_d = din("cvec", [128, 64])
    w_ada_a = din("w_ada_a", [D, 3 * D])
    w_ada_b = din("w_ada_b", [D, 3 * D])
    b_ada_col = din("b_ada_col", [128, 192])
    b_ada_row = din("b_ada_row", [1, 6 * D])
    gcols_d = din("gcols", [128, 64])
    goutc_d = din("goutc", [128, 32])
    gpost_mix_row = din("gpost_mix_row", [1, D])
    gpost_ffn_row = din("gpost_ffn_row", [1, D])
    lng_row = din("lng_row", [1, 2048])
    lnb_row = din("lnb_row", [1, 2048])
    gq_row = din("gq_row", [1, 128])
    gk_row = din("gk_row", [1, 128])
    w_in_h = din("w_in_h", [28, 128, 32 * 256])
    wsT_d = din("wsT", [128, 16 * 128])
    bs_col_d = din("bs_col", [128, 16])
    w_out_h = din("w_out_h", [8, 128, 32 * 512])
    w_up_g = din("w_up_g", [NFC, 128, 32 * 128])
    w_up_u = din("w_up_u", [NFC, 128, 32 * 128])
    convp_d = din("convp", [128, 2 * NFC * 4])
    w_down_h = din("w_down_h", [8, 128, NFC * 512])
    hmask_d = din("hmask", [128, 2])
    ident_d = din("ident", [128, 128])
    out = nc.dram_tensor("out", [TOK, D], F32, kind="ExternalOutput").ap()

    def dscr(name, shape, dt):
        if debug:
            return nc.dram_tensor(name, list(shape), dt, kind="ExternalOutput").ap()
        return nc.dram_tensor(name, list(shape), dt).ap()

    KT_scr = dscr("KT_scr", [4, 128, NKEY], BF16)
    V_scr = dscr("V_scr", [4, 128, NT * VW], BF16)
    hT_scr = dscr("hT_scr", [NOWN, 128, 32 * 128], BF16)
    oT_scr = dscr("oT_scr", [32, 128, NOWN * 128], BF16)
    o_scr = dscr("o_scr", [NOWN * 128, D], F32)
    xm_scr = dscr("xm_scr", [TOK, D], F32)
    f_scr = dscr("f_scr", [TOK, D], F32)
    gg_scr = dscr("gg_scr", [2, 128, D], F32)

    P = Prog(nc)
    sb = SB(nc)
    ps = [nc.alloc_psum_tensor("ps%d" % i, [128, 512], F32).ap() for i in range(8)]
    psb = [p.bitcast(BF16) for p in ps]

    ident = sb.alloc([128, 128], BF16)
    onesf = sb.alloc([128, 128], F32)
    modc = sb.alloc([128, 192], F32)
    modx = sb.alloc([128, 64], F32)
    Ga = sb.alloc([128, 32], F32)
    Gc = sb.alloc([128, 32], F32)
    Gf = sb.alloc([128, 32], F32)
    gcols = sb.alloc([128, 64], F32)
    goutc = sb.alloc([128, 32], F32)
    cst = sb.alloc([128, 8], F32)
    freq = sb.alloc([128, 32], F32)
    pos = sb.alloc([128, NT * 2], F32)
    gq_t = sb.alloc([128, 128], F32)
    gk_t = sb.alloc([128, 128], F32)
    bs_col = sb.alloc([128, 16], F32)
    hmask = sb.alloc([128, 2], F32)
    convp = sb.alloc([128, 2 * NFC * 4], F32)
    ssA = sb.alloc([128, NOWN * 8], F32)
    ssB = sb.alloc([128, NOWN * 16], F32)
    ssO = sb.alloc([128, NOWN * 8], F32)
    ssF = sb.alloc([128, 8 * 8], F32)
    rA = sb.alloc([128, NOWN], F32)
    rB = sb.alloc([128, NOWN], F32)
    rO = sb.alloc([128, NOWN], F32)
    rF = sb.alloc([128, 8], F32)
    sv = sb.alloc([128, 64], F32)
    st = sb.alloc([128, 64], F32)
    SHa = modc[:, 0:32]
    SHf = modc[:, 96:128]
    SHc = modx[:, 0:32]
    epsc = cst[:, 0:1]
    negpi = cst[:, 1:2]
    pospi = cst[:, 2:3]

    def ld(eng, dst, src, key, reads=(), writes=None):
        return P.op(eng, lambda e, dst=dst, src=src: e.dma_start(out=dst, in_=src), reads=reads,
                    writes=[key] if writes is None else writes, dma="L:" + key)

    def bc(row_ap, n=128):
        return row_ap.partition_broadcast(n).rearrange("p a b -> p (a b)")

    P.op("pool", lambda e: e.memset(cst[:, 0:1], EPS), writes=["cst"])
    P.op("pool", lambda e: e.memset(cst[:, 1:2], -math.pi), writes=["cst"])
    P.op("pool", lambda e: e.memset(cst[:, 2:3], math.pi), writes=["cst"])
    P.op("pool", lambda e: e.memset(cst[:, 3:4], 1.0), writes=["cst"])
    P.op("pool", lambda e: e.memset(onesf, 1.0), writes=["onesf"])
    ld("pool", ident, ident_d, "ident")
    ld("sp", sv, cvec_d, "sv")
    ld("sp", gcols, gcols_d, "gcols")
    ld("sp", goutc, goutc_d, "goutc")
    ld("sp", pos, pos_d, "pos")
    ld("sp", freq, jidx_d, "freq")
    ld("sp", gq_t, bc(gq_row), "gq_t")
    ld("sp", gk_t, bc(gk_row), "gk_t")
    ld("sp", bs_col, bs_col_d, "bs_col")
    ld("sp", hmask, hmask_d, "hmask")
    ld("sp", convp, convp_d, "convp")
    ld("sp", modc, b_ada_col, "modc")
    P.op("act", lambda e: e.activation(out=freq, in_=freq, func=AF.Exp, scale=-math.log(10000.0) / 32.0),
         reads=["freq"], writes=["freq"])
    P.op("act", lambda e: e.activation(out=sv, in_=sv, func=AF.Silu), reads=["sv"], writes=["sv"])

    mA = sb.mark()
    NWT = 8
    wt = [sb.alloc([128, 2048], F32) for _ in range(NWT)]
    rowbuf = sb.alloc([2, 6 * D], F32)
    brow = sb.alloc([128, 2048], F32)
    grow = sb.alloc([128, 2048], F32)
    ggs = sb.alloc([128, 2048], F32)
    identf = sb.alloc([128, 128], F32)
    sel0 = sb.alloc([2, 128], F32)
    ld("sp", identf, ident_d, "identf")
    P.op("pool", lambda e: e.memset(sel0, 0.0), writes=["sel0"])
    P.op("pool", lambda e: e.memset(sel0[0:1, :], 1.0), writes=["sel0"])
    sv3 = sv.rearrange("p (two k) -> p k two", two=2)
    nload = 0
    for cb in range(12):
        base = 4 * (cb % 2)
        for kc in range(32):
            w = wt[nload % NWT]
            wk = "wt%d" % (nload % NWT)
            nload += 1
            ld("sp", w, (w_ada_a if cb < 6 else w_ada_b)[kc * 128:(kc + 1) * 128, (cb % 6) * 2048:(cb % 6 + 1) * 2048], wk)
            for q in range(4):
                P.op("pe", lambda e, w=w, q=q, kc=kc, base=base: e.matmul(ps[base + q][0:2, :], lhsT=sv3[:, kc, :], rhs=w[:, q * 512:(q + 1) * 512],
                                                                         start=(kc == 0), stop=(kc == 31)),
                     reads=[wk, "sv"], writes=["ps%d" % (base + q)])
        for q in range(4):
            P.op("dve", lambda e, q=q, cb=cb, base=base: e.tensor_copy(out=rowbuf[:, cb * 2048 + q * 512:cb * 2048 + (q + 1) * 512], in_=ps[base + q][0:2, :]),
                 reads=["ps%d" % (base + q)], writes=["rowbuf%d" % cb])
    allrow = ["rowbuf%d" % cb for cb in range(12)]
    for j in range(192):
        P.op("pe", lambda e, j=j: e.matmul(ps[0][:, 2 * j:2 * j + 2], lhsT=rowbuf[:, j * 128:(j + 1) * 128], rhs=identf[0:2, 0:2], start=True, stop=True),
             reads=allrow + ["identf"], writes=["ps0"])
    pc3 = ps[0][:, 0:384].rearrange("p (j two) -> p j two", two=2)
    P.op("dve", lambda e: e.tensor_tensor(out=modx, in0=pc3[:, 0:64, 1], in1=modc[:, 0:64], op=ALU.add), reads=["ps0", "modc"], writes=["modx"])
    P.op("dve", lambda e: e.tensor_tensor(out=modc, in0=pc3[:, :, 0], in1=modc, op=ALU.add), reads=["ps0", "modc", "modx"], writes=["modc"])
    nq = 0
    for which, cb0 in ((0, 4), (1, 10)):
        for half in range(2):
            cb = cb0 + half
            ld("sp", brow, bc(b_ada_row[:, cb * 2048:(cb + 1) * 2048]), "brow")
            ld("sp", grow, bc((gpost_mix_row if which == 0 else gpost_ffn_row)[:, half * 2048:(half + 1) * 2048]), "grow")
            for q in range(4):
                bank = 1 + (nq % 4)
                nq += 1
                P.op("pe", lambda e, q=q, cb=cb, bank=bank: e.matmul(ps[bank], lhsT=sel0, rhs=rowbuf[:, cb * 2048 + q * 512:cb * 2048 + (q + 1) * 512], start=True, stop=True),
                     reads=allrow + ["sel0"], writes=["ps%d" % bank])
                P.op("dve", lambda e, q=q, bank=bank: e.tensor_tensor(out=ggs[:, q * 512:(q + 1) * 512], in0=ps[bank], in1=brow[:, q * 512:(q + 1) * 512], op=ALU.add),
                     reads=["ps%d" % bank, "brow"], writes=["ggs"])
            P.op("dve", lambda e: e.tensor_tensor(out=ggs, in0=ggs, in1=grow, op=ALU.mult), reads=["ggs", "grow"], writes=["ggs"])
            P.op("sp", lambda e, which=which, half=half: e.dma_start(out=gg_scr[which, :, half * 2048:(half + 1) * 2048], in_=ggs),
                 reads=["ggs"], writes=["gg_scr"], dma="S:gg")
    for (G, scl, gcol, key) in ((Ga, modc[:, 32:64], gcols[:, 0:32], "Ga"), (Gc, modx[:, 32:64], gcols[:, 0:32], "Gc"),
                                (Gf, modc[:, 128:160], gcols[:, 32:64], "Gf")):
        P.op("dve", lambda e, G=G, scl=scl, gcol=gcol: e.scalar_tensor_tensor(out=G, in0=scl, scalar=1.0, in1=gcol, op0=ALU.add, op1=ALU.mult),
             reads=["modc", "modx", "gcols"], writes=[key])
    P.barrier()
    sb.release(mA)

    if stop == "A":
        dbgA = nc.dram_tensor("dbgA", [128, 512], F32, kind="ExternalOutput").ap()
        P.op("sp", lambda e: e.dma_start(out=dbgA[:, 0:192], in_=modc), reads=["modc"], dma="dbg")
        P.op("sp", lambda e: e.dma_start(out=dbgA[:, 192:256], in_=modx), reads=["modx"], dma="dbg")
        P.op("sp", lambda e: e.dma_start(out=dbgA[:, 256:288], in_=Ga), reads=["Ga"], dma="dbg")
        P.op("sp", lambda e: e.dma_start(out=dbgA[:, 288:320], in_=Gc), reads=["Gc"], dma="dbg")
        P.op("sp", lambda e: e.dma_start(out=dbgA[:, 320:352], in_=Gf), reads=["Gf"], dma="dbg")
        P.emit(final_waits=None)
        return nc
    def rstd_from_ss(ss_ap, out_ap, n, reads, wkey):
        P.op("act", lambda e: e.activation(out=out_ap, in_=ss_ap, func=AF.Sqrt, scale=1.0 / n, bias=epsc), reads=list(reads) + ["cst"], writes=[wkey])
        P.op("dve", lambda e: e.reciprocal(out=out_ap, in_=out_ap), reads=[wkey], writes=[wkey])

    def rope_tables(t, cos2, sins, ang, angk, angi, key, skey=None):
        skey = key if skey is None else skey
        P.op("dve", lambda e: e.tensor_scalar(out=ang[:, 0:32], in0=freq, scalar1=pos[:, 2 * t:2 * t + 1], scalar2=None, op0=ALU.mult),
             reads=["freq", "pos"], writes=[skey + "ang"])
        P.op("dve", lambda e: e.tensor_scalar(out=ang[:, 32:64], in0=freq, scalar1=pos[:, 2 * t + 1:2 * t + 2], scalar2=None, op0=ALU.mult),
             reads=["freq", "pos"], writes=[skey + "ang"])
        P.op("dve", lambda e: e.tensor_scalar(out=ang[:, 64:128], in0=ang[:, 0:64], scalar1=0.5 * math.pi, scalar2=None, op0=ALU.add),
             reads=[skey + "ang"], writes=[skey + "ang2"])
        P.op("dve", lambda e: e.tensor_scalar(out=angk, in0=ang, scalar1=1.0 / (2 * math.pi), scalar2=None, op0=ALU.mult),
             reads=[skey + "ang", skey + "ang2"], writes=[skey + "angk"])
        P.op("dve", lambda e: e.tensor_copy(out=angi, in_=angk), reads=[skey + "angk"], writes=[skey + "angi"])
        P.op("dve", lambda e: e.tensor_copy(out=angk, in_=angi), reads=[skey + "angi"], writes=[skey + "angk"])
        P.op("dve", lambda e: e.scalar_tensor_tensor(out=ang, in0=angk, scalar=-2.0 * math.pi, in1=ang, op0=ALU.mult, op1=ALU.add),
             reads=[skey + "angk", skey + "ang", skey + "ang2"], writes=[skey + "ang", skey + "ang2"])
        P.op("dve", lambda e: e.tensor_scalar(out=ang, in0=ang, scalar1=3.1415925, scalar2=-3.1415925, op0=ALU.min, op1=ALU.max),
             reads=[skey + "ang", skey + "ang2"], writes=[skey + "ang", skey + "ang2"])
        c4 = cos2.rearrange("p (a b j) -> p a b j", a=2, b=2)
        s4 = sins.rearrange("p (a b j) -> p a b j", a=2, b=2)
        sarg = ang[:, 0:64].rearrange("p (a j) -> p a j", a=2)
        carg = ang[:, 64:128].rearrange("p (a j) -> p a j", a=2)
        for b in range(2):
            P.op("act", lambda e, b=b: e.activation(out=c4[:, :, b, :], in_=carg, func=AF.Sin), reads=[skey + "ang2"], writes=[key + "cos"])
        P.op("act", lambda e: e.activation(out=s4[:, :, 1, :], in_=sarg, func=AF.Sin), reads=[skey + "ang"], writes=[key + "sin"])
        P.op("act", lambda e: e.activation(out=s4[:, :, 0, :], in_=sarg, func=AF.Sin, scale=-1.0), reads=[skey + "ang"], writes=[key + "sin"])

    def qk_norm_rope(psrc, pkey, nh, g_t, cos2, sins, tkeys, tmp, outb, okey, pfx):
        sq, qn, t1, t2, ssq = tmp["sq"], tmp["qn"], tmp["t1"], tmp["t2"], tmp["ss"]
        P.op("act", lambda e: e.activation(out=sq, in_=psrc, func=AF.Square), reads=[pkey], writes=[pfx + "sq"])
        P.op("dve", lambda e: e.reduce_sum(out=ssq[:, 0:nh], in_=sq.rearrange("p (h d) -> p h d", h=nh), axis=AX.X), reads=[pfx + "sq"], writes=[pfx + "ss"])
        rstd_from_ss(ssq[:, 0:nh], ssq[:, 0:nh], 128, [pfx + "ss"], pfx + "ss")
        for h in range(nh):
            P.op("act", lambda e, h=h: e.activation(out=qn[:, h * 128:(h + 1) * 128], in_=psrc[:, h * 128:(h + 1) * 128], func=AF.Copy, scale=ssq[:, h:h + 1]),
                 reads=[pkey, pfx + "ss"], writes=[pfx + "qn"])
        q3 = qn.rearrange("p (h d) -> p h d", h=nh)
        P.op("dve", lambda e: e.tensor_tensor(out=q3, in0=q3, in1=g_t.unsqueeze(1).broadcast_to([128, nh, 128]), op=ALU.mult),
             reads=[pfx + "qn", "gq_t", "gk_t"], writes=[pfx + "qn"])
        P.op("dve", lambda e: e.tensor_tensor(out=t1.rearrange("p (h d) -> p h d", h=nh), in0=q3, in1=cos2.unsqueeze(1).broadcast_to([128, nh, 128]), op=ALU.mult),
             reads=[pfx + "qn"] + tkeys, writes=[pfx + "t1"])
        q5 = qn.rearrange("p (h a b j) -> p h a b j", h=nh, a=2, b=2)
        t5 = t2.rearrange("p (h a b j) -> p h a b j", h=nh, a=2, b=2)
        s4 = sins.rearrange("p (a b j) -> p a b j", a=2, b=2)
        for b in range(2):
            P.op("dve", lambda e, b=b: e.tensor_tensor(out=t5[:, :, :, b, :], in0=q5[:, :, :, 1 - b, :],
                                                      in1=s4[:, :, b, :].unsqueeze(1).broadcast_to([128, nh, 2, 32]), op=ALU.mult),
                 reads=[pfx + "qn"] + tkeys, writes=[pfx + "t2"])
        P.op("dve", lambda e: e.tensor_tensor(out=outb, in0=t1, in1=t2, op=ALU.add), reads=[pfx + "t1", pfx + "t2"], writes=[okey])

    def norm_transpose(xt, xkey, G, SHv, gkeys, junk, xh, xhkey, hT_dst, hkey, pfx, banks=(0, 1), only_col=None, mask_col=None, coltmp=None):
        P.op("act", lambda e: e.activation(out=junk, in_=xt, func=AF.Square, accum_out=st[:, 0:1]), reads=[xkey], writes=[pfx + "st"])
        rstd_from_ss(st[:, 0:1], st[:, 1:2], D, [pfx + "st"], pfx + "st1")
        P.op("act", lambda e: e.activation(out=xh, in_=xt, func=AF.Copy, scale=st[:, 1:2]), reads=[xkey, pfx + "st1"], writes=[xhkey])
        for g4 in range(4):
            bank = psb[banks[g4 % 2]]
            bk = "ps%d" % banks[g4 % 2]
            for j in range(8):
                kc = g4 * 8 + j
                P.op("pe", lambda e, bank=bank, j=j, kc=kc: e.transpose(out=bank[:, j * 128:(j + 1) * 128], in_=xh[:, kc * 128:(kc + 1) * 128], identity=ident),
                     reads=[xhkey, "ident"], writes=[bk])
            if only_col is None:
                for j in range(8):
                    kc = g4 * 8 + j
                    if g4 % 2 == 0:
                        P.op("act", lambda e, bank=bank, j=j, kc=kc: e.activation(out=hT_dst[:, kc, :], in_=bank[:, j * 128:(j + 1) * 128], func=AF.Identity,
                                                                                 scale=G[:, kc:kc + 1], bias=SHv[:, kc:kc + 1]),
                             reads=[bk] + gkeys, writes=[hkey + "_%d" % kc])
                    else:
                        P.op("dve", lambda e, bank=bank, j=j, kc=kc: e.tensor_scalar(out=hT_dst[:, kc, :], in0=bank[:, j * 128:(j + 1) * 128],
                                                                                    scalar1=G[:, kc:kc + 1], scalar2=SHv[:, kc:kc + 1], op0=ALU.mult, op1=ALU.add),
                             reads=[bk] + gkeys, writes=[hkey + "_%d" % kc])
            else:
                src = bank[:, 0:1024].rearrange("p (j t) -> p j t", j=8)[:, :, only_col]
                sl = slice(g4 * 8, g4 * 8 + 8)
                P.op("dve", lambda e, src=src, sl=sl: e.tensor_tensor(out=coltmp[:, sl], in0=src, in1=G[:, sl], op=ALU.mult), reads=[bk] + gkeys, writes=[pfx + "ct"])
                P.op("dve", lambda e, sl=sl: e.tensor_tensor(out=coltmp[:, sl], in0=coltmp[:, sl], in1=SHv[:, sl], op=ALU.add), reads=[pfx + "ct"] + gkeys, writes=[pfx + "ct"])
                P.op("dve", lambda e, sl=sl: e.tensor_scalar(out=hT_dst[:, sl], in0=coltmp[:, sl], scalar1=mask_col, scalar2=None, op0=ALU.mult),
                     reads=[pfx + "ct", "hmask"], writes=[hkey])

    mTab = sb.mark()
    COSo = sb.alloc([128, NOWN * 64], F32)
    SINo = sb.alloc([128, NOWN * 64], F32)
    mBig = sb.mark()
    COS = sb.alloc([128, NT * 64], F32)
    SIN = sb.alloc([128, NT * 64], F32)
    mR = sb.mark()
    angA = sb.alloc([128, NT * 64], F32)
    angK = sb.alloc([128, NT * 64], F32)
    angI = sb.alloc([128, NT * 64], mybir.dt.int32)
    P.op("dve", lambda e: e.tensor_tensor(out=angA.rearrange("p (m j) -> p m j", j=32), in0=pos.unsqueeze(2).broadcast_to([128, NT * 2, 32]),
                                          in1=freq.unsqueeze(1).broadcast_to([128, NT * 2, 32]), op=ALU.mult),
         reads=["pos", "freq"], writes=["angA"])
    for (dst, dkey, shift) in ((SIN, "SIN", 0.0), (COS, "COS", 0.5 * math.pi)):
        if shift:
            P.op("dve", lambda e, shift=shift: e.tensor_scalar(out=angA, in0=angA, scalar1=shift, scalar2=None, op0=ALU.add), reads=["angA"], writes=["angA"])
        P.op("dve", lambda e: e.tensor_scalar(out=angK, in0=angA, scalar1=1.0 / (2 * math.pi), scalar2=None, op0=ALU.mult), reads=["angA"], writes=["angK"])
        P.op("dve", lambda e: e.tensor_copy(out=angI, in_=angK), reads=["angK"], writes=["angI"])
        P.op("dve", lambda e: e.tensor_copy(out=angK, in_=angI), reads=["angI"], writes=["angK"])
        P.op("dve", lambda e: e.scalar_tensor_tensor(out=angK, in0=angK, scalar=-2.0 * math.pi, in1=angA, op0=ALU.mult, op1=ALU.add), reads=["angK", "angA"], writes=["angK"])
        P.op("dve", lambda e: e.tensor_scalar(out=angK, in0=angK, scalar1=3.1415925, scalar2=-3.1415925, op0=ALU.min, op1=ALU.max), reads=["angK"], writes=["angK"])
        P.op("act", lambda e, dst=dst: e.activation(out=dst, in_=angK, func=AF.Sin), reads=["angK"], writes=[dkey])
    P.op("dve", lambda e: e.tensor_copy(out=COSo, in_=COS[:, 0:NOWN * 64]), reads=["COS"], writes=["COSo"])
    P.op("dve", lambda e: e.tensor_copy(out=SINo, in_=SIN[:, 0:NOWN * 64]), reads=["SIN"], writes=["SINo"])
    P.barrier()
    sb.release(mR)

    def rope_apply(kn, nh, t, outb, t1, t2, rkeys, wkey, tkey, tabs=None):
        Ct, St, ck, sk = (COS, SIN, "COS", "SIN") if tabs is None else tabs
        cs = Ct[:, t * 64:(t + 1) * 64].rearrange("p (a j) -> p a j", a=2).unsqueeze(1).broadcast_to([128, nh, 2, 32])
        sn = St[:, t * 64:(t + 1) * 64].rearrange("p (a j) -> p a j", a=2).unsqueeze(1).broadcast_to([128, nh, 2, 32])
        q5 = kn.rearrange("p (h a b j) -> p h a b j", h=nh, a=2, b=2)
        a5 = t1.rearrange("p (h a b j) -> p h a b j", h=nh, a=2, b=2)
        b5 = t2.rearrange("p (h a b j) -> p h a b j", h=nh, a=2, b=2)
        o5 = outb.rearrange("p (h a b j) -> p h a b j", h=nh, a=2, b=2)
        for b_ in range(2):
            P.op("dve", lambda e, b_=b_: e.tensor_tensor(out=a5[:, :, :, b_, :], in0=q5[:, :, :, b_, :], in1=cs, op=ALU.mult), reads=rkeys + [ck], writes=[tkey + "t1"])
            P.op("dve", lambda e, b_=b_: e.tensor_tensor(out=b5[:, :, :, b_, :], in0=q5[:, :, :, 1 - b_, :], in1=sn, op=ALU.mult), reads=rkeys + [sk], writes=[tkey + "t2"])
        P.op("dve", lambda e: e.tensor_tensor(out=o5[:, :, :, 0, :], in0=a5[:, :, :, 0, :], in1=b5[:, :, :, 0, :], op=ALU.subtract), reads=[tkey + "t1", tkey + "t2"], writes=[wkey])
        P.op("dve", lambda e: e.tensor_tensor(out=o5[:, :, :, 1, :], in0=a5[:, :, :, 1, :], in1=b5[:, :, :, 1, :], op=ALU.add), reads=[tkey + "t1", tkey + "t2"], writes=[wkey])

    mB = sb.mark()
    xt = [sb.alloc([128, D], F32) for _ in range(2)]
    xh = [sb.alloc([128, D], BF16) for _ in range(2)]
    hT = [sb.alloc([128, 32, 128], BF16) for _ in range(2)]
    wkv = sb.alloc([128, 32, 1024], BF16)
    junk = sb.alloc([128, D], BF16)
    kraw = [sb.alloc([128, 512], F32) for _ in range(2)]
    kn = sb.alloc([128, 512], F32)
    kt1 = sb.alloc([128, 512], F32)
    kt2 = sb.alloc([128, 512], F32)
    kr = [sb.alloc([128, 512], BF16) for _ in range(2)]
    kTs = [sb.alloc([128, 4, 128], BF16) for _ in range(2)]
    vaug = [sb.alloc([128, 4, VW], BF16) for _ in range(2)]
    ssx = sb.alloc([128, 4], F32)
    ssk = sb.alloc([128, 8], F32)
    for blk in range(4):
        P.op("pool", lambda e, blk=blk: e.dma_start(out=wkv[:, :, blk * 256:(blk + 1) * 256], in_=w_in_h[24 + blk].rearrange("p (k c) -> p k c", k=32)),
             writes=["wkv"], dma="L:wkv")
    for s_ in range(2):
        P.op("pool", lambda e, s_=s_: e.memset(vaug[s_][:, :, 128:VW], 1.0), writes=["vaug%d" % s_])

    def b_s0(t):
        s_ = t % 2
        src = xr[t * 128:(t + 1) * 128, :] if t < 64 else ctx[(t - 64) * 128:(t - 63) * 128, :]
        ld("sp", xt[s_], src, "xt%d" % s_)

    def b_s1(t):
        s_, c4 = t % 2, t % 4
        P.op("act", lambda e: e.activation(out=junk, in_=xt[s_], func=AF.Square, accum_out=ssx[:, c4:c4 + 1]), reads=["xt%d" % s_], writes=["ssx%d" % c4])
        rstd_from_ss(ssx[:, c4:c4 + 1], ssx[:, c4:c4 + 1], D, ["ssx%d" % c4], "ssx%d" % c4)

    def b_s2(t):
        s_, c4 = t % 2, t % 4
        P.op("act", lambda e: e.activation(out=xh[s_], in_=xt[s_], func=AF.Copy, scale=ssx[:, c4:c4 + 1]), reads=["xt%d" % s_, "ssx%d" % c4], writes=["xh%d" % s_])

    def b_s3(t):
        s_ = t % 2
        G, SHv, gk = (Ga, SHa, ["Ga", "modc"]) if t < 64 else (Gc, SHc, ["Gc", "modx"])
        TB = (0, 1, 4, 5)
        for g4 in range(4):
            bank = psb[TB[g4]]
            bk = "ps%d" % TB[g4]
            for j in range(8):
                kc = g4 * 8 + j
                P.op("pe", lambda e, bank=bank, j=j, kc=kc: e.transpose(out=bank[:, j * 128:(j + 1) * 128], in_=xh[s_][:, kc * 128:(kc + 1) * 128], identity=ident),
                     reads=["xh%d" % s_, "ident"], writes=[bk])
        for g4 in range(4):
            bank = psb[TB[g4]]
            bk = "ps%d" % TB[g4]
            for j in range(8):
                kc = g4 * 8 + j
                if g4 % 2 == 0:
                    P.op("act", lambda e, bank=bank, j=j, kc=kc: e.activation(out=hT[s_][:, kc, :], in_=bank[:, j * 128:(j + 1) * 128], func=AF.Identity,
                                                                             scale=G[:, kc:kc + 1], bias=SHv[:, kc:kc + 1]),
                         reads=[bk] + gk, writes=["hT%d_%d" % (s_, kc)])
                else:
                    P.op("dve", lambda e, bank=bank, j=j, kc=kc: e.tensor_scalar(out=hT[s_][:, kc, :], in0=bank[:, j * 128:(j + 1) * 128],
                                                                                scalar1=G[:, kc:kc + 1], scalar2=SHv[:, kc:kc + 1], op0=ALU.mult, op1=ALU.add),
                         reads=[bk] + gk, writes=["hT%d_%d" % (s_, kc)])
        if t < NOWN:
            P.op("sp", lambda e: e.dma_start(out=hT_scr[t], in_=hT[s_].rearrange("p k c -> p (k c)")),
                 reads=["hT%d_%d" % (s_, kc_) for kc_ in range(32)], writes=["hT_scr%d" % t], dma="S:hT%d" % s_)

    def b_s4(t):
        s_ = t % 2
        for (pp, half) in ((2, 0), (3, 1)):
            for kc in range(32):
                P.op("pe", lambda e, pp=pp, kc=kc, half=half: e.matmul(ps[pp], lhsT=hT[s_][:, kc, :], rhs=wkv[:, kc, half * 512:(half + 1) * 512],
                                                                      start=(kc == 0), stop=(kc == 31)),
                     reads=["hT%d_%d" % (s_, kc), "wkv"], writes=["ps%d" % pp])

    def b_s5(t):
        s_ = t % 2
        pk, pv = ps[2], ps[3]
        pkk, pvk = "ps2", "ps3"
        P.op("act", lambda e: e.activation(out=kraw[s_], in_=pk, func=AF.Copy), reads=[pkk], writes=["kraw%d" % s_])
        P.op("act", lambda e: e.activation(out=vaug[s_][:, :, 0:128], in_=pv.rearrange("p (h d) -> p h d", h=4), func=AF.Copy), reads=[pvk], writes=["vaug%d" % s_])
        P.op("sp", lambda e: e.dma_start(out=V_scr.rearrange("h p (t c) -> p h t c", t=NT)[:, :, t, :], in_=vaug[s_]),
             reads=["vaug%d" % s_], writes=["V_scr"], dma="S:v%d" % s_)
        for h in range(4):
            P.op("act", lambda e, h=h: e.activation(out=junk[:, h * 128:(h + 1) * 128], in_=kraw[s_][:, h * 128:(h + 1) * 128], func=AF.Square,
                                                   accum_out=ssk[:, s_ * 4 + h:s_ * 4 + h + 1]),
                 reads=["kraw%d" % s_], writes=["rk%d" % s_])
        P.op("act", lambda e: e.activation(out=ssk[:, s_ * 4:s_ * 4 + 4], in_=ssk[:, s_ * 4:s_ * 4 + 4], func=AF.Sqrt, scale=1.0 / 128, bias=epsc),
             reads=["rk%d" % s_, "cst"], writes=["rk%d" % s_])

    def b_s6(t):
        s_ = t % 2
        rk = ssk[:, s_ * 4:s_ * 4 + 4]
        P.op("dve", lambda e: e.reciprocal(out=rk, in_=rk), reads=["rk%d" % s_], writes=["rk%d" % s_])
        k3 = kn.rearrange("p (h d) -> p h d", h=4)
        P.op("dve", lambda e: e.tensor_tensor(out=k3, in0=kraw[s_].rearrange("p (h d) -> p h d", h=4), in1=rk.unsqueeze(2).broadcast_to([128, 4, 128]), op=ALU.mult),
             reads=["kraw%d" % s_, "rk%d" % s_], writes=["kn"])
        P.op("dve", lambda e: e.tensor_tensor(out=k3, in0=k3, in1=gk_t.unsqueeze(1).broadcast_to([128, 4, 128]), op=ALU.mult), reads=["kn", "gk_t"], writes=["kn"])
        rope_apply(kn, 4, t, kr[s_], kt1, kt2, ["kn"], "kr%d" % s_, "B")

    def b_s7(t):
        s_ = t % 2
        for h in range(4):
            P.op("pe", lambda e, h=h: e.transpose(out=psb[6][:, h * 128:(h + 1) * 128], in_=kr[s_][:, h * 128:(h + 1) * 128], identity=ident),
                 reads=["kr%d" % s_, "ident"], writes=["ps6"])
        P.op("dve", lambda e: e.tensor_copy(out=kTs[s_], in_=psb[6][:, 0:512].rearrange("p (h t) -> p h t", h=4)), reads=["ps6"], writes=["kTs%d" % s_])
        P.op("sp", lambda e: e.dma_start(out=KT_scr.rearrange("h d n -> d h n")[:, :, t * 128:(t + 1) * 128], in_=kTs[s_]),
             reads=["kTs%d" % s_], writes=["KT_scr"], dma="S:k%d" % s_)

    pipeline(NT, [b_s0, b_s1, b_s2, b_s3, b_s4, b_s5, b_s6, b_s7], [0, 1, 2, 3, 4, 5, 6, 7], order=[5, 3, 7, 6, 4, 2, 1, 0])
    P.barrier()
    sb.release(mBig)

    if stop == "B":
        P.emit(final_waits=None)
        return nc
    qT = sb.alloc([128, 16, NOWN * 128], BF16)
    mC = sb.mark()
    hTo = sb.alloc([128, NG, 32 * 128], BF16)
    wblk = [sb.alloc([128, 32, 256], BF16) for _ in range(2)]
    gv = sb.alloc([128, NG, 2048], BF16)
    lng = sb.alloc([128, 2048], F32)
    lnb = sb.alloc([128, 2048], F32)
    wsT = sb.alloc([128, 16, 128], BF16)
    gtmp = [sb.alloc([128, 256], F32) for _ in range(2)]
    oa = [sb.alloc([128, 256], F32) for _ in range(2)]
    oab = [sb.alloc([128, 256], BF16) for _ in range(2)]
    oTs = [sb.alloc([128, 2, 128], BF16) for _ in range(2)]
    qn = sb.alloc([128, 256], F32)
    qt1 = sb.alloc([128, 256], F32)
    qt2 = sb.alloc([128, 256], F32)
    ssq = [sb.alloc([128, 2], F32) for _ in range(2)]
    qr = [sb.alloc([128, 256], BF16) for _ in range(2)]
    vsum = sb.alloc([128, NOWN * 8], F32)
    vsq = sb.alloc([128, NOWN * 8], F32)
    vst = sb.alloc([128, NOWN * 4], F32)
    junkC = sb.alloc([128, 256], F32)
    ld("sp", lng, bc(lng_row), "lng")
    ld("sp", lnb, bc(lnb_row), "lnb")
    P.op("pool", lambda e: e.dma_start(out=wsT.rearrange("p g c -> p (g c)"), in_=wsT_d), writes=["wsT"], dma="L:wsT")

    nblk = [0]
    wslot = {}

    def load_wblk(blk, tag):
        if (blk, tag) in wslot:
            return
        s_ = nblk[0] % 2
        nblk[0] += 1
        wslot[(blk, tag)] = s_
        P.op("pool", lambda e: e.dma_start(out=wblk[s_].rearrange("p k c -> p (k c)"), in_=w_in_h[blk]), writes=["wblk%d" % s_], dma="L:wblk%d" % s_)

    def run_family(grp, blk0, stages_fn, delays):
        TL = list(range(grp * NG, grp * NG + NG))
        units = [(j, t) for j in range(8) for t in TL]

        def s0(u):
            j, t = units[u]
            load_wblk(blk0 + j, grp)
            if t == TL[1] and j + 1 < 8:
                load_wblk(blk0 + j + 1, grp)
            s_ = wslot[(blk0 + j, grp)]
            bank = u % 3
            for kc in range(32):
                P.op("pe", lambda e, kc=kc: e.matmul(ps[bank][:, 0:256], lhsT=hTo[:, t % NG, kc * 128:(kc + 1) * 128], rhs=wblk[s_][:, kc, :],
                                                     start=(kc == 0), stop=(kc == 31)),
                     reads=["hTo%d" % (t % NG), "wblk%d" % s_], writes=["ps%d" % bank])
        stages = [s0] + stages_fn(units)
        pipeline(len(units), stages, delays)

    def v_stages(units):
        def s1(u):
            j, t = units[u]
            bank, g_ = u % 3, u % 2
            P.op("act", lambda e: e.activation(out=gtmp[g_], in_=ps[bank][:, 0:256], func=AF.Gelu, accum_out=vsum[:, t * 8 + j:t * 8 + j + 1]),
                 reads=["ps%d" % bank], writes=["gtmp%d" % g_, "vsum%d" % (t * 8 + j)])
            P.op("act", lambda e: e.activation(out=junkC, in_=gtmp[g_], func=AF.Square, accum_out=vsq[:, t * 8 + j:t * 8 + j + 1]),
                 reads=["gtmp%d" % g_], writes=["vsq%d" % (t * 8 + j)])

        def s2(u):
            j, t = units[u]
            g_ = u % 2
            P.op("dve", lambda e: e.tensor_copy(out=gv[:, t % NG, j * 256:(j + 1) * 256], in_=gtmp[g_]), reads=["gtmp%d" % g_], writes=["gv%d" % (t % NG)])
        return [s1, s2]

    def u_stages(units):
        def s0b(u):
            j, t = units[u]
            mb = 3 + u % 3
            for gi in range(2):
                g = 2 * j + gi
                P.op("pe", lambda e, gi=gi, g=g: e.matmul(ps[mb][:, gi * 128:(gi + 1) * 128], lhsT=wsT[:, g, :], rhs=gv[:, t % NG, g * 128:(g + 1) * 128], start=True, stop=True),
                     reads=["wsT", "gv%d" % (t % NG)], writes=["ps%d" % mb])

        def s1(u):
            bank, g_ = u % 3, u % 2
            P.op("act", lambda e: e.activation(out=gtmp[g_], in_=ps[bank][:, 0:256], func=AF.Gelu), reads=["ps%d" % bank], writes=["gtmp%d" % g_])

        def s2(u):
            j, t = units[u]
            mb, g_ = 3 + u % 3, u % 2
            for gi in range(2):
                g = 2 * j + gi
                P.op("dve", lambda e, gi=gi, g=g: e.scalar_tensor_tensor(out=oa[g_][:, gi * 128:(gi + 1) * 128], in0=ps[mb][:, gi * 128:(gi + 1) * 128],
                                                                        scalar=bs_col[:, g:g + 1], in1=gtmp[g_][:, gi * 128:(gi + 1) * 128], op0=ALU.add, op1=ALU.mult),
                     reads=["ps%d" % mb, "bs_col", "gtmp%d" % g_], writes=["oa%d" % g_])
            P.op("dve", lambda e: e.tensor_copy(out=oab[g_], in_=oa[g_]), reads=["oa%d" % g_], writes=["oab%d" % g_])

        def s3(u):
            j, t = units[u]
            g_ = u % 2
            tb = 6 + u % 2
            for gi in range(2):
                P.op("pe", lambda e, gi=gi: e.transpose(out=psb[tb][:, gi * 128:(gi + 1) * 128], in_=oab[g_][:, gi * 128:(gi + 1) * 128], identity=ident),
                     reads=["oab%d" % g_, "ident"], writes=["ps%d" % tb])
            col = t * 8 + j
            P.op("act", lambda e: e.activation(out=junkC, in_=oa[g_], func=AF.Square, accum_out=ssA[:, col:col + 1]), reads=["oa%d" % g_], writes=["ssA%d" % col])

        def s4(u):
            j, t = units[u]
            g_ = u % 2
            tb = 6 + u % 2
            for gi in range(2):
                kc = 2 * j + gi
                P.op("act", lambda e, gi=gi, kc=kc: e.activation(out=oTs[g_][:, gi, :], in_=psb[tb][:, gi * 128:(gi + 1) * 128], func=AF.Copy, scale=goutc[:, kc:kc + 1]),
                     reads=["ps%d" % tb, "goutc"], writes=["oTs%d" % g_])
            P.op("sp", lambda e: e.dma_start(out=oT_scr[2 * j:2 * j + 2].rearrange("k p n -> p k n")[:, :, t * 128:(t + 1) * 128], in_=oTs[g_]),
                 reads=["oTs%d" % g_], writes=["oT_scr"], dma="S:oT%d" % g_)
        return [s0b, s1, s2, s3, s4]

    def q_stages(units):
        def s1(u):
            bank, g_ = u % 3, u % 2
            for h in range(2):
                P.op("act", lambda e, h=h: e.activation(out=junkC[:, h * 128:(h + 1) * 128], in_=ps[bank][:, h * 128:(h + 1) * 128], func=AF.Square, accum_out=ssq[g_][:, h:h + 1]),
                     reads=["ps%d" % bank], writes=["ssq%d" % g_])
            P.op("act", lambda e: e.activation(out=ssq[g_], in_=ssq[g_], func=AF.Sqrt, scale=1.0 / 128, bias=epsc), reads=["ssq%d" % g_, "cst"], writes=["ssq%d" % g_])

        def s2(u):
            j, t = units[u]
            bank, g_ = u % 3, u % 2
            P.op("dve", lambda e: e.reciprocal(out=ssq[g_], in_=ssq[g_]), reads=["ssq%d" % g_], writes=["ssq%d" % g_])
            q3 = qn.rearrange("p (h d) -> p h d", h=2)
            P.op("dve", lambda e: e.tensor_tensor(out=q3, in0=ps[bank][:, 0:256].rearrange("p (h d) -> p h d", h=2), in1=ssq[g_].unsqueeze(2).broadcast_to([128, 2, 128]), op=ALU.mult),
                 reads=["ps%d" % bank, "ssq%d" % g_], writes=["Cqn"])
            P.op("dve", lambda e: e.tensor_tensor(out=q3, in0=q3, in1=gq_t.unsqueeze(1).broadcast_to([128, 2, 128]), op=ALU.mult), reads=["Cqn", "gq_t"], writes=["Cqn"])
            rope_apply(qn, 2, t, qr[g_], qt1, qt2, ["Cqn"], "qr%d" % g_, "Cq", tabs=(COSo, SINo, "COSo", "SINo"))

        def s3(u):
            g_ = u % 2
            tb = 6 + u % 2
            for gi in range(2):
                P.op("pe", lambda e, gi=gi: e.transpose(out=psb[tb][:, gi * 128:(gi + 1) * 128], in_=qr[g_][:, gi * 128:(gi + 1) * 128], identity=ident),
                     reads=["qr%d" % g_, "ident"], writes=["ps%d" % tb])

        def s4(u):
            j, t = units[u]
            tb = 6 + u % 2
            P.op("act", lambda e: e.activation(out=qT[:, 2 * j:2 * j + 2, t * 128:(t + 1) * 128], in_=psb[tb][:, 0:256].rearrange("p (h n) -> p h n", h=2), func=AF.Copy),
                 reads=["ps%d" % tb], writes=["qT"])
        return [s1, s2, s3, s4]

    mean = vst[:, 2 * NOWN:3 * NOWN]
    var = vst[:, 3 * NOWN:4 * NOWN]
    for grp in range(NOWN // NG):
        TL = list(range(grp * NG, grp * NG + NG))
        for t in TL:
            ld("sp", hTo[:, t % NG, :], hT_scr[t], "hTo%d" % (t % NG), reads=["hT_scr%d" % t])
        run_family(grp, 8, v_stages, [0, 1, 2])
        allv = ["vsum%d" % c_ for c_ in range(NOWN * 8)] + ["vsq%d" % c_ for c_ in range(NOWN * 8)]
        P.op("dve", lambda e: e.reduce_sum(out=vst[:, 0:NOWN], in_=vsum.rearrange("p (t j) -> p t j", j=8), axis=AX.X), reads=allv, writes=["vst"])
        P.op("dve", lambda e: e.reduce_sum(out=vst[:, NOWN:2 * NOWN], in_=vsq.rearrange("p (t j) -> p t j", j=8), axis=AX.X), reads=allv, writes=["vst"])
        P.op("dve", lambda e: e.tensor_scalar(out=mean, in0=vst[:, 0:NOWN], scalar1=1.0 / 2048, scalar2=None, op0=ALU.mult), reads=["vst"], writes=["vmean"])
        P.op("dve", lambda e: e.tensor_tensor(out=var, in0=mean, in1=mean, op=ALU.mult), reads=["vmean"], writes=["vvar"])
        P.op("dve", lambda e: e.scalar_tensor_tensor(out=var, in0=vst[:, NOWN:2 * NOWN], scalar=1.0 / 2048, in1=var, op0=ALU.mult, op1=ALU.subtract),
             reads=["vst", "vvar"], writes=["vvar"])
        P.op("act", lambda e: e.activation(out=var, in_=var, func=AF.Sqrt, bias=epsc), reads=["vvar", "cst"], writes=["vvar"])
        P.op("dve", lambda e: e.reciprocal(out=var, in_=var), reads=["vvar"], writes=["vvar"])
        for t in TL:
            P.op("dve", lambda e, t=t: e.tensor_scalar(out=gv[:, t % NG, :], in0=gv[:, t % NG, :], scalar1=mean[:, t:t + 1], scalar2=var[:, t:t + 1], op0=ALU.subtract, op1=ALU.mult),
                 reads=["gv%d" % (t % NG), "vmean", "vvar"], writes=["gv%d" % (t % NG)])
            P.op("pool", lambda e, t=t: e.tensor_tensor(out=gv[:, t % NG, :], in0=gv[:, t % NG, :], in1=lng, op=ALU.mult), reads=["gv%d" % (t % NG), "lng"], writes=["gv%d" % (t % NG)])
            P.op("dve", lambda e, t=t: e.tensor_tensor(out=gv[:, t % NG, :], in0=gv[:, t % NG, :], in1=lnb, op=ALU.add), reads=["gv%d" % (t % NG), "lnb"], writes=["gv%d" % (t % NG)])
        run_family(grp, 0, u_stages, [0, 0, 1, 2, 3, 4])
        run_family(grp, 16, q_stages, [0, 1, 2, 3, 4])
    allssA = ["ssA%d" % c_ for c_ in range(NOWN * 8)]
    P.barrier()
    sb.release(mC)

    if stop == "C":
        dbgC = nc.dram_tensor("dbgC", [128, 16 * NOWN * 128], BF16, kind="ExternalOutput").ap()
        P.op("sp", lambda e: e.dma_start(out=dbgC, in_=qT.rearrange("p h n -> p (h n)")), reads=["qT"], dma="dbg")
        dbgC2 = nc.dram_tensor("dbgC2", [128, NOWN * 8], F32, kind="ExternalOutput").ap()
        P.op("sp", lambda e: e.dma_start(out=dbgC2, in_=ssA), reads=allssA, dma="dbg")
        P.emit(final_waits=None)
        return nc
    mD = sb.mark()
    KTh = [sb.alloc([128, NKEY], BF16) for _ in range(2)]
    Vh = [sb.alloc([128, NT, VW], BF16) for _ in range(2)]
    NPT = 3
    PT = [sb.alloc([128, 512], BF16) for _ in range(NPT)]
    ob = [[sb.alloc([128, 128], F32) for _ in range(4)] for _ in range(2)]
    obb = [sb.alloc([128, 128], BF16) for _ in range(4)]
    rden = sb.alloc([128, 4], F32)
    obT = [sb.alloc([128, 512], BF16) for _ in range(2)]
    junkD = sb.alloc([128, 128], F32)
    SCALE = 1.0 / math.sqrt(128.0)
    QB = [(0, 512), (512, 512), (1024, 256)]
    blocks = []
    for kvh in range(4):
        for hh in range(4):
            for (q0, nq) in QB:
                blocks.append((kvh, kvh * 4 + hh, q0, nq))
    units = [(bi, kt) for bi in range(len(blocks)) for kt in range(NT)]
    loaded = set()

    def load_kv(kvh):
        if kvh in loaded or kvh >= 4:
            return
        loaded.add(kvh)
        s_ = kvh % 2
        ld("sp", KTh[s_], KT_scr[kvh], "KTh%d" % s_, reads=["KT_scr"])
        ld("sp", Vh[s_].rearrange("p t c -> p (t c)"), V_scr[kvh], "Vh%d" % s_, reads=["V_scr"])

    def st_S(u):
        bi, kt = units[u]
        kvh, head, q0, nq = blocks[bi]
        load_kv(kvh)
        s_ = kvh % 2
        sbk = u % 3
        P.op("pe", lambda e: e.matmul(ps[sbk][:, 0:nq], lhsT=KTh[s_][:, kt * 128:(kt + 1) * 128], rhs=qT[:, head, q0:q0 + nq], start=True, stop=True),
             reads=["KTh%d" % s_, "qT"], writes=["ps%d" % sbk])

    def st_exp(u):
        bi, kt = units[u]
        kvh, head, q0, nq = blocks[bi]
        sbk = u % 3
        pk = u % NPT
        P.op("act", lambda e: e.activation(out=PT[pk][:, 0:nq], in_=ps[sbk][:, 0:nq], func=AF.Exp, scale=SCALE),
             reads=["ps%d" % sbk], writes=["PT%d" % pk])

    def epi_dve(bi):
        kvh, head, q0, nq = blocks[bi]
        nsub = nq // 128
        so = bi % 2
        for qs in range(nsub):
            pb, pbk = ps[3 + qs], "ps%d" % (3 + qs)
            P.op("dve", lambda e, pb=pb, qs=qs: e.reciprocal(out=rden[:, qs:qs + 1], in_=pb[:, 128:129]), reads=[pbk], writes=["rden%d" % qs])
            P.op("dve", lambda e, pb=pb, qs=qs: e.tensor_scalar(out=ob[so][qs], in0=pb[:, 0:128], scalar1=rden[:, qs:qs + 1], scalar2=None, op0=ALU.mult),
                 reads=[pbk, "rden%d" % qs], writes=["ob%d_%d" % (so, qs)])
        for qs in range(nsub):
            P.op("dve", lambda e, qs=qs: e.tensor_copy(out=obb[qs], in_=ob[so][qs]), reads=["ob%d_%d" % (so, qs)], writes=["obb%d" % qs])
        for qs in range(nsub):
            P.op("pe", lambda e, qs=qs: e.transpose(out=psb[7][:, qs * 128:(qs + 1) * 128], in_=obb[qs], identity=ident), reads=["obb%d" % qs, "ident"], writes=["ps7"])
        P.op("dve", lambda e: e.tensor_scalar(out=obT[so][:, 0:nq], in0=psb[7][:, 0:nq], scalar1=goutc[:, 16 + head:17 + head], scalar2=None, op0=ALU.mult),
             reads=["ps7", "goutc"], writes=["obT%d" % so])
        P.op("sp", lambda e: e.dma_start(out=oT_scr[16 + head, :, q0:q0 + nq], in_=obT[so][:, 0:nq]), reads=["obT%d" % so], writes=["oT_scr"], dma="S:obT%d" % so)

    def epi_act(bi):
        kvh, head, q0, nq = blocks[bi]
        so = bi % 2
        for qs in range(nq // 128):
            t = q0 // 128 + qs
            col = t * 16 + head
            P.op("act", lambda e, qs=qs, col=col: e.activation(out=junkD, in_=ob[so][qs], func=AF.Square, accum_out=ssB[:, col:col + 1]),
                 reads=["ob%d_%d" % (so, qs)], writes=["ssB%d" % col])

    def st_PV(u):
        bi, kt = units[u]
        kvh, head, q0, nq = blocks[bi]
        s_ = kvh % 2
        pk = u % NPT
        for qs in range(nq // 128):
            P.op("pe", lambda e, qs=qs: e.matmul(ps[3 + qs][:, 0:129], lhsT=PT[pk][:, qs * 128:(qs + 1) * 128], rhs=Vh[s_][:, kt, 0:129],
                                                 start=(kt == 0), stop=(kt == NT - 1)),
                 reads=["PT%d" % pk, "Vh%d" % s_], writes=["ps%d" % (3 + qs)])
        if kt == NT - 1:
            epi_dve(bi)
        if kt == 8 and bi > 0:
            epi_act(bi - 1)
        if kt == 0 and bi % 12 == 1:
            load_kv(kvh + 1)

    pipeline(len(units), [st_S, st_exp, st_PV], [0, 1, 3])
    epi_act(len(blocks) - 1)
    allssB = ["ssB%d" % c_ for c_ in range(NOWN * 16)]
    P.barrier()
    sb.release(mD)
    sb.release(mC)
    sb.release(mTab)

    if stop == "D":
        dbgD = nc.dram_tensor("dbgD", [128, NOWN * 16], F32, kind="ExternalOutput").ap()
        P.op("sp", lambda e: e.dma_start(out=dbgD, in_=ssB), reads=allssB, dma="dbg")
        P.emit(final_waits=None)
        return nc
    mE = sb.mark()
    oT = sb.alloc([128, 32, NOWN * 128], BF16)
    wo = [sb.alloc([128, 32, 512], BF16) for _ in range(2)]
    osb = [sb.alloc([128, 512], F32) for _ in range(2)]
    otmp = sb.alloc([128, 512], F32)
    junkE = sb.alloc([128, 512], F32)
    for kc in range(32):
        ld("sp", oT[:, kc, :], oT_scr[kc], "oT", reads=["oT_scr"], writes=["oT"])
    P.op("dve", lambda e: e.reduce_sum(out=rA, in_=ssA.rearrange("p (t j) -> p t j", j=8), axis=AX.X), reads=allssA, writes=["rA"])
    P.op("dve", lambda e: e.reduce_sum(out=rB, in_=ssB.rearrange("p (t j) -> p t j", j=16), axis=AX.X), reads=allssB, writes=["rB"])
    rstd_from_ss(rA, rA, 2048, ["rA"], "rA")
    rstd_from_ss(rB, rB, 2048, ["rB"], "rB")
    nE = 0
    for cbk in range(8):
        s_ = cbk % 2
        P.op("pool", lambda e, s_=s_, cbk=cbk: e.dma_start(out=wo[s_].rearrange("p k c -> p (k c)"), in_=w_out_h[cbk]), writes=["wo%d" % s_], dma="L:wo%d" % s_)
        for t in range(NOWN):
            pa, pbn = nE % 2, 2 + (nE % 2)
            so = nE % 2
            nE += 1
            for kc in range(16):
                P.op("pe", lambda e, pa=pa, kc=kc, t=t, s_=s_: e.matmul(ps[pa], lhsT=oT[:, kc, t * 128:(t + 1) * 128], rhs=wo[s_][:, kc, :], start=(kc == 0), stop=(kc == 15)),
                     reads=["oT", "wo%d" % s_], writes=["ps%d" % pa])
            for kc in range(16, 32):
                P.op("pe", lambda e, pbn=pbn, kc=kc, t=t, s_=s_: e.matmul(ps[pbn], lhsT=oT[:, kc, t * 128:(t + 1) * 128], rhs=wo[s_][:, kc, :], start=(kc == 16), stop=(kc == 31)),
                     reads=["oT", "wo%d" % s_], writes=["ps%d" % pbn])
            P.op("act", lambda e, pa=pa, t=t: e.activation(out=otmp, in_=ps[pa], func=AF.Copy, scale=rA[:, t:t + 1]), reads=["ps%d" % pa, "rA"], writes=["otmp"])
            P.op("dve", lambda e, pbn=pbn, t=t, so=so: e.scalar_tensor_tensor(out=osb[so], in0=ps[pbn], scalar=rB[:, t:t + 1], in1=otmp, op0=ALU.mult, op1=ALU.add),
                 reads=["ps%d" % pbn, "rB", "otmp"], writes=["osb%d" % so])
            P.op("act", lambda e, so=so, t=t, cbk=cbk: e.activation(out=junkE, in_=osb[so], func=AF.Square, accum_out=ssO[:, t * 8 + cbk:t * 8 + cbk + 1]),
                 reads=["osb%d" % so], writes=["ssO"])
            P.op("sp", lambda e, so=so, t=t, cbk=cbk: e.dma_start(out=o_scr[t * 128:(t + 1) * 128, cbk * 512:(cbk + 1) * 512], in_=osb[so]),
                 reads=["osb%d" % so], writes=["o_scr"], dma="S:osb%d" % so)
    P.op("dve", lambda e: e.reduce_sum(out=rO, in_=ssO.rearrange("p (t j) -> p t j", j=8), axis=AX.X), reads=["ssO"], writes=["rO"])
    rstd_from_ss(rO, rO, D, ["rO"], "rO")
    P.barrier()
    sb.release(mE)

    if stop == "E":
        dbgE = nc.dram_tensor("dbgE", [128, NOWN], F32, kind="ExternalOutput").ap()
        P.op("sp", lambda e: e.dma_start(out=dbgE, in_=rO), reads=["rO"], dma="dbg")
        P.emit(final_waits=None)
        return nc
    out_ops = []
    for blk in range(2):
        mF = sb.mark()
        HTF_BYTES = 32 * 514 * 2
        HTF_OFF = (SB.LIMIT - HTF_BYTES) // 64 * 64
        hTf = nc.alloc_sbuf_tensor_at("hTf%d" % blk, [128, 32, 514], BF16, offset=HTF_OFF).ap()
        mF2 = sb.mark()
        orow = [sb.alloc([128, D], F32) for _ in range(2)]
        xrow = [sb.alloc([128, D], F32) for _ in range(2)]
        xm = [sb.alloc([128, D], F32) for _ in range(2)]
        ggrow = sb.alloc([128, D], F32)
        junkF = sb.alloc([128, D], BF16)
        xhF = [sb.alloc([128, D], BF16) for _ in range(2)]
        coltmp = sb.alloc([128, 32], F32)
        stF = sb.alloc([128, 2], F32)
        ld("sp", ggrow, gg_scr[0], "ggrow", reads=["gg_scr"])

        def f_s0(ti):
            t, s_ = 4 * blk + ti, ti % 2
            ld("sp", orow[s_], o_scr[t * 128:(t + 1) * 128, :], "orow%d" % s_, reads=["o_scr"])
            ld("sp", xrow[s_], xr[t * 128:(t + 1) * 128, :], "xrow%d" % s_)

        def f_s1(ti):
            t, s_ = 4 * blk + ti, ti % 2
            P.op("dve", lambda e: e.scalar_tensor_tensor(out=xm[s_], in0=orow[s_], scalar=rO[:, t:t + 1], in1=ggrow, op0=ALU.mult, op1=ALU.mult),
                 reads=["orow%d" % s_, "rO", "ggrow"], writes=["xm%d" % s_])
            P.op("dve", lambda e: e.tensor_tensor(out=xm[s_], in0=xm[s_], in1=xrow[s_], op=ALU.add), reads=["xm%d" % s_, "xrow%d" % s_], writes=["xm%d" % s_])
            if 1 <= ti <= 4:
                own = t - 1
                P.op("sp", lambda e: e.dma_start(out=xm_scr[own * 128:(own + 1) * 128, :], in_=xm[s_]), reads=["xm%d" % s_], writes=["xm_scr"], dma="S:xm%d" % s_)

        def f_s2(ti):
            s_ = ti % 2
            P.op("act", lambda e: e.activation(out=junkF, in_=xm[s_], func=AF.Square, accum_out=stF[:, s_:s_ + 1]), reads=["xm%d" % s_], writes=["stF%d" % s_])
            rstd_from_ss(stF[:, s_:s_ + 1], stF[:, s_:s_ + 1], D, ["stF%d" % s_], "stF%d" % s_)

        def f_s3(ti):
            s_ = ti % 2
            P.op("act", lambda e: e.activation(out=xhF[s_], in_=xm[s_], func=AF.Copy, scale=stF[:, s_:s_ + 1]), reads=["xm%d" % s_, "stF%d" % s_], writes=["xhF%d" % s_])

        def f_s4(ti):
            s_ = ti % 2
            TB = (0, 1, 2, 3)
            for g4 in range(4):
                for j in range(8):
                    kc = g4 * 8 + j
                    P.op("pe", lambda e, g4=g4, j=j, kc=kc: e.transpose(out=psb[TB[g4]][:, j * 128:(j + 1) * 128], in_=xhF[s_][:, kc * 128:(kc + 1) * 128], identity=ident),
                         reads=["xhF%d" % s_, "ident"], writes=["ps%d" % TB[g4]])
            if 1 <= ti <= 4:
                for g4 in range(4):
                    bank, bk = psb[TB[g4]], "ps%d" % TB[g4]
                    for j in range(8):
                        kc = g4 * 8 + j
                        dst = hTf[:, kc, (ti - 1) * 128:ti * 128]
                        if g4 % 2 == 0:
                            P.op("act", lambda e, bank=bank, j=j, kc=kc, dst=dst: e.activation(out=dst, in_=bank[:, j * 128:(j + 1) * 128], func=AF.Identity,
                                                                                              scale=Gf[:, kc:kc + 1], bias=SHf[:, kc:kc + 1]),
                                 reads=[bk, "Gf", "modc"], writes=["hTf_%d_%d" % (ti, kc)])
                        else:
                            P.op("dve", lambda e, bank=bank, j=j, kc=kc, dst=dst: e.tensor_scalar(out=dst, in0=bank[:, j * 128:(j + 1) * 128],
                                                                                                 scalar1=Gf[:, kc:kc + 1], scalar2=SHf[:, kc:kc + 1], op0=ALU.mult, op1=ALU.add),
                                 reads=[bk, "Gf", "modc"], writes=["hTf_%d_%d" % (ti, kc)])
            else:
                col = 127 if ti == 0 else 0
                dstc = 512 if ti == 0 else 513
                if blk == 0 and ti == 0:
                    mcol = hmask[:, 0:1]
                elif blk == 1 and ti == 5:
                    mcol = hmask[:, 1:2]
                else:
                    mcol = cst[:, 3:4]
                for g4 in range(4):
                    bank, bk = psb[TB[g4]], "ps%d" % TB[g4]
                    src = bank[:, 0:1024].rearrange("p (j t) -> p j t", j=8)[:, :, col]
                    sl = slice(g4 * 8, g4 * 8 + 8)
                    P.op("dve", lambda e, src=src, sl=sl: e.tensor_tensor(out=coltmp[:, sl], in0=src, in1=Gf[:, sl], op=ALU.mult), reads=[bk, "Gf"], writes=["Fct"])
                    P.op("dve", lambda e, sl=sl: e.tensor_tensor(out=coltmp[:, sl], in0=coltmp[:, sl], in1=SHf[:, sl], op=ALU.add), reads=["Fct", "modc"], writes=["Fct"])
                    P.op("dve", lambda e, sl=sl, dstc=dstc, mcol=mcol: e.tensor_scalar(out=hTf[:, sl, dstc], in0=coltmp[:, sl], scalar1=mcol, scalar2=None, op0=ALU.mult),
                         reads=["Fct", "hmask", "cst"], writes=["hTf_h%d_%d" % (ti, g4)])

        pipeline(6, [f_s0, f_s1, f_s2, f_s3, f_s4], [0, 1, 2, 3, 4])
        assert sb.off <= HTF_OFF, sb.off
        hTf_keys = ["hTf_%d_%d" % (ti, kc) for ti in range(1, 5) for kc in range(32)] + ["hTf_h%d_%d" % (ti, g4) for ti in (0, 5) for g4 in range(4)]
        P.barrier()
        sb.release(mF2)
        act = sb.alloc([128, NFC, 512], BF16)
        mG = sb.mark()
        wg = [sb.alloc([128, 32, 128], BF16) for _ in range(2)]
        wu = [sb.alloc([128, 32, 128], BF16) for _ in range(2)]
        ag = sb.alloc([128, 514], F32)
        au = sb.alloc([128, 514], F32)
        cg = sb.alloc([128, 512], F32)
        cu = sb.alloc([128, 512], F32)
        sg = sb.alloc([128, 512], F32)
        cp4 = convp.rearrange("p (c f) -> p c f", f=4)
        for i in range(NFC):
            s_ = i % 2
            P.op("pool", lambda e, s_=s_, i=i: e.dma_start(out=wg[s_].rearrange("p k c -> p (k c)"), in_=w_up_g[i]), writes=["wg%d" % s_], dma="L:wg%d" % s_)
            P.op("pool", lambda e, s_=s_, i=i: e.dma_start(out=wu[s_].rearrange("p k c -> p (k c)"), in_=w_up_u[i]), writes=["wu%d" % s_], dma="L:wu%d" % s_)
            for (wsrc, wkey, pm, ph, abuf, akey, cbuf, ckey, ci) in ((wg[s_], "wg%d" % s_, s_, 4, ag, "ag", cg, "cg", i),
                                                                   (wu[s_], "wu%d" % s_, 2 + s_, 5, au, "au", cu, "cu", NFC + i)):
                for kc in range(32):
                    P.op("pe", lambda e, pm=pm, kc=kc, wsrc=wsrc: e.matmul(ps[pm], lhsT=wsrc[:, kc, :], rhs=hTf[:, kc, 0:512], start=(kc == 0), stop=(kc == 31)),
                         reads=[wkey], writes=["ps%d" % pm])
                for kc in range(32):
                    P.op("pe", lambda e, ph=ph, kc=kc, wsrc=wsrc: e.matmul(ps[ph][:, 0:2], lhsT=wsrc[:, kc, :], rhs=hTf[:, kc, 512:514], start=(kc == 0), stop=(kc == 31)),
                         reads=[wkey], writes=["ps%d" % ph])
                P.op("act", lambda e, pm=pm, abuf=abuf: e.activation(out=abuf[:, 1:513], in_=ps[pm], func=AF.Copy), reads=["ps%d" % pm], writes=[akey])
                P.op("dve", lambda e, ph=ph, abuf=abuf: e.tensor_copy(out=abuf[:, 0:1], in_=ps[ph][:, 0:1]), reads=["ps%d" % ph], writes=[akey])
                P.op("dve", lambda e, ph=ph, abuf=abuf: e.tensor_copy(out=abuf[:, 513:514], in_=ps[ph][:, 1:2]), reads=["ps%d" % ph], writes=[akey])
                P.op("dve", lambda e, abuf=abuf, cbuf=cbuf, ci=ci: e.tensor_scalar(out=cbuf, in0=abuf[:, 0:512], scalar1=cp4[:, ci, 0:1], scalar2=cp4[:, ci, 3:4], op0=ALU.mult, op1=ALU.add),
                     reads=[akey, "convp"], writes=[ckey])
                P.op("dve", lambda e, abuf=abuf, cbuf=cbuf, ci=ci: e.scalar_tensor_tensor(out=cbuf, in0=abuf[:, 1:513], scalar=cp4[:, ci, 1:2], in1=cbuf, op0=ALU.mult, op1=ALU.add),
                     reads=[akey, "convp", ckey], writes=[ckey])
                P.op("dve", lambda e, abuf=abuf, cbuf=cbuf, ci=ci: e.scalar_tensor_tensor(out=cbuf, in0=abuf[:, 2:514], scalar=cp4[:, ci, 2:3], in1=cbuf, op0=ALU.mult, op1=ALU.add),
                     reads=[akey, "convp", ckey], writes=[ckey])
            P.op("act", lambda e: e.activation(out=sg, in_=cg, func=AF.Silu), reads=["cg"], writes=["sg"])
            P.op("dve", lambda e, i=i: e.tensor_tensor(out=act[:, i, :], in0=sg, in1=cu, op=ALU.mult), reads=["sg", "cu"], writes=["act"])
        assert sb.off <= HTF_OFF, sb.off
        P.barrier()
        sb.release(mG)
        KG = 8
        groups = [(k0, min(k0 + KG, NFC)) for k0 in range(0, NFC, KG)]
        wd = [sb.alloc([128, KG, 512], BF16) for _ in range(3)]
        fsb = [sb.alloc([128, 512], F32) for _ in range(2)]
        junkG = sb.alloc([128, 512], F32)
        nwd = 0
        nf = 0
        for cbk in range(8):
            base = 4 * (cbk % 2)
            for (k0, k1) in groups:
                s_ = nwd % 3
                nwd += 1
                P.op("pool", lambda e, s_=s_, cbk=cbk, k0=k0, k1=k1: e.dma_start(out=wd[s_][:, 0:k1 - k0, :].rearrange("p k c -> p (k c)"),
                                                                               in_=w_down_h[cbk][:, k0 * 512:k1 * 512]),
                     writes=["wd%d" % s_], dma="L:wd%d" % s_)
                for kc in range(k0, k1):
                    for q in range(4):
                        P.op("pe", lambda e, base=base, q=q, kc=kc, k0=k0, s_=s_: e.matmul(ps[base + q], lhsT=act[:, kc, q * 128:(q + 1) * 128], rhs=wd[s_][:, kc - k0, :],
                                                                                       start=(kc == 0), stop=(kc == NFC - 1)),
                             reads=["act", "wd%d" % s_], writes=["ps%d" % (base + q)])
            for q in range(4):
                so = nf % 2
                nf += 1
                own = 4 * blk + q
                P.op("act", lambda e, base=base, q=q, so=so: e.activation(out=fsb[so], in_=ps[base + q], func=AF.Copy), reads=["ps%d" % (base + q)], writes=["fsb%d" % so])
                P.op("act", lambda e, so=so, q=q, cbk=cbk: e.activation(out=junkG, in_=fsb[so], func=AF.Square, accum_out=ssF[:, q * 8 + cbk:q * 8 + cbk + 1]),
                     reads=["fsb%d" % so], writes=["ssF"])
                P.op("sp", lambda e, so=so, own=own, cbk=cbk: e.dma_start(out=f_scr[own * 128:(own + 1) * 128, cbk * 512:(cbk + 1) * 512], in_=fsb[so]),
                     reads=["fsb%d" % so], writes=["f_scr"], dma="S:fsb%d" % so)
        P.op("dve", lambda e: e.reduce_sum(out=rF[:, 0:4], in_=ssF[:, 0:32].rearrange("p (t j) -> p t j", j=8), axis=AX.X), reads=["ssF"], writes=["rF"])
        rstd_from_ss(rF[:, 0:4], rF[:, 0:4], D, ["rF"], "rF")
        frow = [sb.alloc([128, D], F32) for _ in range(2)]
        xmrow = [sb.alloc([128, D], F32) for _ in range(2)]
        ggf = sb.alloc([128, D], F32)
        ld("sp", ggf, gg_scr[1], "ggf", reads=["gg_scr"])

        def z_s0(q):
            own, s_ = 4 * blk + q, q % 2
            ld("sp", frow[s_], f_scr[own * 128:(own + 1) * 128, :], "frow%d" % s_, reads=["f_scr"])
            ld("sp", xmrow[s_], xm_scr[own * 128:(own + 1) * 128, :], "xmrow%d" % s_, reads=["xm_scr"])

        def z_s1(q):
            own, s_ = 4 * blk + q, q % 2
            P.op("dve", lambda e: e.scalar_tensor_tensor(out=frow[s_], in0=frow[s_], scalar=rF[:, q:q + 1], in1=ggf, op0=ALU.mult, op1=ALU.mult),
                 reads=["frow%d" % s_, "rF", "ggf"], writes=["frow%d" % s_])
            P.op("dve", lambda e: e.tensor_tensor(out=frow[s_], in0=frow[s_], in1=xmrow[s_], op=ALU.add), reads=["frow%d" % s_, "xmrow%d" % s_], writes=["frow%d" % s_])
            o = P.op("sp", lambda e: e.dma_start(out=out[own * 128:(own + 1) * 128, :], in_=frow[s_]), reads=["frow%d" % s_], dma="S:out%d" % s_)
            out_ops.append(o.idx)

        pipeline(4, [z_s0, z_s1], [0, 1])
        P.barrier()
        sb.release(mF)

    P.emit(final_waits=out_ops)
    return nc


_CACHE = {}


def _host_layouts(inp):
    f = np.float32
    A = {}

    def col(v):
        v = np.asarray(v, f).reshape(-1, 128)
        return np.ascontiguousarray(v.T)

    A["ctx"] = np.ascontiguousarray(inp["ctx"][0], dtype=f)
    A["jidx"] = np.ascontiguousarray(np.broadcast_to(np.arange(32, dtype=f)[None, :], (128, 32)))
    A["cvec"] = np.concatenate([col(inp["c"][0]), col(inp["c_ctx"])], axis=1)
    A["w_ada_a"] = np.ascontiguousarray(inp["w_ada"][0][:, :3 * D], dtype=f)
    A["w_ada_b"] = np.ascontiguousarray(inp["w_ada"][0][:, 3 * D:], dtype=f)
    A["b_ada_col"] = col(inp["b_ada"][0])
    A["b_ada_row"] = np.ascontiguousarray(inp["b_ada"][0][None, :], dtype=f)
    A["gcols"] = np.concatenate([col(inp["g_pre_mix"][0]), col(inp["g_pre_ffn"][0])], axis=1)
    A["goutc"] = col(np.concatenate([inp["g_out_a"][0], inp["g_out_b"][0]]))
    A["gpost_mix_row"] = np.ascontiguousarray(inp["g_post_mix"][0][None, :], dtype=f)
    A["gpost_ffn_row"] = np.ascontiguousarray(inp["g_post_ffn"][0][None, :], dtype=f)
    A["lng_row"] = np.ascontiguousarray(inp["ln_v_g"][0][None, :], dtype=f)
    A["lnb_row"] = np.ascontiguousarray(inp["ln_v_b"][0][None, :], dtype=f)
    A["gq_row"] = np.ascontiguousarray(inp["g_q"][0][None, :], dtype=f)
    A["gk_row"] = np.ascontiguousarray(inp["g_k"][0][None, :], dtype=f)
    w_in = np.asarray(inp["w_in"][0], f)
    A["w_in_h"] = np.ascontiguousarray(w_in.reshape(32, 128, 28, 256).transpose(2, 1, 0, 3)).reshape(28, 128, 32 * 256)
    ws = np.asarray(inp["w_s"][0], f)
    A["wsT"] = np.ascontiguousarray(ws.transpose(2, 0, 1)).reshape(128, 16 * 128)
    A["bs_col"] = np.ascontiguousarray(np.asarray(inp["b_s"][0], f).T)
    w_out = np.asarray(inp["w_out"][0], f)
    A["w_out_h"] = np.ascontiguousarray(w_out.reshape(32, 128, 8, 512).transpose(2, 1, 0, 3)).reshape(8, 128, 32 * 512)
    w_up = np.asarray(inp["w_up"][0], f)
    w_up_h = np.ascontiguousarray(w_up.reshape(32, 128, 2 * NFC, 128).transpose(2, 1, 0, 3)).reshape(2 * NFC, 128, 32 * 128)
    A["w_up_g"] = np.ascontiguousarray(w_up_h[:NFC])
    A["w_up_u"] = np.ascontiguousarray(w_up_h[NFC:])
    cw = np.asarray(inp["conv_w"][0], f)
    cb = np.asarray(inp["conv_b"][0], f)
    cp = np.concatenate([cw, cb[None, :]], axis=0)
    A["convp"] = np.ascontiguousarray(cp.reshape(4, 2 * NFC, 128).transpose(2, 1, 0)).reshape(128, 2 * NFC * 4)
    w_down = np.asarray(inp["w_down"][0], f)
    A["w_down_h"] = np.ascontiguousarray(w_down.reshape(NFC, 128, 8, 512).transpose(2, 1, 0, 3)).reshape(8, 128, NFC * 512)
    A["ident"] = np.eye(128, dtype=f)
    return A


def _run(inputs, stop=None, debug=False, cores=NCORE):
    nc = build_program(stop=stop, debug=debug)
    in_maps = _in_maps(inputs)[:cores]
    return run_bass_kernel_spmd(nc, in_maps, core_ids=list(range(cores)))


def _in_maps(inputs):
    shared = _host_layouts(inputs)
    x = np.asarray(inputs["x"][0], np.float32)
    tok = np.arange(S)
    in_maps = []
    for i in range(NCORE):
        shift = (TOK * i - 128) % S
        order = (tok + shift) % S
        m = dict(shared)
        m["xr"] = np.ascontiguousarray(x[order])
        rowi = (order // 64).astype(np.float32)
        coli = (order % 64).astype(np.float32)
        pos = np.zeros((128, NT, 2), np.float32)
        pos[:, :64, 0] = rowi.reshape(64, 128).T
        pos[:, :64, 1] = coli.reshape(64, 128).T
        m["pos"] = pos.reshape(128, NT * 2)
        hm = np.ones((128, 2), np.float32)
        if i == 0:
            hm[:, 0] = 0.0
        if i == NCORE - 1:
            hm[:, 1] = 0.0
        m["hmask"] = hm
        in_maps.append(m)
    return in_maps


def kernel(**inputs):
    if "nc" not in _CACHE:
        _CACHE["nc"] = build_program()
    nc = _CACHE["nc"]
    in_maps = _in_maps(inputs)
    res = run_bass_kernel_spmd(nc, in_maps, core_ids=list(range(NCORE)))
    outs = [np.asarray(r["out"], np.float32) for r in res.results]
    return np.concatenate(outs, axis=0)[None, :, :]
```

```python
import math
import numpy as np
import concourse.bass as bass
import concourse.mybir as mybir
from concourse.bass_utils import run_bass_kernel_spmd
from contextlib import ExitStack

F32 = mybir.dt.float32
BF16 = mybir.dt.bfloat16
AF = mybir.ActivationFunctionType
ALU = mybir.AluOpType
AX = mybir.AxisListType

ENGS = ("pe", "act", "dve", "pool", "sp")
D = 4096
S = 8192
NCORE = 8
TOK = S // NCORE
DFF = 11008
NFC = DFF // 128
EPS = 1e-6
NT = 66
NKEY = NT * 128
NOWN = 10
NG = 5
VW = 130


class Op:
    __slots__ = ("eng", "fn", "reads", "writes", "dma", "deps", "idx", "need_inc", "seq", "dma_cnt")

    def __init__(self, eng, fn, reads, writes, dma):
        self.eng, self.fn, self.reads, self.writes, self.dma = eng, fn, reads, writes, dma
        self.deps = set()
        self.need_inc = False
        self.seq = 0
        self.dma_cnt = 0


class Prog:
    def __init__(self, nc):
        self.nc = nc
        self.ops = []
        self.last_w = {}
        self.readers = {}
        self.last_eng = {}
        self.dma_since = []
        self.pending_barrier = {}

    def op(self, eng, fn, reads=(), writes=(), dma=None):
        o = Op(eng, fn, tuple(reads), tuple(writes), dma)
        o.idx = len(self.ops)
        for r in o.reads:
            w = self.last_w.get(r)
            if w is not None:
                o.deps.add(w)
        for w_ in o.writes:
            w = self.last_w.get(w_)
            if w is not None:
                o.deps.add(w)
            for rd in self.readers.get(w_, ()):
                o.deps.add(rd)
        for r in o.reads:
            self.readers.setdefault(r, []).append(o.idx)
        for w_ in o.writes:
            self.last_w[w_] = o.idx
            self.readers[w_] = []
        if eng in self.pending_barrier:
            o.deps |= self.pending_barrier.pop(eng)
        o.deps.discard(o.idx)
        self.ops.append(o)
        if dma is None:
            self.last_cmp = getattr(self, "last_cmp", {})
            self.last_cmp[eng] = o.idx
        self.last_eng[eng] = o.idx
        if dma is not None:
            self.dma_since.append(o.idx)
        return o

    def barrier(self):
        b = set(self.last_eng.values()) | set(self.dma_since)
        for e in ENGS:
            self.pending_barrier[e] = set(b) | self.pending_barrier.get(e, set())
        self.dma_since = []

    def emit(self, final_waits=()):
        nc = self.nc
        ops = self.ops
        for o in ops:
            for d in o.deps:
                p = ops[d]
                if p.dma is not None:
                    continue
                if p.eng == "pe" and o.eng == "pe":
                    continue
                p.need_inc = True
        if final_waits is None:
            for e, i_ in getattr(self, "last_cmp", {}).items():
                ops[i_].need_inc = True
            for o in ops:
                if o.dma is None and o.eng != "pe" and False:
                    o.need_inc = True
        cnt = {e: 0 for e in ENGS}
        dcnt = {}
        for o in ops:
            if o.dma is not None:
                dcnt[o.dma] = dcnt.get(o.dma, 0) + 1
                o.dma_cnt = dcnt[o.dma]
            elif o.need_inc:
                cnt[o.eng] += 1
                o.seq = cnt[o.eng]
        with ExitStack() as es:
            esem = {e: es.enter_context(nc.semaphore("s_" + e)) for e in ENGS}
            dsem = {k: es.enter_context(nc.semaphore("d_%d" % i)) for i, k in enumerate(dcnt)}
            block = es.enter_context(nc.Block())
            per_eng = {e: [o for o in ops if o.eng == e] for e in ENGS}

            def body(ename):
                def run(engine):
                    waited = {}
                    for o in per_eng[ename]:
                        need = {}
                        for d in o.deps:
                            p = ops[d]
                            if p.dma is not None:
                                key = ("d", p.dma)
                                val = 16 * p.dma_cnt
                                sem = dsem[p.dma]
                            else:
                                if p.eng == "pe" and ename == "pe":
                                    continue
                                key = ("e", p.eng)
                                val = p.seq
                                sem = esem[p.eng]
                            if val > need.get(key, (0, None))[0]:
                                need[key] = (val, sem)
                        for key, (val, sem) in need.items():
                            if waited.get(key, 0) >= val:
                                continue
                            engine.wait_ge(sem, val)
                            waited[key] = val
                        ins = o.fn(engine)
                        if o.dma is not None:
                            ins.then_inc(dsem[o.dma], 16)
                        elif o.need_inc:
                            ins.then_inc(esem[ename], 1)
                    if ename == "sp" and final_waits is None:
                        for e2 in ENGS:
                            if cnt[e2] > 0:
                                engine.wait_ge(esem[e2], cnt[e2])
                        for k2 in dcnt:
                            engine.wait_ge(dsem[k2], 16 * dcnt[k2])
                    elif ename == "sp":
                        for i in final_waits:
                            p = ops[i]
                            engine.wait_ge(dsem[p.dma], 16 * dcnt[p.dma])
                return run

            block.tensor(body("pe"))
            block.scalar(body("act"))
            block.vector(body("dve"))
            block.gpsimd(body("pool"))
            block.sync(body("sp"))


class SB:
    BASE = 16512
    LIMIT = 229376 - 64

    def __init__(self, nc):
        self.nc = nc
        self.off = SB.BASE
        self.n = 0

    def alloc(self, shape, dtype):
        nbytes = int(np.prod(shape[1:])) * (2 if dtype == BF16 else 4)
        nbytes = (nbytes + 63) // 64 * 64
        assert self.off + nbytes <= SB.LIMIT, ("SBUF overflow", self.off, nbytes)
        self.n += 1
        t = self.nc.alloc_sbuf_tensor_at("sb%d" % self.n, list(shape), dtype, offset=self.off)
        self.off += nbytes
        return t.ap()

    def mark(self):
        return self.off

    def release(self, m):
        self.off = m


def pipeline(n, stages, delays, order=None):
    maxd = max(delays)
    order = list(reversed(range(len(stages)))) if order is None else order
    for i in range(n + maxd):
        for s_ in order:
            u = i - delays[s_]
            if 0 <= u < n:
                stages[s_](u)


def build_program(stop=None, debug=False):
    nc = bass.Bass("TRN2", target_bir_lowering=False)

    def din(name, shape):
        return nc.dram_tensor(name, list(shape), F32, kind="ExternalInput").ap()

    xr = din("xr", [S, D])
    ctx = din("ctx", [256, D])
    pos_d = din("pos", [128, NT * 2])
    jidx_d = din("jidx", [128, 32])
    cvec_d = din("cvec", [128, 64])
    w_ada_a = din("w_ada_a", [D, 3 * D])
    w_ada_b = din("w_ada_b", [D, 3 * D])
    b_ada_col = din("b_ada_col", [128, 192])
    b_ada_row = din("b_ada_row", [1, 6 * D])
    gcols_d = din("gcols", [128, 64])
    goutc_d = din("goutc", [128, 32])
    gpost_mix_row = din("gpost_mix_row", [1, D])
    gpost_ffn_row = din("gpost_ffn_row", [1, D])
    lng_row = din("lng_row", [1, 2048])
    lnb_row = din("lnb_row", [1, 2048])
    gq_row = din("gq_row", [1, 128])
    gk_row = din("gk_row", [1, 128])
    w_in_h = din("w_in_h", [28, 128, 32 * 256])
    wsT_d = din("wsT", [128, 16 * 128])
    bs_col_d = din("bs_col", [128, 16])
    w_out_h = din("w_out_h", [8, 128, 32 * 512])
    w_up_g = din("w_up_g", [NFC, 128, 32 * 128])
    w_up_u = din("w_up_u", [NFC, 128, 32 * 128])
    convp_d = din("convp", [128, 2 * NFC * 4])
    w_down_h = din("w_down_h", [8, 128, NFC * 512])
    hmask_d = din("hmask", [128, 2])
    ident_d = din("ident", [128, 128])
    out = nc.dram_tensor("out", [TOK, D], F32, kind="ExternalOutput").ap()

    def dscr(name, shape, dt):
        if debug:
            return nc.dram_tensor(name, list(shape), dt, kind="ExternalOutput").ap()
        return nc.dram_tensor(name, list(shape), dt).ap()

    KT_scr = dscr("KT_scr", [4, 128, NKEY], BF16)
    V_scr = dscr("V_scr", [4, 128, NT * VW], BF16)
    hT_scr = dscr("hT_scr", [NOWN, 128, 32 * 128], BF16)
    oT_scr = dscr("oT_scr", [32, 128, NOWN * 128], BF16)
    o_scr = dscr("o_scr", [NOWN * 128, D], F32)
    xm_scr = dscr("xm_scr", [TOK, D], F32)
    f_scr = dscr("f_scr", [TOK, D], F32)
    gg_scr = dscr("gg_scr", [2, 128, D], F32)

    P = Prog(nc)
    sb = SB(nc)
    ps = [nc.alloc_psum_tensor("ps%d" % i, [128, 512], F32).ap() for i in range(8)]
    psb = [p.bitcast(BF16) for p in ps]

    ident = sb.alloc([128, 128], BF16)
    onesf = sb.alloc([128, 128], F32)
    modc = sb.alloc([128, 192], F32)
    modx = sb.alloc([128, 64], F32)
    Ga = sb.alloc([128, 32], F32)
    Gc = sb.alloc([128, 32], F32)
    Gf = sb.alloc([128, 32], F32)
    gcols = sb.alloc([128, 64], F32)
    goutc = sb.alloc([128, 32], F32)
    cst = sb.alloc([128, 8], F32)
    freq = sb.alloc([128, 32], F32)
    pos = sb.alloc([128, NT * 2], F32)
    gq_t = sb.alloc([128, 128], F32)
    gk_t = sb.alloc([128, 128], F32)
    bs_col = sb.alloc([128, 16], F32)
    hmask = sb.alloc([128, 2], F32)
    convp = sb.alloc([128, 2 * NFC * 4], F32)
    ssA = sb.alloc([128, NOWN * 8], F32)
    ssB = sb.alloc([128, NOWN * 16], F32)
    ssO = sb.alloc([128, NOWN * 8], F32)
    ssF = sb.alloc([128, 8 * 8], F32)
    rA = sb.alloc([128, NOWN], F32)
    rB = sb.alloc([128, NOWN], F32)
    rO = sb.alloc([128, NOWN], F32)
    rF = sb.alloc([128, 8], F32)
    sv = sb.alloc([128, 64], F32)
    st = sb.alloc([128, 64], F32)
    SHa = modc[:, 0:32]
    SHf = modc[:, 96:128]
    SHc = modx[:, 0:32]
    epsc = cst[:, 0:1]
    negpi = cst[:, 1:2]
    pospi = cst[:, 2:3]

    def ld(eng, dst, src, key, reads=(), writes=None):
        return P.op(eng, lambda e, dst=dst, src=src: e.dma_start(out=dst, in_=src), reads=reads,
                    writes=[key] if writes is None else writes, dma="L:" + key)

    def bc(row_ap, n=128):
        return row_ap.partition_broadcast(n).rearrange("p a b -> p (a b)")

    P.op("pool", lambda e: e.memset(cst[:, 0:1], EPS), writes=["cst"])
    P.op("pool", lambda e: e.memset(cst[:, 1:2], -math.pi), writes=["cst"])
    P.op("pool", lambda e: e.memset(cst[:, 2:3], math.pi), writes=["cst"])
    P.op("pool", lambda e: e.memset(cst[:, 3:4], 1.0), writes=["cst"])
    P.op("pool", lambda e: e.memset(onesf, 1.0), writes=["onesf"])
    ld("pool", ident, ident_d, "ident")
    ld("sp", sv, cvec_d, "sv")
    ld("sp", gcols, gcols_d, "gcols")
    ld("sp", goutc, goutc_d, "goutc")
    ld("sp", pos, pos_d, "pos")
    ld("sp", freq, jidx_d, "freq")
    ld("sp", gq_t, bc(gq_row), "gq_t")
    ld("sp", gk_t, bc(gk_row), "gk_t")
    ld("sp", bs_col, bs_col_d, "bs_col")
    ld("sp", hmask, hmask_d, "hmask")
    ld("sp", convp, convp_d, "convp")
    ld("sp", modc, b_ada_col, "modc")
    P.op("act", lambda e: e.activation(out=freq, in_=freq, func=AF.Exp, scale=-math.log(10000.0) / 32.0),
         reads=["freq"], writes=["freq"])
    P.op("act", lambda e: e.activation(out=sv, in_=sv, func=AF.Silu), reads=["sv"], writes=["sv"])

    mA = sb.mark()
    NWT = 8
    wt = [sb.alloc([128, 2048], F32) for _ in range(NWT)]
    rowbuf = sb.alloc([2, 6 * D], F32)
    brow = sb.alloc([128, 2048], F32)
    grow = sb.alloc([128, 2048], F32)
    ggs = sb.alloc([128, 2048], F32)
    identf = sb.alloc([128, 128], F32)
    sel0 = sb.alloc([2, 128], F32)
    ld("sp", identf, ident_d, "identf")
    P.op("pool", lambda e: e.memset(sel0, 0.0), writes=["sel0"])
    P.op("pool", lambda e: e.memset(sel0[0:1, :], 1.0), writes=["sel0"])
    sv3 = sv.rearrange("p (two k) -> p k two", two=2)
    nload = 0
    for cb in range(12):
        base = 4 * (cb % 2)
        for kc in range(32):
            w = wt[nload % NWT]
            wk = "wt%d" % (nload % NWT)
            nload += 1
            ld("sp", w, (w_ada_a if cb < 6 else w_ada_b)[kc * 128:(kc + 1) * 128, (cb % 6) * 2048:(cb % 6 + 1) * 2048], wk)
            for q in range(4):
                P.op("pe", lambda e, w=w, q=q, kc=kc, base=base: e.matmul(ps[base + q][0:2, :], lhsT=sv3[:, kc, :], rhs=w[:, q * 512:(q + 1) * 512],
                                                                         start=(kc == 0), stop=(kc == 31)),
                     reads=[wk, "sv"], writes=["ps%d" % (base + q)])
        for q in range(4):
            P.op("dve", lambda e, q=q, cb=cb, base=base: e.tensor_copy(out=rowbuf[:, cb * 2048 + q * 512:cb * 2048 + (q + 1) * 512], in_=ps[base + q][0:2, :]),
                 reads=["ps%d" % (base + q)], writes=["rowbuf%d" % cb])
    allrow = ["rowbuf%d" % cb for cb in range(12)]
    for j in range(192):
        P.op("pe", lambda e, j=j: e.matmul(ps[0][:, 2 * j:2 * j + 2], lhsT=rowbuf[:, j * 128:(j + 1) * 128], rhs=identf[0:2, 0:2], start=True, stop=True),
             reads=allrow + ["identf"], writes=["ps0"])
    pc3 = ps[0][:, 0:384].rearrange("p (j two) -> p j two", two=2)
    P.op("dve", lambda e: e.tensor_tensor(out=modx, in0=pc3[:, 0:64, 1], in1=modc[:, 0:64], op=ALU.add), reads=["ps0", "modc"], writes=["modx"])
    P.op("dve", lambda e: e.tensor_tensor(out=modc, in0=pc3[:, :, 0], in1=modc, op=ALU.add), reads=["ps0", "modc", "modx"], writes=["modc"])
    nq = 0
    for which, cb0 in ((0, 4), (1, 10)):
        for half in range(2):
            cb = cb0 + half
            ld("sp", brow, bc(b_ada_row[:, cb * 2048:(cb + 1) * 2048]), "brow")
            ld("sp", grow, bc((gpost_mix_row if which == 0 else gpost_ffn_row)[:, half * 2048:(half + 1) * 2048]), "grow")
            for q in range(4):
                bank = 1 + (nq % 4)
                nq += 1
                P.op("pe", lambda e, q=q, cb=cb, bank=bank: e.matmul(ps[bank], lhsT=sel0, rhs=rowbuf[:, cb * 2048 + q * 512:cb * 2048 + (q + 1) * 512], start=True, stop=True),
                     reads=allrow + ["sel0"], writes=["ps%d" % bank])
                P.op("dve", lambda e, q=q, bank=bank: e.tensor_tensor(out=ggs[:, q * 512:(q + 1) * 512], in0=ps[bank], in1=brow[:, q * 512:(q + 1) * 512], op=ALU.add),
                     reads=["ps%d" % bank, "brow"], writes=["ggs"])
            P.op("dve", lambda e: e.tensor_tensor(out=ggs, in0=ggs, in1=grow, op=ALU.mult), reads=["ggs", "grow"], writes=["ggs"])
            P.op("sp", lambda e, which=which, half=half: e.dma_start(out=gg_scr[which, :, half * 2048:(half + 1) * 2048], in_=ggs),
                 reads=["ggs"], writes=["gg_scr"], dma="S:gg")
    for (G, scl, gcol, key) in ((Ga, modc[:, 32:64], gcols[:, 0:32], "Ga"), (Gc, modx[:, 32:64], gcols[:, 0:32], "Gc"),
                                (Gf, modc[:, 128:160], gcols[:, 32:64], "Gf")):
        P.op("dve", lambda e, G=G, scl=scl, gcol=gcol: e.scalar_tensor_tensor(out=G, in0=scl, scalar=1.0, in1=gcol, op0=ALU.add, op1=ALU.mult),
             reads=["modc", "modx", "gcols"], writes=[key])
    P.barrier()
    sb.release(mA)

    if stop == "A":
        dbgA = nc.dram_tensor("dbgA", [128, 512], F32, kind="ExternalOutput").ap()
        P.op("sp", lambda e: e.dma_start(out=dbgA[:, 0:192], in_=modc), reads=["modc"], dma="dbg")
        P.op("sp", lambda e: e.dma_start(out=dbgA[:, 192:256], in_=modx), reads=["modx"], dma="dbg")
        P.op("sp", lambda e: e.dma_start(out=dbgA[:, 256:288], in_=Ga), reads=["Ga"], dma="dbg")
        P.op("sp", lambda e: e.dma_start(out=dbgA[:, 288:320], in_=Gc), reads=["Gc"], dma="dbg")
        P.op("sp", lambda e: e.dma_start(out=dbgA[:, 320:352], in_=Gf), reads=["Gf"], dma="dbg")
        P.emit(final_waits=None)
        return nc
    def rstd_from_ss(ss_ap, out_ap, n, reads, wkey):
        P.op("act", lambda e: e.activation(out=out_ap, in_=ss_ap, func=AF.Sqrt, scale=1.0 / n, bias=epsc), reads=list(reads) + ["cst"], writes=[wkey])
        P.op("dve", lambda e: e.reciprocal(out=out_ap, in_=out_ap), reads=[wkey], writes=[wkey])

    def rope_tables(t, cos2, sins, ang, angk, angi, key, skey=None):
        skey = key if skey is None else skey
        P.op("dve", lambda e: e.tensor_scalar(out=ang[:, 0:32], in0=freq, scalar1=pos[:, 2 * t:2 * t + 1], scalar2=None, op0=ALU.mult),
             reads=["freq", "pos"], writes=[skey + "ang"])
        P.op("dve", lambda e: e.tensor_scalar(out=ang[:, 32:64], in0=freq, scalar1=pos[:, 2 * t + 1:2 * t + 2], scalar2=None, op0=ALU.mult),
             reads=["freq", "pos"], writes=[skey + "ang"])
        P.op("dve", lambda e: e.tensor_scalar(out=ang[:, 64:128], in0=ang[:, 0:64], scalar1=0.5 * math.pi, scalar2=None, op0=ALU.add),
             reads=[skey + "ang"], writes=[skey + "ang2"])
        P.op("dve", lambda e: e.tensor_scalar(out=angk, in0=ang, scalar1=1.0 / (2 * math.pi), scalar2=None, op0=ALU.mult),
             reads=[skey + "ang", skey + "ang2"], writes=[skey + "angk"])
        P.op("dve", lambda e: e.tensor_copy(out=angi, in_=angk), reads=[skey + "angk"], writes=[skey + "angi"])
        P.op("dve", lambda e: e.tensor_copy(out=angk, in_=angi), reads=[skey + "angi"], writes=[skey + "angk"])
        P.op("dve", lambda e: e.scalar_tensor_tensor(out=ang, in0=angk, scalar=-2.0 * math.pi, in1=ang, op0=ALU.mult, op1=ALU.add),
             reads=[skey + "angk", skey + "ang", skey + "ang2"], writes=[skey + "ang", skey + "ang2"])
        P.op("dve", lambda e: e.tensor_scalar(out=ang, in0=ang, scalar1=3.1415925, scalar2=-3.1415925, op0=ALU.min, op1=ALU.max),
             reads=[skey + "ang", skey + "ang2"], writes=[skey + "ang", skey + "ang2"])
        c4 = cos2.rearrange("p (a b j) -> p a b j", a=2, b=2)
        s4 = sins.rearrange("p (a b j) -> p a b j", a=2, b=2)
        sarg = ang[:, 0:64].rearrange("p (a j) -> p a j", a=2)
        carg = ang[:, 64:128].rearrange("p (a j) -> p a j", a=2)
        for b in range(2):
            P.op("act", lambda e, b=b: e.activation(out=c4[:, :, b, :], in_=carg, func=AF.Sin), reads=[skey + "ang2"], writes=[key + "cos"])
        P.op("act", lambda e: e.activation(out=s4[:, :, 1, :], in_=sarg, func=AF.Sin), reads=[skey + "ang"], writes=[key + "sin"])
        P.op("act", lambda e: e.activation(out=s4[:, :, 0, :], in_=sarg, func=AF.Sin, scale=-1.0), reads=[skey + "ang"], writes=[key + "sin"])

    def qk_norm_rope(psrc, pkey, nh, g_t, cos2, sins, tkeys, tmp, outb, okey, pfx):
        sq, qn, t1, t2, ssq = tmp["sq"], tmp["qn"], tmp["t1"], tmp["t2"], tmp["ss"]
        P.op("act", lambda e: e.activation(out=sq, in_=psrc, func=AF.Square), reads=[pkey], writes=[pfx + "sq"])
        P.op("dve", lambda e: e.reduce_sum(out=ssq[:, 0:nh], in_=sq.rearrange("p (h d) -> p h d", h=nh), axis=AX.X), reads=[pfx + "sq"], writes=[pfx + "ss"])
        rstd_from_ss(ssq[:, 0:nh], ssq[:, 0:nh], 128, [pfx + "ss"], pfx + "ss")
        for h in range(nh):
            P.op("act", lambda e, h=h: e.activation(out=qn[:, h * 128:(h + 1) * 128], in_=psrc[:, h * 128:(h + 1) * 128], func=AF.Copy, scale=ssq[:, h:h + 1]),
                 reads=[pkey, pfx + "ss"], writes=[pfx + "qn"])
        q3 = qn.rearrange("p (h d) -> p h d", h=nh)
        P.op("dve", lambda e: e.tensor_tensor(out=q3, in0=q3, in1=g_t.unsqueeze(1).broadcast_to([128, nh, 128]), op=ALU.mult),
             reads=[pfx + "qn", "gq_t", "gk_t"], writes=[pfx + "qn"])
        P.op("dve", lambda e: e.tensor_tensor(out=t1.rearrange("p (h d) -> p h d", h=nh), in0=q3, in1=cos2.unsqueeze(1).broadcast_to([128, nh, 128]), op=ALU.mult),
             reads=[pfx + "qn"] + tkeys, writes=[pfx + "t1"])
        q5 = qn.rearrange("p (h a b j) -> p h a b j", h=nh, a=2, b=2)
        t5 = t2.rearrange("p (h a b j) -> p h a b j", h=nh, a=2, b=2)
        s4 = sins.rearrange("p (a b j) -> p a b j", a=2, b=2)
        for b in range(2):
            P.op("dve", lambda e, b=b: e.tensor_tensor(out=t5[:, :, :, b, :], in0=q5[:, :, :, 1 - b, :],
                                                      in1=s4[:, :, b, :].unsqueeze(1).broadcast_to([128, nh, 2, 32]), op=ALU.mult),
                 reads=[pfx + "qn"] + tkeys, writes=[pfx + "t2"])
        P.op("dve", lambda e: e.tensor_tensor(out=outb, in0=t1, in1=t2, op=ALU.add), reads=[pfx + "t1", pfx + "t2"], writes=[okey])

    def norm_transpose(xt, xkey, G, SHv, gkeys, junk, xh, xhkey, hT_dst, hkey, pfx, banks=(0, 1), only_col=None, mask_col=None, coltmp=None):
        P.op("act", lambda e: e.activation(out=junk, in_=xt, func=AF.Square, accum_out=st[:, 0:1]), reads=[xkey], writes=[pfx + "st"])
        rstd_from_ss(st[:, 0:1], st[:, 1:2], D, [pfx + "st"], pfx + "st1")
        P.op("act", lambda e: e.activation(out=xh, in_=xt, func=AF.Copy, scale=st[:, 1:2]), reads=[xkey, pfx + "st1"], writes=[xhkey])
        for g4 in range(4):
            bank = psb[banks[g4 % 2]]
            bk = "ps%d" % banks[g4 % 2]
            for j in range(8):
                kc = g4 * 8 + j
                P.op("pe", lambda e, bank=bank, j=j, kc=kc: e.transpose(out=bank[:, j * 128:(j + 1) * 128], in_=xh[:, kc * 128:(kc + 1) * 128], identity=ident),
                     reads=[xhkey, "ident"], writes=[bk])
            if only_col is None:
                for j in range(8):
                    kc = g4 * 8 + j
                    if g4 % 2 == 0:
                        P.op("act", lambda e, bank=bank, j=j, kc=kc: e.activation(out=hT_dst[:, kc, :], in_=bank[:, j * 128:(j + 1) * 128], func=AF.Identity,
                                                                                 scale=G[:, kc:kc + 1], bias=SHv[:, kc:kc + 1]),
                             reads=[bk] + gkeys, writes=[hkey + "_%d" % kc])
                    else:
                        P.op("dve", lambda e, bank=bank, j=j, kc=kc: e.tensor_scalar(out=hT_dst[:, kc, :], in0=bank[:, j * 128:(j + 1) * 128],
                                                                                    scalar1=G[:, kc:kc + 1], scalar2=SHv[:, kc:kc + 1], op0=ALU.mult, op1=ALU.add),
                             reads=[bk] + gkeys, writes=[hkey + "_%d" % kc])
            else:
                src = bank[:, 0:1024].rearrange("p (j t) -> p j t", j=8)[:, :, only_col]
                sl = slice(g4 * 8, g4 * 8 + 8)
                P.op("dve", lambda e, src=src, sl=sl: e.tensor_tensor(out=coltmp[:, sl], in0=src, in1=G[:, sl], op=ALU.mult), reads=[bk] + gkeys, writes=[pfx + "ct"])
                P.op("dve", lambda e, sl=sl: e.tensor_tensor(out=coltmp[:, sl], in0=coltmp[:, sl], in1=SHv[:, sl], op=ALU.add), reads=[pfx + "ct"] + gkeys, writes=[pfx + "ct"])
                P.op("dve", lambda e, sl=sl: e.tensor_scalar(out=hT_dst[:, sl], in0=coltmp[:, sl], scalar1=mask_col, scalar2=None, op0=ALU.mult),
                     reads=[pfx + "ct", "hmask"], writes=[hkey])

    mTab = sb.mark()
    COSo = sb.alloc([128, NOWN * 64], F32)
    SINo = sb.alloc([128, NOWN * 64], F32)
    mBig = sb.mark()
    COS = sb.alloc([128, NT * 64], F32)
    SIN = sb.alloc([128, NT * 64], F32)
    mR = sb.mark()
    angA = sb.alloc([128, NT * 64], F32)
    angK = sb.alloc([128, NT * 64], F32)
    angI = sb.alloc([128, NT * 64], mybir.dt.int32)
    P.op("dve", lambda e: e.tensor_tensor(out=angA.rearrange("p (m j) -> p m j", j=32), in0=pos.unsqueeze(2).broadcast_to([128, NT * 2, 32]),
                                          in1=freq.unsqueeze(1).broadcast_to([128, NT * 2, 32]), op=ALU.mult),
         reads=["pos", "freq"], writes=["angA"])
    for (dst, dkey, shift) in ((SIN, "SIN", 0.0), (COS, "COS", 0.5 * math.pi)):
        if shift:
            P.op("dve", lambda e, shift=shift: e.tensor_scalar(out=angA, in0=angA, scalar1=shift, scalar2=None, op0=ALU.add), reads=["angA"], writes=["angA"])
        P.op("dve", lambda e: e.tensor_scalar(out=angK, in0=angA, scalar1=1.0 / (2 * math.pi), scalar2=None, op0=ALU.mult), reads=["angA"], writes=["angK"])
        P.op("dve", lambda e: e.tensor_copy(out=angI, in_=angK), reads=["angK"], writes=["angI"])
        P.op("dve", lambda e: e.tensor_copy(out=angK, in_=angI), reads=["angI"], writes=["angK"])
        P.op("dve", lambda e: e.scalar_tensor_tensor(out=angK, in0=angK, scalar=-2.0 * math.pi, in1=angA, op0=ALU.mult, op1=ALU.add), reads=["angK", "angA"], writes=["angK"])
        P.op("dve", lambda e: e.tensor_scalar(out=angK, in0=angK, scalar1=3.1415925, scalar2=-3.1415925, op0=ALU.min, op1=ALU.max), reads=["angK"], writes=["angK"])
        P.op("act", lambda e, dst=dst: e.activation(out=dst, in_=angK, func=AF.Sin), reads=["angK"], writes=[dkey])
    P.op("dve", lambda e: e.tensor_copy(out=COSo, in_=COS[:, 0:NOWN * 64]), reads=["COS"], writes=["COSo"])
    P.op("dve", lambda e: e.tensor_copy(out=SINo, in_=SIN[:, 0:NOWN * 64]), reads=["SIN"], writes=["SINo"])
    P.barrier()
    sb.release(mR)

    def rope_apply(kn, nh, t, outb, t1, t2, rkeys, wkey, tkey, tabs=None):
        Ct, St, ck, sk = (COS, SIN, "COS", "SIN") if tabs is None else tabs
        cs = Ct[:, t * 64:(t + 1) * 64].rearrange("p (a j) -> p a j", a=2).unsqueeze(1).broadcast_to([128, nh, 2, 32])
        sn = St[:, t * 64:(t + 1) * 64].rearrange("p (a j) -> p a j", a=2).unsqueeze(1).broadcast_to([128, nh, 2, 32])
        q5 = kn.rearrange("p (h a b j) -> p h a b j", h=nh, a=2, b=2)
        a5 = t1.rearrange("p (h a b j) -> p h a b j", h=nh, a=2, b=2)
        b5 = t2.rearrange("p (h a b j) -> p h a b j", h=nh, a=2, b=2)
        o5 = outb.rearrange("p (h a b j) -> p h a b j", h=nh, a=2, b=2)
        for b_ in range(2):
            P.op("dve", lambda e, b_=b_: e.tensor_tensor(out=a5[:, :, :, b_, :], in0=q5[:, :, :, b_, :], in1=cs, op=ALU.mult), reads=rkeys + [ck], writes=[tkey + "t1"])
            P.op("dve", lambda e, b_=b_: e.tensor_tensor(out=b5[:, :, :, b_, :], in0=q5[:, :, :, 1 - b_, :], in1=sn, op=ALU.mult), reads=rkeys + [sk], writes=[tkey + "t2"])
        P.op("dve", lambda e: e.tensor_tensor(out=o5[:, :, :, 0, :], in0=a5[:, :, :, 0, :], in1=b5[:, :, :, 0, :], op=ALU.subtract), reads=[tkey + "t1", tkey + "t2"], writes=[wkey])
        P.op("dve", lambda e: e.tensor_tensor(out=o5[:, :, :, 1, :], in0=a5[:, :, :, 1, :], in1=b5[:, :, :, 1, :], op=ALU.add), reads=[tkey + "t1", tkey + "t2"], writes=[wkey])

    mB = sb.mark()
    xt = [sb.alloc([128, D], F32) for _ in range(2)]
    xh = [sb.alloc([128, D], BF16) for _ in range(2)]
    hT = [sb.alloc([128, 32, 128], BF16) for _ in range(2)]
    wkv = sb.alloc([128, 32, 1024], BF16)
    junk = sb.alloc([128, D], BF16)
    kraw = [sb.alloc([128, 512], F32) for _ in range(2)]
    kn = sb.alloc([128, 512], F32)
    kt1 = sb.alloc([128, 512], F32)
    kt2 = sb.alloc([128, 512], F32)
    kr = [sb.alloc([128, 512], BF16) for _ in range(2)]
    kTs = [sb.alloc([128, 4, 128], BF16) for _ in range(2)]
    vaug = [sb.alloc([128, 4, VW], BF16) for _ in range(2)]
    ssx = sb.alloc([128, 4], F32)
    ssk = sb.alloc([128, 8], F32)
    for blk in range(4):
        P.op("pool", lambda e, blk=blk: e.dma_start(out=wkv[:, :, blk * 256:(blk + 1) * 256], in_=w_in_h[24 + blk].rearrange("p (k c) -> p k c", k=32)),
             writes=["wkv"], dma="L:wkv")
    for s_ in range(2):
        P.op("pool", lambda e, s_=s_: e.memset(vaug[s_][:, :, 128:VW], 1.0), writes=["vaug%d" % s_])

    def b_s0(t):
        s_ = t % 2
        src = xr[t * 128:(t + 1) * 128, :] if t < 64 else ctx[(t - 64) * 128:(t - 63) * 128, :]
        ld("sp", xt[s_], src, "xt%d" % s_)

    def b_s1(t):
        s_, c4 = t % 2, t % 4
        P.op("act", lambda e: e.activation(out=junk, in_=xt[s_], func=AF.Square, accum_out=ssx[:, c4:c4 + 1]), reads=["xt%d" % s_], writes=["ssx%d" % c4])
        rstd_from_ss(ssx[:, c4:c4 + 1], ssx[:, c4:c4 + 1], D, ["ssx%d" % c4], "ssx%d" % c4)

    def b_s2(t):
        s_, c4 = t % 2, t % 4
        P.op("act", lambda e: e.activation(out=xh[s_], in_=xt[s_], func=AF.Copy, scale=ssx[:, c4:c4 + 1]), reads=["xt%d" % s_, "ssx%d" % c4], writes=["xh%d" % s_])

    def b_s3(t):
        s_ = t % 2
        G, SHv, gk = (Ga, SHa, ["Ga", "modc"]) if t < 64 else (Gc, SHc, ["Gc", "modx"])
        TB = (0, 1, 4, 5)
        for g4 in range(4):
            bank = psb[TB[g4]]
            bk = "ps%d" % TB[g4]
            for j in range(8):
                kc = g4 * 8 + j
                P.op("pe", lambda e, bank=bank, j=j, kc=kc: e.transpose(out=bank[:, j * 128:(j + 1) * 128], in_=xh[s_][:, kc * 128:(kc + 1) * 128], identity=ident),
                     reads=["xh%d" % s_, "ident"], writes=[bk])
        for g4 in range(4):
            bank = psb[TB[g4]]
            bk = "ps%d" % TB[g4]
            for j in range(8):
                kc = g4 * 8 + j
                if g4 % 2 == 0:
                    P.op("act", lambda e, bank=bank, j=j, kc=kc: e.activation(out=hT[s_][:, kc, :], in_=bank[:, j * 128:(j + 1) * 128], func=AF.Identity,
                                                                             scale=G[:, kc:kc + 1], bias=SHv[:, kc:kc + 1]),
                         reads=[bk] + gk, writes=["hT%d_%d" % (s_, kc)])
                else:
                    P.op("dve", lambda e, bank=bank, j=j, kc=kc: e.tensor_scalar(out=hT[s_][:, kc, :], in0=bank[:, j * 128:(j + 1) * 128],
                                                                                scalar1=G[:, kc:kc + 1], scalar2=SHv[:, kc:kc + 1], op0=ALU.mult, op1=ALU.add),
                         reads=[bk] + gk, writes=["hT%d_%d" % (s_, kc)])
        if t < NOWN:
            P.op("sp", lambda e: e.dma_start(out=hT_scr[t], in_=hT[s_].rearrange("p k c -> p (k c)")),
                 reads=["hT%d_%d" % (s_, kc_) for kc_ in range(32)], writes=["hT_scr%d" % t], dma="S:hT%d" % s_)

    def b_s4(t):
        s_ = t % 2
        for (pp, half) in ((2, 0), (3, 1)):
            for kc in range(32):
                P.op("pe", lambda e, pp=pp, kc=kc, half=half: e.matmul(ps[pp], lhsT=hT[s_][:, kc, :], rhs=wkv[:, kc, half * 512:(half + 1) * 512],
                                                                      start=(kc == 0), stop=(kc == 31)),
                     reads=["hT%d_%d" % (s_, kc), "wkv"], writes=["ps%d" % pp])

    def b_s5(t):
        s_ = t % 2
        pk, pv = ps[2], ps[3]
        pkk, pvk = "ps2", "ps3"
        P.op("act", lambda e: e.activation(out=kraw[s_], in_=pk, func=AF.Copy), reads=[pkk], writes=["kraw%d" % s_])
        P.op("act", lambda e: e.activation(out=vaug[s_][:, :, 0:128], in_=pv.rearrange("p (h d) -> p h d", h=4), func=AF.Copy), reads=[pvk], writes=["vaug%d" % s_])
        P.op("sp", lambda e: e.dma_start(out=V_scr.rearrange("h p (t c) -> p h t c", t=NT)[:, :, t, :], in_=vaug[s_]),
             reads=["vaug%d" % s_], writes=["V_scr"], dma="S:v%d" % s_)
        for h in range(4):
            P.op("act", lambda e, h=h: e.activation(out=junk[:, h * 128:(h + 1) * 128], in_=kraw[s_][:, h * 128:(h + 1) * 128], func=AF.Square,
                                                   accum_out=ssk[:, s_ * 4 + h:s_ * 4 + h + 1]),
                 reads=["kraw%d" % s_], writes=["rk%d" % s_])
        P.op("act", lambda e: e.activation(out=ssk[:, s_ * 4:s_ * 4 + 4], in_=ssk[:, s_ * 4:s_ * 4 + 4], func=AF.Sqrt, scale=1.0 / 128, bias=epsc),
             reads=["rk%d" % s_, "cst"], writes=["rk%d" % s_])

    def b_s6(t):
        s_ = t % 2
        rk = ssk[:, s_ * 4:s_ * 4 + 4]
        P.op("dve", lambda e: e.reciprocal(out=rk, in_=rk), reads=["rk%d" % s_], writes=["rk%d" % s_])
        k3 = kn.rearrange("p (h d) -> p h d", h=4)
        P.op("dve", lambda e: e.tensor_tensor(out=k3, in0=kraw[s_].rearrange("p (h d) -> p h d", h=4), in1=rk.unsqueeze(2).broadcast_to([128, 4, 128]), op=ALU.mult),
             reads=["kraw%d" % s_, "rk%d" % s_], writes=["kn"])
        P.op("dve", lambda e: e.tensor_tensor(out=k3, in0=k3, in1=gk_t.unsqueeze(1).broadcast_to([128, 4, 128]), op=ALU.mult), reads=["kn", "gk_t"], writes=["kn"])
        rope_apply(kn, 4, t, kr[s_], kt1, kt2, ["kn"], "kr%d" % s_, "B")

    def b_s7(t):
        s_ = t % 2
        for h in range(4):
            P.op("pe", lambda e, h=h: e.transpose(out=psb[6][:, h * 128:(h + 1) * 128], in_=kr[s_][:, h * 128:(h + 1) * 128], identity=ident),
                 reads=["kr%d" % s_, "ident"], writes=["ps6"])
        P.op("dve", lambda e: e.tensor_copy(out=kTs[s_], in_=psb[6][:, 0:512].rearrange("p (h t) -> p h t", h=4)), reads=["ps6"], writes=["kTs%d" % s_])
        P.op("sp", lambda e: e.dma_start(out=KT_scr.rearrange("h d n -> d h n")[:, :, t * 128:(t + 1) * 128], in_=kTs[s_]),
             reads=["kTs%d" % s_], writes=["KT_scr"], dma="S:k%d" % s_)

    pipeline(NT, [b_s0, b_s1, b_s2, b_s3, b_s4, b_s5, b_s6, b_s7], [0, 1, 2, 3, 4, 5, 6, 7], order=[5, 3, 7, 6, 4, 2, 1, 0])
    P.barrier()
    sb.release(mBig)

    if stop == "B":
        P.emit(final_waits=None)
        return nc
    qT = sb.alloc([128, 16, NOWN * 128], BF16)
    mC = sb.mark()
    hTo = sb.alloc([128, NG, 32 * 128], BF16)
    wblk = [sb.alloc([128, 32, 256], BF16) for _ in range(2)]
    gv = sb.alloc([128, NG, 2048], BF16)
    lng = sb.alloc([128, 2048], F32)
    lnb = sb.alloc([128, 2048], F32)
    wsT = sb.alloc([128, 16, 128], BF16)
    gtmp = [sb.alloc([128, 256], F32) for _ in range(2)]
    oa = [sb.alloc([128, 256], F32) for _ in range(2)]
    oab = [sb.alloc([128, 256], BF16) for _ in range(2)]
    oTs = [sb.alloc([128, 2, 128], BF16) for _ in range(2)]
    qn = sb.alloc([128, 256], F32)
    qt1 = sb.alloc([128, 256], F32)
    qt2 = sb.alloc([128, 256], F32)
    ssq = [sb.alloc([128, 2], F32) for _ in range(2)]
    qr = [sb.alloc([128, 256], BF16) for _ in range(2)]
    vsum = sb.alloc([128, NOWN * 8], F32)
    vsq = sb.alloc([128, NOWN * 8], F32)
    vst = sb.alloc([128, NOWN * 4], F32)
    junkC = sb.alloc([128, 256], F32)
    ld("sp", lng, bc(lng_row), "lng")
    ld("sp", lnb, bc(lnb_row), "lnb")
    P.op("pool", lambda e: e.dma_start(out=wsT.rearrange("p g c -> p (g c)"), in_=wsT_d), writes=["wsT"], dma="L:wsT")

    nblk = [0]
    wslot = {}

    def load_wblk(blk, tag):
        if (blk, tag) in wslot:
            return
        s_ = nblk[0] % 2
        nblk[0] += 1
        wslot[(blk, tag)] = s_
        P.op("pool", lambda e: e.dma_start(out=wblk[s_].rearrange("p k c -> p (k c)"), in_=w_in_h[blk]), writes=["wblk%d" % s_], dma="L:wblk%d" % s_)

    def run_family(grp, blk0, stages_fn, delays):
        TL = list(range(grp * NG, grp * NG + NG))
        units = [(j, t) for j in range(8) for t in TL]

        def s0(u):
            j, t = units[u]
            load_wblk(blk0 + j, grp)
            if t == TL[1] and j + 1 < 8:
                load_wblk(blk0 + j + 1, grp)
            s_ = wslot[(blk0 + j, grp)]
            bank = u % 3
            for kc in range(32):
                P.op("pe", lambda e, kc=kc: e.matmul(ps[bank][:, 0:256], lhsT=hTo[:, t % NG, kc * 128:(kc + 1) * 128], rhs=wblk[s_][:, kc, :],
                                                     start=(kc == 0), stop=(kc == 31)),
                     reads=["hTo%d" % (t % NG), "wblk%d" % s_], writes=["ps%d" % bank])
        stages = [s0] + stages_fn(units)
        pipeline(len(units), stages, delays)

    def v_stages(units):
        def s1(u):
            j, t = units[u]
            bank, g_ = u % 3, u % 2
            P.op("act", lambda e: e.activation(out=gtmp[g_], in_=ps[bank][:, 0:256], func=AF.Gelu, accum_out=vsum[:, t * 8 + j:t * 8 + j + 1]),
                 reads=["ps%d" % bank], writes=["gtmp%d" % g_, "vsum%d" % (t * 8 + j)])
            P.op("act", lambda e: e.activation(out=junkC, in_=gtmp[g_], func=AF.Square, accum_out=vsq[:, t * 8 + j:t * 8 + j + 1]),
                 reads=["gtmp%d" % g_], writes=["vsq%d" % (t * 8 + j)])

        def s2(u):
            j, t = units[u]
            g_ = u % 2
            P.op("dve", lambda e: e.tensor_copy(out=gv[:, t % NG, j * 256:(j + 1) * 256], in_=gtmp[g_]), reads=["gtmp%d" % g_], writes=["gv%d" % (t % NG)])
        return [s1, s2]

    def u_stages(units):
        def s0b(u):
            j, t = units[u]
            mb = 3 + u % 3
            for gi in range(2):
                g = 2 * j + gi
                P.op("pe", lambda e, gi=gi, g=g: e.matmul(ps[mb][:, gi * 128:(gi + 1) * 128], lhsT=wsT[:, g, :], rhs=gv[:, t % NG, g * 128:(g + 1) * 128], start=True, stop=True),
                     reads=["wsT", "gv%d" % (t % NG)], writes=["ps%d" % mb])

        def s1(u):
            bank, g_ = u % 3, u % 2
            P.op("act", lambda e: e.activation(out=gtmp[g_], in_=ps[bank][:, 0:256], func=AF.Gelu), reads=["ps%d" % bank], writes=["gtmp%d" % g_])

        def s2(u):
            j, t = units[u]
            mb, g_ = 3 + u % 3, u % 2
            for gi in range(2):
                g = 2 * j + gi
                P.op("dve", lambda e, gi=gi, g=g: e.scalar_tensor_tensor(out=oa[g_][:, gi * 128:(gi + 1) * 128], in0=ps[mb][:, gi * 128:(gi + 1) * 128],
                                                                        scalar=bs_col[:, g:g + 1], in1=gtmp[g_][:, gi * 128:(gi + 1) * 128], op0=ALU.add, op1=ALU.mult),
                     reads=["ps%d" % mb, "bs_col", "gtmp%d" % g_], writes=["oa%d" % g_])
            P.op("dve", lambda e: e.tensor_copy(out=oab[g_], in_=oa[g_]), reads=["oa%d" % g_], writes=["oab%d" % g_])

        def s3(u):
            j, t = units[u]
            g_ = u % 2
            tb = 6 + u % 2
            for gi in range(2):
                P.op("pe", lambda e, gi=gi: e.transpose(out=psb[tb][:, gi * 128:(gi + 1) * 128], in_=oab[g_][:, gi * 128:(gi + 1) * 128], identity=ident),
                     reads=["oab%d" % g_, "ident"], writes=["ps%d" % tb])
            col = t * 8 + j
            P.op("act", lambda e: e.activation(out=junkC, in_=oa[g_], func=AF.Square, accum_out=ssA[:, col:col + 1]), reads=["oa%d" % g_], writes=["ssA%d" % col])

        def s4(u):
            j, t = units[u]
            g_ = u % 2
            tb = 6 + u % 2
            for gi in range(2):
                kc = 2 * j + gi
                P.op("act", lambda e, gi=gi, kc=kc: e.activation(out=oTs[g_][:, gi, :], in_=psb[tb][:, gi * 128:(gi + 1) * 128], func=AF.Copy, scale=goutc[:, kc:kc + 1]),
                     reads=["ps%d" % tb, "goutc"], writes=["oTs%d" % g_])
            P.op("sp", lambda e: e.dma_start(out=oT_scr[2 * j:2 * j + 2].rearrange("k p n -> p k n")[:, :, t * 128:(t + 1) * 128], in_=oTs[g_]),
                 reads=["oTs%d" % g_], writes=["oT_scr"], dma="S:oT%d" % g_)
        return [s0b, s1, s2, s3, s4]

    def q_stages(units):
        def s1(u):
            bank, g_ = u % 3, u % 2
            for h in range(2):
                P.op("act", lambda e, h=h: e.activation(out=junkC[:, h * 128:(h + 1) * 128], in_=ps[bank][:, h * 128:(h + 1) * 128], func=AF.Square, accum_out=ssq[g_][:, h:h + 1]),
                     reads=["ps%d" % bank], writes=["ssq%d" % g_])
            P.op("act", lambda e: e.activation(out=ssq[g_], in_=ssq[g_], func=AF.Sqrt, scale=1.0 / 128, bias=epsc), reads=["ssq%d" % g_, "cst"], writes=["ssq%d" % g_])

        def s2(u):
            j, t = units[u]
            bank, g_ = u % 3, u % 2
            P.op("dve", lambda e: e.reciprocal(out=ssq[g_], in_=ssq[g_]), reads=["ssq%d" % g_], writes=["ssq%d" % g_])
            q3 = qn.rearrange("p (h d) -> p h d", h=2)
            P.op("dve", lambda e: e.tensor_tensor(out=q3, in0=ps[bank][:, 0:256].rearrange("p (h d) -> p h d", h=2), in1=ssq[g_].unsqueeze(2).broadcast_to([128, 2, 128]), op=ALU.mult),
                 reads=["ps%d" % bank, "ssq%d" % g_], writes=["Cqn"])
            P.op("dve", lambda e: e.tensor_tensor(out=q3, in0=q3, in1=gq_t.unsqueeze(1).broadcast_to([128, 2, 128]), op=ALU.mult), reads=["Cqn", "gq_t"], writes=["Cqn"])
            rope_apply(qn, 2, t, qr[g_], qt1, qt2, ["Cqn"], "qr%d" % g_, "Cq", tabs=(COSo, SINo, "COSo", "SINo"))

        def s3(u):
            g_ = u % 2
            tb = 6 + u % 2
            for gi in range(2):
                P.op("pe", lambda e, gi=gi: e.transpose(out=psb[tb][:, gi * 128:(gi + 1) * 128], in_=qr[g_][:, gi * 128:(gi + 1) * 128], identity=ident),
                     reads=["qr%d" % g_, "ident"], writes=["ps%d" % tb])

        def s4(u):
            j, t = units[u]
            tb = 6 + u % 2
            P.op("act", lambda e: e.activation(out=qT[:, 2 * j:2 * j + 2, t * 128:(t + 1) * 128], in_=psb[tb][:, 0:256].rearrange("p (h n) -> p h n", h=2), func=AF.Copy),
                 reads=["ps%d" % tb], writes=["qT"])
        return [s1, s2, s3, s4]

    mean = vst[:, 2 * NOWN:3 * NOWN]
    var = vst[:, 3 * NOWN:4 * NOWN]
    for grp in range(NOWN // NG):
        TL = list(range(grp * NG, grp * NG + NG))
        for t in TL:
            ld("sp", hTo[:, t % NG, :], hT_scr[t], "hTo%d" % (t % NG), reads=["hT_scr%d" % t])
        run_family(grp, 8, v_stages, [0, 1, 2])
        allv = ["vsum%d" % c_ for c_ in range(NOWN * 8)] + ["vsq%d" % c_ for c_ in range(NOWN * 8)]
        P.op("dve", lambda e: e.reduce_sum(out=vst[:, 0:NOWN], in_=vsum.rearrange("p (t j) -> p t j", j=8), axis=AX.X), reads=allv, writes=["vst"])
        P.op("dve", lambda e: e.reduce_sum(out=vst[:, NOWN:2 * NOWN], in_=vsq.rearrange("p (t j) -> p t j", j=8), axis=AX.X), reads=allv, writes=["vst"])
        P.op("dve", lambda e: e.tensor_scalar(out=mean, in0=vst[:, 0:NOWN], scalar1=1.0 / 2048, scalar2=None, op0=ALU.mult), reads=["vst"], writes=["vmean"])
        P.op("dve", lambda e: e.tensor_tensor(out=var, in0=mean, in1=mean, op=ALU.mult), reads=["vmean"], writes=["vvar"])
        P.op("dve", lambda e: e.scalar_tensor_tensor(out=var, in0=vst[:, NOWN:2 * NOWN], scalar=1.0 / 2048, in1=var, op0=ALU.mult, op1=ALU.subtract),
             reads=["vst", "vvar"], writes=["vvar"])
        P.op("act", lambda e: e.activation(out=var, in_=var, func=AF.Sqrt, bias=epsc), reads=["vvar", "cst"], writes=["vvar"])
        P.op("dve", lambda e: e.reciprocal(out=var, in_=var), reads=["vvar"], writes=["vvar"])
        for t in TL:
            P.op("dve", lambda e, t=t: e.tensor_scalar(out=gv[:, t % NG, :], in0=gv[:, t % NG, :], scalar1=mean[:, t:t + 1], scalar2=var[:, t:t + 1], op0=ALU.subtract, op1=ALU.mult),
                 reads=["gv%d" % (t % NG), "vmean", "vvar"], writes=["gv%d" % (t % NG)])
            P.op("dve", lambda e, t=t: e.tensor_tensor(out=gv[:, t % NG, :], in0=gv[:, t % NG, :], in1=lng, op=ALU.mult), reads=["gv%d" % (t % NG), "lng"], writes=["gv%d" % (t % NG)])
            P.op("dve", lambda e, t=t: e.tensor_tensor(out=gv[:, t % NG, :], in0=gv[:, t % NG, :], in1=lnb, op=ALU.add), reads=["gv%d" % (t % NG), "lnb"], writes=["gv%d" % (t % NG)])
        run_family(grp, 0, u_stages, [0, 0, 1, 2, 3, 4])
        run_family(grp, 16, q_stages, [0, 1, 2, 3, 4])
    allssA = ["ssA%d" % c_ for c_ in range(NOWN * 8)]
    P.barrier()
    sb.release(mC)

    if stop == "C":
        dbgC = nc.dram_tensor("dbgC", [128, 16 * NOWN * 128], BF16, kind="ExternalOutput").ap()
        P.op("sp", lambda e: e.dma_start(out=dbgC, in_=qT.rearrange("p h n -> p (h n)")), reads=["qT"], dma="dbg")
        dbgC2 = nc.dram_tensor("dbgC2", [128, NOWN * 8], F32, kind="ExternalOutput").ap()
        P.op("sp", lambda e: e.dma_start(out=dbgC2, in_=ssA), reads=allssA, dma="dbg")
        P.emit(final_waits=None)
        return nc
    mD = sb.mark()
    KTh = [sb.alloc([128, NKEY], BF16) for _ in range(2)]
    Vh = [sb.alloc([128, NT, VW], BF16) for _ in range(2)]
    NPT = 3
    PT = [sb.alloc([128, 512], BF16) for _ in range(NPT)]
    ob = [[sb.alloc([128, 128], F32) for _ in range(4)] for _ in range(2)]
    obb = [sb.alloc([128, 128], BF16) for _ in range(4)]
    rden = sb.alloc([128, 4], F32)
    obT = [sb.alloc([128, 512], BF16) for _ in range(2)]
    junkD = sb.alloc([128, 128], F32)
    SCALE = 1.0 / math.sqrt(128.0)
    QB = [(0, 512), (512, 512), (1024, 256)]
    blocks = []
    for kvh in range(4):
        for hh in range(4):
            for (q0, nq) in QB:
                blocks.append((kvh, kvh * 4 + hh, q0, nq))
    units = [(bi, kt) for bi in range(len(blocks)) for kt in range(NT)]
    loaded = set()

    def load_kv(kvh):
        if kvh in loaded or kvh >= 4:
            return
        loaded.add(kvh)
        s_ = kvh % 2
        ld("sp", KTh[s_], KT_scr[kvh], "KTh%d" % s_, reads=["KT_scr"])
        ld("sp", Vh[s_].rearrange("p t c -> p (t c)"), V_scr[kvh], "Vh%d" % s_, reads=["V_scr"])

    def st_S(u):
        bi, kt = units[u]
        kvh, head, q0, nq = blocks[bi]
        load_kv(kvh)
        s_ = kvh % 2
        sbk = u % 3
        P.op("pe", lambda e: e.matmul(ps[sbk][:, 0:nq], lhsT=KTh[s_][:, kt * 128:(kt + 1) * 128], rhs=qT[:, head, q0:q0 + nq], start=True, stop=True),
             reads=["KTh%d" % s_, "qT"], writes=["ps%d" % sbk])

    def st_exp(u):
        bi, kt = units[u]
        kvh, head, q0, nq = blocks[bi]
        sbk = u % 3
        pk = u % NPT
        P.op("act", lambda e: e.activation(out=PT[pk][:, 0:nq], in_=ps[sbk][:, 0:nq], func=AF.Exp, scale=SCALE),
             reads=["ps%d" % sbk], writes=["PT%d" % pk])

    def epi_dve(bi):
        kvh, head, q0, nq = blocks[bi]
        nsub = nq // 128
        so = bi % 2
        for qs in range(nsub):
            pb, pbk = ps[3 + qs], "ps%d" % (3 + qs)
            P.op("dve", lambda e, pb=pb, qs=qs: e.reciprocal(out=rden[:, qs:qs + 1], in_=pb[:, 128:129]), reads=[pbk], writes=["rden%d" % qs])
            P.op("dve", lambda e, pb=pb, qs=qs: e.tensor_scalar(out=ob[so][qs], in0=pb[:, 0:128], scalar1=rden[:, qs:qs + 1], scalar2=None, op0=ALU.mult),
                 reads=[pbk, "rden%d" % qs], writes=["ob%d_%d" % (so, qs)])
        for qs in range(nsub):
            P.op("dve", lambda e, qs=qs: e.tensor_copy(out=obb[qs], in_=ob[so][qs]), reads=["ob%d_%d" % (so, qs)], writes=["obb%d" % qs])
        for qs in range(nsub):
            P.op("pe", lambda e, qs=qs: e.transpose(out=psb[7][:, qs * 128:(qs + 1) * 128], in_=obb[qs], identity=ident), reads=["obb%d" % qs, "ident"], writes=["ps7"])
        P.op("dve", lambda e: e.tensor_scalar(out=obT[so][:, 0:nq], in0=psb[7][:, 0:nq], scalar1=goutc[:, 16 + head:17 + head], scalar2=None, op0=ALU.mult),
             reads=["ps7", "goutc"], writes=["obT%d" % so])
        P.op("sp", lambda e: e.dma_start(out=oT_scr[16 + head, :, q0:q0 + nq], in_=obT[so][:, 0:nq]), reads=["obT%d" % so], writes=["oT_scr"], dma="S:obT%d" % so)

    def epi_act(bi):
        kvh, head, q0, nq = blocks[bi]
        so = bi % 2
        for qs in range(nq // 128):
            t = q0 // 128 + qs
            col = t * 16 + head
            P.op("act", lambda e, qs=qs, col=col: e.activation(out=junkD, in_=ob[so][qs], func=AF.Square, accum_out=ssB[:, col:col + 1]),
                 reads=["ob%d_%d" % (so, qs)], writes=["ssB%d" % col])

    def st_PV(u):
        bi, kt = units[u]
        kvh, head, q0, nq = blocks[bi]
        s_ = kvh % 2
        pk = u % NPT
        for qs in range(nq // 128):
            P.op("pe", lambda e, qs=qs: e.matmul(ps[3 + qs][:, 0:129], lhsT=PT[pk][:, qs * 128:(qs + 1) * 128], rhs=Vh[s_][:, kt, 0:129],
                                                 start=(kt == 0), stop=(kt == NT - 1)),
                 reads=["PT%d" % pk, "Vh%d" % s_], writes=["ps%d" % (3 + qs)])
        if kt == NT - 1:
            epi_dve(bi)
        if kt == 8 and bi > 0:
            epi_act(bi - 1)
        if kt == 0 and bi % 12 == 1:
            load_kv(kvh + 1)

    pipeline(len(units), [st_S, st_exp, st_PV], [0, 1, 3])
    epi_act(len(blocks) - 1)
    allssB = ["ssB%d" % c_ for c_ in range(NOWN * 16)]
    P.barrier()
    sb.release(mD)
    sb.release(mC)
    sb.release(mTab)

    if stop == "D":
        dbgD = nc.dram_tensor("dbgD", [128, NOWN * 16], F32, kind="ExternalOutput").ap()
        P.op("sp", lambda e: e.dma_start(out=dbgD, in_=ssB), reads=allssB, dma="dbg")
        P.emit(final_waits=None)
        return nc
    mE = sb.mark()
    oT = sb.alloc([128, 32, NOWN * 128], BF16)
    wo = [sb.alloc([128, 32, 512], BF16) for _ in range(2)]
    osb = [sb.alloc([128, 512], F32) for _ in range(2)]
    otmp = sb.alloc([128, 512], F32)
    junkE = sb.alloc([128, 512], F32)
    for kc in range(32):
        ld("sp", oT[:, kc, :], oT_scr[kc], "oT", reads=["oT_scr"], writes=["oT"])
    P.op("dve", lambda e: e.reduce_sum(out=rA, in_=ssA.rearrange("p (t j) -> p t j", j=8), axis=AX.X), reads=allssA, writes=["rA"])
    P.op("dve", lambda e: e.reduce_sum(out=rB, in_=ssB.rearrange("p (t j) -> p t j", j=16), axis=AX.X), reads=allssB, writes=["rB"])
    rstd_from_ss(rA, rA, 2048, ["rA"], "rA")
    rstd_from_ss(rB, rB, 2048, ["rB"], "rB")
    nE = 0
    for cbk in range(8):
        s_ = cbk % 2
        P.op("pool", lambda e, s_=s_, cbk=cbk: e.dma_start(out=wo[s_].rearrange("p k c -> p (k c)"), in_=w_out_h[cbk]), writes=["wo%d" % s_], dma="L:wo%d" % s_)
        for t in range(NOWN):
            pa, pbn = nE % 2, 2 + (nE % 2)
            so = nE % 2
            nE += 1
            for kc in range(16):
                P.op("pe", lambda e, pa=pa, kc=kc, t=t, s_=s_: e.matmul(ps[pa], lhsT=oT[:, kc, t * 128:(t + 1) * 128], rhs=wo[s_][:, kc, :], start=(kc == 0), stop=(kc == 15)),
                     reads=["oT", "wo%d" % s_], writes=["ps%d" % pa])
            for kc in range(16, 32):
                P.op("pe", lambda e, pbn=pbn, kc=kc, t=t, s_=s_: e.matmul(ps[pbn], lhsT=oT[:, kc, t * 128:(t + 1) * 128], rhs=wo[s_][:, kc, :], start=(kc == 16), stop=(kc == 31)),
                     reads=["oT", "wo%d" % s_], writes=["ps%d" % pbn])
            P.op("act", lambda e, pa=pa, t=t: e.activation(out=otmp, in_=ps[pa], func=AF.Copy, scale=rA[:, t:t + 1]), reads=["ps%d" % pa, "rA"], writes=["otmp"])
            P.op("dve", lambda e, pbn=pbn, t=t, so=so: e.scalar_tensor_tensor(out=osb[so], in0=ps[pbn], scalar=rB[:, t:t + 1], in1=otmp, op0=ALU.mult, op1=ALU.add),
                 reads=["ps%d" % pbn, "rB", "otmp"], writes=["osb%d" % so])
            P.op("act", lambda e, so=so, t=t, cbk=cbk: e.activation(out=junkE, in_=osb[so], func=AF.Square, accum_out=ssO[:, t * 8 + cbk:t * 8 + cbk + 1]),
                 reads=["osb%d" % so], writes=["ssO"])
            P.op("sp", lambda e, so=so, t=t, cbk=cbk: e.dma_start(out=o_scr[t * 128:(t + 1) * 128, cbk * 512:(cbk + 1) * 512], in_=osb[so]),
                 reads=["osb%d" % so], writes=["o_scr"], dma="S:osb%d" % so)
    P.op("dve", lambda e: e.reduce_sum(out=rO, in_=ssO.rearrange("p (t j) -> p t j", j=8), axis=AX.X), reads=["ssO"], writes=["rO"])
    rstd_from_ss(rO, rO, D, ["rO"], "rO")
    P.barrier()
    sb.release(mE)

    if stop == "E":
        dbgE = nc.dram_tensor("dbgE", [128, NOWN], F32, kind="ExternalOutput").ap()
        P.op("sp", lambda e: e.dma_start(out=dbgE, in_=rO), reads=["rO"], dma="dbg")
        P.emit(final_waits=None)
        return nc
    out_ops = []
    for blk in range(2):
        mF = sb.mark()
        HTF_BYTES = 32 * 514 * 2
        HTF_OFF = (SB.LIMIT - HTF_BYTES) // 64 * 64
        hTf = nc.alloc_sbuf_tensor_at("hTf%d" % blk, [128, 32, 514], BF16, offset=HTF_OFF).ap()
        mF2 = sb.mark()
        orow = [sb.alloc([128, D], F32) for _ in range(2)]
        xrow = [sb.alloc([128, D], F32) for _ in range(2)]
        xm = [sb.alloc([128, D], F32) for _ in range(2)]
        ggrow = sb.alloc([128, D], F32)
        junkF = sb.alloc([128, D], BF16)
        xhF = [sb.alloc([128, D], BF16) for _ in range(2)]
        coltmp = sb.alloc([128, 32], F32)
        stF = sb.alloc([128, 2], F32)
        ld("sp", ggrow, gg_scr[0], "ggrow", reads=["gg_scr"])

        def f_s0(ti):
            t, s_ = 4 * blk + ti, ti % 2
            ld("sp", orow[s_], o_scr[t * 128:(t + 1) * 128, :], "orow%d" % s_, reads=["o_scr"])
            ld("sp", xrow[s_], xr[t * 128:(t + 1) * 128, :], "xrow%d" % s_)

        def f_s1(ti):
            t, s_ = 4 * blk + ti, ti % 2
            P.op("dve", lambda e: e.scalar_tensor_tensor(out=xm[s_], in0=orow[s_], scalar=rO[:, t:t + 1], in1=ggrow, op0=ALU.mult, op1=ALU.mult),
                 reads=["orow%d" % s_, "rO", "ggrow"], writes=["xm%d" % s_])
            P.op("dve", lambda e: e.tensor_tensor(out=xm[s_], in0=xm[s_], in1=xrow[s_], op=ALU.add), reads=["xm%d" % s_, "xrow%d" % s_], writes=["xm%d" % s_])
            if 1 <= ti <= 4:
                own = t - 1
                P.op("sp", lambda e: e.dma_start(out=xm_scr[own * 128:(own + 1) * 128, :], in_=xm[s_]), reads=["xm%d" % s_], writes=["xm_scr"], dma="S:xm%d" % s_)

        def f_s2(ti):
            s_ = ti % 2
            P.op("act", lambda e: e.activation(out=junkF, in_=xm[s_], func=AF.Square, accum_out=stF[:, s_:s_ + 1]), reads=["xm%d" % s_], writes=["stF%d" % s_])
            rstd_from_ss(stF[:, s_:s_ + 1], stF[:, s_:s_ + 1], D, ["stF%d" % s_], "stF%d" % s_)

        def f_s3(ti):
            s_ = ti % 2
            P.op("act", lambda e: e.activation(out=xhF[s_], in_=xm[s_], func=AF.Copy, scale=stF[:, s_:s_ + 1]), reads=["xm%d" % s_, "stF%d" % s_], writes=["xhF%d" % s_])

        def f_s4(ti):
            s_ = ti % 2
            TB = (0, 1, 2, 3)
            for g4 in range(4):
                for j in range(8):
                    kc = g4 * 8 + j
                    P.op("pe", lambda e, g4=g4, j=j, kc=kc: e.transpose(out=psb[TB[g4]][:, j * 128:(j + 1) * 128], in_=xhF[s_][:, kc * 128:(kc + 1) * 128], identity=ident),
                         reads=["xhF%d" % s_, "ident"], writes=["ps%d" % TB[g4]])
            if 1 <= ti <= 4:
                for g4 in range(4):
                    bank, bk = psb[TB[g4]], "ps%d" % TB[g4]
                    for j in range(8):
                        kc = g4 * 8 + j
                        dst = hTf[:, kc, (ti - 1) * 128:ti * 128]
                        if g4 % 2 == 0:
                            P.op("act", lambda e, bank=bank, j=j, kc=kc, dst=dst: e.activation(out=dst, in_=bank[:, j * 128:(j + 1) * 128], func=AF.Identity,
                                                                                              scale=Gf[:, kc:kc + 1], bias=SHf[:, kc:kc + 1]),
                                 reads=[bk, "Gf", "modc"], writes=["hTf_%d_%d" % (ti, kc)])
                        else:
                            P.op("dve", lambda e, bank=bank, j=j, kc=kc, dst=dst: e.tensor_scalar(out=dst, in0=bank[:, j * 128:(j + 1) * 128],
                                                                                                 scalar1=Gf[:, kc:kc + 1], scalar2=SHf[:, kc:kc + 1], op0=ALU.mult, op1=ALU.add),
                                 reads=[bk, "Gf", "modc"], writes=["hTf_%d_%d" % (ti, kc)])
            else:
                col = 127 if ti == 0 else 0
                dstc = 512 if ti == 0 else 513
                if blk == 0 and ti == 0:
                    mcol = hmask[:, 0:1]
                elif blk == 1 and ti == 5:
                    mcol = hmask[:, 1:2]
                else:
                    mcol = cst[:, 3:4]
                for g4 in range(4):
                    bank, bk = psb[TB[g4]], "ps%d" % TB[g4]
                    src = bank[:, 0:1024].rearrange("p (j t) -> p j t", j=8)[:, :, col]
                    sl = slice(g4 * 8, g4 * 8 + 8)
                    P.op("dve", lambda e, src=src, sl=sl: e.tensor_tensor(out=coltmp[:, sl], in0=src, in1=Gf[:, sl], op=ALU.mult), reads=[bk, "Gf"], writes=["Fct"])
                    P.op("dve", lambda e, sl=sl: e.tensor_tensor(out=coltmp[:, sl], in0=coltmp[:, sl], in1=SHf[:, sl], op=ALU.add), reads=["Fct", "modc"], writes=["Fct"])
                    P.op("dve", lambda e, sl=sl, dstc=dstc, mcol=mcol: e.tensor_scalar(out=hTf[:, sl, dstc], in0=coltmp[:, sl], scalar1=mcol, scalar2=None, op0=ALU.mult),
                         reads=["Fct", "hmask", "cst"], writes=["hTf_h%d_%d" % (ti, g4)])

        pipeline(6, [f_s0, f_s1, f_s2, f_s3, f_s4], [0, 1, 2, 3, 4])
        assert sb.off <= HTF_OFF, sb.off
        hTf_keys = ["hTf_%d_%d" % (ti, kc) for ti in range(1, 5) for kc in range(32)] + ["hTf_h%d_%d" % (ti, g4) for ti in (0, 5) for g4 in range(4)]
        P.barrier()
        sb.release(mF2)
        act = sb.alloc([128, NFC, 512], BF16)
        mG = sb.mark()
        wg = [sb.alloc([128, 32, 128], BF16) for _ in range(2)]
        wu = [sb.alloc([128, 32, 128], BF16) for _ in range(2)]
        ag = sb.alloc([128, 514], F32)
        au = sb.alloc([128, 514], F32)
        cg = sb.alloc([128, 512], F32)
        cu = sb.alloc([128, 512], F32)
        sg = sb.alloc([128, 512], F32)
        cp4 = convp.rearrange("p (c f) -> p c f", f=4)
        for i in range(NFC):
            s_ = i % 2
            P.op("pool", lambda e, s_=s_, i=i: e.dma_start(out=wg[s_].rearrange("p k c -> p (k c)"), in_=w_up_g[i]), writes=["wg%d" % s_], dma="L:wg%d" % s_)
            P.op("pool", lambda e, s_=s_, i=i: e.dma_start(out=wu[s_].rearrange("p k c -> p (k c)"), in_=w_up_u[i]), writes=["wu%d" % s_], dma="L:wu%d" % s_)
            for (wsrc, wkey, pm, ph, abuf, akey, cbuf, ckey, ci) in ((wg[s_], "wg%d" % s_, s_, 4, ag, "ag", cg, "cg", i),
                                                                   (wu[s_], "wu%d" % s_, 2 + s_, 5, au, "au", cu, "cu", NFC + i)):
                for kc in range(32):
                    P.op("pe", lambda e, pm=pm, kc=kc, wsrc=wsrc: e.matmul(ps[pm], lhsT=wsrc[:, kc, :], rhs=hTf[:, kc, 0:512], start=(kc == 0), stop=(kc == 31)),
                         reads=[wkey], writes=["ps%d" % pm])
                for kc in range(32):
                    P.op("pe", lambda e, ph=ph, kc=kc, wsrc=wsrc: e.matmul(ps[ph][:, 0:2], lhsT=wsrc[:, kc, :], rhs=hTf[:, kc, 512:514], start=(kc == 0), stop=(kc == 31)),
                         reads=[wkey], writes=["ps%d" % ph])
                P.op("act", lambda e, pm=pm, abuf=abuf: e.activation(out=abuf[:, 1:513], in_=ps[pm], func=AF.Copy), reads=["ps%d" % pm], writes=[akey])
                P.op("dve", lambda e, ph=ph, abuf=abuf: e.tensor_copy(out=abuf[:, 0:1], in_=ps[ph][:, 0:1]), reads=["ps%d" % ph], writes=[akey])
                P.op("dve", lambda e, ph=ph, abuf=abuf: e.tensor_copy(out=abuf[:, 513:514], in_=ps[ph][:, 1:2]), reads=["ps%d" % ph], writes=[akey])
                P.op("dve", lambda e, abuf=abuf, cbuf=cbuf, ci=ci: e.tensor_scalar(out=cbuf, in0=abuf[:, 0:512], scalar1=cp4[:, ci, 0:1], scalar2=cp4[:, ci, 3:4], op0=ALU.mult, op1=ALU.add),
                     reads=[akey, "convp"], writes=[ckey])
                P.op("dve", lambda e, abuf=abuf, cbuf=cbuf, ci=ci: e.scalar_tensor_tensor(out=cbuf, in0=abuf[:, 1:513], scalar=cp4[:, ci, 1:2], in1=cbuf, op0=ALU.mult, op1=ALU.add),
                     reads=[akey, "convp", ckey], writes=[ckey])
                P.op("dve", lambda e, abuf=abuf, cbuf=cbuf, ci=ci: e.scalar_tensor_tensor(out=cbuf, in0=abuf[:, 2:514], scalar=cp4[:, ci, 2:3], in1=cbuf, op0=ALU.mult, op1=ALU.add),
                     reads=[akey, "convp", ckey], writes=[ckey])
            P.op("act", lambda e: e.activation(out=sg, in_=cg, func=AF.Silu), reads=["cg"], writes=["sg"])
            P.op("dve", lambda e, i=i: e.tensor_tensor(out=act[:, i, :], in0=sg, in1=cu, op=ALU.mult), reads=["sg", "cu"], writes=["act"])
        assert sb.off <= HTF_OFF, sb.off
        P.barrier()
        sb.release(mG)
        KG = 8
        groups = [(k0, min(k0 + KG, NFC)) for k0 in range(0, NFC, KG)]
        wd = [sb.alloc([128, KG, 512], BF16) for _ in range(3)]
        fsb = [sb.alloc([128, 512], F32) for _ in range(2)]
        junkG = sb.alloc([128, 512], F32)
        nwd = 0
        nf = 0
        for cbk in range(8):
            base = 4 * (cbk % 2)
            for (k0, k1) in groups:
                s_ = nwd % 3
                nwd += 1
                P.op("pool", lambda e, s_=s_, cbk=cbk, k0=k0, k1=k1: e.dma_start(out=wd[s_][:, 0:k1 - k0, :].rearrange("p k c -> p (k c)"),
                                                                               in_=w_down_h[cbk][:, k0 * 512:k1 * 512]),
                     writes=["wd%d" % s_], dma="L:wd%d" % s_)
                for kc in range(k0, k1):
                    for q in range(4):
                        P.op("pe", lambda e, base=base, q=q, kc=kc, k0=k0, s_=s_: e.matmul(ps[base + q], lhsT=act[:, kc, q * 128:(q + 1) * 128], rhs=wd[s_][:, kc - k0, :],
                                                                                       start=(kc == 0), stop=(kc == NFC - 1)),
                             reads=["act", "wd%d" % s_], writes=["ps%d" % (base + q)])
            for q in range(4):
                so = nf % 2
                nf += 1
                own = 4 * blk + q
                P.op("act", lambda e, base=base, q=q, so=so: e.activation(out=fsb[so], in_=ps[base + q], func=AF.Copy), reads=["ps%d" % (base + q)], writes=["fsb%d" % so])
                P.op("act", lambda e, so=so, q=q, cbk=cbk: e.activation(out=junkG, in_=fsb[so], func=AF.Square, accum_out=ssF[:, q * 8 + cbk:q * 8 + cbk + 1]),
                     reads=["fsb%d" % so], writes=["ssF"])
                P.op("sp", lambda e, so=so, own=own, cbk=cbk: e.dma_start(out=f_scr[own * 128:(own + 1) * 128, cbk * 512:(cbk + 1) * 512], in_=fsb[so]),
                     reads=["fsb%d" % so], writes=["f_scr"], dma="S:fsb%d" % so)
        P.op("dve", lambda e: e.reduce_sum(out=rF[:, 0:4], in_=ssF[:, 0:32].rearrange("p (t j) -> p t j", j=8), axis=AX.X), reads=["ssF"], writes=["rF"])
        rstd_from_ss(rF[:, 0:4], rF[:, 0:4], D, ["rF"], "rF")
        frow = [sb.alloc([128, D], F32) for _ in range(2)]
        xmrow = [sb.alloc([128, D], F32) for _ in range(2)]
        ggf = sb.alloc([128, D], F32)
        ld("sp", ggf, gg_scr[1], "ggf", reads=["gg_scr"])

        def z_s0(q):
            own, s_ = 4 * blk + q, q % 2
            ld("sp", frow[s_], f_scr[own * 128:(own + 1) * 128, :], "frow%d" % s_, reads=["f_scr"])
            ld("sp", xmrow[s_], xm_scr[own * 128:(own + 1) * 128, :], "xmrow%d" % s_, reads=["xm_scr"])

        def z_s1(q):
            own, s_ = 4 * blk + q, q % 2
            P.op("dve", lambda e: e.scalar_tensor_tensor(out=frow[s_], in0=frow[s_], scalar=rF[:, q:q + 1], in1=ggf, op0=ALU.mult, op1=ALU.mult),
                 reads=["frow%d" % s_, "rF", "ggf"], writes=["frow%d" % s_])
            P.op("dve", lambda e: e.tensor_tensor(out=frow[s_], in0=frow[s_], in1=xmrow[s_], op=ALU.add), reads=["frow%d" % s_, "xmrow%d" % s_], writes=["frow%d" % s_])
            o = P.op("sp", lambda e: e.dma_start(out=out[own * 128:(own + 1) * 128, :], in_=frow[s_]), reads=["frow%d" % s_], dma="S:out%d" % s_)
            out_ops.append(o.idx)

        pipeline(4, [z_s0, z_s1], [0, 1])
        P.barrier()
        sb.release(mF)

    P.emit(final_waits=out_ops)
    return nc


_CACHE = {}


def _host_layouts(inp):
    f = np.float32
    A = {}

    def col(v):
        v = np.asarray(v, f).reshape(-1, 128)
        return np.ascontiguousarray(v.T)

    A["ctx"] = np.ascontiguousarray(inp["ctx"][0], dtype=f)
    A["jidx"] = np.ascontiguousarray(np.broadcast_to(np.arange(32, dtype=f)[None, :], (128, 32)))
    A["cvec"] = np.concatenate([col(inp["c"][0]), col(inp["c_ctx"])], axis=1)
    A["w_ada_a"] = np.ascontiguousarray(inp["w_ada"][0][:, :3 * D], dtype=f)
    A["w_ada_b"] = np.ascontiguousarray(inp["w_ada"][0][:, 3 * D:], dtype=f)
    A["b_ada_col"] = col(inp["b_ada"][0])
    A["b_ada_row"] = np.ascontiguousarray(inp["b_ada"][0][None, :], dtype=f)
    A["gcols"] = np.concatenate([col(inp["g_pre_mix"][0]), col(inp["g_pre_ffn"][0])], axis=1)
    A["goutc"] = col(np.concatenate([inp["g_out_a"][0], inp["g_out_b"][0]]))
    A["gpost_mix_row"] = np.ascontiguousarray(inp["g_post_mix"][0][None, :], dtype=f)
    A["gpost_ffn_row"] = np.ascontiguousarray(inp["g_post_ffn"][0][None, :], dtype=f)
    A["lng_row"] = np.ascontiguousarray(inp["ln_v_g"][0][None, :], dtype=f)
    A["lnb_row"] = np.ascontiguousarray(inp["ln_v_b"][0][None, :], dtype=f)
    A["gq_row"] = np.ascontiguousarray(inp["g_q"][0][None, :], dtype=f)
    A["gk_row"] = np.ascontiguousarray(inp["g_k"][0][None, :], dtype=f)
    w_in = np.asarray(inp["w_in"][0], f)
    A["w_in_h"] = np.ascontiguousarray(w_in.reshape(32, 128, 28, 256).transpose(2, 1, 0, 3)).reshape(28, 128, 32 * 256)
    ws = np.asarray(inp["w_s"][0], f)
    A["wsT"] = np.ascontiguousarray(ws.transpose(2, 0, 1)).reshape(128, 16 * 128)
    A["bs_col"] = np.ascontiguousarray(np.asarray(inp["b_s"][0], f).T)
    w_out = np.asarray(inp["w_out"][0], f)
    A["w_out_h"] = np.ascontiguousarray(w_out.reshape(32, 128, 8, 512).transpose(2, 1, 0, 3)).reshape(8, 128, 32 * 512)
    w_up = np.asarray(inp["w_up"][0], f)
    w_up_h = np.ascontiguousarray(w_up.reshape(32, 128, 2 * NFC, 128).transpose(2, 1, 0, 3)).reshape(2 * NFC, 128, 32 * 128)
    A["w_up_g"] = np.ascontiguousarray(w_up_h[:NFC])
    A["w_up_u"] = np.ascontiguousarray(w_up_h[NFC:])
    cw = np.asarray(inp["conv_w"][0], f)
    cb = np.asarray(inp["conv_b"][0], f)
    cp = np.concatenate([cw, cb[None, :]], axis=0)
    A["convp"] = np.ascontiguousarray(cp.reshape(4, 2 * NFC, 128).transpose(2, 1, 0)).reshape(128, 2 * NFC * 4)
    w_down = np.asarray(inp["w_down"][0], f)
    A["w_down_h"] = np.ascontiguousarray(w_down.reshape(NFC, 128, 8, 512).transpose(2, 1, 0, 3)).reshape(8, 128, NFC * 512)
    A["ident"] = np.eye(128, dtype=f)
    return A


def _run(inputs, stop=None, debug=False, cores=NCORE):
    nc = build_program(stop=stop, debug=debug)
    in_maps = _in_maps(inputs)[:cores]
    return run_bass_kernel_spmd(nc, in_maps, core_ids=list(range(cores)))


def _in_maps(inputs):
    shared = _host_layouts(inputs)
    x = np.asarray(inputs["x"][0], np.float32)
    tok = np.arange(S)
    in_maps = []
    for i in range(NCORE):
        shift = (TOK * i - 128) % S
        order = (tok + shift) % S
        m = dict(shared)
        m["xr"] = np.ascontiguousarray(x[order])
        rowi = (order // 64).astype(np.float32)
        coli = (order % 64).astype(np.float32)
        pos = np.zeros((128, NT, 2), np.float32)
        pos[:, :64, 0] = rowi.reshape(64, 128).T
        pos[:, :64, 1] = coli.reshape(64, 128).T
        m["pos"] = pos.reshape(128, NT * 2)
        hm = np.ones((128, 2), np.float32)
        if i == 0:
            hm[:, 0] = 0.0
        if i == NCORE - 1:
            hm[:, 1] = 0.0
        m["hmask"] = hm
        in_maps.append(m)
    return in_maps


def kernel(**inputs):
    if "nc" not in _CACHE:
        _CACHE["nc"] = build_program()
    nc = _CACHE["nc"]
    in_maps = _in_maps(inputs)
    res = run_bass_kernel_spmd(nc, in_maps, core_ids=list(range(NCORE)))
    outs = [np.asarray(r["out"], np.float32) for r in res.results]
    return np.concatenate(outs, axis=0)[None, :, :]
```

```python
import math
import numpy as np
import concourse.bass as bass
import concourse.mybir as mybir
from concourse.bass_utils import run_bass_kernel_spmd
from contextlib import ExitStack

F32 = mybir.dt.float32
BF16 = mybir.dt.bfloat16
AF = mybir.ActivationFunctionType
ALU = mybir.AluOpType
AX = mybir.AxisListType

ENGS = ("pe", "act", "dve", "pool", "sp")
D = 4096
S = 8192
NCORE = 8
TOK = S // NCORE
DFF = 11008
NFC = DFF // 128
EPS = 1e-6
NT = 66
NKEY = NT * 128
NOWN = 10
NG = 5
VW = 130


class Op:
    __slots__ = ("eng", "fn", "reads", "writes", "dma", "deps", "idx", "need_inc", "seq", "dma_cnt")

    def __init__(self, eng, fn, reads, writes, dma):
        self.eng, self.fn, self.reads, self.writes, self.dma = eng, fn, reads, writes, dma
        self.deps = set()
        self.need_inc = False
        self.seq = 0
        self.dma_cnt = 0


class Prog:
    def __init__(self, nc):
        self.nc = nc
        self.ops = []
        self.last_w = {}
        self.readers = {}
        self.last_eng = {}
        self.dma_since = []
        self.pending_barrier = {}

    def op(self, eng, fn, reads=(), writes=(), dma=None):
        o = Op(eng, fn, tuple(reads), tuple(writes), dma)
        o.idx = len(self.ops)
        for r in o.reads:
            w = self.last_w.get(r)
            if w is not None:
                o.deps.add(w)
        for w_ in o.writes:
            w = self.last_w.get(w_)
            if w is not None:
                o.deps.add(w)
            for rd in self.readers.get(w_, ()):
                o.deps.add(rd)
        for r in o.reads:
            self.readers.setdefault(r, []).append(o.idx)
        for w_ in o.writes:
            self.last_w[w_] = o.idx
            self.readers[w_] = []
        if eng in self.pending_barrier:
            o.deps |= self.pending_barrier.pop(eng)
        o.deps.discard(o.idx)
        self.ops.append(o)
        if dma is None:
            self.last_cmp = getattr(self, "last_cmp", {})
            self.last_cmp[eng] = o.idx
        self.last_eng[eng] = o.idx
        if dma is not None:
            self.dma_since.append(o.idx)
        return o

    def barrier(self):
        b = set(self.last_eng.values()) | set(self.dma_since)
        for e in ENGS:
            self.pending_barrier[e] = set(b) | self.pending_barrier.get(e, set())
        self.dma_since = []

    def emit(self, final_waits=()):
        nc = self.nc
        ops = self.ops
        for o in ops:
            for d in o.deps:
                p = ops[d]
                if p.dma is not None:
                    continue
                if p.eng == "pe" and o.eng == "pe":
                    continue
                p.need_inc = True
        if final_waits is None:
            for e, i_ in getattr(self, "last_cmp", {}).items():
                ops[i_].need_inc = True
            for o in ops:
                if o.dma is None and o.eng != "pe" and False:
                    o.need_inc = True
        cnt = {e: 0 for e in ENGS}
        dcnt = {}
        for o in ops:
            if o.dma is not None:
                dcnt[o.dma] = dcnt.get(o.dma, 0) + 1
                o.dma_cnt = dcnt[o.dma]
            elif o.need_inc:
                cnt[o.eng] += 1
                o.seq = cnt[o.eng]
        with ExitStack() as es:
            esem = {e: es.enter_context(nc.semaphore("s_" + e)) for e in ENGS}
            dsem = {k: es.enter_context(nc.semaphore("d_%d" % i)) for i, k in enumerate(dcnt)}
            block = es.enter_context(nc.Block())
            per_eng = {e: [o for o in ops if o.eng == e] for e in ENGS}

            def body(ename):
                def run(engine):
                    waited = {}
                    for o in per_eng[ename]:
                        need = {}
                        for d in o.deps:
                            p = ops[d]
                            if p.dma is not None:
                                key = ("d", p.dma)
                                val = 16 * p.dma_cnt
                                sem = dsem[p.dma]
                            else:
                                if p.eng == "pe" and ename == "pe":
                                    continue
                                key = ("e", p.eng)
                                val = p.seq
                                sem = esem[p.eng]
                            if val > need.get(key, (0, None))[0]:
                                need[key] = (val, sem)
                        for key, (val, sem) in need.items():
                            if waited.get(key, 0) >= val:
                                continue
                            engine.wait_ge(sem, val)
                            waited[key] = val
                        ins = o.fn(engine)
                        if o.dma is not None:
                            ins.then_inc(dsem[o.dma], 16)
                        elif o.need_inc:
                            ins.then_inc(esem[ename], 1)
                    if ename == "sp" and final_waits is None:
                        for e2 in ENGS:
                            if cnt[e2] > 0:
                                engine.wait_ge(esem[e2], cnt[e2])
                        for k2 in dcnt:
                            engine.wait_ge(dsem[k2], 16 * dcnt[k2])
                    elif ename == "sp":
                        for i in final_waits:
                            p = ops[i]
                            engine.wait_ge(dsem[p.dma], 16 * dcnt[p.dma])
                return run

            block.tensor(body("pe"))
            block.scalar(body("act"))
            block.vector(body("dve"))
            block.gpsimd(body("pool"))
            block.sync(body("sp"))


class SB:
    BASE = 16512
    LIMIT = 229376 - 64

    def __init__(self, nc):
        self.nc = nc
        self.off = SB.BASE
        self.n = 0

    def alloc(self, shape, dtype):
        nbytes = int(np.prod(shape[1:])) * (2 if dtype == BF16 else 4)
        nbytes = (nbytes + 63) // 64 * 64
        assert self.off + nbytes <= SB.LIMIT, ("SBUF overflow", self.off, nbytes)
        self.n += 1
        t = self.nc.alloc_sbuf_tensor_at("sb%d" % self.n, list(shape), dtype, offset=self.off)
        self.off += nbytes
        return t.ap()

    def mark(self):
        return self.off

    def release(self, m):
        self.off = m


def pipeline(n, stages, delays, order=None):
    maxd = max(delays)
    order = list(reversed(range(len(stages)))) if order is None else order
    for i in range(n + maxd):
        for s_ in order:
            u = i - delays[s_]
            if 0 <= u < n:
                stages[s_](u)


def build_program(stop=None, debug=False):
    nc = bass.Bass("TRN2", target_bir_lowering=False)

    def din(name, shape):
        return nc.dram_tensor(name, list(shape), F32, kind="ExternalInput").ap()

    xr = din("xr", [S, D])
    ctx = din("ctx", [256, D])
    pos_d = din("pos", [128, NT * 2])
    jidx_d = din("jidx", [128, 32])
    cvec_d = din("cvec", [128, 64])
    w_ada_a = din("w_ada_a", [96, 128, 32 * 128])
    w_ada_b = din("w_ada_b", [96, 128, 32 * 128])
    b_ada_col = din("b_ada_col", [128, 192])
    b_ada_row = din("b_ada_row", [1, 6 * D])
    gcols_d = din("gcols", [128, 64])
    goutc_d = din("goutc", [128, 32])
    gpost_mix_row = din("gpost_mix_row", [1, D])
    gpost_ffn_row = din("gpost_ffn_row", [1, D])
    lng_row = din("lng_row", [1, 2048])
    lnb_row = din("lnb_row", [1, 2048])
    gq_row = din("gq_row", [1, 128])
    gk_row = din("gk_row", [1, 128])
    w_in_h = din("w_in_h", [28, 128, 32 * 256])
    wsT_d = din("wsT", [128, 16 * 128])
    bs_col_d = din("bs_col", [128, 16])
    w_out_h = din("w_out_h", [8, 128, 32 * 512])
    w_up_g = din("w_up_g", [NFC, 128, 32 * 128])
    w_up_u = din("w_up_u", [NFC, 128, 32 * 128])
    convp_d = din("convp", [128, 2 * NFC * 4])
    w_down_h = din("w_down_h", [8, 128, NFC * 512])
    hmask_d = din("hmask", [128, 2])
    ident_d = din("ident", [128, 128])
    out = nc.dram_tensor("out", [TOK, D], F32, kind="ExternalOutput").ap()

    def dscr(name, shape, dt):
        if debug:
            return nc.dram_tensor(name, list(shape), dt, kind="ExternalOutput").ap()
        return nc.dram_tensor(name, list(shape), dt).ap()

    KT_scr = dscr("KT_scr", [4, 128, NKEY], BF16)
    V_scr = dscr("V_scr", [4, 128, NT * VW], BF16)
    hT_scr = dscr("hT_scr", [NOWN, 128, 32 * 128], BF16)
    oT_scr = dscr("oT_scr", [32, 128, NOWN * 128], BF16)
    o_scr = dscr("o_scr", [NOWN * 128, D], F32)
    xm_scr = dscr("xm_scr", [TOK, D], F32)
    f_scr = dscr("f_scr", [TOK, D], F32)
    gg_scr = dscr("gg_scr", [2, 128, D], F32)

    P = Prog(nc)
    sb = SB(nc)
    ps = [nc.alloc_psum_tensor("ps%d" % i, [128, 512], F32).ap() for i in range(8)]
    psb = [p.bitcast(BF16) for p in ps]

    ident = sb.alloc([128, 128], BF16)
    onesf = sb.alloc([128, 128], F32)
    modc = sb.alloc([128, 192], F32)
    modx = sb.alloc([128, 64], F32)
    Ga = sb.alloc([128, 32], F32)
    Gc = sb.alloc([128, 32], F32)
    Gf = sb.alloc([128, 32], F32)
    gcols = sb.alloc([128, 64], F32)
    goutc = sb.alloc([128, 32], F32)
    cst = sb.alloc([128, 8], F32)
    freq = sb.alloc([128, 32], F32)
    pos = sb.alloc([128, NT * 2], F32)
    gq_t = sb.alloc([128, 128], F32)
    gk_t = sb.alloc([128, 128], F32)
    bs_col = sb.alloc([128, 16], F32)
    hmask = sb.alloc([128, 2], F32)
    convp = sb.alloc([128, 2 * NFC * 4], F32)
    ssA = sb.alloc([128, NOWN * 8], F32)
    ssB = sb.alloc([128, NOWN * 16], F32)
    ssO = sb.alloc([128, NOWN * 8], F32)
    ssF = sb.alloc([128, 8 * 8], F32)
    rA = sb.alloc([128, NOWN], F32)
    rB = sb.alloc([128, NOWN], F32)
    rO = sb.alloc([128, NOWN], F32)
    rF = sb.alloc([128, 8], F32)
    sv = sb.alloc([128, 64], F32)
    st = sb.alloc([128, 64], F32)
    SHa = modc[:, 0:32]
    SHf = modc[:, 96:128]
    SHc = modx[:, 0:32]
    epsc = cst[:, 0:1]
    negpi = cst[:, 1:2]
    pospi = cst[:, 2:3]

    def ld(eng, dst, src, key, reads=(), writes=None):
        return P.op(eng, lambda e, dst=dst, src=src: e.dma_start(out=dst, in_=src), reads=reads,
                    writes=[key] if writes is None else writes, dma="L:" + key)

    def bc(row_ap, n=128):
        return row_ap.partition_broadcast(n).rearrange("p a b -> p (a b)")

    P.op("pool", lambda e: e.memset(cst[:, 0:1], EPS), writes=["cst"])
    P.op("pool", lambda e: e.memset(cst[:, 1:2], -math.pi), writes=["cst"])
    P.op("pool", lambda e: e.memset(cst[:, 2:3], math.pi), writes=["cst"])
    P.op("pool", lambda e: e.memset(cst[:, 3:4], 1.0), writes=["cst"])
    P.op("pool", lambda e: e.memset(onesf, 1.0), writes=["onesf"])
    ld("pool", ident, ident_d, "ident")
    ld("sp", sv, cvec_d, "sv")
    ld("sp", gcols, gcols_d, "gcols")
    ld("sp", goutc, goutc_d, "goutc")
    ld("sp", pos, pos_d, "pos")
    ld("sp", freq, jidx_d, "freq")
    ld("sp", gq_t, bc(gq_row), "gq_t")
    ld("sp", gk_t, bc(gk_row), "gk_t")
    ld("sp", bs_col, bs_col_d, "bs_col")
    ld("sp", hmask, hmask_d, "hmask")
    ld("sp", convp, convp_d, "convp")
    ld("sp", modc, b_ada_col, "modc")
    P.op("act", lambda e: e.activation(out=freq, in_=freq, func=AF.Exp, scale=-math.log(10000.0) / 32.0),
         reads=["freq"], writes=["freq"])
    P.op("act", lambda e: e.activation(out=sv, in_=sv, func=AF.Silu), reads=["sv"], writes=["sv"])

    svb = sb.alloc([128, 32, 2], BF16)
    P.op("dve", lambda e: e.tensor_copy(out=svb, in_=sv.rearrange("p (two k) -> p k two", two=2)), reads=["sv"], writes=["svb"])

    def wchunk(ch):
        return (w_ada_a if ch < 96 else w_ada_b)[ch % 96]

    mA = sb.mark()
    wa = [sb.alloc([128, 32, 128], BF16) for _ in range(4)]
    for ch in range(64):
        s_ = ch % 4
        P.op("pool", lambda e, s_=s_, ch=ch: e.dma_start(out=wa[s_].rearrange("p k c -> p (k c)"), in_=wchunk(ch)), writes=["wa%d" % s_], dma="L:wa%d" % s_)
        for kc in range(32):
            P.op("pe", lambda e, s_=s_, ch=ch, kc=kc: e.matmul(ps[0][:, 2 * ch:2 * ch + 2], lhsT=wa[s_][:, kc, :], rhs=svb[:, kc, :], start=(kc == 0), stop=(kc == 31)),
                 reads=["wa%d" % s_, "svb"], writes=["ps0"])
    pc3 = ps[0][:, 0:128].rearrange("p (j two) -> p j two", two=2)
    P.op("dve", lambda e: e.tensor_tensor(out=modx, in0=pc3[:, :, 1], in1=modc[:, 0:64], op=ALU.add), reads=["ps0", "modc"], writes=["modx"])
    P.op("dve", lambda e: e.tensor_tensor(out=modc[:, 0:64], in0=pc3[:, :, 0], in1=modc[:, 0:64], op=ALU.add), reads=["ps0", "modc", "modx"], writes=["modc"])
    for (G, scl, gcol, key) in ((Ga, modc[:, 32:64], gcols[:, 0:32], "Ga"), (Gc, modx[:, 32:64], gcols[:, 0:32], "Gc")):
        P.op("dve", lambda e, G=G, scl=scl, gcol=gcol: e.scalar_tensor_tensor(out=G, in0=scl, scalar=1.0, in1=gcol, op0=ALU.add, op1=ALU.mult),
             reads=["modc", "modx", "gcols"], writes=[key])
    P.barrier()
    sb.release(mA)

    if stop == "A":
        dbgA = nc.dram_tensor("dbgA", [128, 512], F32, kind="ExternalOutput").ap()
        P.op("sp", lambda e: e.dma_start(out=dbgA[:, 0:192], in_=modc), reads=["modc"], dma="dbg")
        P.op("sp", lambda e: e.dma_start(out=dbgA[:, 192:256], in_=modx), reads=["modx"], dma="dbg")
        P.op("sp", lambda e: e.dma_start(out=dbgA[:, 256:288], in_=Ga), reads=["Ga"], dma="dbg")
        P.op("sp", lambda e: e.dma_start(out=dbgA[:, 288:320], in_=Gc), reads=["Gc"], dma="dbg")
        P.op("sp", lambda e: e.dma_start(out=dbgA[:, 320:352], in_=Gf), reads=["Gf"], dma="dbg")
        P.emit(final_waits=None)
        return nc
    def rstd_from_ss(ss_ap, out_ap, n, reads, wkey):
        P.op("act", lambda e: e.activation(out=out_ap, in_=ss_ap, func=AF.Sqrt, scale=1.0 / n, bias=epsc), reads=list(reads) + ["cst"], writes=[wkey])
        P.op("dve", lambda e: e.reciprocal(out=out_ap, in_=out_ap), reads=[wkey], writes=[wkey])

    def rope_tables(t, cos2, sins, ang, angk, angi, key, skey=None):
        skey = key if skey is None else skey
        P.op("dve", lambda e: e.tensor_scalar(out=ang[:, 0:32], in0=freq, scalar1=pos[:, 2 * t:2 * t + 1], scalar2=None, op0=ALU.mult),
             reads=["freq", "pos"], writes=[skey + "ang"])
        P.op("dve", lambda e: e.tensor_scalar(out=ang[:, 32:64], in0=freq, scalar1=pos[:, 2 * t + 1:2 * t + 2], scalar2=None, op0=ALU.mult),
             reads=["freq", "pos"], writes=[skey + "ang"])
        P.op("dve", lambda e: e.tensor_scalar(out=ang[:, 64:128], in0=ang[:, 0:64], scalar1=0.5 * math.pi, scalar2=None, op0=ALU.add),
             reads=[skey + "ang"], writes=[skey + "ang2"])
        P.op("dve", lambda e: e.tensor_scalar(out=angk, in0=ang, scalar1=1.0 / (2 * math.pi), scalar2=None, op0=ALU.mult),
             reads=[skey + "ang", skey + "ang2"], writes=[skey + "angk"])
        P.op("dve", lambda e: e.tensor_copy(out=angi, in_=angk), reads=[skey + "angk"], writes=[skey + "angi"])
        P.op("dve", lambda e: e.tensor_copy(out=angk, in_=angi), reads=[skey + "angi"], writes=[skey + "angk"])
        P.op("dve", lambda e: e.scalar_tensor_tensor(out=ang, in0=angk, scalar=-2.0 * math.pi, in1=ang, op0=ALU.mult, op1=ALU.add),
             reads=[skey + "angk", skey + "ang", skey + "ang2"], writes=[skey + "ang", skey + "ang2"])
        P.op("dve", lambda e: e.tensor_scalar(out=ang, in0=ang, scalar1=3.1415925, scalar2=-3.1415925, op0=ALU.min, op1=ALU.max),
             reads=[skey + "ang", skey + "ang2"], writes=[skey + "ang", skey + "ang2"])
        c4 = cos2.rearrange("p (a b j) -> p a b j", a=2, b=2)
        s4 = sins.rearrange("p (a b j) -> p a b j", a=2, b=2)
        sarg = ang[:, 0:64].rearrange("p (a j) -> p a j", a=2)
        carg = ang[:, 64:128].rearrange("p (a j) -> p a j", a=2)
        for b in range(2):
            P.op("act", lambda e, b=b: e.activation(out=c4[:, :, b, :], in_=carg, func=AF.Sin), reads=[skey + "ang2"], writes=[key + "cos"])
        P.op("act", lambda e: e.activation(out=s4[:, :, 1, :], in_=sarg, func=AF.Sin), reads=[skey + "ang"], writes=[key + "sin"])
        P.op("act", lambda e: e.activation(out=s4[:, :, 0, :], in_=sarg, func=AF.Sin, scale=-1.0), reads=[skey + "ang"], writes=[key + "sin"])

    def qk_norm_rope(psrc, pkey, nh, g_t, cos2, sins, tkeys, tmp, outb, okey, pfx):
        sq, qn, t1, t2, ssq = tmp["sq"], tmp["qn"], tmp["t1"], tmp["t2"], tmp["ss"]
        P.op("act", lambda e: e.activation(out=sq, in_=psrc, func=AF.Square), reads=[pkey], writes=[pfx + "sq"])
        P.op("dve", lambda e: e.reduce_sum(out=ssq[:, 0:nh], in_=sq.rearrange("p (h d) -> p h d", h=nh), axis=AX.X), reads=[pfx + "sq"], writes=[pfx + "ss"])
        rstd_from_ss(ssq[:, 0:nh], ssq[:, 0:nh], 128, [pfx + "ss"], pfx + "ss")
        for h in range(nh):
            P.op("act", lambda e, h=h: e.activation(out=qn[:, h * 128:(h + 1) * 128], in_=psrc[:, h * 128:(h + 1) * 128], func=AF.Copy, scale=ssq[:, h:h + 1]),
                 reads=[pkey, pfx + "ss"], writes=[pfx + "qn"])
        q3 = qn.rearrange("p (h d) -> p h d", h=nh)
        P.op("dve", lambda e: e.tensor_tensor(out=q3, in0=q3, in1=g_t.unsqueeze(1).broadcast_to([128, nh, 128]), op=ALU.mult),
             reads=[pfx + "qn", "gq_t", "gk_t"], writes=[pfx + "qn"])
        P.op("dve", lambda e: e.tensor_tensor(out=t1.rearrange("p (h d) -> p h d", h=nh), in0=q3, in1=cos2.unsqueeze(1).broadcast_to([128, nh, 128]), op=ALU.mult),
             reads=[pfx + "qn"] + tkeys, writes=[pfx + "t1"])
        q5 = qn.rearrange("p (h a b j) -> p h a b j", h=nh, a=2, b=2)
        t5 = t2.rearrange("p (h a b j) -> p h a b j", h=nh, a=2, b=2)
        s4 = sins.rearrange("p (a b j) -> p a b j", a=2, b=2)
        for b in range(2):
            P.op("dve", lambda e, b=b: e.tensor_tensor(out=t5[:, :, :, b, :], in0=q5[:, :, :, 1 - b, :],
                                                      in1=s4[:, :, b, :].unsqueeze(1).broadcast_to([128, nh, 2, 32]), op=ALU.mult),
                 reads=[pfx + "qn"] + tkeys, writes=[pfx + "t2"])
        P.op("dve", lambda e: e.tensor_tensor(out=outb, in0=t1, in1=t2, op=ALU.add), reads=[pfx + "t1", pfx + "t2"], writes=[okey])

    def norm_transpose(xt, xkey, G, SHv, gkeys, junk, xh, xhkey, hT_dst, hkey, pfx, banks=(0, 1), only_col=None, mask_col=None, coltmp=None):
        P.op("act", lambda e: e.activation(out=junk, in_=xt, func=AF.Square, accum_out=st[:, 0:1]), reads=[xkey], writes=[pfx + "st"])
        rstd_from_ss(st[:, 0:1], st[:, 1:2], D, [pfx + "st"], pfx + "st1")
        P.op("act", lambda e: e.activation(out=xh, in_=xt, func=AF.Copy, scale=st[:, 1:2]), reads=[xkey, pfx + "st1"], writes=[xhkey])
        for g4 in range(4):
            bank = psb[banks[g4 % 2]]
            bk = "ps%d" % banks[g4 % 2]
            for j in range(8):
                kc = g4 * 8 + j
                P.op("pe", lambda e, bank=bank, j=j, kc=kc: e.transpose(out=bank[:, j * 128:(j + 1) * 128], in_=xh[:, kc * 128:(kc + 1) * 128], identity=ident),
                     reads=[xhkey, "ident"], writes=[bk])
            if only_col is None:
                for j in range(8):
                    kc = g4 * 8 + j
                    if g4 % 2 == 0:
                        P.op("act", lambda e, bank=bank, j=j, kc=kc: e.activation(out=hT_dst[:, kc, :], in_=bank[:, j * 128:(j + 1) * 128], func=AF.Identity,
                                                                                 scale=G[:, kc:kc + 1], bias=SHv[:, kc:kc + 1]),
                             reads=[bk] + gkeys, writes=[hkey + "_%d" % kc])
                    else:
                        P.op("dve", lambda e, bank=bank, j=j, kc=kc: e.tensor_scalar(out=hT_dst[:, kc, :], in0=bank[:, j * 128:(j + 1) * 128],
                                                                                    scalar1=G[:, kc:kc + 1], scalar2=SHv[:, kc:kc + 1], op0=ALU.mult, op1=ALU.add),
                             reads=[bk] + gkeys, writes=[hkey + "_%d" % kc])
            else:
                src = bank[:, 0:1024].rearrange("p (j t) -> p j t", j=8)[:, :, only_col]
                sl = slice(g4 * 8, g4 * 8 + 8)
                P.op("dve", lambda e, src=src, sl=sl: e.tensor_tensor(out=coltmp[:, sl], in0=src, in1=G[:, sl], op=ALU.mult), reads=[bk] + gkeys, writes=[pfx + "ct"])
                P.op("dve", lambda e, sl=sl: e.tensor_tensor(out=coltmp[:, sl], in0=coltmp[:, sl], in1=SHv[:, sl], op=ALU.add), reads=[pfx + "ct"] + gkeys, writes=[pfx + "ct"])
                P.op("dve", lambda e, sl=sl: e.tensor_scalar(out=hT_dst[:, sl], in0=coltmp[:, sl], scalar1=mask_col, scalar2=None, op0=ALU.mult),
                     reads=[pfx + "ct", "hmask"], writes=[hkey])

    mTab = sb.mark()
    COSo = sb.alloc([128, NOWN * 64], F32)
    SINo = sb.alloc([128, NOWN * 64], F32)
    mBig = sb.mark()
    COS = sb.alloc([128, NT * 64], F32)
    SIN = sb.alloc([128, NT * 64], F32)
    mR = sb.mark()
    angA = sb.alloc([128, NT * 64], F32)
    angK = sb.alloc([128, NT * 64], F32)
    angI = sb.alloc([128, NT * 64], mybir.dt.int32)
    P.op("dve", lambda e: e.tensor_tensor(out=angA.rearrange("p (m j) -> p m j", j=32), in0=pos.unsqueeze(2).broadcast_to([128, NT * 2, 32]),
                                          in1=freq.unsqueeze(1).broadcast_to([128, NT * 2, 32]), op=ALU.mult),
         reads=["pos", "freq"], writes=["angA"])
    for (dst, dkey, shift) in ((SIN, "SIN", 0.0), (COS, "COS", 0.5 * math.pi)):
        if shift:
            P.op("dve", lambda e, shift=shift: e.tensor_scalar(out=angA, in0=angA, scalar1=shift, scalar2=None, op0=ALU.add), reads=["angA"], writes=["angA"])
        P.op("dve", lambda e: e.tensor_scalar(out=angK, in0=angA, scalar1=1.0 / (2 * math.pi), scalar2=None, op0=ALU.mult), reads=["angA"], writes=["angK"])
        P.op("dve", lambda e: e.tensor_copy(out=angI, in_=angK), reads=["angK"], writes=["angI"])
        P.op("dve", lambda e: e.tensor_copy(out=angK, in_=angI), reads=["angI"], writes=["angK"])
        P.op("dve", lambda e: e.scalar_tensor_tensor(out=angK, in0=angK, scalar=-2.0 * math.pi, in1=angA, op0=ALU.mult, op1=ALU.add), reads=["angK", "angA"], writes=["angK"])
        P.op("dve", lambda e: e.tensor_scalar(out=angK, in0=angK, scalar1=3.1415925, scalar2=-3.1415925, op0=ALU.min, op1=ALU.max), reads=["angK"], writes=["angK"])
        P.op("act", lambda e, dst=dst: e.activation(out=dst, in_=angK, func=AF.Sin), reads=["angK"], writes=[dkey])
    P.op("dve", lambda e: e.tensor_copy(out=COSo, in_=COS[:, 0:NOWN * 64]), reads=["COS"], writes=["COSo"])
    P.op("dve", lambda e: e.tensor_copy(out=SINo, in_=SIN[:, 0:NOWN * 64]), reads=["SIN"], writes=["SINo"])
    P.barrier()
    sb.release(mR)

    def rope_apply(kn, nh, t, outb, t1, t2, rkeys, wkey, tkey, tabs=None):
        Ct, St, ck, sk = (COS, SIN, "COS", "SIN") if tabs is None else tabs
        cs = Ct[:, t * 64:(t + 1) * 64].rearrange("p (a j) -> p a j", a=2).unsqueeze(1).broadcast_to([128, nh, 2, 32])
        sn = St[:, t * 64:(t + 1) * 64].rearrange("p (a j) -> p a j", a=2).unsqueeze(1).broadcast_to([128, nh, 2, 32])
        q5 = kn.rearrange("p (h a b j) -> p h a b j", h=nh, a=2, b=2)
        a5 = t1.rearrange("p (h a b j) -> p h a b j", h=nh, a=2, b=2)
        b5 = t2.rearrange("p (h a b j) -> p h a b j", h=nh, a=2, b=2)
        o5 = outb.rearrange("p (h a b j) -> p h a b j", h=nh, a=2, b=2)
        for b_ in range(2):
            P.op("dve", lambda e, b_=b_: e.tensor_tensor(out=a5[:, :, :, b_, :], in0=q5[:, :, :, b_, :], in1=cs, op=ALU.mult), reads=rkeys + [ck], writes=[tkey + "t1"])
            P.op("dve", lambda e, b_=b_: e.tensor_tensor(out=b5[:, :, :, b_, :], in0=q5[:, :, :, 1 - b_, :], in1=sn, op=ALU.mult), reads=rkeys + [sk], writes=[tkey + "t2"])
        P.op("dve", lambda e: e.tensor_tensor(out=o5[:, :, :, 0, :], in0=a5[:, :, :, 0, :], in1=b5[:, :, :, 0, :], op=ALU.subtract), reads=[tkey + "t1", tkey + "t2"], writes=[wkey])
        P.op("dve", lambda e: e.tensor_tensor(out=o5[:, :, :, 1, :], in0=a5[:, :, :, 1, :], in1=b5[:, :, :, 1, :], op=ALU.add), reads=[tkey + "t1", tkey + "t2"], writes=[wkey])

    mB = sb.mark()
    xt = [sb.alloc([128, D], F32) for _ in range(2)]
    xh = [sb.alloc([128, D], BF16) for _ in range(2)]
    hT = [sb.alloc([128, 32, 128], BF16) for _ in range(2)]
    wkv = sb.alloc([128, 32, 1024], BF16)
    junk = sb.alloc([128, 512], BF16)
    wa2 = [sb.alloc([128, 16, 128], BF16) for _ in range(3)]
    kraw = [sb.alloc([128, 512], F32) for _ in range(2)]
    kn = sb.alloc([128, 512], F32)
    kt1 = sb.alloc([128, 512], F32)
    kt2 = sb.alloc([128, 512], F32)
    kr = [sb.alloc([128, 512], BF16) for _ in range(2)]
    kTs = [sb.alloc([128, 4, 128], BF16) for _ in range(2)]
    vaug = [sb.alloc([128, 4, VW], BF16) for _ in range(2)]
    ssx = sb.alloc([128, 4], F32)
    ssk = sb.alloc([128, 8], F32)
    for blk in range(4):
        P.op("pool", lambda e, blk=blk: e.dma_start(out=wkv[:, :, blk * 256:(blk + 1) * 256], in_=w_in_h[24 + blk].rearrange("p (k c) -> p k c", k=32)),
             writes=["wkv"], dma="L:wkv")
    for s_ in range(2):
        P.op("pool", lambda e, s_=s_: e.memset(vaug[s_][:, :, 128:VW], 1.0), writes=["vaug%d" % s_])

    def b_s0(t):
        s_ = t % 2
        src = xr[t * 128:(t + 1) * 128, :] if t < 64 else ctx[(t - 64) * 128:(t - 63) * 128, :]
        ld("sp", xt[s_], src, "xt%d" % s_)

    def b_s1(t):
        s_, c4 = t % 2, t % 4
        P.op("act", lambda e: e.activation(out=xh[s_], in_=xt[s_], func=AF.Square, accum_out=ssx[:, c4:c4 + 1]), reads=["xt%d" % s_], writes=["ssx%d" % c4, "xh%d" % s_])
        rstd_from_ss(ssx[:, c4:c4 + 1], ssx[:, c4:c4 + 1], D, ["ssx%d" % c4], "ssx%d" % c4)

    def b_s2(t):
        s_, c4 = t % 2, t % 4
        P.op("act", lambda e: e.activation(out=xh[s_], in_=xt[s_], func=AF.Copy, scale=ssx[:, c4:c4 + 1]), reads=["xt%d" % s_, "ssx%d" % c4], writes=["xh%d" % s_])

    def b_s3(t):
        s_ = t % 2
        G, SHv, gk = (Ga, SHa, ["Ga", "modc"]) if t < 64 else (Gc, SHc, ["Gc", "modx"])
        TB = (0, 1, 4, 5)
        for g4 in range(4):
            bank = psb[TB[g4]]
            bk = "ps%d" % TB[g4]
            for j in range(8):
                kc = g4 * 8 + j
                P.op("pe", lambda e, bank=bank, j=j, kc=kc: e.transpose(out=bank[:, j * 128:(j + 1) * 128], in_=xh[s_][:, kc * 128:(kc + 1) * 128], identity=ident),
                     reads=["xh%d" % s_, "ident"], writes=[bk])
        for g4 in range(4):
            bank = psb[TB[g4]]
            bk = "ps%d" % TB[g4]
            for j in range(8):
                kc = g4 * 8 + j
                if g4 % 2 == 0:
                    P.op("act", lambda e, bank=bank, j=j, kc=kc: e.activation(out=hT[s_][:, kc, :], in_=bank[:, j * 128:(j + 1) * 128], func=AF.Identity,
                                                                             scale=G[:, kc:kc + 1], bias=SHv[:, kc:kc + 1]),
                         reads=[bk] + gk, writes=["hT%d_%d" % (s_, kc)])
                else:
                    P.op("dve", lambda e, bank=bank, j=j, kc=kc: e.tensor_scalar(out=hT[s_][:, kc, :], in0=bank[:, j * 128:(j + 1) * 128],
                                                                                scalar1=G[:, kc:kc + 1], scalar2=SHv[:, kc:kc + 1], op0=ALU.mult, op1=ALU.add),
                         reads=[bk] + gk, writes=["hT%d_%d" % (s_, kc)])
        if t < NOWN:
            P.op("sp", lambda e: e.dma_start(out=hT_scr[t], in_=hT[s_].rearrange("p k c -> p (k c)")),
                 reads=["hT%d_%d" % (s_, kc_) for kc_ in range(32)], writes=["hT_scr%d" % t], dma="S:hT%d" % s_)

    def b_s4(t):
        s_ = t % 2
        for (pp, half) in ((2, 0), (3, 1)):
            for kc in range(32):
                P.op("pe", lambda e, pp=pp, kc=kc, half=half: e.matmul(ps[pp], lhsT=hT[s_][:, kc, :], rhs=wkv[:, kc, half * 512:(half + 1) * 512],
                                                                      start=(kc == 0), stop=(kc == 31)),
                     reads=["hT%d_%d" % (s_, kc), "wkv"], writes=["ps%d" % pp])

    def b_s5(t):
        s_ = t % 2
        pk, pv = ps[2], ps[3]
        pkk, pvk = "ps2", "ps3"
        P.op("act", lambda e: e.activation(out=kraw[s_], in_=pk, func=AF.Copy), reads=[pkk], writes=["kraw%d" % s_])
        P.op("act", lambda e: e.activation(out=vaug[s_][:, :, 0:128], in_=pv.rearrange("p (h d) -> p h d", h=4), func=AF.Copy), reads=[pvk], writes=["vaug%d" % s_])
        P.op("sp", lambda e: e.dma_start(out=V_scr.rearrange("h p (t c) -> p h t c", t=NT)[:, :, t, :], in_=vaug[s_]),
             reads=["vaug%d" % s_], writes=["V_scr"], dma="S:v%d" % s_)
        for h in range(4):
            P.op("act", lambda e, h=h: e.activation(out=junk[:, h * 128:(h + 1) * 128], in_=kraw[s_][:, h * 128:(h + 1) * 128], func=AF.Square,
                                                   accum_out=ssk[:, s_ * 4 + h:s_ * 4 + h + 1]),
                 reads=["kraw%d" % s_], writes=["rk%d" % s_])
        P.op("act", lambda e: e.activation(out=ssk[:, s_ * 4:s_ * 4 + 4], in_=ssk[:, s_ * 4:s_ * 4 + 4], func=AF.Sqrt, scale=1.0 / 128, bias=epsc),
             reads=["rk%d" % s_, "cst"], writes=["rk%d" % s_])

    def b_s6(t):
        s_ = t % 2
        rk = ssk[:, s_ * 4:s_ * 4 + 4]
        P.op("dve", lambda e: e.reciprocal(out=rk, in_=rk), reads=["rk%d" % s_], writes=["rk%d" % s_])
        k3 = kn.rearrange("p (h d) -> p h d", h=4)
        P.op("dve", lambda e: e.tensor_tensor(out=k3, in0=kraw[s_].rearrange("p (h d) -> p h d", h=4), in1=rk.unsqueeze(2).broadcast_to([128, 4, 128]), op=ALU.mult),
             reads=["kraw%d" % s_, "rk%d" % s_], writes=["kn"])
        P.op("dve", lambda e: e.tensor_tensor(out=k3, in0=k3, in1=gk_t.unsqueeze(1).broadcast_to([128, 4, 128]), op=ALU.mult), reads=["kn", "gk_t"], writes=["kn"])
        rope_apply(kn, 4, t, kr[s_], kt1, kt2, ["kn"], "kr%d" % s_, "B")

    def b_s7(t):
        s_ = t % 2
        for h in range(4):
            P.op("pe", lambda e, h=h: e.transpose(out=psb[6][:, h * 128:(h + 1) * 128], in_=kr[s_][:, h * 128:(h + 1) * 128], identity=ident),
                 reads=["kr%d" % s_, "ident"], writes=["ps6"])
        P.op("dve", lambda e: e.tensor_copy(out=kTs[s_], in_=psb[6][:, 0:512].rearrange("p (h t) -> p h t", h=4)), reads=["ps6"], writes=["kTs%d" % s_])
        P.op("sp", lambda e: e.dma_start(out=KT_scr.rearrange("h d n -> d h n")[:, :, t * 128:(t + 1) * 128], in_=kTs[s_]),
             reads=["kTs%d" % s_], writes=["KT_scr"], dma="S:k%d" % s_)

    NG2 = 256

    def a2_ld(g):
        ch, hh, sl = 64 + g // 2, g % 2, g % 3
        P.op("pool", lambda e: e.dma_start(out=wa2[sl].rearrange("p k c -> p (k c)"), in_=wchunk(ch)[:, hh * 2048:(hh + 1) * 2048]),
             writes=["wa2_%d" % sl], dma="L:wa2_%d" % sl)

    def a2_mmul(g):
        ch, hh, sl = 64 + g // 2, g % 2, g % 3
        c2 = 2 * (ch - 64)
        for k16 in range(16):
            kc = hh * 16 + k16
            P.op("pe", lambda e, k16=k16, kc=kc: e.matmul(ps[7][:, c2:c2 + 2], lhsT=wa2[sl][:, k16, :], rhs=svb[:, kc, :], start=(kc == 0), stop=(kc == 31)),
                 reads=["wa2_%d" % sl, "svb"], writes=["ps7"])

    def a2(t):
        if t >= 64:
            return
        for g in range(4 * t, 4 * t + 4):
            a2_ld(g)
            if g >= 2:
                a2_mmul(g - 2)

    pipeline(NT, [b_s0, b_s1, b_s2, b_s3, b_s4, b_s5, b_s6, b_s7, a2], [0, 1, 2, 3, 4, 5, 6, 7, 0], order=[5, 3, 7, 6, 4, 8, 2, 1, 0])
    a2_mmul(NG2 - 2)
    a2_mmul(NG2 - 1)
    pc7 = ps[7][:, 0:256].rearrange("p (j two) -> p j two", two=2)
    P.op("dve", lambda e: e.tensor_tensor(out=modc[:, 64:192], in0=pc7[:, :, 0], in1=modc[:, 64:192], op=ALU.add), reads=["ps7", "modc"], writes=["modc"])
    P.op("dve", lambda e: e.scalar_tensor_tensor(out=Gf, in0=modc[:, 128:160], scalar=1.0, in1=gcols[:, 32:64], op0=ALU.add, op1=ALU.mult),
         reads=["modc", "gcols"], writes=["Gf"])
    P.barrier()
    sb.release(mBig)

    if stop == "B":
        P.emit(final_waits=None)
        return nc
    mGG = sb.mark()
    identf = sb.alloc([128, 128], F32)
    dg = [sb.alloc([128, 128], F32) for _ in range(2)]
    grow = sb.alloc([128, 2048], F32)
    ggs = sb.alloc([128, 2048], F32)
    ld("sp", identf, ident_d, "identf")
    ndg = 0
    for which, c0 in ((0, 64), (1, 160)):
        for half in range(2):
            ld("sp", grow, bc((gpost_mix_row if which == 0 else gpost_ffn_row)[:, half * 2048:(half + 1) * 2048]), "grow")
            for q4 in range(4):
                bank = 1 + (q4 % 2)
                for jj in range(4):
                    j = c0 + half * 16 + q4 * 4 + jj
                    d_ = ndg % 2
                    ndg += 1
                    P.op("dve", lambda e, d_=d_, j=j: e.tensor_scalar(out=dg[d_], in0=identf, scalar1=modc[:, j:j + 1], scalar2=None, op0=ALU.mult),
                         reads=["identf", "modc"], writes=["dg%d" % d_])
                    P.op("pe", lambda e, d_=d_, jj=jj, bank=bank: e.matmul(ps[bank][:, jj * 128:(jj + 1) * 128], lhsT=onesf, rhs=dg[d_], start=True, stop=True),
                         reads=["dg%d" % d_, "onesf"], writes=["ps%d" % bank])
                P.op("dve", lambda e, q4=q4, bank=bank: e.tensor_tensor(out=ggs[:, q4 * 512:(q4 + 1) * 512], in0=ps[bank], in1=grow[:, q4 * 512:(q4 + 1) * 512], op=ALU.mult),
                     reads=["ps%d" % bank, "grow"], writes=["ggs"])
            P.op("sp", lambda e, which=which, half=half: e.dma_start(out=gg_scr[which, :, half * 2048:(half + 1) * 2048], in_=ggs),
                 reads=["ggs"], writes=["gg_scr"], dma="S:gg")
    P.barrier()
    sb.release(mGG)

    qT = sb.alloc([128, 16, NOWN * 128], BF16)
    mC = sb.mark()
    hTo = sb.alloc([128, NG, 32 * 128], BF16)
    wblk = [sb.alloc([128, 32, 256], BF16) for _ in range(2)]
    gv = sb.alloc([128, NG, 2048], BF16)
    lng = sb.alloc([128, 2048], F32)
    lnb = sb.alloc([128, 2048], F32)
    wsT = sb.alloc([128, 16, 128], BF16)
    gtmp = [sb.alloc([128, 256], F32) for _ in range(2)]
    oa = [sb.alloc([128, 256], F32) for _ in range(2)]
    oab = [sb.alloc([128, 256], BF16) for _ in range(2)]
    oTs = [sb.alloc([128, 2, 128], BF16) for _ in range(2)]
    qn = sb.alloc([128, 256], F32)
    qt1 = sb.alloc([128, 256], F32)
    qt2 = sb.alloc([128, 256], F32)
    ssq = [sb.alloc([128, 2], F32) for _ in range(2)]
    qr = [sb.alloc([128, 256], BF16) for _ in range(2)]
    vsum = sb.alloc([128, NOWN * 8], F32)
    vsq = sb.alloc([128, NOWN * 8], F32)
    vst = sb.alloc([128, NOWN * 4], F32)
    junkC = sb.alloc([128, 256], F32)
    ld("sp", lng, bc(lng_row), "lng")
    ld("sp", lnb, bc(lnb_row), "lnb")
    P.op("pool", lambda e: e.dma_start(out=wsT.rearrange("p g c -> p (g c)"), in_=wsT_d), writes=["wsT"], dma="L:wsT")

    nblk = [0]
    wslot = {}

    def load_wblk(blk, tag):
        if (blk, tag) in wslot:
            return
        s_ = nblk[0] % 2
        nblk[0] += 1
        wslot[(blk, tag)] = s_
        P.op("pool", lambda e: e.dma_start(out=wblk[s_].rearrange("p k c -> p (k c)"), in_=w_in_h[blk]), writes=["wblk%d" % s_], dma="L:wblk%d" % s_)

    def run_family(grp, blk0, stages_fn, delays):
        TL = list(range(grp * NG, grp * NG + NG))
        units = [(j, t) for j in range(8) for t in TL]

        def s0(u):
            j, t = units[u]
            load_wblk(blk0 + j, grp)
            if t == TL[1] and j + 1 < 8:
                load_wblk(blk0 + j + 1, grp)
            s_ = wslot[(blk0 + j, grp)]
            bank = u % 3
            for kc in range(32):
                P.op("pe", lambda e, kc=kc: e.matmul(ps[bank][:, 0:256], lhsT=hTo[:, t % NG, kc * 128:(kc + 1) * 128], rhs=wblk[s_][:, kc, :],
                                                     start=(kc == 0), stop=(kc == 31)),
                     reads=["hTo%d" % (t % NG), "wblk%d" % s_], writes=["ps%d" % bank])
        stages = [s0] + stages_fn(units)
        pipeline(len(units), stages, delays)

    def v_stages(units):
        def s1(u):
            j, t = units[u]
            bank, g_ = u % 3, u % 2
            P.op("act", lambda e: e.activation(out=gtmp[g_], in_=ps[bank][:, 0:256], func=AF.Gelu, accum_out=vsum[:, t * 8 + j:t * 8 + j + 1]),
                 reads=["ps%d" % bank], writes=["gtmp%d" % g_, "vsum%d" % (t * 8 + j)])
            P.op("act", lambda e: e.activation(out=junkC, in_=gtmp[g_], func=AF.Square, accum_out=vsq[:, t * 8 + j:t * 8 + j + 1]),
                 reads=["gtmp%d" % g_], writes=["vsq%d" % (t * 8 + j)])

        def s2(u):
            j, t = units[u]
            g_ = u % 2
            P.op("dve", lambda e: e.tensor_copy(out=gv[:, t % NG, j * 256:(j + 1) * 256], in_=gtmp[g_]), reads=["gtmp%d" % g_], writes=["gv%d" % (t % NG)])
        return [s1, s2]

    def u_stages(units):
        def s0b(u):
            j, t = units[u]
            mb = 3 + u % 3
            for gi in range(2):
                g = 2 * j + gi
                P.op("pe", lambda e, gi=gi, g=g: e.matmul(ps[mb][:, gi * 128:(gi + 1) * 128], lhsT=wsT[:, g, :], rhs=gv[:, t % NG, g * 128:(g + 1) * 128], start=True, stop=True),
                     reads=["wsT", "gv%d" % (t % NG)], writes=["ps%d" % mb])

        def s1(u):
            bank, g_ = u % 3, u % 2
            P.op("act", lambda e: e.activation(out=gtmp[g_], in_=ps[bank][:, 0:256], func=AF.Gelu), reads=["ps%d" % bank], writes=["gtmp%d" % g_])

        def s2(u):
            j, t = units[u]
            mb, g_ = 3 + u % 3, u % 2
            for gi in range(2):
                g = 2 * j + gi
                P.op("dve", lambda e, gi=gi, g=g: e.scalar_tensor_tensor(out=oa[g_][:, gi * 128:(gi + 1) * 128], in0=ps[mb][:, gi * 128:(gi + 1) * 128],
                                                                        scalar=bs_col[:, g:g + 1], in1=gtmp[g_][:, gi * 128:(gi + 1) * 128], op0=ALU.add, op1=ALU.mult),
                     reads=["ps%d" % mb, "bs_col", "gtmp%d" % g_], writes=["oa%d" % g_])
            P.op("dve", lambda e: e.tensor_copy(out=oab[g_], in_=oa[g_]), reads=["oa%d" % g_], writes=["oab%d" % g_])

        def s3(u):
            j, t = units[u]
            g_ = u % 2
            tb = 6 + u % 2
            for gi in range(2):
                P.op("pe", lambda e, gi=gi: e.transpose(out=psb[tb][:, gi * 128:(gi + 1) * 128], in_=oab[g_][:, gi * 128:(gi + 1) * 128], identity=ident),
                     reads=["oab%d" % g_, "ident"], writes=["ps%d" % tb])
            col = t * 8 + j
            P.op("act", lambda e: e.activation(out=junkC, in_=oa[g_], func=AF.Square, accum_out=ssA[:, col:col + 1]), reads=["oa%d" % g_], writes=["ssA%d" % col])

        def s4(u):
            j, t = units[u]
            g_ = u % 2
            tb = 6 + u % 2
            for gi in range(2):
                kc = 2 * j + gi
                P.op("act", lambda e, gi=gi, kc=kc: e.activation(out=oTs[g_][:, gi, :], in_=psb[tb][:, gi * 128:(gi + 1) * 128], func=AF.Copy, scale=goutc[:, kc:kc + 1]),
                     reads=["ps%d" % tb, "goutc"], writes=["oTs%d" % g_])
            P.op("sp", lambda e: e.dma_start(out=oT_scr[2 * j:2 * j + 2].rearrange("k p n -> p k n")[:, :, t * 128:(t + 1) * 128], in_=oTs[g_]),
                 reads=["oTs%d" % g_], writes=["oT_scr"], dma="S:oT%d" % g_)
        return [s0b, s1, s2, s3, s4]

    def q_stages(units):
        def s1(u):
            bank, g_ = u % 3, u % 2
            for h in range(2):
                P.op("act", lambda e, h=h: e.activation(out=junkC[:, h * 128:(h + 1) * 128], in_=ps[bank][:, h * 128:(h + 1) * 128], func=AF.Square, accum_out=ssq[g_][:, h:h + 1]),
                     reads=["ps%d" % bank], writes=["ssq%d" % g_])
            P.op("act", lambda e: e.activation(out=ssq[g_], in_=ssq[g_], func=AF.Sqrt, scale=1.0 / 128, bias=epsc), reads=["ssq%d" % g_, "cst"], writes=["ssq%d" % g_])

        def s2(u):
            j, t = units[u]
            bank, g_ = u % 3, u % 2
            P.op("dve", lambda e: e.reciprocal(out=ssq[g_], in_=ssq[g_]), reads=["ssq%d" % g_], writes=["ssq%d" % g_])
            q3 = qn.rearrange("p (h d) -> p h d", h=2)
            P.op("dve", lambda e: e.tensor_tensor(out=q3, in0=ps[bank][:, 0:256].rearrange("p (h d) -> p h d", h=2), in1=ssq[g_].unsqueeze(2).broadcast_to([128, 2, 128]), op=ALU.mult),
                 reads=["ps%d" % bank, "ssq%d" % g_], writes=["Cqn"])
            P.op("dve", lambda e: e.tensor_tensor(out=q3, in0=q3, in1=gq_t.unsqueeze(1).broadcast_to([128, 2, 128]), op=ALU.mult), reads=["Cqn", "gq_t"], writes=["Cqn"])
            rope_apply(qn, 2, t, qr[g_], qt1, qt2, ["Cqn"], "qr%d" % g_, "Cq", tabs=(COSo, SINo, "COSo", "SINo"))

        def s3(u):
            g_ = u % 2
            tb = 6 + u % 2
            for gi in range(2):
                P.op("pe", lambda e, gi=gi: e.transpose(out=psb[tb][:, gi * 128:(gi + 1) * 128], in_=qr[g_][:, gi * 128:(gi + 1) * 128], identity=ident),
                     reads=["qr%d" % g_, "ident"], writes=["ps%d" % tb])

        def s4(u):
            j, t = units[u]
            tb = 6 + u % 2
            P.op("act", lambda e: e.activation(out=qT[:, 2 * j:2 * j + 2, t * 128:(t + 1) * 128], in_=psb[tb][:, 0:256].rearrange("p (h n) -> p h n", h=2), func=AF.Copy),
                 reads=["ps%d" % tb], writes=["qT"])
        return [s1, s2, s3, s4]

    mean = vst[:, 2 * NOWN:3 * NOWN]
    var = vst[:, 3 * NOWN:4 * NOWN]
    for grp in range(NOWN // NG):
        TL = list(range(grp * NG, grp * NG + NG))
        for t in TL:
            ld("sp", hTo[:, t % NG, :], hT_scr[t], "hTo%d" % (t % NG), reads=["hT_scr%d" % t])
        run_family(grp, 8, v_stages, [0, 1, 2])
        allv = ["vsum%d" % c_ for c_ in range(NOWN * 8)] + ["vsq%d" % c_ for c_ in range(NOWN * 8)]
        P.op("dve", lambda e: e.reduce_sum(out=vst[:, 0:NOWN], in_=vsum.rearrange("p (t j) -> p t j", j=8), axis=AX.X), reads=allv, writes=["vst"])
        P.op("dve", lambda e: e.reduce_sum(out=vst[:, NOWN:2 * NOWN], in_=vsq.rearrange("p (t j) -> p t j", j=8), axis=AX.X), reads=allv, writes=["vst"])
        P.op("dve", lambda e: e.tensor_scalar(out=mean, in0=vst[:, 0:NOWN], scalar1=1.0 / 2048, scalar2=None, op0=ALU.mult), reads=["vst"], writes=["vmean"])
        P.op("dve", lambda e: e.tensor_tensor(out=var, in0=mean, in1=mean, op=ALU.mult), reads=["vmean"], writes=["vvar"])
        P.op("dve", lambda e: e.scalar_tensor_tensor(out=var, in0=vst[:, NOWN:2 * NOWN], scalar=1.0 / 2048, in1=var, op0=ALU.mult, op1=ALU.subtract),
             reads=["vst", "vvar"], writes=["vvar"])
        P.op("act", lambda e: e.activation(out=var, in_=var, func=AF.Sqrt, bias=epsc), reads=["vvar", "cst"], writes=["vvar"])
        P.op("dve", lambda e: e.reciprocal(out=var, in_=var), reads=["vvar"], writes=["vvar"])
        for t in TL:
            P.op("dve", lambda e, t=t: e.tensor_scalar(out=gv[:, t % NG, :], in0=gv[:, t % NG, :], scalar1=mean[:, t:t + 1], scalar2=var[:, t:t + 1], op0=ALU.subtract, op1=ALU.mult),
                 reads=["gv%d" % (t % NG), "vmean", "vvar"], writes=["gv%d" % (t % NG)])
            P.op("dve", lambda e, t=t: e.tensor_tensor(out=gv[:, t % NG, :], in0=gv[:, t % NG, :], in1=lng, op=ALU.mult), reads=["gv%d" % (t % NG), "lng"], writes=["gv%d" % (t % NG)])
            P.op("dve", lambda e, t=t: e.tensor_tensor(out=gv[:, t % NG, :], in0=gv[:, t % NG, :], in1=lnb, op=ALU.add), reads=["gv%d" % (t % NG), "lnb"], writes=["gv%d" % (t % NG)])
        run_family(grp, 0, u_stages, [0, 0, 1, 2, 3, 4])
        run_family(grp, 16, q_stages, [0, 1, 2, 3, 4])
    allssA = ["ssA%d" % c_ for c_ in range(NOWN * 8)]
    P.barrier()
    sb.release(mC)

    if stop == "C":
        dbgC = nc.dram_tensor("dbgC", [128, 16 * NOWN * 128], BF16, kind="ExternalOutput").ap()
        P.op("sp", lambda e: e.dma_start(out=dbgC, in_=qT.rearrange("p h n -> p (h n)")), reads=["qT"], dma="dbg")
        dbgC2 = nc.dram_tensor("dbgC2", [128, NOWN * 8], F32, kind="ExternalOutput").ap()
        P.op("sp", lambda e: e.dma_start(out=dbgC2, in_=ssA), reads=allssA, dma="dbg")
        P.emit(final_waits=None)
        return nc
    mD = sb.mark()
    KTh = [sb.alloc([128, NKEY], BF16) for _ in range(2)]
    Vh = [sb.alloc([128, NT, VW], BF16) for _ in range(2)]
    NPT = 3
    PT = [sb.alloc([128, 512], BF16) for _ in range(NPT)]
    ob = [[sb.alloc([128, 128], F32) for _ in range(4)] for _ in range(2)]
    obb = [sb.alloc([128, 128], BF16) for _ in range(4)]
    rden = sb.alloc([128, 4], F32)
    obT = [sb.alloc([128, 512], BF16) for _ in range(2)]
    junkD = sb.alloc([128, 128], F32)
    SCALE = 1.0 / math.sqrt(128.0)
    QB = [(0, 512), (512, 512), (1024, 256)]
    blocks = []
    for kvh in range(4):
        for hh in range(4):
            for (q0, nq) in QB:
                blocks.append((kvh, kvh * 4 + hh, q0, nq))
    units = [(bi, kt) for bi in range(len(blocks)) for kt in range(NT)]
    loaded = set()

    def load_kv(kvh):
        if kvh in loaded or kvh >= 4:
            return
        loaded.add(kvh)
        s_ = kvh % 2
        ld("sp", KTh[s_], KT_scr[kvh], "KTh%d" % s_, reads=["KT_scr"])
        ld("sp", Vh[s_].rearrange("p t c -> p (t c)"), V_scr[kvh], "Vh%d" % s_, reads=["V_scr"])

    def st_S(u):
        bi, kt = units[u]
        kvh, head, q0, nq = blocks[bi]
        load_kv(kvh)
        s_ = kvh % 2
        sbk = u % 3
        P.op("pe", lambda e: e.matmul(ps[sbk][:, 0:nq], lhsT=KTh[s_][:, kt * 128:(kt + 1) * 128], rhs=qT[:, head, q0:q0 + nq], start=True, stop=True),
             reads=["KTh%d" % s_, "qT"], writes=["ps%d" % sbk])

    def st_exp(u):
        bi, kt = units[u]
        kvh, head, q0, nq = blocks[bi]
        sbk = u % 3
        pk = u % NPT
        P.op("act", lambda e: e.activation(out=PT[pk][:, 0:nq], in_=ps[sbk][:, 0:nq], func=AF.Exp, scale=SCALE),
             reads=["ps%d" % sbk], writes=["PT%d" % pk])

    def epi_dve(bi):
        kvh, head, q0, nq = blocks[bi]
        nsub = nq // 128
        so = bi % 2
        for qs in range(nsub):
            pb, pbk = ps[3 + qs], "ps%d" % (3 + qs)
            P.op("dve", lambda e, pb=pb, qs=qs: e.reciprocal(out=rden[:, qs:qs + 1], in_=pb[:, 128:129]), reads=[pbk], writes=["rden%d" % qs])
            P.op("dve", lambda e, pb=pb, qs=qs: e.tensor_scalar(out=ob[so][qs], in0=pb[:, 0:128], scalar1=rden[:, qs:qs + 1], scalar2=None, op0=ALU.mult),
                 reads=[pbk, "rden%d" % qs], writes=["ob%d_%d" % (so, qs)])
        for qs in range(nsub):
            P.op("dve", lambda e, qs=qs: e.tensor_copy(out=obb[qs], in_=ob[so][qs]), reads=["ob%d_%d" % (so, qs)], writes=["obb%d" % qs])
        for qs in range(nsub):
            P.op("pe", lambda e, qs=qs: e.transpose(out=psb[7][:, qs * 128:(qs + 1) * 128], in_=obb[qs], identity=ident), reads=["obb%d" % qs, "ident"], writes=["ps7"])
        P.op("dve", lambda e: e.tensor_scalar(out=obT[so][:, 0:nq], in0=psb[7][:, 0:nq], scalar1=goutc[:, 16 + head:17 + head], scalar2=None, op0=ALU.mult),
             reads=["ps7", "goutc"], writes=["obT%d" % so])
        P.op("sp", lambda e: e.dma_start(out=oT_scr[16 + head, :, q0:q0 + nq], in_=obT[so][:, 0:nq]), reads=["obT%d" % so], writes=["oT_scr"], dma="S:obT%d" % so)

    def epi_act(bi):
        kvh, head, q0, nq = blocks[bi]
        so = bi % 2
        for qs in range(nq // 128):
            t = q0 // 128 + qs
            col = t * 16 + head
            P.op("act", lambda e, qs=qs, col=col: e.activation(out=junkD, in_=ob[so][qs], func=AF.Square, accum_out=ssB[:, col:col + 1]),
                 reads=["ob%d_%d" % (so, qs)], writes=["ssB%d" % col])

    def st_PV(u):
        bi, kt = units[u]
        kvh, head, q0, nq = blocks[bi]
        s_ = kvh % 2
        pk = u % NPT
        for qs in range(nq // 128):
            P.op("pe", lambda e, qs=qs: e.matmul(ps[3 + qs][:, 0:129], lhsT=PT[pk][:, qs * 128:(qs + 1) * 128], rhs=Vh[s_][:, kt, 0:129],
                                                 start=(kt == 0), stop=(kt == NT - 1)),
                 reads=["PT%d" % pk, "Vh%d" % s_], writes=["ps%d" % (3 + qs)])
        if kt == NT - 1:
            epi_dve(bi)
        if kt == 8 and bi > 0:
            epi_act(bi - 1)
        if kt == 0 and bi % 12 == 1:
            load_kv(kvh + 1)

    pipeline(len(units), [st_S, st_exp, st_PV], [0, 1, 3])
    epi_act(len(blocks) - 1)
    allssB = ["ssB%d" % c_ for c_ in range(NOWN * 16)]
    P.barrier()
    sb.release(mD)
    sb.release(mC)
    sb.release(mTab)

    if stop == "D":
        dbgD = nc.dram_tensor("dbgD", [128, NOWN * 16], F32, kind="ExternalOutput").ap()
        P.op("sp", lambda e: e.dma_start(out=dbgD, in_=ssB), reads=allssB, dma="dbg")
        P.emit(final_waits=None)
        return nc
    mE = sb.mark()
    oT = sb.alloc([128, 32, NOWN * 128], BF16)
    wo = [sb.alloc([128, 32, 512], BF16) for _ in range(2)]
    osb = [sb.alloc([128, 512], F32) for _ in range(2)]
    otmp = sb.alloc([128, 512], F32)
    junkE = sb.alloc([128, 512], F32)
    for kc in range(32):
        ld("sp", oT[:, kc, :], oT_scr[kc], "oT", reads=["oT_scr"], writes=["oT"])
    P.op("dve", lambda e: e.reduce_sum(out=rA, in_=ssA.rearrange("p (t j) -> p t j", j=8), axis=AX.X), reads=allssA, writes=["rA"])
    P.op("dve", lambda e: e.reduce_sum(out=rB, in_=ssB.rearrange("p (t j) -> p t j", j=16), axis=AX.X), reads=allssB, writes=["rB"])
    rstd_from_ss(rA, rA, 2048, ["rA"], "rA")
    rstd_from_ss(rB, rB, 2048, ["rB"], "rB")
    nE = 0
    for cbk in range(8):
        s_ = cbk % 2
        P.op("pool", lambda e, s_=s_, cbk=cbk: e.dma_start(out=wo[s_].rearrange("p k c -> p (k c)"), in_=w_out_h[cbk]), writes=["wo%d" % s_], dma="L:wo%d" % s_)
        for t in range(NOWN):
            pa, pbn = nE % 2, 2 + (nE % 2)
            so = nE % 2
            nE += 1
            for kc in range(16):
                P.op("pe", lambda e, pa=pa, kc=kc, t=t, s_=s_: e.matmul(ps[pa], lhsT=oT[:, kc, t * 128:(t + 1) * 128], rhs=wo[s_][:, kc, :], start=(kc == 0), stop=(kc == 15)),
                     reads=["oT", "wo%d" % s_], writes=["ps%d" % pa])
            for kc in range(16, 32):
                P.op("pe", lambda e, pbn=pbn, kc=kc, t=t, s_=s_: e.matmul(ps[pbn], lhsT=oT[:, kc, t * 128:(t + 1) * 128], rhs=wo[s_][:, kc, :], start=(kc == 16), stop=(kc == 31)),
                     reads=["oT", "wo%d" % s_], writes=["ps%d" % pbn])
            P.op("act", lambda e, pa=pa, t=t: e.activation(out=otmp, in_=ps[pa], func=AF.Copy, scale=rA[:, t:t + 1]), reads=["ps%d" % pa, "rA"], writes=["otmp"])
            P.op("dve", lambda e, pbn=pbn, t=t, so=so: e.scalar_tensor_tensor(out=osb[so], in0=ps[pbn], scalar=rB[:, t:t + 1], in1=otmp, op0=ALU.mult, op1=ALU.add),
                 reads=["ps%d" % pbn, "rB", "otmp"], writes=["osb%d" % so])
            P.op("act", lambda e, so=so, t=t, cbk=cbk: e.activation(out=junkE, in_=osb[so], func=AF.Square, accum_out=ssO[:, t * 8 + cbk:t * 8 + cbk + 1]),
                 reads=["osb%d" % so], writes=["ssO"])
            P.op("sp", lambda e, so=so, t=t, cbk=cbk: e.dma_start(out=o_scr[t * 128:(t + 1) * 128, cbk * 512:(cbk + 1) * 512], in_=osb[so]),
                 reads=["osb%d" % so], writes=["o_scr"], dma="S:osb%d" % so)
    P.op("dve", lambda e: e.reduce_sum(out=rO, in_=ssO.rearrange("p (t j) -> p t j", j=8), axis=AX.X), reads=["ssO"], writes=["rO"])
    rstd_from_ss(rO, rO, D, ["rO"], "rO")
    P.barrier()
    sb.release(mE)

    if stop == "E":
        dbgE = nc.dram_tensor("dbgE", [128, NOWN], F32, kind="ExternalOutput").ap()
        P.op("sp", lambda e: e.dma_start(out=dbgE, in_=rO), reads=["rO"], dma="dbg")
        P.emit(final_waits=None)
        return nc
    out_ops = []
    for blk in range(2):
        mF = sb.mark()
        HTF_BYTES = 32 * 514 * 2
        HTF_OFF = (SB.LIMIT - HTF_BYTES) // 64 * 64
        hTf = nc.alloc_sbuf_tensor_at("hTf%d" % blk, [128, 32, 514], BF16, offset=HTF_OFF).ap()
        mF2 = sb.mark()
        orow = [sb.alloc([128, D], F32) for _ in range(2)]
        xrow = [sb.alloc([128, D], F32) for _ in range(2)]
        xm = [sb.alloc([128, D], F32) for _ in range(2)]
        ggrow = sb.alloc([128, D], F32)
        junkF = sb.alloc([128, D], BF16)
        xhF = [sb.alloc([128, D], BF16) for _ in range(2)]
        coltmp = sb.alloc([128, 32], F32)
        stF = sb.alloc([128, 2], F32)
        ld("sp", ggrow, gg_scr[0], "ggrow", reads=["gg_scr"])

        def f_s0(ti):
            t, s_ = 4 * blk + ti, ti % 2
            ld("sp", orow[s_], o_scr[t * 128:(t + 1) * 128, :], "orow%d" % s_, reads=["o_scr"])
            ld("sp", xrow[s_], xr[t * 128:(t + 1) * 128, :], "xrow%d" % s_)

        def f_s1(ti):
            t, s_ = 4 * blk + ti, ti % 2
            P.op("dve", lambda e: e.scalar_tensor_tensor(out=xm[s_], in0=orow[s_], scalar=rO[:, t:t + 1], in1=ggrow, op0=ALU.mult, op1=ALU.mult),
                 reads=["orow%d" % s_, "rO", "ggrow"], writes=["xm%d" % s_])
            P.op("dve", lambda e: e.tensor_tensor(out=xm[s_], in0=xm[s_], in1=xrow[s_], op=ALU.add), reads=["xm%d" % s_, "xrow%d" % s_], writes=["xm%d" % s_])
            if 1 <= ti <= 4:
                own = t - 1
                P.op("sp", lambda e: e.dma_start(out=xm_scr[own * 128:(own + 1) * 128, :], in_=xm[s_]), reads=["xm%d" % s_], writes=["xm_scr"], dma="S:xm%d" % s_)

        def f_s2(ti):
            s_ = ti % 2
            P.op("act", lambda e: e.activation(out=junkF, in_=xm[s_], func=AF.Square, accum_out=stF[:, s_:s_ + 1]), reads=["xm%d" % s_], writes=["stF%d" % s_])
            rstd_from_ss(stF[:, s_:s_ + 1], stF[:, s_:s_ + 1], D, ["stF%d" % s_], "stF%d" % s_)

        def f_s3(ti):
            s_ = ti % 2
            P.op("act", lambda e: e.activation(out=xhF[s_], in_=xm[s_], func=AF.Copy, scale=stF[:, s_:s_ + 1]), reads=["xm%d" % s_, "stF%d" % s_], writes=["xhF%d" % s_])

        def f_s4(ti):
            s_ = ti % 2
            TB = (0, 1, 2, 3)
            for g4 in range(4):
                for j in range(8):
                    kc = g4 * 8 + j
                    P.op("pe", lambda e, g4=g4, j=j, kc=kc: e.transpose(out=psb[TB[g4]][:, j * 128:(j + 1) * 128], in_=xhF[s_][:, kc * 128:(kc + 1) * 128], identity=ident),
                         reads=["xhF%d" % s_, "ident"], writes=["ps%d" % TB[g4]])
            if 1 <= ti <= 4:
                for g4 in range(4):
                    bank, bk = psb[TB[g4]], "ps%d" % TB[g4]
                    for j in range(8):
                        kc = g4 * 8 + j
                        dst = hTf[:, kc, (ti - 1) * 128:ti * 128]
                        if g4 % 2 == 0:
                            P.op("act", lambda e, bank=bank, j=j, kc=kc, dst=dst: e.activation(out=dst, in_=bank[:, j * 128:(j + 1) * 128], func=AF.Identity,
                                                                                              scale=Gf[:, kc:kc + 1], bias=SHf[:, kc:kc + 1]),
                                 reads=[bk, "Gf", "modc"], writes=["hTf_%d_%d" % (ti, kc)])
                        else:
                            P.op("dve", lambda e, bank=bank, j=j, kc=kc, dst=dst: e.tensor_scalar(out=dst, in0=bank[:, j * 128:(j + 1) * 128],
                                                                                                 scalar1=Gf[:, kc:kc + 1], scalar2=SHf[:, kc:kc + 1], op0=ALU.mult, op1=ALU.add),
                                 reads=[bk, "Gf", "modc"], writes=["hTf_%d_%d" % (ti, kc)])
            else:
                col = 127 if ti == 0 else 0
                dstc = 512 if ti == 0 else 513
                if blk == 0 and ti == 0:
                    mcol = hmask[:, 0:1]
                elif blk == 1 and ti == 5:
                    mcol = hmask[:, 1:2]
                else:
                    mcol = cst[:, 3:4]
                for g4 in range(4):
                    bank, bk = psb[TB[g4]], "ps%d" % TB[g4]
                    src = bank[:, 0:1024].rearrange("p (j t) -> p j t", j=8)[:, :, col]
                    sl = slice(g4 * 8, g4 * 8 + 8)
                    P.op("dve", lambda e, src=src, sl=sl: e.tensor_tensor(out=coltmp[:, sl], in0=src, in1=Gf[:, sl], op=ALU.mult), reads=[bk, "Gf"], writes=["Fct"])
                    P.op("dve", lambda e, sl=sl: e.tensor_tensor(out=coltmp[:, sl], in0=coltmp[:, sl], in1=SHf[:, sl], op=ALU.add), reads=["Fct", "modc"], writes=["Fct"])
                    P.op("dve", lambda e, sl=sl, dstc=dstc, mcol=mcol: e.tensor_scalar(out=hTf[:, sl, dstc], in0=coltmp[:, sl], scalar1=mcol, scalar2=None, op0=ALU.mult),
                         reads=["Fct", "hmask", "cst"], writes=["hTf_h%d_%d" % (ti, g4)])

        pipeline(6, [f_s0, f_s1, f_s2, f_s3, f_s4], [0, 1, 2, 3, 4])
        assert sb.off <= HTF_OFF, sb.off
        hTf_keys = ["hTf_%d_%d" % (ti, kc) for ti in range(1, 5) for kc in range(32)] + ["hTf_h%d_%d" % (ti, g4) for ti in (0, 5) for g4 in range(4)]
        P.barrier()
        sb.release(mF2)
        act = sb.alloc([128, NFC, 512], BF16)
        mG = sb.mark()
        wg = [sb.alloc([128, 32, 128], BF16) for _ in range(2)]
        wu = [sb.alloc([128, 32, 128], BF16) for _ in range(2)]
        ag = sb.alloc([128, 514], F32)
        au = sb.alloc([128, 514], F32)
        cg = sb.alloc([128, 512], F32)
        cu = sb.alloc([128, 512], F32)
        sg = sb.alloc([128, 512], F32)
        cp4 = convp.rearrange("p (c f) -> p c f", f=4)
        for i in range(NFC):
            s_ = i % 2
            P.op("pool", lambda e, s_=s_, i=i: e.dma_start(out=wg[s_].rearrange("p k c -> p (k c)"), in_=w_up_g[i]), writes=["wg%d" % s_], dma="L:wg%d" % s_)
            P.op("pool", lambda e, s_=s_, i=i: e.dma_start(out=wu[s_].rearrange("p k c -> p (k c)"), in_=w_up_u[i]), writes=["wu%d" % s_], dma="L:wu%d" % s_)
            for (wsrc, wkey, pm, ph, abuf, akey, cbuf, ckey, ci) in ((wg[s_], "wg%d" % s_, s_, 4, ag, "ag", cg, "cg", i),
                                                                   (wu[s_], "wu%d" % s_, 2 + s_, 5, au, "au", cu, "cu", NFC + i)):
                for kc in range(32):
                    P.op("pe", lambda e, pm=pm, kc=kc, wsrc=wsrc: e.matmul(ps[pm], lhsT=wsrc[:, kc, :], rhs=hTf[:, kc, 0:512], start=(kc == 0), stop=(kc == 31)),
                         reads=[wkey], writes=["ps%d" % pm])
                for kc in range(32):
                    P.op("pe", lambda e, ph=ph, kc=kc, wsrc=wsrc: e.matmul(ps[ph][:, 0:2], lhsT=wsrc[:, kc, :], rhs=hTf[:, kc, 512:514], start=(kc == 0), stop=(kc == 31)),
                         reads=[wkey], writes=["ps%d" % ph])
                P.op("act", lambda e, pm=pm, abuf=abuf: e.activation(out=abuf[:, 1:513], in_=ps[pm], func=AF.Copy), reads=["ps%d" % pm], writes=[akey])
                P.op("dve", lambda e, ph=ph, abuf=abuf: e.tensor_copy(out=abuf[:, 0:1], in_=ps[ph][:, 0:1]), reads=["ps%d" % ph], writes=[akey])
                P.op("dve", lambda e, ph=ph, abuf=abuf: e.tensor_copy(out=abuf[:, 513:514], in_=ps[ph][:, 1:2]), reads=["ps%d" % ph], writes=[akey])
                P.op("dve", lambda e, abuf=abuf, cbuf=cbuf, ci=ci: e.tensor_scalar(out=cbuf, in0=abuf[:, 0:512], scalar1=cp4[:, ci, 0:1], scalar2=cp4[:, ci, 3:4], op0=ALU.mult, op1=ALU.add),
                     reads=[akey, "convp"], writes=[ckey])
                P.op("dve", lambda e, abuf=abuf, cbuf=cbuf, ci=ci: e.scalar_tensor_tensor(out=cbuf, in0=abuf[:, 1:513], scalar=cp4[:, ci, 1:2], in1=cbuf, op0=ALU.mult, op1=ALU.add),
                     reads=[akey, "convp", ckey], writes=[ckey])
                P.op("dve", lambda e, abuf=abuf, cbuf=cbuf, ci=ci: e.scalar_tensor_tensor(out=cbuf, in0=abuf[:, 2:514], scalar=cp4[:, ci, 2:3], in1=cbuf, op0=ALU.mult, op1=ALU.add),
                     reads=[akey, "convp", ckey], writes=[ckey])
            P.op("act", lambda e: e.activation(out=sg, in_=cg, func=AF.Silu), reads=["cg"], writes=["sg"])
            P.op("dve", lambda e, i=i: e.tensor_tensor(out=act[:, i, :], in0=sg, in1=cu, op=ALU.mult), reads=["sg", "cu"], writes=["act"])
        assert sb.off <= HTF_OFF, sb.off
        P.barrier()
        sb.release(mG)
        KG = 8
        groups = [(k0, min(k0 + KG, NFC)) for k0 in range(0, NFC, KG)]
        wd = [sb.alloc([128, KG, 512], BF16) for _ in range(3)]
        fsb = [sb.alloc([128, 512], F32) for _ in range(2)]
        junkG = sb.alloc([128, 512], F32)
        nwd = 0
        nf = 0
        for cbk in range(8):
            base = 4 * (cbk % 2)
            for (k0, k1) in groups:
                s_ = nwd % 3
                nwd += 1
                P.op("pool", lambda e, s_=s_, cbk=cbk, k0=k0, k1=k1: e.dma_start(out=wd[s_][:, 0:k1 - k0, :].rearrange("p k c -> p (k c)"),
                                                                               in_=w_down_h[cbk][:, k0 * 512:k1 * 512]),
                     writes=["wd%d" % s_], dma="L:wd%d" % s_)
                for kc in range(k0, k1):
                    for q in range(4):
                        P.op("pe", lambda e, base=base, q=q, kc=kc, k0=k0, s_=s_: e.matmul(ps[base + q], lhsT=act[:, kc, q * 128:(q + 1) * 128], rhs=wd[s_][:, kc - k0, :],
                                                                                       start=(kc == 0), stop=(kc == NFC - 1)),
                             reads=["act", "wd%d" % s_], writes=["ps%d" % (base + q)])
            for q in range(4):
                so = nf % 2
                nf += 1
                own = 4 * blk + q
                P.op("act", lambda e, base=base, q=q, so=so: e.activation(out=fsb[so], in_=ps[base + q], func=AF.Copy), reads=["ps%d" % (base + q)], writes=["fsb%d" % so])
                P.op("act", lambda e, so=so, q=q, cbk=cbk: e.activation(out=junkG, in_=fsb[so], func=AF.Square, accum_out=ssF[:, q * 8 + cbk:q * 8 + cbk + 1]),
                     reads=["fsb%d" % so], writes=["ssF"])
                P.op("sp", lambda e, so=so, own=own, cbk=cbk: e.dma_start(out=f_scr[own * 128:(own + 1) * 128, cbk * 512:(cbk + 1) * 512], in_=fsb[so]),
                     reads=["fsb%d" % so], writes=["f_scr"], dma="S:fsb%d" % so)
        P.op("dve", lambda e: e.reduce_sum(out=rF[:, 0:4], in_=ssF[:, 0:32].rearrange("p (t j) -> p t j", j=8), axis=AX.X), reads=["ssF"], writes=["rF"])
        rstd_from_ss(rF[:, 0:4], rF[:, 0:4], D, ["rF"], "rF")
        frow = [sb.alloc([128, D], F32) for _ in range(2)]
        xmrow = [sb.alloc([128, D], F32) for _ in range(2)]
        ggf = sb.alloc([128, D], F32)
        ld("sp", ggf, gg_scr[1], "ggf", reads=["gg_scr"])

        def z_s0(q):
            own, s_ = 4 * blk + q, q % 2
            ld("sp", frow[s_], f_scr[own * 128:(own + 1) * 128, :], "frow%d" % s_, reads=["f_scr"])
            ld("sp", xmrow[s_], xm_scr[own * 128:(own + 1) * 128, :], "xmrow%d" % s_, reads=["xm_scr"])

        def z_s1(q):
            own, s_ = 4 * blk + q, q % 2
            P.op("dve", lambda e: e.scalar_tensor_tensor(out=frow[s_], in0=frow[s_], scalar=rF[:, q:q + 1], in1=ggf, op0=ALU.mult, op1=ALU.mult),
                 reads=["frow%d" % s_, "rF", "ggf"], writes=["frow%d" % s_])
            P.op("dve", lambda e: e.tensor_tensor(out=frow[s_], in0=frow[s_], in1=xmrow[s_], op=ALU.add), reads=["frow%d" % s_, "xmrow%d" % s_], writes=["frow%d" % s_])
            o = P.op("sp", lambda e: e.dma_start(out=out[own * 128:(own + 1) * 128, :], in_=frow[s_]), reads=["frow%d" % s_], dma="S:out%d" % s_)
            out_ops.append(o.idx)

        pipeline(4, [z_s0, z_s1], [0, 1])
        P.barrier()
        sb.release(mF)

    P.emit(final_waits=out_ops)
    return nc


_CACHE = {}


def _host_layouts(inp):
    f = np.float32
    A = {}

    def col(v):
        v = np.asarray(v, f).reshape(-1, 128)
        return np.ascontiguousarray(v.T)

    A["ctx"] = np.ascontiguousarray(inp["ctx"][0], dtype=f)
    A["jidx"] = np.ascontiguousarray(np.broadcast_to(np.arange(32, dtype=f)[None, :], (128, 32)))
    A["cvec"] = np.concatenate([col(inp["c"][0]), col(inp["c_ctx"])], axis=1)
    w_ada_h = np.ascontiguousarray(np.asarray(inp["w_ada"][0], f).reshape(32, 128, 192, 128).transpose(2, 1, 0, 3)).reshape(192, 128, 32 * 128)
    A["w_ada_a"] = np.ascontiguousarray(w_ada_h[:96])
    A["w_ada_b"] = np.ascontiguousarray(w_ada_h[96:])
    A["b_ada_col"] = col(inp["b_ada"][0])
    A["b_ada_row"] = np.ascontiguousarray(inp["b_ada"][0][None, :], dtype=f)
    A["gcols"] = np.concatenate([col(inp["g_pre_mix"][0]), col(inp["g_pre_ffn"][0])], axis=1)
    A["goutc"] = col(np.concatenate([inp["g_out_a"][0], inp["g_out_b"][0]]))
    A["gpost_mix_row"] = np.ascontiguousarray(inp["g_post_mix"][0][None, :], dtype=f)
    A["gpost_ffn_row"] = np.ascontiguousarray(inp["g_post_ffn"][0][None, :], dtype=f)
    A["lng_row"] = np.ascontiguousarray(inp["ln_v_g"][0][None, :], dtype=f)
    A["lnb_row"] = np.ascontiguousarray(inp["ln_v_b"][0][None, :], dtype=f)
    A["gq_row"] = np.ascontiguousarray(inp["g_q"][0][None, :], dtype=f)
    A["gk_row"] = np.ascontiguousarray(inp["g_k"][0][None, :], dtype=f)
    w_in = np.asarray(inp["w_in"][0], f)
    A["w_in_h"] = np.ascontiguousarray(w_in.reshape(32, 128, 28, 256).transpose(2, 1, 0, 3)).reshape(28, 128, 32 * 256)
    ws = np.asarray(inp["w_s"][0], f)
    A["wsT"] = np.ascontiguousarray(ws.transpose(2, 0, 1)).reshape(128, 16 * 128)
    A["bs_col"] = np.ascontiguousarray(np.asarray(inp["b_s"][0], f).T)
    w_out = np.asarray(inp["w_out"][0], f)
    A["w_out_h"] = np.ascontiguousarray(w_out.reshape(32, 128, 8, 512).transpose(2, 1, 0, 3)).reshape(8, 128, 32 * 512)
    w_up = np.asarray(inp["w_up"][0], f)
    w_up_h = np.ascontiguousarray(w_up.reshape(32, 128, 2 * NFC, 128).transpose(2, 1, 0, 3)).reshape(2 * NFC, 128, 32 * 128)
    A["w_up_g"] = np.ascontiguousarray(w_up_h[:NFC])
    A["w_up_u"] = np.ascontiguousarray(w_up_h[NFC:])
    cw = np.asarray(inp["conv_w"][0], f)
    cb = np.asarray(inp["conv_b"][0], f)
    cp = np.concatenate([cw, cb[None, :]], axis=0)
    A["convp"] = np.ascontiguousarray(cp.reshape(4, 2 * NFC, 128).transpose(2, 1, 0)).reshape(128, 2 * NFC * 4)
    w_down = np.asarray(inp["w_down"][0], f)
    A["w_down_h"] = np.ascontiguousarray(w_down.reshape(NFC, 128, 8, 512).transpose(2, 1, 0, 3)).reshape(8, 128, NFC * 512)
    A["ident"] = np.eye(128, dtype=f)
    return A


def _run(inputs, stop=None, debug=False, cores=NCORE):
    nc = build_program(stop=stop, debug=debug)
    in_maps = _in_maps(inputs)[:cores]
    return run_bass_kernel_spmd(nc, in_maps, core_ids=list(range(cores)))


def _in_maps(inputs):
    shared = _host_layouts(inputs)
    x = np.asarray(inputs["x"][0], np.float32)
    tok = np.arange(S)
    in_maps = []
    for i in range(NCORE):
        shift = (TOK * i - 128) % S
        order = (tok + shift) % S
        m = dict(shared)
        m["xr"] = np.ascontiguousarray(x[order])
        rowi = (order // 64).astype(np.float32)
        coli = (order % 64).astype(np.float32)
        pos = np.zeros((128, NT, 2), np.float32)
        pos[:, :64, 0] = rowi.reshape(64, 128).T
        pos[:, :64, 1] = coli.reshape(64, 128).T
        m["pos"] = pos.reshape(128, NT * 2)
        hm = np.ones((128, 2), np.float32)
        if i == 0:
            hm[:, 0] = 0.0
        if i == NCORE - 1:
            hm[:, 1] = 0.0
        m["hmask"] = hm
        in_maps.append(m)
    return in_maps


def kernel(**inputs):
    if "nc" not in _CACHE:
        _CACHE["nc"] = build_program()
    nc = _CACHE["nc"]
    in_maps = _in_maps(inputs)
    res = run_bass_kernel_spmd(nc, in_maps, core_ids=list(range(NCORE)))
    outs = [np.asarray(r["out"], np.float32) for r in res.results]
    return np.concatenate(outs, axis=0)[None, :, :]
```

```python
import math
import numpy as np
import concourse.bass as bass
import concourse.mybir as mybir
from concourse.bass_utils import run_bass_kernel_spmd
from contextlib import ExitStack

F32 = mybir.dt.float32
BF16 = mybir.dt.bfloat16
AF = mybir.ActivationFunctionType
ALU = mybir.AluOpType
AX = mybir.AxisListType

ENGS = ("pe", "act", "dve", "pool", "sp")
D = 4096
S = 8192
NCORE = 8
TOK = S // NCORE
DFF = 11008
NFC = DFF // 128
EPS = 1e-6
NT = 66
NKEY = NT * 128
NOWN = 10
NG = 5
VW = 130


class Op:
    __slots__ = ("eng", "fn", "reads", "writes", "dma", "deps", "idx", "need_inc", "seq", "dma_cnt")

    def __init__(self, eng, fn, reads, writes, dma):
        self.eng, self.fn, self.reads, self.writes, self.dma = eng, fn, reads, writes, dma
        self.deps = set()
        self.need_inc = False
        self.seq = 0
        self.dma_cnt = 0


class Prog:
    def __init__(self, nc):
        self.nc = nc
        self.ops = []
        self.last_w = {}
        self.readers = {}
        self.last_eng = {}
        self.dma_since = []
        self.pending_barrier = {}

    def op(self, eng, fn, reads=(), writes=(), dma=None):
        o = Op(eng, fn, tuple(reads), tuple(writes), dma)
        o.idx = len(self.ops)
        for r in o.reads:
            w = self.last_w.get(r)
            if w is not None:
                o.deps.add(w)
        for w_ in o.writes:
            w = self.last_w.get(w_)
            if w is not None:
                o.deps.add(w)
            for rd in self.readers.get(w_, ()):
                o.deps.add(rd)
        for r in o.reads:
            self.readers.setdefault(r, []).append(o.idx)
        for w_ in o.writes:
            self.last_w[w_] = o.idx
            self.readers[w_] = []
        if eng in self.pending_barrier:
            o.deps |= self.pending_barrier.pop(eng)
        o.deps.discard(o.idx)
        self.ops.append(o)
        if dma is None:
            self.last_cmp = getattr(self, "last_cmp", {})
            self.last_cmp[eng] = o.idx
        self.last_eng[eng] = o.idx
        if dma is not None:
            self.dma_since.append(o.idx)
        return o

    def barrier(self):
        b = set(self.last_eng.values()) | set(self.dma_since)
        for e in ENGS:
            self.pending_barrier[e] = set(b) | self.pending_barrier.get(e, set())
        self.dma_since = []

    def emit(self, final_waits=()):
        nc = self.nc
        ops = self.ops
        for o in ops:
            for d in o.deps:
                p = ops[d]
                if p.dma is not None:
                    continue
                if p.eng == "pe" and o.eng == "pe":
                    continue
                p.need_inc = True
        if final_waits is None:
            for e, i_ in getattr(self, "last_cmp", {}).items():
                ops[i_].need_inc = True
            for o in ops:
                if o.dma is None and o.eng != "pe" and False:
                    o.need_inc = True
        cnt = {e: 0 for e in ENGS}
        dcnt = {}
        for o in ops:
            if o.dma is not None:
                dcnt[o.dma] = dcnt.get(o.dma, 0) + 1
                o.dma_cnt = dcnt[o.dma]
            elif o.need_inc:
                cnt[o.eng] += 1
                o.seq = cnt[o.eng]
        with ExitStack() as es:
            esem = {e: es.enter_context(nc.semaphore("s_" + e)) for e in ENGS}
            dsem = {k: es.enter_context(nc.semaphore("d_%d" % i)) for i, k in enumerate(dcnt)}
            block = es.enter_context(nc.Block())
            per_eng = {e: [o for o in ops if o.eng == e] for e in ENGS}

            def body(ename):
                def run(engine):
                    waited = {}
                    for o in per_eng[ename]:
                        need = {}
                        for d in o.deps:
                            p = ops[d]
                            if p.dma is not None:
                                key = ("d", p.dma)
                                val = 16 * p.dma_cnt
                                sem = dsem[p.dma]
                            else:
                                if p.eng == "pe" and ename == "pe":
                                    continue
                                key = ("e", p.eng)
                                val = p.seq
                                sem = esem[p.eng]
                            if val > need.get(key, (0, None))[0]:
                                need[key] = (val, sem)
                        for key, (val, sem) in need.items():
                            if waited.get(key, 0) >= val:
                                continue
                            engine.wait_ge(sem, val)
                            waited[key] = val
                        ins = o.fn(engine)
                        if o.dma is not None:
                            ins.then_inc(dsem[o.dma], 16)
                        elif o.need_inc:
                            ins.then_inc(esem[ename], 1)
                    if ename == "sp" and final_waits is None:
                        for e2 in ENGS:
                            if cnt[e2] > 0:
                                engine.wait_ge(esem[e2], cnt[e2])
                        for k2 in dcnt:
                            engine.wait_ge(dsem[k2], 16 * dcnt[k2])
                    elif ename == "sp":
                        for i in final_waits:
                            p = ops[i]
                            engine.wait_ge(dsem[p.dma], 16 * dcnt[p.dma])
                return run

            block.tensor(body("pe"))
            block.scalar(body("act"))
            block.vector(body("dve"))
            block.gpsimd(body("pool"))
            block.sync(body("sp"))


class SB:
    BASE = 16512
    LIMIT = 229376 - 64

    def __init__(self, nc):
        self.nc = nc
        self.off = SB.BASE
        self.n = 0

    def alloc(self, shape, dtype):
        nbytes = int(np.prod(shape[1:])) * (2 if dtype == BF16 else 4)
        nbytes = (nbytes + 63) // 64 * 64
        assert self.off + nbytes <= SB.LIMIT, ("SBUF overflow", self.off, nbytes)
        self.n += 1
        t = self.nc.alloc_sbuf_tensor_at("sb%d" % self.n, list(shape), dtype, offset=self.off)
        self.off += nbytes
        return t.ap()

    def mark(self):
        return self.off

    def release(self, m):
        self.off = m


def pipeline(n, stages, delays, order=None):
    maxd = max(delays)
    order = list(reversed(range(len(stages)))) if order is None else order
    for i in range(n + maxd):
        for s_ in order:
            u = i - delays[s_]
            if 0 <= u < n:
                stages[s_](u)


def build_program(stop=None, debug=False):
    nc = bass.Bass("TRN2", target_bir_lowering=False)

    def din(name, shape):
        return nc.dram_tensor(name, list(shape), F32, kind="ExternalInput").ap()

    xr = din("xr", [S, D])
    ctx = din("ctx", [256, D])
    pos_d = din("pos", [128, NT * 2])
    jidx_d = din("jidx", [128, 32])
    cvec_d = din("cvec", [128, 64])
    w_ada_a = din("w_ada_a", [96, 128, 32 * 128])
    w_ada_b = din("w_ada_b", [96, 128, 32 * 128])
    b_ada_col = din("b_ada_col", [128, 192])
    b_ada_row = din("b_ada_row", [1, 6 * D])
    gcols_d = din("gcols", [128, 64])
    goutc_d = din("goutc", [128, 32])
    gpost_mix_row = din("gpost_mix_row", [1, D])
    gpost_ffn_row = din("gpost_ffn_row", [1, D])
    lng_row = din("lng_row", [1, 2048])
    lnb_row = din("lnb_row", [1, 2048])
    gq_row = din("gq_row", [1, 128])
    gk_row = din("gk_row", [1, 128])
    w_in_h = din("w_in_h", [28, 128, 32 * 256])
    wsT_d = din("wsT", [128, 16 * 128])
    bs_col_d = din("bs_col", [128, 16])
    w_out_h = din("w_out_h", [8, 128, 32 * 512])
    w_up_g = din("w_up_g", [NFC, 128, 32 * 128])
    w_up_u = din("w_up_u", [NFC, 128, 32 * 128])
    convp_d = din("convp", [128, 2 * NFC * 4])
    w_down_h = din("w_down_h", [8, 128, NFC * 512])
    hmask_d = din("hmask", [128, 2])
    ident_d = din("ident", [128, 128])
    out = nc.dram_tensor("out", [TOK, D], F32, kind="ExternalOutput").ap()

    def dscr(name, shape, dt):
        if debug:
            return nc.dram_tensor(name, list(shape), dt, kind="ExternalOutput").ap()
        return nc.dram_tensor(name, list(shape), dt).ap()

    KT_scr = dscr("KT_scr", [4, 128, NKEY], BF16)
    V_scr = dscr("V_scr", [4, 128, NT * VW], BF16)
    hT_scr = dscr("hT_scr", [NOWN, 128, 32 * 128], BF16)
    oT_scr = dscr("oT_scr", [32, 128, NOWN * 128], BF16)
    o_scr = dscr("o_scr", [NOWN * 128, D], F32)
    xm_scr = dscr("xm_scr", [TOK, D], F32)
    f_scr = dscr("f_scr", [TOK, D], F32)
    gg_scr = dscr("gg_scr", [2, 128, D], F32)

    P = Prog(nc)
    sb = SB(nc)
    ps = [nc.alloc_psum_tensor("ps%d" % i, [128, 512], F32).ap() for i in range(8)]
    psb = [p.bitcast(BF16) for p in ps]

    ident = sb.alloc([128, 128], BF16)
    onesf = sb.alloc([128, 128], F32)
    modc = sb.alloc([128, 192], F32)
    modx = sb.alloc([128, 64], F32)
    Ga = sb.alloc([128, 32], F32)
    Gc = sb.alloc([128, 32], F32)
    Gf = sb.alloc([128, 32], F32)
    gcols = sb.alloc([128, 64], F32)
    goutc = sb.alloc([128, 32], F32)
    cst = sb.alloc([128, 8], F32)
    freq = sb.alloc([128, 32], F32)
    pos = sb.alloc([128, NT * 2], F32)
    gq_t = sb.alloc([128, 128], F32)
    gk_t = sb.alloc([128, 128], F32)
    bs_col = sb.alloc([128, 16], F32)
    hmask = sb.alloc([128, 2], F32)
    convp = sb.alloc([128, 2 * NFC * 4], F32)
    ssA = sb.alloc([128, NOWN * 8], F32)
    ssB = sb.alloc([128, NOWN * 16], F32)
    ssO = sb.alloc([128, NOWN * 8], F32)
    ssF = sb.alloc([128, 8 * 8], F32)
    rA = sb.alloc([128, NOWN], F32)
    rB = sb.alloc([128, NOWN], F32)
    rO = sb.alloc([128, NOWN], F32)
    rF = sb.alloc([128, 8], F32)
    sv = sb.alloc([128, 64], F32)
    st = sb.alloc([128, 64], F32)
    SHa = modc[:, 0:32]
    SHf = modc[:, 96:128]
    SHc = modx[:, 0:32]
    epsc = cst[:, 0:1]
    negpi = cst[:, 1:2]
    pospi = cst[:, 2:3]

    def ld(eng, dst, src, key, reads=(), writes=None):
        return P.op(eng, lambda e, dst=dst, src=src: e.dma_start(out=dst, in_=src), reads=reads,
                    writes=[key] if writes is None else writes, dma="L:" + key)

    def bc(row_ap, n=128):
        return row_ap.partition_broadcast(n).rearrange("p a b -> p (a b)")

    P.op("pool", lambda e: e.memset(cst[:, 0:1], EPS), writes=["cst"])
    P.op("pool", lambda e: e.memset(cst[:, 1:2], -math.pi), writes=["cst"])
    P.op("pool", lambda e: e.memset(cst[:, 2:3], math.pi), writes=["cst"])
    P.op("pool", lambda e: e.memset(cst[:, 3:4], 1.0), writes=["cst"])
    P.op("pool", lambda e: e.memset(onesf, 1.0), writes=["onesf"])
    ld("pool", ident, ident_d, "ident")
    ld("sp", sv, cvec_d, "sv")
    ld("sp", gcols, gcols_d, "gcols")
    ld("sp", goutc, goutc_d, "goutc")
    ld("sp", pos, pos_d, "pos")
    ld("sp", freq, jidx_d, "freq")
    ld("sp", gq_t, bc(gq_row), "gq_t")
    ld("sp", gk_t, bc(gk_row), "gk_t")
    ld("sp", bs_col, bs_col_d, "bs_col")
    ld("sp", hmask, hmask_d, "hmask")
    ld("sp", convp, convp_d, "convp")
    ld("sp", modc, b_ada_col, "modc")
    P.op("act", lambda e: e.activation(out=freq, in_=freq, func=AF.Exp, scale=-math.log(10000.0) / 32.0),
         reads=["freq"], writes=["freq"])
    P.op("act", lambda e: e.activation(out=sv, in_=sv, func=AF.Silu), reads=["sv"], writes=["sv"])

    svb = sb.alloc([128, 32, 2], BF16)
    P.op("dve", lambda e: e.tensor_copy(out=svb, in_=sv.rearrange("p (two k) -> p k two", two=2)), reads=["sv"], writes=["svb"])

    def wchunk(ch):
        return (w_ada_a if ch < 96 else w_ada_b)[ch % 96]

    mA = sb.mark()
    NWA = 8
    wa = [sb.alloc([128, 32, 128], BF16) for _ in range(NWA)]
    for ch in range(64):
        s_ = ch % NWA
        P.op("pool", lambda e, s_=s_, ch=ch: e.dma_start(out=wa[s_].rearrange("p k c -> p (k c)"), in_=wchunk(ch)), writes=["wa%d" % s_], dma="L:wa%d" % s_)
        for kc in range(32):
            P.op("pe", lambda e, s_=s_, ch=ch, kc=kc: e.matmul(ps[0][:, 2 * ch:2 * ch + 2], lhsT=wa[s_][:, kc, :], rhs=svb[:, kc, :], start=(kc == 0), stop=(kc == 31)),
                 reads=["wa%d" % s_, "svb"], writes=["ps0"])
    pc3 = ps[0][:, 0:128].rearrange("p (j two) -> p j two", two=2)
    P.op("dve", lambda e: e.tensor_tensor(out=modx, in0=pc3[:, :, 1], in1=modc[:, 0:64], op=ALU.add), reads=["ps0", "modc"], writes=["modx"])
    P.op("dve", lambda e: e.tensor_tensor(out=modc[:, 0:64], in0=pc3[:, :, 0], in1=modc[:, 0:64], op=ALU.add), reads=["ps0", "modc", "modx"], writes=["modc"])
    for (G, scl, gcol, key) in ((Ga, modc[:, 32:64], gcols[:, 0:32], "Ga"), (Gc, modx[:, 32:64], gcols[:, 0:32], "Gc")):
        P.op("dve", lambda e, G=G, scl=scl, gcol=gcol: e.scalar_tensor_tensor(out=G, in0=scl, scalar=1.0, in1=gcol, op0=ALU.add, op1=ALU.mult),
             reads=["modc", "modx", "gcols"], writes=[key])
    P.barrier()
    sb.release(mA)

    if stop == "A":
        dbgA = nc.dram_tensor("dbgA", [128, 512], F32, kind="ExternalOutput").ap()
        P.op("sp", lambda e: e.dma_start(out=dbgA[:, 0:192], in_=modc), reads=["modc"], dma="dbg")
        P.op("sp", lambda e: e.dma_start(out=dbgA[:, 192:256], in_=modx), reads=["modx"], dma="dbg")
        P.op("sp", lambda e: e.dma_start(out=dbgA[:, 256:288], in_=Ga), reads=["Ga"], dma="dbg")
        P.op("sp", lambda e: e.dma_start(out=dbgA[:, 288:320], in_=Gc), reads=["Gc"], dma="dbg")
        P.op("sp", lambda e: e.dma_start(out=dbgA[:, 320:352], in_=Gf), reads=["Gf"], dma="dbg")
        P.emit(final_waits=None)
        return nc
    def rstd_from_ss(ss_ap, out_ap, n, reads, wkey):
        P.op("act", lambda e: e.activation(out=out_ap, in_=ss_ap, func=AF.Sqrt, scale=1.0 / n, bias=epsc), reads=list(reads) + ["cst"], writes=[wkey])
        P.op("dve", lambda e: e.reciprocal(out=out_ap, in_=out_ap), reads=[wkey], writes=[wkey])

    def rope_tables(t, cos2, sins, ang, angk, angi, key, skey=None):
        skey = key if skey is None else skey
        P.op("dve", lambda e: e.tensor_scalar(out=ang[:, 0:32], in0=freq, scalar1=pos[:, 2 * t:2 * t + 1], scalar2=None, op0=ALU.mult),
             reads=["freq", "pos"], writes=[skey + "ang"])
        P.op("dve", lambda e: e.tensor_scalar(out=ang[:, 32:64], in0=freq, scalar1=pos[:, 2 * t + 1:2 * t + 2], scalar2=None, op0=ALU.mult),
             reads=["freq", "pos"], writes=[skey + "ang"])
        P.op("dve", lambda e: e.tensor_scalar(out=ang[:, 64:128], in0=ang[:, 0:64], scalar1=0.5 * math.pi, scalar2=None, op0=ALU.add),
             reads=[skey + "ang"], writes=[skey + "ang2"])
        P.op("dve", lambda e: e.tensor_scalar(out=angk, in0=ang, scalar1=1.0 / (2 * math.pi), scalar2=None, op0=ALU.mult),
             reads=[skey + "ang", skey + "ang2"], writes=[skey + "angk"])
        P.op("dve", lambda e: e.tensor_copy(out=angi, in_=angk), reads=[skey + "angk"], writes=[skey + "angi"])
        P.op("dve", lambda e: e.tensor_copy(out=angk, in_=angi), reads=[skey + "angi"], writes=[skey + "angk"])
        P.op("dve", lambda e: e.scalar_tensor_tensor(out=ang, in0=angk, scalar=-2.0 * math.pi, in1=ang, op0=ALU.mult, op1=ALU.add),
             reads=[skey + "angk", skey + "ang", skey + "ang2"], writes=[skey + "ang", skey + "ang2"])
        P.op("dve", lambda e: e.tensor_scalar(out=ang, in0=ang, scalar1=3.1415925, scalar2=-3.1415925, op0=ALU.min, op1=ALU.max),
             reads=[skey + "ang", skey + "ang2"], writes=[skey + "ang", skey + "ang2"])
        c4 = cos2.rearrange("p (a b j) -> p a b j", a=2, b=2)
        s4 = sins.rearrange("p (a b j) -> p a b j", a=2, b=2)
        sarg = ang[:, 0:64].rearrange("p (a j) -> p a j", a=2)
        carg = ang[:, 64:128].rearrange("p (a j) -> p a j", a=2)
        for b in range(2):
            P.op("act", lambda e, b=b: e.activation(out=c4[:, :, b, :], in_=carg, func=AF.Sin), reads=[skey + "ang2"], writes=[key + "cos"])
        P.op("act", lambda e: e.activation(out=s4[:, :, 1, :], in_=sarg, func=AF.Sin), reads=[skey + "ang"], writes=[key + "sin"])
        P.op("act", lambda e: e.activation(out=s4[:, :, 0, :], in_=sarg, func=AF.Sin, scale=-1.0), reads=[skey + "ang"], writes=[key + "sin"])

    def qk_norm_rope(psrc, pkey, nh, g_t, cos2, sins, tkeys, tmp, outb, okey, pfx):
        sq, qn, t1, t2, ssq = tmp["sq"], tmp["qn"], tmp["t1"], tmp["t2"], tmp["ss"]
        P.op("act", lambda e: e.activation(out=sq, in_=psrc, func=AF.Square), reads=[pkey], writes=[pfx + "sq"])
        P.op("dve", lambda e: e.reduce_sum(out=ssq[:, 0:nh], in_=sq.rearrange("p (h d) -> p h d", h=nh), axis=AX.X), reads=[pfx + "sq"], writes=[pfx + "ss"])
        rstd_from_ss(ssq[:, 0:nh], ssq[:, 0:nh], 128, [pfx + "ss"], pfx + "ss")
        for h in range(nh):
            P.op("act", lambda e, h=h: e.activation(out=qn[:, h * 128:(h + 1) * 128], in_=psrc[:, h * 128:(h + 1) * 128], func=AF.Copy, scale=ssq[:, h:h + 1]),
                 reads=[pkey, pfx + "ss"], writes=[pfx + "qn"])
        q3 = qn.rearrange("p (h d) -> p h d", h=nh)
        P.op("dve", lambda e: e.tensor_tensor(out=q3, in0=q3, in1=g_t.unsqueeze(1).broadcast_to([128, nh, 128]), op=ALU.mult),
             reads=[pfx + "qn", "gq_t", "gk_t"], writes=[pfx + "qn"])
        P.op("dve", lambda e: e.tensor_tensor(out=t1.rearrange("p (h d) -> p h d", h=nh), in0=q3, in1=cos2.unsqueeze(1).broadcast_to([128, nh, 128]), op=ALU.mult),
             reads=[pfx + "qn"] + tkeys, writes=[pfx + "t1"])
        q5 = qn.rearrange("p (h a b j) -> p h a b j", h=nh, a=2, b=2)
        t5 = t2.rearrange("p (h a b j) -> p h a b j", h=nh, a=2, b=2)
        s4 = sins.rearrange("p (a b j) -> p a b j", a=2, b=2)
        for b in range(2):
            P.op("dve", lambda e, b=b: e.tensor_tensor(out=t5[:, :, :, b, :], in0=q5[:, :, :, 1 - b, :],
                                                      in1=s4[:, :, b, :].unsqueeze(1).broadcast_to([128, nh, 2, 32]), op=ALU.mult),
                 reads=[pfx + "qn"] + tkeys, writes=[pfx + "t2"])
        P.op("dve", lambda e: e.tensor_tensor(out=outb, in0=t1, in1=t2, op=ALU.add), reads=[pfx + "t1", pfx + "t2"], writes=[okey])

    def norm_transpose(xt, xkey, G, SHv, gkeys, junk, xh, xhkey, hT_dst, hkey, pfx, banks=(0, 1), only_col=None, mask_col=None, coltmp=None):
        P.op("act", lambda e: e.activation(out=junk, in_=xt, func=AF.Square, accum_out=st[:, 0:1]), reads=[xkey], writes=[pfx + "st"])
        rstd_from_ss(st[:, 0:1], st[:, 1:2], D, [pfx + "st"], pfx + "st1")
        P.op("act", lambda e: e.activation(out=xh, in_=xt, func=AF.Copy, scale=st[:, 1:2]), reads=[xkey, pfx + "st1"], writes=[xhkey])
        for g4 in range(4):
            bank = psb[banks[g4 % 2]]
            bk = "ps%d" % banks[g4 % 2]
            for j in range(8):
                kc = g4 * 8 + j
                P.op("pe", lambda e, bank=bank, j=j, kc=kc: e.transpose(out=bank[:, j * 128:(j + 1) * 128], in_=xh[:, kc * 128:(kc + 1) * 128], identity=ident),
                     reads=[xhkey, "ident"], writes=[bk])
            if only_col is None:
                for j in range(8):
                    kc = g4 * 8 + j
                    if g4 % 2 == 0:
                        P.op("act", lambda e, bank=bank, j=j, kc=kc: e.activation(out=hT_dst[:, kc, :], in_=bank[:, j * 128:(j + 1) * 128], func=AF.Identity,
                                                                                 scale=G[:, kc:kc + 1], bias=SHv[:, kc:kc + 1]),
                             reads=[bk] + gkeys, writes=[hkey + "_%d" % kc])
                    else:
                        P.op("dve", lambda e, bank=bank, j=j, kc=kc: e.tensor_scalar(out=hT_dst[:, kc, :], in0=bank[:, j * 128:(j + 1) * 128],
                                                                                    scalar1=G[:, kc:kc + 1], scalar2=SHv[:, kc:kc + 1], op0=ALU.mult, op1=ALU.add),
                             reads=[bk] + gkeys, writes=[hkey + "_%d" % kc])
            else:
                src = bank[:, 0:1024].rearrange("p (j t) -> p j t", j=8)[:, :, only_col]
                sl = slice(g4 * 8, g4 * 8 + 8)
                P.op("dve", lambda e, src=src, sl=sl: e.tensor_tensor(out=coltmp[:, sl], in0=src, in1=G[:, sl], op=ALU.mult), reads=[bk] + gkeys, writes=[pfx + "ct"])
                P.op("dve", lambda e, sl=sl: e.tensor_tensor(out=coltmp[:, sl], in0=coltmp[:, sl], in1=SHv[:, sl], op=ALU.add), reads=[pfx + "ct"] + gkeys, writes=[pfx + "ct"])
                P.op("dve", lambda e, sl=sl: e.tensor_scalar(out=hT_dst[:, sl], in0=coltmp[:, sl], scalar1=mask_col, scalar2=None, op0=ALU.mult),
                     reads=[pfx + "ct", "hmask"], writes=[hkey])

    mTab = sb.mark()
    COSo = sb.alloc([128, NOWN * 64], F32)
    SINo = sb.alloc([128, NOWN * 64], F32)
    mBig = sb.mark()
    COS = sb.alloc([128, NT * 64], F32)
    SIN = sb.alloc([128, NT * 64], F32)
    mR = sb.mark()
    angA = sb.alloc([128, NT * 64], F32)
    angK = sb.alloc([128, NT * 64], F32)
    angI = sb.alloc([128, NT * 64], mybir.dt.int32)
    P.op("dve", lambda e: e.tensor_tensor(out=angA.rearrange("p (m j) -> p m j", j=32), in0=pos.unsqueeze(2).broadcast_to([128, NT * 2, 32]),
                                          in1=freq.unsqueeze(1).broadcast_to([128, NT * 2, 32]), op=ALU.mult),
         reads=["pos", "freq"], writes=["angA"])
    for (dst, dkey, shift) in ((SIN, "SIN", 0.0), (COS, "COS", 0.5 * math.pi)):
        if shift:
            P.op("dve", lambda e, shift=shift: e.tensor_scalar(out=angA, in0=angA, scalar1=shift, scalar2=None, op0=ALU.add), reads=["angA"], writes=["angA"])
        P.op("dve", lambda e: e.tensor_scalar(out=angK, in0=angA, scalar1=1.0 / (2 * math.pi), scalar2=None, op0=ALU.mult), reads=["angA"], writes=["angK"])
        P.op("dve", lambda e: e.tensor_copy(out=angI, in_=angK), reads=["angK"], writes=["angI"])
        P.op("dve", lambda e: e.tensor_copy(out=angK, in_=angI), reads=["angI"], writes=["angK"])
        P.op("dve", lambda e: e.scalar_tensor_tensor(out=angK, in0=angK, scalar=-2.0 * math.pi, in1=angA, op0=ALU.mult, op1=ALU.add), reads=["angK", "angA"], writes=["angK"])
        P.op("dve", lambda e: e.tensor_scalar(out=angK, in0=angK, scalar1=3.1415925, scalar2=-3.1415925, op0=ALU.min, op1=ALU.max), reads=["angK"], writes=["angK"])
        P.op("act", lambda e, dst=dst: e.activation(out=dst, in_=angK, func=AF.Sin), reads=["angK"], writes=[dkey])
    P.op("dve", lambda e: e.tensor_copy(out=COSo, in_=COS[:, 0:NOWN * 64]), reads=["COS"], writes=["COSo"])
    P.op("dve", lambda e: e.tensor_copy(out=SINo, in_=SIN[:, 0:NOWN * 64]), reads=["SIN"], writes=["SINo"])
    P.barrier()
    sb.release(mR)

    def rope_apply(kn, nh, t, outb, t1, t2, rkeys, wkey, tkey, tabs=None):
        Ct, St, ck, sk = (COS, SIN, "COS", "SIN") if tabs is None else tabs
        cs = Ct[:, t * 64:(t + 1) * 64].rearrange("p (a j) -> p a j", a=2).unsqueeze(1).broadcast_to([128, nh, 2, 32])
        sn = St[:, t * 64:(t + 1) * 64].rearrange("p (a j) -> p a j", a=2).unsqueeze(1).broadcast_to([128, nh, 2, 32])
        q5 = kn.rearrange("p (h a b j) -> p h a b j", h=nh, a=2, b=2)
        a5 = t1.rearrange("p (h a b j) -> p h a b j", h=nh, a=2, b=2)
        b5 = t2.rearrange("p (h a b j) -> p h a b j", h=nh, a=2, b=2)
        o5 = outb.rearrange("p (h a b j) -> p h a b j", h=nh, a=2, b=2)
        for b_ in range(2):
            P.op("dve", lambda e, b_=b_: e.tensor_tensor(out=a5[:, :, :, b_, :], in0=q5[:, :, :, b_, :], in1=cs, op=ALU.mult), reads=rkeys + [ck], writes=[tkey + "t1"])
            P.op("dve", lambda e, b_=b_: e.tensor_tensor(out=b5[:, :, :, b_, :], in0=q5[:, :, :, 1 - b_, :], in1=sn, op=ALU.mult), reads=rkeys + [sk], writes=[tkey + "t2"])
        P.op("dve", lambda e: e.tensor_tensor(out=o5[:, :, :, 0, :], in0=a5[:, :, :, 0, :], in1=b5[:, :, :, 0, :], op=ALU.subtract), reads=[tkey + "t1", tkey + "t2"], writes=[wkey])
        P.op("dve", lambda e: e.tensor_tensor(out=o5[:, :, :, 1, :], in0=a5[:, :, :, 1, :], in1=b5[:, :, :, 1, :], op=ALU.add), reads=[tkey + "t1", tkey + "t2"], writes=[wkey])

    mB = sb.mark()
    xt = [sb.alloc([128, D], F32) for _ in range(2)]
    xh = [sb.alloc([128, D], BF16) for _ in range(2)]
    hT = [sb.alloc([128, 32, 128], BF16) for _ in range(2)]
    wkv = sb.alloc([128, 32, 1024], BF16)
    junk = sb.alloc([128, 512], BF16)
    wa2 = [sb.alloc([128, 16, 128], BF16) for _ in range(3)]
    kraw = [sb.alloc([128, 512], F32) for _ in range(2)]
    kn = sb.alloc([128, 512], F32)
    kt1 = sb.alloc([128, 512], F32)
    kt2 = sb.alloc([128, 512], F32)
    kr = [sb.alloc([128, 512], BF16) for _ in range(2)]
    kTs = [sb.alloc([128, 4, 128], BF16) for _ in range(2)]
    vaug = [sb.alloc([128, 4, VW], BF16) for _ in range(2)]
    ssx = sb.alloc([128, 4], F32)
    ssk = sb.alloc([128, 8], F32)
    for blk in range(4):
        P.op("pool", lambda e, blk=blk: e.dma_start(out=wkv[:, :, blk * 256:(blk + 1) * 256], in_=w_in_h[24 + blk].rearrange("p (k c) -> p k c", k=32)),
             writes=["wkv"], dma="L:wkv")
    for s_ in range(2):
        P.op("pool", lambda e, s_=s_: e.memset(vaug[s_][:, :, 128:VW], 1.0), writes=["vaug%d" % s_])

    def b_s0(t):
        s_ = t % 2
        src = xr[t * 128:(t + 1) * 128, :] if t < 64 else ctx[(t - 64) * 128:(t - 63) * 128, :]
        ld("sp", xt[s_], src, "xt%d" % s_)

    def b_s1(t):
        s_, c4 = t % 2, t % 4
        P.op("act", lambda e: e.activation(out=xh[s_], in_=xt[s_], func=AF.Square, accum_out=ssx[:, c4:c4 + 1]), reads=["xt%d" % s_], writes=["ssx%d" % c4, "xh%d" % s_])
        rstd_from_ss(ssx[:, c4:c4 + 1], ssx[:, c4:c4 + 1], D, ["ssx%d" % c4], "ssx%d" % c4)

    def b_s2(t):
        s_, c4 = t % 2, t % 4
        P.op("act", lambda e: e.activation(out=xh[s_], in_=xt[s_], func=AF.Copy, scale=ssx[:, c4:c4 + 1]), reads=["xt%d" % s_, "ssx%d" % c4], writes=["xh%d" % s_])

    def b_s3(t):
        s_ = t % 2
        G, SHv, gk = (Ga, SHa, ["Ga", "modc"]) if t < 64 else (Gc, SHc, ["Gc", "modx"])
        TB = (0, 1, 4, 5)
        for g4 in range(4):
            bank = psb[TB[g4]]
            bk = "ps%d" % TB[g4]
            for j in range(8):
                kc = g4 * 8 + j
                P.op("pe", lambda e, bank=bank, j=j, kc=kc: e.transpose(out=bank[:, j * 128:(j + 1) * 128], in_=xh[s_][:, kc * 128:(kc + 1) * 128], identity=ident),
                     reads=["xh%d" % s_, "ident"], writes=[bk])
        for g4 in range(4):
            bank = psb[TB[g4]]
            bk = "ps%d" % TB[g4]
            for j in range(8):
                kc = g4 * 8 + j
                if g4 % 2 == 0:
                    P.op("act", lambda e, bank=bank, j=j, kc=kc: e.activation(out=hT[s_][:, kc, :], in_=bank[:, j * 128:(j + 1) * 128], func=AF.Identity,
                                                                             scale=G[:, kc:kc + 1], bias=SHv[:, kc:kc + 1]),
                         reads=[bk] + gk, writes=["hT%d_%d" % (s_, kc)])
                else:
                    P.op("dve", lambda e, bank=bank, j=j, kc=kc: e.tensor_scalar(out=hT[s_][:, kc, :], in0=bank[:, j * 128:(j + 1) * 128],
                                                                                scalar1=G[:, kc:kc + 1], scalar2=SHv[:, kc:kc + 1], op0=ALU.mult, op1=ALU.add),
                         reads=[bk] + gk, writes=["hT%d_%d" % (s_, kc)])
        if t < NOWN:
            P.op("sp", lambda e: e.dma_start(out=hT_scr[t], in_=hT[s_].rearrange("p k c -> p (k c)")),
                 reads=["hT%d_%d" % (s_, kc_) for kc_ in range(32)], writes=["hT_scr%d" % t], dma="S:hT%d" % s_)

    def b_s4(t):
        s_ = t % 2
        for (pp, half) in ((2, 0), (3, 1)):
            for kc in range(32):
                P.op("pe", lambda e, pp=pp, kc=kc, half=half: e.matmul(ps[pp], lhsT=hT[s_][:, kc, :], rhs=wkv[:, kc, half * 512:(half + 1) * 512],
                                                                      start=(kc == 0), stop=(kc == 31)),
                     reads=["hT%d_%d" % (s_, kc), "wkv"], writes=["ps%d" % pp])

    def b_s5(t):
        s_ = t % 2
        pk, pv = ps[2], ps[3]
        pkk, pvk = "ps2", "ps3"
        P.op("act", lambda e: e.activation(out=kraw[s_], in_=pk, func=AF.Copy), reads=[pkk], writes=["kraw%d" % s_])
        P.op("act", lambda e: e.activation(out=vaug[s_][:, :, 0:128], in_=pv.rearrange("p (h d) -> p h d", h=4), func=AF.Copy), reads=[pvk], writes=["vaug%d" % s_])
        P.op("sp", lambda e: e.dma_start(out=V_scr.rearrange("h p (t c) -> p h t c", t=NT)[:, :, t, :], in_=vaug[s_]),
             reads=["vaug%d" % s_], writes=["V_scr"], dma="S:v%d" % s_)
        for h in range(4):
            P.op("act", lambda e, h=h: e.activation(out=junk[:, h * 128:(h + 1) * 128], in_=kraw[s_][:, h * 128:(h + 1) * 128], func=AF.Square,
                                                   accum_out=ssk[:, s_ * 4 + h:s_ * 4 + h + 1]),
                 reads=["kraw%d" % s_], writes=["rk%d" % s_])
        P.op("act", lambda e: e.activation(out=ssk[:, s_ * 4:s_ * 4 + 4], in_=ssk[:, s_ * 4:s_ * 4 + 4], func=AF.Sqrt, scale=1.0 / 128, bias=epsc),
             reads=["rk%d" % s_, "cst"], writes=["rk%d" % s_])

    def b_s6(t):
        s_ = t % 2
        rk = ssk[:, s_ * 4:s_ * 4 + 4]
        P.op("dve", lambda e: e.reciprocal(out=rk, in_=rk), reads=["rk%d" % s_], writes=["rk%d" % s_])
        k3 = kn.rearrange("p (h d) -> p h d", h=4)
        P.op("dve", lambda e: e.tensor_tensor(out=k3, in0=kraw[s_].rearrange("p (h d) -> p h d", h=4), in1=rk.unsqueeze(2).broadcast_to([128, 4, 128]), op=ALU.mult),
             reads=["kraw%d" % s_, "rk%d" % s_], writes=["kn"])
        P.op("dve", lambda e: e.tensor_tensor(out=k3, in0=k3, in1=gk_t.unsqueeze(1).broadcast_to([128, 4, 128]), op=ALU.mult), reads=["kn", "gk_t"], writes=["kn"])
        rope_apply(kn, 4, t, kr[s_], kt1, kt2, ["kn"], "kr%d" % s_, "B")

    def b_s7(t):
        s_ = t % 2
        for h in range(4):
            P.op("pe", lambda e, h=h: e.transpose(out=psb[6][:, h * 128:(h + 1) * 128], in_=kr[s_][:, h * 128:(h + 1) * 128], identity=ident),
                 reads=["kr%d" % s_, "ident"], writes=["ps6"])
        P.op("dve", lambda e: e.tensor_copy(out=kTs[s_], in_=psb[6][:, 0:512].rearrange("p (h t) -> p h t", h=4)), reads=["ps6"], writes=["kTs%d" % s_])
        P.op("sp", lambda e: e.dma_start(out=KT_scr.rearrange("h d n -> d h n")[:, :, t * 128:(t + 1) * 128], in_=kTs[s_]),
             reads=["kTs%d" % s_], writes=["KT_scr"], dma="S:k%d" % s_)

    NG2 = 256

    def a2_ld(g):
        ch, hh, sl = 64 + g // 2, g % 2, g % 3
        P.op("pool", lambda e: e.dma_start(out=wa2[sl].rearrange("p k c -> p (k c)"), in_=wchunk(ch)[:, hh * 2048:(hh + 1) * 2048]),
             writes=["wa2_%d" % sl], dma="L:wa2_%d" % sl)

    def a2_mmul(g):
        ch, hh, sl = 64 + g // 2, g % 2, g % 3
        c2 = 2 * (ch - 64)
        for k16 in range(16):
            kc = hh * 16 + k16
            P.op("pe", lambda e, k16=k16, kc=kc: e.matmul(ps[7][:, c2:c2 + 2], lhsT=wa2[sl][:, k16, :], rhs=svb[:, kc, :], start=(kc == 0), stop=(kc == 31)),
                 reads=["wa2_%d" % sl, "svb"], writes=["ps7"])

    def a2(t):
        if t >= 64:
            return
        for g in range(4 * t, 4 * t + 4):
            a2_ld(g)
            if g >= 2:
                a2_mmul(g - 2)

    pipeline(NT, [b_s0, b_s1, b_s2, b_s3, b_s4, b_s5, b_s6, b_s7, a2], [0, 1, 2, 3, 4, 5, 6, 7, 0], order=[5, 3, 7, 6, 4, 8, 2, 1, 0])
    a2_mmul(NG2 - 2)
    a2_mmul(NG2 - 1)
    pc7 = ps[7][:, 0:256].rearrange("p (j two) -> p j two", two=2)
    P.op("dve", lambda e: e.tensor_tensor(out=modc[:, 64:192], in0=pc7[:, :, 0], in1=modc[:, 64:192], op=ALU.add), reads=["ps7", "modc"], writes=["modc"])
    P.op("dve", lambda e: e.scalar_tensor_tensor(out=Gf, in0=modc[:, 128:160], scalar=1.0, in1=gcols[:, 32:64], op0=ALU.add, op1=ALU.mult),
         reads=["modc", "gcols"], writes=["Gf"])
    P.barrier()
    sb.release(mBig)

    if stop == "B":
        P.emit(final_waits=None)
        return nc
    mGG = sb.mark()
    identf = sb.alloc([128, 128], F32)
    dg = [sb.alloc([128, 128], F32) for _ in range(2)]
    grow = sb.alloc([128, 2048], F32)
    ggs = sb.alloc([128, 2048], F32)
    ld("sp", identf, ident_d, "identf")
    ndg = 0
    for which, c0 in ((0, 64), (1, 160)):
        for half in range(2):
            ld("sp", grow, bc((gpost_mix_row if which == 0 else gpost_ffn_row)[:, half * 2048:(half + 1) * 2048]), "grow")
            for q4 in range(4):
                bank = 1 + (q4 % 2)
                for jj in range(4):
                    j = c0 + half * 16 + q4 * 4 + jj
                    d_ = ndg % 2
                    ndg += 1
                    P.op("dve", lambda e, d_=d_, j=j: e.tensor_scalar(out=dg[d_], in0=identf, scalar1=modc[:, j:j + 1], scalar2=None, op0=ALU.mult),
                         reads=["identf", "modc"], writes=["dg%d" % d_])
                    P.op("pe", lambda e, d_=d_, jj=jj, bank=bank: e.matmul(ps[bank][:, jj * 128:(jj + 1) * 128], lhsT=onesf, rhs=dg[d_], start=True, stop=True),
                         reads=["dg%d" % d_, "onesf"], writes=["ps%d" % bank])
                P.op("dve", lambda e, q4=q4, bank=bank: e.tensor_tensor(out=ggs[:, q4 * 512:(q4 + 1) * 512], in0=ps[bank], in1=grow[:, q4 * 512:(q4 + 1) * 512], op=ALU.mult),
                     reads=["ps%d" % bank, "grow"], writes=["ggs"])
            P.op("sp", lambda e, which=which, half=half: e.dma_start(out=gg_scr[which, :, half * 2048:(half + 1) * 2048], in_=ggs),
                 reads=["ggs"], writes=["gg_scr"], dma="S:gg")
    P.barrier()
    sb.release(mGG)

    qT = sb.alloc([128, 16, NOWN * 128], BF16)
    mC = sb.mark()
    hTo = sb.alloc([128, NG, 32 * 128], BF16)
    wblk = [sb.alloc([128, 32, 256], BF16) for _ in range(2)]
    gv = sb.alloc([128, NG, 2048], BF16)
    lng = sb.alloc([128, 2048], F32)
    lnb = sb.alloc([128, 2048], F32)
    wsT = sb.alloc([128, 16, 128], BF16)
    gtmp = [sb.alloc([128, 256], F32) for _ in range(2)]
    oa = [sb.alloc([128, 256], F32) for _ in range(2)]
    oab = [sb.alloc([128, 256], BF16) for _ in range(2)]
    oTs = [sb.alloc([128, 2, 128], BF16) for _ in range(2)]
    qn = sb.alloc([128, 256], F32)
    qt1 = sb.alloc([128, 256], F32)
    qt2 = sb.alloc([128, 256], F32)
    ssq = [sb.alloc([128, 2], F32) for _ in range(2)]
    qr = [sb.alloc([128, 256], BF16) for _ in range(2)]
    vsum = sb.alloc([128, NOWN * 8], F32)
    vsq = sb.alloc([128, NOWN * 8], F32)
    vst = sb.alloc([128, NOWN * 4], F32)
    junkC = sb.alloc([128, 256], F32)
    ld("sp", lng, bc(lng_row), "lng")
    ld("sp", lnb, bc(lnb_row), "lnb")
    P.op("pool", lambda e: e.dma_start(out=wsT.rearrange("p g c -> p (g c)"), in_=wsT_d), writes=["wsT"], dma="L:wsT")

    nblk = [0]
    wslot = {}

    def load_wblk(blk, tag):
        if (blk, tag) in wslot:
            return
        s_ = nblk[0] % 2
        nblk[0] += 1
        wslot[(blk, tag)] = s_
        P.op("pool", lambda e: e.dma_start(out=wblk[s_].rearrange("p k c -> p (k c)"), in_=w_in_h[blk]), writes=["wblk%d" % s_], dma="L:wblk%d" % s_)

    def run_family(grp, blk0, stages_fn, delays):
        TL = list(range(grp * NG, grp * NG + NG))
        units = [(j, t) for j in range(8) for t in TL]

        def s0(u):
            j, t = units[u]
            load_wblk(blk0 + j, grp)
            if t == TL[1] and j + 1 < 8:
                load_wblk(blk0 + j + 1, grp)
            s_ = wslot[(blk0 + j, grp)]
            bank = u % 3
            for kc in range(32):
                P.op("pe", lambda e, kc=kc: e.matmul(ps[bank][:, 0:256], lhsT=hTo[:, t % NG, kc * 128:(kc + 1) * 128], rhs=wblk[s_][:, kc, :],
                                                     start=(kc == 0), stop=(kc == 31)),
                     reads=["hTo%d" % (t % NG), "wblk%d" % s_], writes=["ps%d" % bank])
        stages = [s0] + stages_fn(units)
        pipeline(len(units), stages, delays)

    def v_stages(units):
        def s1(u):
            j, t = units[u]
            bank, g_ = u % 3, u % 2
            P.op("act", lambda e: e.activation(out=gtmp[g_], in_=ps[bank][:, 0:256], func=AF.Gelu, accum_out=vsum[:, t * 8 + j:t * 8 + j + 1]),
                 reads=["ps%d" % bank], writes=["gtmp%d" % g_, "vsum%d" % (t * 8 + j)])
            P.op("act", lambda e: e.activation(out=junkC, in_=gtmp[g_], func=AF.Square, accum_out=vsq[:, t * 8 + j:t * 8 + j + 1]),
                 reads=["gtmp%d" % g_], writes=["vsq%d" % (t * 8 + j)])

        def s2(u):
            j, t = units[u]
            g_ = u % 2
            P.op("dve", lambda e: e.tensor_copy(out=gv[:, t % NG, j * 256:(j + 1) * 256], in_=gtmp[g_]), reads=["gtmp%d" % g_], writes=["gv%d" % (t % NG)])
        return [s1, s2]

    def u_stages(units):
        def s0b(u):
            j, t = units[u]
            mb = 3 + u % 3
            for gi in range(2):
                g = 2 * j + gi
                P.op("pe", lambda e, gi=gi, g=g: e.matmul(ps[mb][:, gi * 128:(gi + 1) * 128], lhsT=wsT[:, g, :], rhs=gv[:, t % NG, g * 128:(g + 1) * 128], start=True, stop=True),
                     reads=["wsT", "gv%d" % (t % NG)], writes=["ps%d" % mb])

        def s1(u):
            bank, g_ = u % 3, u % 2
            P.op("act", lambda e: e.activation(out=gtmp[g_], in_=ps[bank][:, 0:256], func=AF.Gelu), reads=["ps%d" % bank], writes=["gtmp%d" % g_])

        def s2(u):
            j, t = units[u]
            mb, g_ = 3 + u % 3, u % 2
            for gi in range(2):
                g = 2 * j + gi
                P.op("dve", lambda e, gi=gi, g=g: e.scalar_tensor_tensor(out=oa[g_][:, gi * 128:(gi + 1) * 128], in0=ps[mb][:, gi * 128:(gi + 1) * 128],
                                                                        scalar=bs_col[:, g:g + 1], in1=gtmp[g_][:, gi * 128:(gi + 1) * 128], op0=ALU.add, op1=ALU.mult),
                     reads=["ps%d" % mb, "bs_col", "gtmp%d" % g_], writes=["oa%d" % g_])
            P.op("dve", lambda e: e.tensor_copy(out=oab[g_], in_=oa[g_]), reads=["oa%d" % g_], writes=["oab%d" % g_])

        def s3(u):
            j, t = units[u]
            g_ = u % 2
            tb = 6 + u % 2
            for gi in range(2):
                P.op("pe", lambda e, gi=gi: e.transpose(out=psb[tb][:, gi * 128:(gi + 1) * 128], in_=oab[g_][:, gi * 128:(gi + 1) * 128], identity=ident),
                     reads=["oab%d" % g_, "ident"], writes=["ps%d" % tb])
            col = t * 8 + j
            P.op("act", lambda e: e.activation(out=junkC, in_=oa[g_], func=AF.Square, accum_out=ssA[:, col:col + 1]), reads=["oa%d" % g_], writes=["ssA%d" % col])

        def s4(u):
            j, t = units[u]
            g_ = u % 2
            tb = 6 + u % 2
            for gi in range(2):
                kc = 2 * j + gi
                P.op("act", lambda e, gi=gi, kc=kc: e.activation(out=oTs[g_][:, gi, :], in_=psb[tb][:, gi * 128:(gi + 1) * 128], func=AF.Copy, scale=goutc[:, kc:kc + 1]),
                     reads=["ps%d" % tb, "goutc"], writes=["oTs%d" % g_])
            P.op("sp", lambda e: e.dma_start(out=oT_scr[2 * j:2 * j + 2].rearrange("k p n -> p k n")[:, :, t * 128:(t + 1) * 128], in_=oTs[g_]),
                 reads=["oTs%d" % g_], writes=["oT_scr"], dma="S:oT%d" % g_)
        return [s0b, s1, s2, s3, s4]

    def q_stages(units):
        def s1(u):
            bank, g_ = u % 3, u % 2
            for h in range(2):
                P.op("act", lambda e, h=h: e.activation(out=junkC[:, h * 128:(h + 1) * 128], in_=ps[bank][:, h * 128:(h + 1) * 128], func=AF.Square, accum_out=ssq[g_][:, h:h + 1]),
                     reads=["ps%d" % bank], writes=["ssq%d" % g_])
            P.op("act", lambda e: e.activation(out=ssq[g_], in_=ssq[g_], func=AF.Sqrt, scale=1.0 / 128, bias=epsc), reads=["ssq%d" % g_, "cst"], writes=["ssq%d" % g_])

        def s2(u):
            j, t = units[u]
            bank, g_ = u % 3, u % 2
            P.op("dve", lambda e: e.reciprocal(out=ssq[g_], in_=ssq[g_]), reads=["ssq%d" % g_], writes=["ssq%d" % g_])
            q3 = qn.rearrange("p (h d) -> p h d", h=2)
            P.op("dve", lambda e: e.tensor_tensor(out=q3, in0=ps[bank][:, 0:256].rearrange("p (h d) -> p h d", h=2), in1=ssq[g_].unsqueeze(2).broadcast_to([128, 2, 128]), op=ALU.mult),
                 reads=["ps%d" % bank, "ssq%d" % g_], writes=["Cqn"])
            P.op("dve", lambda e: e.tensor_tensor(out=q3, in0=q3, in1=gq_t.unsqueeze(1).broadcast_to([128, 2, 128]), op=ALU.mult), reads=["Cqn", "gq_t"], writes=["Cqn"])
            rope_apply(qn, 2, t, qr[g_], qt1, qt2, ["Cqn"], "qr%d" % g_, "Cq", tabs=(COSo, SINo, "COSo", "SINo"))

        def s3(u):
            g_ = u % 2
            tb = 6 + u % 2
            for gi in range(2):
                P.op("pe", lambda e, gi=gi: e.transpose(out=psb[tb][:, gi * 128:(gi + 1) * 128], in_=qr[g_][:, gi * 128:(gi + 1) * 128], identity=ident),
                     reads=["qr%d" % g_, "ident"], writes=["ps%d" % tb])

        def s4(u):
            j, t = units[u]
            tb = 6 + u % 2
            P.op("act", lambda e: e.activation(out=qT[:, 2 * j:2 * j + 2, t * 128:(t + 1) * 128], in_=psb[tb][:, 0:256].rearrange("p (h n) -> p h n", h=2), func=AF.Copy),
                 reads=["ps%d" % tb], writes=["qT"])
        return [s1, s2, s3, s4]

    mean = vst[:, 2 * NOWN:3 * NOWN]
    var = vst[:, 3 * NOWN:4 * NOWN]
    for grp in range(NOWN // NG):
        TL = list(range(grp * NG, grp * NG + NG))
        for t in TL:
            ld("sp", hTo[:, t % NG, :], hT_scr[t], "hTo%d" % (t % NG), reads=["hT_scr%d" % t])
        run_family(grp, 8, v_stages, [0, 1, 2])
        allv = ["vsum%d" % c_ for c_ in range(NOWN * 8)] + ["vsq%d" % c_ for c_ in range(NOWN * 8)]
        P.op("dve", lambda e: e.reduce_sum(out=vst[:, 0:NOWN], in_=vsum.rearrange("p (t j) -> p t j", j=8), axis=AX.X), reads=allv, writes=["vst"])
        P.op("dve", lambda e: e.reduce_sum(out=vst[:, NOWN:2 * NOWN], in_=vsq.rearrange("p (t j) -> p t j", j=8), axis=AX.X), reads=allv, writes=["vst"])
        P.op("dve", lambda e: e.tensor_scalar(out=mean, in0=vst[:, 0:NOWN], scalar1=1.0 / 2048, scalar2=None, op0=ALU.mult), reads=["vst"], writes=["vmean"])
        P.op("dve", lambda e: e.tensor_tensor(out=var, in0=mean, in1=mean, op=ALU.mult), reads=["vmean"], writes=["vvar"])
        P.op("dve", lambda e: e.scalar_tensor_tensor(out=var, in0=vst[:, NOWN:2 * NOWN], scalar=1.0 / 2048, in1=var, op0=ALU.mult, op1=ALU.subtract),
             reads=["vst", "vvar"], writes=["vvar"])
        P.op("act", lambda e: e.activation(out=var, in_=var, func=AF.Sqrt, bias=epsc), reads=["vvar", "cst"], writes=["vvar"])
        P.op("dve", lambda e: e.reciprocal(out=var, in_=var), reads=["vvar"], writes=["vvar"])
        for t in TL:
            P.op("dve", lambda e, t=t: e.tensor_scalar(out=gv[:, t % NG, :], in0=gv[:, t % NG, :], scalar1=mean[:, t:t + 1], scalar2=var[:, t:t + 1], op0=ALU.subtract, op1=ALU.mult),
                 reads=["gv%d" % (t % NG), "vmean", "vvar"], writes=["gv%d" % (t % NG)])
            P.op("dve", lambda e, t=t: e.tensor_tensor(out=gv[:, t % NG, :], in0=gv[:, t % NG, :], in1=lng, op=ALU.mult), reads=["gv%d" % (t % NG), "lng"], writes=["gv%d" % (t % NG)])
            P.op("dve", lambda e, t=t: e.tensor_tensor(out=gv[:, t % NG, :], in0=gv[:, t % NG, :], in1=lnb, op=ALU.add), reads=["gv%d" % (t % NG), "lnb"], writes=["gv%d" % (t % NG)])
        run_family(grp, 0, u_stages, [0, 0, 1, 2, 3, 4])
        run_family(grp, 16, q_stages, [0, 1, 2, 3, 4])
    allssA = ["ssA%d" % c_ for c_ in range(NOWN * 8)]
    P.barrier()
    sb.release(mC)

    if stop == "C":
        dbgC = nc.dram_tensor("dbgC", [128, 16 * NOWN * 128], BF16, kind="ExternalOutput").ap()
        P.op("sp", lambda e: e.dma_start(out=dbgC, in_=qT.rearrange("p h n -> p (h n)")), reads=["qT"], dma="dbg")
        dbgC2 = nc.dram_tensor("dbgC2", [128, NOWN * 8], F32, kind="ExternalOutput").ap()
        P.op("sp", lambda e: e.dma_start(out=dbgC2, in_=ssA), reads=allssA, dma="dbg")
        P.emit(final_waits=None)
        return nc
    mD = sb.mark()
    KTh = [sb.alloc([128, NKEY], BF16) for _ in range(2)]
    Vh = [sb.alloc([128, NT, VW], BF16) for _ in range(2)]
    NPT = 3
    PT = [sb.alloc([128, 512], BF16) for _ in range(NPT)]
    ob = [[sb.alloc([128, 128], F32) for _ in range(4)] for _ in range(2)]
    obb = [sb.alloc([128, 128], BF16) for _ in range(4)]
    rden = sb.alloc([128, 4], F32)
    obT = [sb.alloc([128, 512], BF16) for _ in range(2)]
    junkD = sb.alloc([128, 128], F32)
    SCALE = 1.0 / math.sqrt(128.0)
    QB = [(0, 512), (512, 512), (1024, 256)]
    blocks = []
    for kvh in range(4):
        for hh in range(4):
            for (q0, nq) in QB:
                blocks.append((kvh, kvh * 4 + hh, q0, nq))
    units = [(bi, kt) for bi in range(len(blocks)) for kt in range(NT)]
    loaded = set()

    def load_kv(kvh):
        if kvh in loaded or kvh >= 4:
            return
        loaded.add(kvh)
        s_ = kvh % 2
        ld("sp", KTh[s_], KT_scr[kvh], "KTh%d" % s_, reads=["KT_scr"])
        ld("sp", Vh[s_].rearrange("p t c -> p (t c)"), V_scr[kvh], "Vh%d" % s_, reads=["V_scr"])

    def st_S(u):
        bi, kt = units[u]
        kvh, head, q0, nq = blocks[bi]
        load_kv(kvh)
        s_ = kvh % 2
        sbk = u % 3
        P.op("pe", lambda e: e.matmul(ps[sbk][:, 0:nq], lhsT=KTh[s_][:, kt * 128:(kt + 1) * 128], rhs=qT[:, head, q0:q0 + nq], start=True, stop=True),
             reads=["KTh%d" % s_, "qT"], writes=["ps%d" % sbk])

    def st_exp(u):
        bi, kt = units[u]
        kvh, head, q0, nq = blocks[bi]
        sbk = u % 3
        pk = u % NPT
        P.op("act", lambda e: e.activation(out=PT[pk][:, 0:nq], in_=ps[sbk][:, 0:nq], func=AF.Exp, scale=SCALE),
             reads=["ps%d" % sbk], writes=["PT%d" % pk])

    def epi_dve(bi):
        kvh, head, q0, nq = blocks[bi]
        nsub = nq // 128
        so = bi % 2
        for qs in range(nsub):
            pb, pbk = ps[3 + qs], "ps%d" % (3 + qs)
            P.op("dve", lambda e, pb=pb, qs=qs: e.reciprocal(out=rden[:, qs:qs + 1], in_=pb[:, 128:129]), reads=[pbk], writes=["rden%d" % qs])
            P.op("dve", lambda e, pb=pb, qs=qs: e.tensor_scalar(out=ob[so][qs], in0=pb[:, 0:128], scalar1=rden[:, qs:qs + 1], scalar2=None, op0=ALU.mult),
                 reads=[pbk, "rden%d" % qs], writes=["ob%d_%d" % (so, qs)])
        for qs in range(nsub):
            P.op("dve", lambda e, qs=qs: e.tensor_copy(out=obb[qs], in_=ob[so][qs]), reads=["ob%d_%d" % (so, qs)], writes=["obb%d" % qs])
        for qs in range(nsub):
            P.op("pe", lambda e, qs=qs: e.transpose(out=psb[7][:, qs * 128:(qs + 1) * 128], in_=obb[qs], identity=ident), reads=["obb%d" % qs, "ident"], writes=["ps7"])
        P.op("dve", lambda e: e.tensor_scalar(out=obT[so][:, 0:nq], in0=psb[7][:, 0:nq], scalar1=goutc[:, 16 + head:17 + head], scalar2=None, op0=ALU.mult),
             reads=["ps7", "goutc"], writes=["obT%d" % so])
        P.op("sp", lambda e: e.dma_start(out=oT_scr[16 + head, :, q0:q0 + nq], in_=obT[so][:, 0:nq]), reads=["obT%d" % so], writes=["oT_scr"], dma="S:obT%d" % so)

    def epi_act(bi):
        kvh, head, q0, nq = blocks[bi]
        so = bi % 2
        for qs in range(nq // 128):
            t = q0 // 128 + qs
            col = t * 16 + head
            P.op("act", lambda e, qs=qs, col=col: e.activation(out=junkD, in_=ob[so][qs], func=AF.Square, accum_out=ssB[:, col:col + 1]),
                 reads=["ob%d_%d" % (so, qs)], writes=["ssB%d" % col])

    def st_PV(u):
        bi, kt = units[u]
        kvh, head, q0, nq = blocks[bi]
        s_ = kvh % 2
        pk = u % NPT
        for qs in range(nq // 128):
            P.op("pe", lambda e, qs=qs: e.matmul(ps[3 + qs][:, 0:129], lhsT=PT[pk][:, qs * 128:(qs + 1) * 128], rhs=Vh[s_][:, kt, 0:129],
                                                 start=(kt == 0), stop=(kt == NT - 1)),
                 reads=["PT%d" % pk, "Vh%d" % s_], writes=["ps%d" % (3 + qs)])
        if kt == NT - 1:
            epi_dve(bi)
        if kt == 8 and bi > 0:
            epi_act(bi - 1)
        if kt == 0 and bi % 12 == 1:
            load_kv(kvh + 1)

    pipeline(len(units), [st_S, st_exp, st_PV], [0, 1, 3])
    epi_act(len(blocks) - 1)
    allssB = ["ssB%d" % c_ for c_ in range(NOWN * 16)]
    P.barrier()
    sb.release(mD)
    sb.release(mC)
    sb.release(mTab)

    if stop == "D":
        dbgD = nc.dram_tensor("dbgD", [128, NOWN * 16], F32, kind="ExternalOutput").ap()
        P.op("sp", lambda e: e.dma_start(out=dbgD, in_=ssB), reads=allssB, dma="dbg")
        P.emit(final_waits=None)
        return nc
    mE = sb.mark()
    oT = sb.alloc([128, 32, NOWN * 128], BF16)
    wo = [sb.alloc([128, 32, 512], BF16) for _ in range(2)]
    osb = [sb.alloc([128, 512], F32) for _ in range(2)]
    otmp = sb.alloc([128, 512], F32)
    junkE = sb.alloc([128, 512], F32)
    for kc in range(32):
        ld("sp", oT[:, kc, :], oT_scr[kc], "oT", reads=["oT_scr"], writes=["oT"])
    P.op("dve", lambda e: e.reduce_sum(out=rA, in_=ssA.rearrange("p (t j) -> p t j", j=8), axis=AX.X), reads=allssA, writes=["rA"])
    P.op("dve", lambda e: e.reduce_sum(out=rB, in_=ssB.rearrange("p (t j) -> p t j", j=16), axis=AX.X), reads=allssB, writes=["rB"])
    rstd_from_ss(rA, rA, 2048, ["rA"], "rA")
    rstd_from_ss(rB, rB, 2048, ["rB"], "rB")
    nE = 0
    for cbk in range(8):
        s_ = cbk % 2
        P.op("pool", lambda e, s_=s_, cbk=cbk: e.dma_start(out=wo[s_].rearrange("p k c -> p (k c)"), in_=w_out_h[cbk]), writes=["wo%d" % s_], dma="L:wo%d" % s_)
        for t in range(NOWN):
            pa, pbn = nE % 2, 2 + (nE % 2)
            so = nE % 2
            nE += 1
            for kc in range(16):
                P.op("pe", lambda e, pa=pa, kc=kc, t=t, s_=s_: e.matmul(ps[pa], lhsT=oT[:, kc, t * 128:(t + 1) * 128], rhs=wo[s_][:, kc, :], start=(kc == 0), stop=(kc == 15)),
                     reads=["oT", "wo%d" % s_], writes=["ps%d" % pa])
            for kc in range(16, 32):
                P.op("pe", lambda e, pbn=pbn, kc=kc, t=t, s_=s_: e.matmul(ps[pbn], lhsT=oT[:, kc, t * 128:(t + 1) * 128], rhs=wo[s_][:, kc, :], start=(kc == 16), stop=(kc == 31)),
                     reads=["oT", "wo%d" % s_], writes=["ps%d" % pbn])
            P.op("act", lambda e, pa=pa, t=t: e.activation(out=otmp, in_=ps[pa], func=AF.Copy, scale=rA[:, t:t + 1]), reads=["ps%d" % pa, "rA"], writes=["otmp"])
            P.op("dve", lambda e, pbn=pbn, t=t, so=so: e.scalar_tensor_tensor(out=osb[so], in0=ps[pbn], scalar=rB[:, t:t + 1], in1=otmp, op0=ALU.mult, op1=ALU.add),
                 reads=["ps%d" % pbn, "rB", "otmp"], writes=["osb%d" % so])
            P.op("act", lambda e, so=so, t=t, cbk=cbk: e.activation(out=junkE, in_=osb[so], func=AF.Square, accum_out=ssO[:, t * 8 + cbk:t * 8 + cbk + 1]),
                 reads=["osb%d" % so], writes=["ssO"])
            P.op("sp", lambda e, so=so, t=t, cbk=cbk: e.dma_start(out=o_scr[t * 128:(t + 1) * 128, cbk * 512:(cbk + 1) * 512], in_=osb[so]),
                 reads=["osb%d" % so], writes=["o_scr"], dma="S:osb%d" % so)
    P.op("dve", lambda e: e.reduce_sum(out=rO, in_=ssO.rearrange("p (t j) -> p t j", j=8), axis=AX.X), reads=["ssO"], writes=["rO"])
    rstd_from_ss(rO, rO, D, ["rO"], "rO")
    P.barrier()
    sb.release(mE)

    if stop == "E":
        dbgE = nc.dram_tensor("dbgE", [128, NOWN], F32, kind="ExternalOutput").ap()
        P.op("sp", lambda e: e.dma_start(out=dbgE, in_=rO), reads=["rO"], dma="dbg")
        P.emit(final_waits=None)
        return nc
    out_ops = []
    for blk in range(2):
        mF = sb.mark()
        HTF_BYTES = 32 * 514 * 2
        HTF_OFF = (SB.LIMIT - HTF_BYTES) // 64 * 64
        hTf = nc.alloc_sbuf_tensor_at("hTf%d" % blk, [128, 32, 514], BF16, offset=HTF_OFF).ap()
        mF2 = sb.mark()
        orow = [sb.alloc([128, D], F32) for _ in range(2)]
        xrow = [sb.alloc([128, D], F32) for _ in range(2)]
        xm = [sb.alloc([128, D], F32) for _ in range(2)]
        ggrow = sb.alloc([128, D], F32)
        junkF = sb.alloc([128, D], BF16)
        xhF = [sb.alloc([128, D], BF16) for _ in range(2)]
        coltmp = sb.alloc([128, 32], F32)
        stF = sb.alloc([128, 2], F32)
        ld("sp", ggrow, gg_scr[0], "ggrow", reads=["gg_scr"])

        def f_s0(ti):
            t, s_ = 4 * blk + ti, ti % 2
            ld("sp", orow[s_], o_scr[t * 128:(t + 1) * 128, :], "orow%d" % s_, reads=["o_scr"])
            ld("sp", xrow[s_], xr[t * 128:(t + 1) * 128, :], "xrow%d" % s_)

        def f_s1(ti):
            t, s_ = 4 * blk + ti, ti % 2
            P.op("dve", lambda e: e.scalar_tensor_tensor(out=xm[s_], in0=orow[s_], scalar=rO[:, t:t + 1], in1=ggrow, op0=ALU.mult, op1=ALU.mult),
                 reads=["orow%d" % s_, "rO", "ggrow"], writes=["xm%d" % s_])
            P.op("dve", lambda e: e.tensor_tensor(out=xm[s_], in0=xm[s_], in1=xrow[s_], op=ALU.add), reads=["xm%d" % s_, "xrow%d" % s_], writes=["xm%d" % s_])
            if 1 <= ti <= 4:
                own = t - 1
                P.op("sp", lambda e: e.dma_start(out=xm_scr[own * 128:(own + 1) * 128, :], in_=xm[s_]), reads=["xm%d" % s_], writes=["xm_scr"], dma="S:xm%d" % s_)

        def f_s2(ti):
            s_ = ti % 2
            P.op("act", lambda e: e.activation(out=junkF, in_=xm[s_], func=AF.Square, accum_out=stF[:, s_:s_ + 1]), reads=["xm%d" % s_], writes=["stF%d" % s_])
            rstd_from_ss(stF[:, s_:s_ + 1], stF[:, s_:s_ + 1], D, ["stF%d" % s_], "stF%d" % s_)

        def f_s3(ti):
            s_ = ti % 2
            P.op("act", lambda e: e.activation(out=xhF[s_], in_=xm[s_], func=AF.Copy, scale=stF[:, s_:s_ + 1]), reads=["xm%d" % s_, "stF%d" % s_], writes=["xhF%d" % s_])

        def f_s4(ti):
            s_ = ti % 2
            TB = (0, 1, 2, 3)
            for g4 in range(4):
                for j in range(8):
                    kc = g4 * 8 + j
                    P.op("pe", lambda e, g4=g4, j=j, kc=kc: e.transpose(out=psb[TB[g4]][:, j * 128:(j + 1) * 128], in_=xhF[s_][:, kc * 128:(kc + 1) * 128], identity=ident),
                         reads=["xhF%d" % s_, "ident"], writes=["ps%d" % TB[g4]])
            if 1 <= ti <= 4:
                for g4 in range(4):
                    bank, bk = psb[TB[g4]], "ps%d" % TB[g4]
                    for j in range(8):
                        kc = g4 * 8 + j
                        dst = hTf[:, kc, (ti - 1) * 128:ti * 128]
                        if g4 % 2 == 0:
                            P.op("act", lambda e, bank=bank, j=j, kc=kc, dst=dst: e.activation(out=dst, in_=bank[:, j * 128:(j + 1) * 128], func=AF.Identity,
                                                                                              scale=Gf[:, kc:kc + 1], bias=SHf[:, kc:kc + 1]),
                                 reads=[bk, "Gf", "modc"], writes=["hTf_%d_%d" % (ti, kc)])
                        else:
                            P.op("dve", lambda e, bank=bank, j=j, kc=kc, dst=dst: e.tensor_scalar(out=dst, in0=bank[:, j * 128:(j + 1) * 128],
                                                                                                 scalar1=Gf[:, kc:kc + 1], scalar2=SHf[:, kc:kc + 1], op0=ALU.mult, op1=ALU.add),
                                 reads=[bk, "Gf", "modc"], writes=["hTf_%d_%d" % (ti, kc)])
            else:
                col = 127 if ti == 0 else 0
                dstc = 512 if ti == 0 else 513
                if blk == 0 and ti == 0:
                    mcol = hmask[:, 0:1]
                elif blk == 1 and ti == 5:
                    mcol = hmask[:, 1:2]
                else:
                    mcol = cst[:, 3:4]
                for g4 in range(4):
                    bank, bk = psb[TB[g4]], "ps%d" % TB[g4]
                    src = bank[:, 0:1024].rearrange("p (j t) -> p j t", j=8)[:, :, col]
                    sl = slice(g4 * 8, g4 * 8 + 8)
                    P.op("dve", lambda e, src=src, sl=sl: e.tensor_tensor(out=coltmp[:, sl], in0=src, in1=Gf[:, sl], op=ALU.mult), reads=[bk, "Gf"], writes=["Fct"])
                    P.op("dve", lambda e, sl=sl: e.tensor_tensor(out=coltmp[:, sl], in0=coltmp[:, sl], in1=SHf[:, sl], op=ALU.add), reads=["Fct", "modc"], writes=["Fct"])
                    P.op("dve", lambda e, sl=sl, dstc=dstc, mcol=mcol: e.tensor_scalar(out=hTf[:, sl, dstc], in0=coltmp[:, sl], scalar1=mcol, scalar2=None, op0=ALU.mult),
                         reads=["Fct", "hmask", "cst"], writes=["hTf_h%d_%d" % (ti, g4)])

        pipeline(6, [f_s0, f_s1, f_s2, f_s3, f_s4], [0, 1, 2, 3, 4])
        assert sb.off <= HTF_OFF, sb.off
        hTf_keys = ["hTf_%d_%d" % (ti, kc) for ti in range(1, 5) for kc in range(32)] + ["hTf_h%d_%d" % (ti, g4) for ti in (0, 5) for g4 in range(4)]
        P.barrier()
        sb.release(mF2)
        act = sb.alloc([128, NFC, 512], BF16)
        mG = sb.mark()
        wg = [sb.alloc([128, 32, 128], BF16) for _ in range(2)]
        wu = [sb.alloc([128, 32, 128], BF16) for _ in range(2)]
        ag = sb.alloc([128, 514], F32)
        au = sb.alloc([128, 514], F32)
        cg = sb.alloc([128, 512], F32)
        cu = sb.alloc([128, 512], F32)
        sg = sb.alloc([128, 512], F32)
        cp4 = convp.rearrange("p (c f) -> p c f", f=4)
        for i in range(NFC):
            s_ = i % 2
            P.op("pool", lambda e, s_=s_, i=i: e.dma_start(out=wg[s_].rearrange("p k c -> p (k c)"), in_=w_up_g[i]), writes=["wg%d" % s_], dma="L:wg%d" % s_)
            P.op("pool", lambda e, s_=s_, i=i: e.dma_start(out=wu[s_].rearrange("p k c -> p (k c)"), in_=w_up_u[i]), writes=["wu%d" % s_], dma="L:wu%d" % s_)
            for (wsrc, wkey, pm, ph, abuf, akey, cbuf, ckey, ci) in ((wg[s_], "wg%d" % s_, s_, 4, ag, "ag", cg, "cg", i),
                                                                   (wu[s_], "wu%d" % s_, 2 + s_, 5, au, "au", cu, "cu", NFC + i)):
                for kc in range(32):
                    P.op("pe", lambda e, pm=pm, kc=kc, wsrc=wsrc: e.matmul(ps[pm], lhsT=wsrc[:, kc, :], rhs=hTf[:, kc, 0:512], start=(kc == 0), stop=(kc == 31)),
                         reads=[wkey], writes=["ps%d" % pm])
                for kc in range(32):
                    P.op("pe", lambda e, ph=ph, kc=kc, wsrc=wsrc: e.matmul(ps[ph][:, 0:2], lhsT=wsrc[:, kc, :], rhs=hTf[:, kc, 512:514], start=(kc == 0), stop=(kc == 31)),
                         reads=[wkey], writes=["ps%d" % ph])
                P.op("act", lambda e, pm=pm, abuf=abuf: e.activation(out=abuf[:, 1:513], in_=ps[pm], func=AF.Copy), reads=["ps%d" % pm], writes=[akey])
                P.op("dve", lambda e, ph=ph, abuf=abuf: e.tensor_copy(out=abuf[:, 0:1], in_=ps[ph][:, 0:1]), reads=["ps%d" % ph], writes=[akey])
                P.op("dve", lambda e, ph=ph, abuf=abuf: e.tensor_copy(out=abuf[:, 513:514], in_=ps[ph][:, 1:2]), reads=["ps%d" % ph], writes=[akey])
                P.op("dve", lambda e, abuf=abuf, cbuf=cbuf, ci=ci: e.tensor_scalar(out=cbuf, in0=abuf[:, 0:512], scalar1=cp4[:, ci, 0:1], scalar2=cp4[:, ci, 3:4], op0=ALU.mult, op1=ALU.add),
                     reads=[akey, "convp"], writes=[ckey])
                P.op("dve", lambda e, abuf=abuf, cbuf=cbuf, ci=ci: e.scalar_tensor_tensor(out=cbuf, in0=abuf[:, 1:513], scalar=cp4[:, ci, 1:2], in1=cbuf, op0=ALU.mult, op1=ALU.add),
                     reads=[akey, "convp", ckey], writes=[ckey])
                P.op("dve", lambda e, abuf=abuf, cbuf=cbuf, ci=ci: e.scalar_tensor_tensor(out=cbuf, in0=abuf[:, 2:514], scalar=cp4[:, ci, 2:3], in1=cbuf, op0=ALU.mult, op1=ALU.add),
                     reads=[akey, "convp", ckey], writes=[ckey])
            P.op("act", lambda e: e.activation(out=sg, in_=cg, func=AF.Silu), reads=["cg"], writes=["sg"])
            P.op("dve", lambda e, i=i: e.tensor_tensor(out=act[:, i, :], in0=sg, in1=cu, op=ALU.mult), reads=["sg", "cu"], writes=["act"])
        assert sb.off <= HTF_OFF, sb.off
        P.barrier()
        sb.release(mG)
        KG = 8
        groups = [(k0, min(k0 + KG, NFC)) for k0 in range(0, NFC, KG)]
        wd = [sb.alloc([128, KG, 512], BF16) for _ in range(3)]
        fsb = [sb.alloc([128, 512], F32) for _ in range(2)]
        junkG = sb.alloc([128, 512], F32)
        nwd = 0
        nf = 0
        for cbk in range(8):
            base = 4 * (cbk % 2)
            for (k0, k1) in groups:
                s_ = nwd % 3
                nwd += 1
                P.op("pool", lambda e, s_=s_, cbk=cbk, k0=k0, k1=k1: e.dma_start(out=wd[s_][:, 0:k1 - k0, :].rearrange("p k c -> p (k c)"),
                                                                               in_=w_down_h[cbk][:, k0 * 512:k1 * 512]),
                     writes=["wd%d" % s_], dma="L:wd%d" % s_)
                for kc in range(k0, k1):
                    for q in range(4):
                        P.op("pe", lambda e, base=base, q=q, kc=kc, k0=k0, s_=s_: e.matmul(ps[base + q], lhsT=act[:, kc, q * 128:(q + 1) * 128], rhs=wd[s_][:, kc - k0, :],
                                                                                       start=(kc == 0), stop=(kc == NFC - 1)),
                             reads=["act", "wd%d" % s_], writes=["ps%d" % (base + q)])
            for q in range(4):
                so = nf % 2
                nf += 1
                own = 4 * blk + q
                P.op("act", lambda e, base=base, q=q, so=so: e.activation(out=fsb[so], in_=ps[base + q], func=AF.Copy), reads=["ps%d" % (base + q)], writes=["fsb%d" % so])
                P.op("act", lambda e, so=so, q=q, cbk=cbk: e.activation(out=junkG, in_=fsb[so], func=AF.Square, accum_out=ssF[:, q * 8 + cbk:q * 8 + cbk + 1]),
                     reads=["fsb%d" % so], writes=["ssF"])
                P.op("sp", lambda e, so=so, own=own, cbk=cbk: e.dma_start(out=f_scr[own * 128:(own + 1) * 128, cbk * 512:(cbk + 1) * 512], in_=fsb[so]),
                     reads=["fsb%d" % so], writes=["f_scr"], dma="S:fsb%d" % so)
        P.op("dve", lambda e: e.reduce_sum(out=rF[:, 0:4], in_=ssF[:, 0:32].rearrange("p (t j) -> p t j", j=8), axis=AX.X), reads=["ssF"], writes=["rF"])
        rstd_from_ss(rF[:, 0:4], rF[:, 0:4], D, ["rF"], "rF")
        frow = [sb.alloc([128, D], F32) for _ in range(2)]
        xmrow = [sb.alloc([128, D], F32) for _ in range(2)]
        ggf = sb.alloc([128, D], F32)
        ld("sp", ggf, gg_scr[1], "ggf", reads=["gg_scr"])

        def z_s0(q):
            own, s_ = 4 * blk + q, q % 2
            ld("sp", frow[s_], f_scr[own * 128:(own + 1) * 128, :], "frow%d" % s_, reads=["f_scr"])
            ld("sp", xmrow[s_], xm_scr[own * 128:(own + 1) * 128, :], "xmrow%d" % s_, reads=["xm_scr"])

        def z_s1(q):
            own, s_ = 4 * blk + q, q % 2
            P.op("dve", lambda e: e.scalar_tensor_tensor(out=frow[s_], in0=frow[s_], scalar=rF[:, q:q + 1], in1=ggf, op0=ALU.mult, op1=ALU.mult),
                 reads=["frow%d" % s_, "rF", "ggf"], writes=["frow%d" % s_])
            P.op("dve", lambda e: e.tensor_tensor(out=frow[s_], in0=frow[s_], in1=xmrow[s_], op=ALU.add), reads=["frow%d" % s_, "xmrow%d" % s_], writes=["frow%d" % s_])
            o = P.op("sp", lambda e: e.dma_start(out=out[own * 128:(own + 1) * 128, :], in_=frow[s_]), reads=["frow%d" % s_], dma="S:out%d" % s_)
            out_ops.append(o.idx)

        pipeline(4, [z_s0, z_s1], [0, 1])
        P.barrier()
        sb.release(mF)

    P.emit(final_waits=out_ops)
    return nc


_CACHE = {}


def _host_layouts(inp):
    f = np.float32
    A = {}

    def col(v):
        v = np.asarray(v, f).reshape(-1, 128)
        return np.ascontiguousarray(v.T)

    A["ctx"] = np.ascontiguousarray(inp["ctx"][0], dtype=f)
    A["jidx"] = np.ascontiguousarray(np.broadcast_to(np.arange(32, dtype=f)[None, :], (128, 32)))
    A["cvec"] = np.concatenate([col(inp["c"][0]), col(inp["c_ctx"])], axis=1)
    w_ada_h = np.ascontiguousarray(np.asarray(inp["w_ada"][0], f).reshape(32, 128, 192, 128).transpose(2, 1, 0, 3)).reshape(192, 128, 32 * 128)
    A["w_ada_a"] = np.ascontiguousarray(w_ada_h[:96])
    A["w_ada_b"] = np.ascontiguousarray(w_ada_h[96:])
    A["b_ada_col"] = col(inp["b_ada"][0])
    A["b_ada_row"] = np.ascontiguousarray(inp["b_ada"][0][None, :], dtype=f)
    A["gcols"] = np.concatenate([col(inp["g_pre_mix"][0]), col(inp["g_pre_ffn"][0])], axis=1)
    A["goutc"] = col(np.concatenate([inp["g_out_a"][0], inp["g_out_b"][0]]))
    A["gpost_mix_row"] = np.ascontiguousarray(inp["g_post_mix"][0][None, :], dtype=f)
    A["gpost_ffn_row"] = np.ascontiguousarray(inp["g_post_ffn"][0][None, :], dtype=f)
    A["lng_row"] = np.ascontiguousarray(inp["ln_v_g"][0][None, :], dtype=f)
    A["lnb_row"] = np.ascontiguousarray(inp["ln_v_b"][0][None, :], dtype=f)
    A["gq_row"] = np.ascontiguousarray(inp["g_q"][0][None, :], dtype=f)
    A["gk_row"] = np.ascontiguousarray(inp["g_k"][0][None, :], dtype=f)
    w_in = np.asarray(inp["w_in"][0], f)
    A["w_in_h"] = np.ascontiguousarray(w_in.reshape(32, 128, 28, 256).transpose(2, 1, 0, 3)).reshape(28, 128, 32 * 256)
    ws = np.asarray(inp["w_s"][0], f)
    A["wsT"] = np.ascontiguousarray(ws.transpose(2, 0, 1)).reshape(128, 16 * 128)
    A["bs_col"] = np.ascontiguousarray(np.asarray(inp["b_s"][0], f).T)
    w_out = np.asarray(inp["w_out"][0], f)
    A["w_out_h"] = np.ascontiguousarray(w_out.reshape(32, 128, 8, 512).transpose(2, 1, 0, 3)).reshape(8, 128, 32 * 512)
    w_up = np.asarray(inp["w_up"][0], f)
    w_up_h = np.ascontiguousarray(w_up.reshape(32, 128, 2 * NFC, 128).transpose(2, 1, 0, 3)).reshape(2 * NFC, 128, 32 * 128)
    A["w_up_g"] = np.ascontiguousarray(w_up_h[:NFC])
    A["w_up_u"] = np.ascontiguousarray(w_up_h[NFC:])
    cw = np.asarray(inp["conv_w"][0], f)
    cb = np.asarray(inp["conv_b"][0], f)
    cp = np.concatenate([cw, cb[None, :]], axis=0)
    A["convp"] = np.ascontiguousarray(cp.reshape(4, 2 * NFC, 128).transpose(2, 1, 0)).reshape(128, 2 * NFC * 4)
    w_down = np.asarray(inp["w_down"][0], f)
    A["w_down_h"] = np.ascontiguousarray(w_down.reshape(NFC, 128, 8, 512).transpose(2, 1, 0, 3)).reshape(8, 128, NFC * 512)
    A["ident"] = np.eye(128, dtype=f)
    return A


def _run(inputs, stop=None, debug=False, cores=NCORE):
    nc = build_program(stop=stop, debug=debug)
    in_maps = _in_maps(inputs)[:cores]
    return run_bass_kernel_spmd(nc, in_maps, core_ids=list(range(cores)))


def _in_maps(inputs):
    shared = _host_layouts(inputs)
    x = np.asarray(inputs["x"][0], np.float32)
    tok = np.arange(S)
    in_maps = []
    for i in range(NCORE):
        shift = (TOK * i - 128) % S
        order = (tok + shift) % S
        m = dict(shared)
        m["xr"] = np.ascontiguousarray(x[order])
        rowi = (order // 64).astype(np.float32)
        coli = (order % 64).astype(np.float32)
        pos = np.zeros((128, NT, 2), np.float32)
        pos[:, :64, 0] = rowi.reshape(64, 128).T
        pos[:, :64, 1] = coli.reshape(64, 128).T
        m["pos"] = pos.reshape(128, NT * 2)
        hm = np.ones((128, 2), np.float32)
        if i == 0:
            hm[:, 0] = 0.0
        if i == NCORE - 1:
            hm[:, 1] = 0.0
        m["hmask"] = hm
        in_maps.append(m)
    return in_maps


def kernel(**inputs):
    if "nc" not in _CACHE:
        _CACHE["nc"] = build_program()
    nc = _CACHE["nc"]
    in_maps = _in_maps(inputs)
    res = run_bass_kernel_spmd(nc, in_maps, core_ids=list(range(NCORE)))
    outs = [np.asarray(r["out"], np.float32) for r in res.results]
    return np.concatenate(outs, axis=0)[None, :, :]
```

```python
import math
import numpy as np
import concourse.bass as bass
import concourse.mybir as mybir
from concourse.bass_utils import run_bass_kernel_spmd
from contextlib import ExitStack

F32 = mybir.dt.float32
BF16 = mybir.dt.bfloat16
AF = mybir.ActivationFunctionType
ALU = mybir.AluOpType
AX = mybir.AxisListType

ENGS = ("pe", "act", "dve", "pool", "sp")
D = 4096
S = 8192
NCORE = 8
TOK = S // NCORE
DFF = 11008
NFC = DFF // 128
EPS = 1e-6
NT = 66
NKEY = NT * 128
NOWN = 10
NG = 5
VW = 130


class Op:
    __slots__ = ("eng", "fn", "reads", "writes", "dma", "deps", "idx", "need_inc", "seq", "dma_cnt")

    def __init__(self, eng, fn, reads, writes, dma):
        self.eng, self.fn, self.reads, self.writes, self.dma = eng, fn, reads, writes, dma
        self.deps = set()
        self.need_inc = False
        self.seq = 0
        self.dma_cnt = 0


class Prog:
    def __init__(self, nc):
        self.nc = nc
        self.ops = []
        self.last_w = {}
        self.readers = {}
        self.last_eng = {}
        self.dma_since = []
        self.pending_barrier = {}

    def op(self, eng, fn, reads=(), writes=(), dma=None):
        o = Op(eng, fn, tuple(reads), tuple(writes), dma)
        o.idx = len(self.ops)
        for r in o.reads:
            w = self.last_w.get(r)
            if w is not None:
                o.deps.add(w)
        for w_ in o.writes:
            w = self.last_w.get(w_)
            if w is not None:
                o.deps.add(w)
            for rd in self.readers.get(w_, ()):
                o.deps.add(rd)
        for r in o.reads:
            self.readers.setdefault(r, []).append(o.idx)
        for w_ in o.writes:
            self.last_w[w_] = o.idx
            self.readers[w_] = []
        if eng in self.pending_barrier:
            o.deps |= self.pending_barrier.pop(eng)
        o.deps.discard(o.idx)
        self.ops.append(o)
        if dma is None:
            self.last_cmp = getattr(self, "last_cmp", {})
            self.last_cmp[eng] = o.idx
        self.last_eng[eng] = o.idx
        if dma is not None:
            self.dma_since.append(o.idx)
        return o

    def barrier(self):
        b = set(self.last_eng.values()) | set(self.dma_since)
        for e in ENGS:
            self.pending_barrier[e] = set(b) | self.pending_barrier.get(e, set())
        self.dma_since = []

    def emit(self, final_waits=()):
        nc = self.nc
        ops = self.ops
        for o in ops:
            for d in o.deps:
                p = ops[d]
                if p.dma is not None:
                    continue
                if p.eng == "pe" and o.eng == "pe":
                    continue
                p.need_inc = True
        if final_waits is None:
            for e, i_ in getattr(self, "last_cmp", {}).items():
                ops[i_].need_inc = True
            for o in ops:
                if o.dma is None and o.eng != "pe" and False:
                    o.need_inc = True
        cnt = {e: 0 for e in ENGS}
        dcnt = {}
        for o in ops:
            if o.dma is not None:
                dcnt[o.dma] = dcnt.get(o.dma, 0) + 1
                o.dma_cnt = dcnt[o.dma]
            elif o.need_inc:
                cnt[o.eng] += 1
                o.seq = cnt[o.eng]
        with ExitStack() as es:
            esem = {e: es.enter_context(nc.semaphore("s_" + e)) for e in ENGS}
            dsem = {k: es.enter_context(nc.semaphore("d_%d" % i)) for i, k in enumerate(dcnt)}
            block = es.enter_context(nc.Block())
            per_eng = {e: [o for o in ops if o.eng == e] for e in ENGS}

            def body(ename):
                def run(engine):
                    waited = {}
                    for o in per_eng[ename]:
                        need = {}
                        for d in o.deps:
                            p = ops[d]
                            if p.dma is not None:
                                key = ("d", p.dma)
                                val = 16 * p.dma_cnt
                                sem = dsem[p.dma]
                            else:
                                if p.eng == "pe" and ename == "pe":
                                    continue
                                key = ("e", p.eng)
                                val = p.seq
                                sem = esem[p.eng]
                            if val > need.get(key, (0, None))[0]:
                                need[key] = (val, sem)
                        for key, (val, sem) in need.items():
                            if waited.get(key, 0) >= val:
                                continue
                            engine.wait_ge(sem, val)
                            waited[key] = val
                        ins = o.fn(engine)
                        if o.dma is not None:
                            ins.then_inc(dsem[o.dma], 16)
                        elif o.need_inc:
                            ins.then_inc(esem[ename], 1)
                    if ename == "sp" and final_waits is None:
                        for e2 in ENGS:
                            if cnt[e2] > 0:
                                engine.wait_ge(esem[e2], cnt[e2])
                        for k2 in dcnt:
                            engine.wait_ge(dsem[k2], 16 * dcnt[k2])
                    elif ename == "sp":
                        for i in final_waits:
                            p = ops[i]
                            engine.wait_ge(dsem[p.dma], 16 * dcnt[p.dma])
                return run

            block.tensor(body("pe"))
            block.scalar(body("act"))
            block.vector(body("dve"))
            block.gpsimd(body("pool"))
            block.sync(body("sp"))


class SB:
    BASE = 16512
    LIMIT = 229376 - 64

    def __init__(self, nc):
        self.nc = nc
        self.off = SB.BASE
        self.n = 0

    def alloc(self, shape, dtype):
        nbytes = int(np.prod(shape[1:])) * (2 if dtype == BF16 else 4)
        nbytes = (nbytes + 63) // 64 * 64
        assert self.off + nbytes <= SB.LIMIT, ("SBUF overflow", self.off, nbytes)
        self.n += 1
        t = self.nc.alloc_sbuf_tensor_at("sb%d" % self.n, list(shape), dtype, offset=self.off)
        self.off += nbytes
        return t.ap()

    def mark(self):
        return self.off

    def release(self, m):
        self.off = m


def pipeline(n, stages, delays, order=None):
    maxd = max(delays)
    order = list(reversed(range(len(stages)))) if order is None else order
    for i in range(n + maxd):
        for s_ in order:
            u = i - delays[s_]
            if 0 <= u < n:
                stages[s_](u)


def build_program(stop=None, debug=False):
    nc = bass.Bass("TRN2", target_bir_lowering=False)

    def din(name, shape):
        return nc.dram_tensor(name, list(shape), F32, kind="ExternalInput").ap()

    xr = din("xr", [S, D])
    ctx = din("ctx", [256, D])
    pos_d = din("pos", [128, NT * 2])
    jidx_d = din("jidx", [128, 32])
    cvec_d = din("cvec", [128, 64])
    w_ada_a = din("w_ada_a", [96, 128, 32 * 128])
    w_ada_b = din("w_ada_b", [96, 128, 32 * 128])
    b_ada_col = din("b_ada_col", [128, 192])
    b_ada_row = din("b_ada_row", [1, 6 * D])
    gcols_d = din("gcols", [128, 64])
    goutc_d = din("goutc", [128, 32])
    gpost_mix_row = din("gpost_mix_row", [1, D])
    gpost_ffn_row = din("gpost_ffn_row", [1, D])
    lng_row = din("lng_row", [1, 2048])
    lnb_row = din("lnb_row", [1, 2048])
    gq_row = din("gq_row", [1, 128])
    gk_row = din("gk_row", [1, 128])
    w_in_h = din("w_in_h", [28, 128, 32 * 256])
    wsT_d = din("wsT", [128, 16 * 128])
    bs_col_d = din("bs_col", [128, 16])
    w_out_h = din("w_out_h", [8, 128, 32 * 512])
    w_up_g = din("w_up_g", [NFC, 128, 32 * 128])
    w_up_u = din("w_up_u", [NFC, 128, 32 * 128])
    convp_d = din("convp", [128, 2 * NFC * 4])
    w_down_h = din("w_down_h", [8, 128, NFC * 512])
    hmask_d = din("hmask", [128, 2])
    ident_d = din("ident", [128, 128])
    out = nc.dram_tensor("out", [TOK, D], F32, kind="ExternalOutput").ap()

    def dscr(name, shape, dt):
        if debug:
            return nc.dram_tensor(name, list(shape), dt, kind="ExternalOutput").ap()
        return nc.dram_tensor(name, list(shape), dt).ap()

    KT_scr = dscr("KT_scr", [4, 128, NKEY], BF16)
    V_scr = dscr("V_scr", [4, 128, NT * VW], BF16)
    hT_scr = dscr("hT_scr", [NOWN, 128, 32 * 128], BF16)
    oT_scr = dscr("oT_scr", [32, 128, NOWN * 128], BF16)
    o_scr = dscr("o_scr", [NOWN * 128, D], F32)
    xm_scr = dscr("xm_scr", [TOK, D], F32)
    f_scr = dscr("f_scr", [TOK, D], F32)
    gg_scr = dscr("gg_scr", [2, 128, D], F32)

    P = Prog(nc)
    sb = SB(nc)
    ps = [nc.alloc_psum_tensor("ps%d" % i, [128, 512], F32).ap() for i in range(8)]
    psb = [p.bitcast(BF16) for p in ps]

    ident = sb.alloc([128, 128], BF16)
    onesf = sb.alloc([128, 128], F32)
    modc = sb.alloc([128, 192], F32)
    modx = sb.alloc([128, 64], F32)
    Ga = sb.alloc([128, 32], F32)
    Gc = sb.alloc([128, 32], F32)
    Gf = sb.alloc([128, 32], F32)
    gcols = sb.alloc([128, 64], F32)
    goutc = sb.alloc([128, 32], F32)
    cst = sb.alloc([128, 8], F32)
    freq = sb.alloc([128, 32], F32)
    pos = sb.alloc([128, NT * 2], F32)
    gq_t = sb.alloc([128, 128], F32)
    gk_t = sb.alloc([128, 128], F32)
    bs_col = sb.alloc([128, 16], F32)
    hmask = sb.alloc([128, 2], F32)
    convp = sb.alloc([128, 2 * NFC * 4], F32)
    ssA = sb.alloc([128, NOWN * 8], F32)
    ssB = sb.alloc([128, NOWN * 16], F32)
    ssO = sb.alloc([128, NOWN * 8], F32)
    ssF = sb.alloc([128, 8 * 8], F32)
    rA = sb.alloc([128, NOWN], F32)
    rB = sb.alloc([128, NOWN], F32)
    rO = sb.alloc([128, NOWN], F32)
    rF = sb.alloc([128, 8], F32)
    sv = sb.alloc([128, 64], F32)
    ahalo = sb.alloc([128, 2 * NFC, 2], F32)
    st = sb.alloc([128, 64], F32)
    SHa = modc[:, 0:32]
    SHf = modc[:, 96:128]
    SHc = modx[:, 0:32]
    epsc = cst[:, 0:1]
    negpi = cst[:, 1:2]
    pospi = cst[:, 2:3]

    def ld(eng, dst, src, key, reads=(), writes=None):
        return P.op(eng, lambda e, dst=dst, src=src: e.dma_start(out=dst, in_=src), reads=reads,
                    writes=[key] if writes is None else writes, dma="L:" + key)

    def bc(row_ap, n=128):
        return row_ap.partition_broadcast(n).rearrange("p a b -> p (a b)")

    P.op("pool", lambda e: e.memset(cst[:, 0:1], EPS), writes=["cst"])
    P.op("pool", lambda e: e.memset(cst[:, 1:2], -math.pi), writes=["cst"])
    P.op("pool", lambda e: e.memset(cst[:, 2:3], math.pi), writes=["cst"])
    P.op("pool", lambda e: e.memset(cst[:, 3:4], 1.0), writes=["cst"])
    P.op("pool", lambda e: e.memset(onesf, 1.0), writes=["onesf"])
    ld("pool", ident, ident_d, "ident")
    ld("sp", sv, cvec_d, "sv")
    ld("sp", gcols, gcols_d, "gcols")
    ld("sp", goutc, goutc_d, "goutc")
    ld("sp", pos, pos_d, "pos")
    ld("sp", freq, jidx_d, "freq")
    ld("sp", gq_t, bc(gq_row), "gq_t")
    ld("sp", gk_t, bc(gk_row), "gk_t")
    ld("sp", bs_col, bs_col_d, "bs_col")
    ld("sp", hmask, hmask_d, "hmask")
    ld("sp", convp, convp_d, "convp")
    ld("sp", modc, b_ada_col, "modc")
    P.op("act", lambda e: e.activation(out=freq, in_=freq, func=AF.Exp, scale=-math.log(10000.0) / 32.0),
         reads=["freq"], writes=["freq"])
    P.op("act", lambda e: e.activation(out=sv, in_=sv, func=AF.Silu), reads=["sv"], writes=["sv"])

    svb = sb.alloc([128, 32, 2], BF16)
    P.op("dve", lambda e: e.tensor_copy(out=svb, in_=sv.rearrange("p (two k) -> p k two", two=2)), reads=["sv"], writes=["svb"])

    def wchunk(ch):
        return (w_ada_a if ch < 96 else w_ada_b)[ch % 96]

    mA = sb.mark()
    NWA = 8
    wa = [sb.alloc([128, 32, 128], BF16) for _ in range(NWA)]
    for ch in range(64):
        s_ = ch % NWA
        P.op("pool", lambda e, s_=s_, ch=ch: e.dma_start(out=wa[s_].rearrange("p k c -> p (k c)"), in_=wchunk(ch)), writes=["wa%d" % s_], dma="L:wa%d" % s_)
        for kc in range(32):
            P.op("pe", lambda e, s_=s_, ch=ch, kc=kc: e.matmul(ps[0][:, 2 * ch:2 * ch + 2], lhsT=wa[s_][:, kc, :], rhs=svb[:, kc, :], start=(kc == 0), stop=(kc == 31)),
                 reads=["wa%d" % s_, "svb"], writes=["ps0"])
    pc3 = ps[0][:, 0:128].rearrange("p (j two) -> p j two", two=2)
    P.op("dve", lambda e: e.tensor_tensor(out=modx, in0=pc3[:, :, 1], in1=modc[:, 0:64], op=ALU.add), reads=["ps0", "modc"], writes=["modx"])
    P.op("dve", lambda e: e.tensor_tensor(out=modc[:, 0:64], in0=pc3[:, :, 0], in1=modc[:, 0:64], op=ALU.add), reads=["ps0", "modc", "modx"], writes=["modc"])
    for (G, scl, gcol, key) in ((Ga, modc[:, 32:64], gcols[:, 0:32], "Ga"), (Gc, modx[:, 32:64], gcols[:, 0:32], "Gc")):
        P.op("dve", lambda e, G=G, scl=scl, gcol=gcol: e.scalar_tensor_tensor(out=G, in0=scl, scalar=1.0, in1=gcol, op0=ALU.add, op1=ALU.mult),
             reads=["modc", "modx", "gcols"], writes=[key])
    P.barrier()
    sb.release(mA)

    if stop == "A":
        dbgA = nc.dram_tensor("dbgA", [128, 512], F32, kind="ExternalOutput").ap()
        P.op("sp", lambda e: e.dma_start(out=dbgA[:, 0:192], in_=modc), reads=["modc"], dma="dbg")
        P.op("sp", lambda e: e.dma_start(out=dbgA[:, 192:256], in_=modx), reads=["modx"], dma="dbg")
        P.op("sp", lambda e: e.dma_start(out=dbgA[:, 256:288], in_=Ga), reads=["Ga"], dma="dbg")
        P.op("sp", lambda e: e.dma_start(out=dbgA[:, 288:320], in_=Gc), reads=["Gc"], dma="dbg")
        P.op("sp", lambda e: e.dma_start(out=dbgA[:, 320:352], in_=Gf), reads=["Gf"], dma="dbg")
        P.emit(final_waits=None)
        return nc
    def rstd_from_ss(ss_ap, out_ap, n, reads, wkey):
        P.op("act", lambda e: e.activation(out=out_ap, in_=ss_ap, func=AF.Sqrt, scale=1.0 / n, bias=epsc), reads=list(reads) + ["cst"], writes=[wkey])
        P.op("dve", lambda e: e.reciprocal(out=out_ap, in_=out_ap), reads=[wkey], writes=[wkey])

    def rope_tables(t, cos2, sins, ang, angk, angi, key, skey=None):
        skey = key if skey is None else skey
        P.op("dve", lambda e: e.tensor_scalar(out=ang[:, 0:32], in0=freq, scalar1=pos[:, 2 * t:2 * t + 1], scalar2=None, op0=ALU.mult),
             reads=["freq", "pos"], writes=[skey + "ang"])
        P.op("dve", lambda e: e.tensor_scalar(out=ang[:, 32:64], in0=freq, scalar1=pos[:, 2 * t + 1:2 * t + 2], scalar2=None, op0=ALU.mult),
             reads=["freq", "pos"], writes=[skey + "ang"])
        P.op("dve", lambda e: e.tensor_scalar(out=ang[:, 64:128], in0=ang[:, 0:64], scalar1=0.5 * math.pi, scalar2=None, op0=ALU.add),
             reads=[skey + "ang"], writes=[skey + "ang2"])
        P.op("dve", lambda e: e.tensor_scalar(out=angk, in0=ang, scalar1=1.0 / (2 * math.pi), scalar2=None, op0=ALU.mult),
             reads=[skey + "ang", skey + "ang2"], writes=[skey + "angk"])
        P.op("dve", lambda e: e.tensor_copy(out=angi, in_=angk), reads=[skey + "angk"], writes=[skey + "angi"])
        P.op("dve", lambda e: e.tensor_copy(out=angk, in_=angi), reads=[skey + "angi"], writes=[skey + "angk"])
        P.op("dve", lambda e: e.scalar_tensor_tensor(out=ang, in0=angk, scalar=-2.0 * math.pi, in1=ang, op0=ALU.mult, op1=ALU.add),
             reads=[skey + "angk", skey + "ang", skey + "ang2"], writes=[skey + "ang", skey + "ang2"])
        P.op("dve", lambda e: e.tensor_scalar(out=ang, in0=ang, scalar1=3.1415925, scalar2=-3.1415925, op0=ALU.min, op1=ALU.max),
             reads=[skey + "ang", skey + "ang2"], writes=[skey + "ang", skey + "ang2"])
        c4 = cos2.rearrange("p (a b j) -> p a b j", a=2, b=2)
        s4 = sins.rearrange("p (a b j) -> p a b j", a=2, b=2)
        sarg = ang[:, 0:64].rearrange("p (a j) -> p a j", a=2)
        carg = ang[:, 64:128].rearrange("p (a j) -> p a j", a=2)
        for b in range(2):
            P.op("act", lambda e, b=b: e.activation(out=c4[:, :, b, :], in_=carg, func=AF.Sin), reads=[skey + "ang2"], writes=[key + "cos"])
        P.op("act", lambda e: e.activation(out=s4[:, :, 1, :], in_=sarg, func=AF.Sin), reads=[skey + "ang"], writes=[key + "sin"])
        P.op("act", lambda e: e.activation(out=s4[:, :, 0, :], in_=sarg, func=AF.Sin, scale=-1.0), reads=[skey + "ang"], writes=[key + "sin"])

    def qk_norm_rope(psrc, pkey, nh, g_t, cos2, sins, tkeys, tmp, outb, okey, pfx):
        sq, qn, t1, t2, ssq = tmp["sq"], tmp["qn"], tmp["t1"], tmp["t2"], tmp["ss"]
        P.op("act", lambda e: e.activation(out=sq, in_=psrc, func=AF.Square), reads=[pkey], writes=[pfx + "sq"])
        P.op("dve", lambda e: e.reduce_sum(out=ssq[:, 0:nh], in_=sq.rearrange("p (h d) -> p h d", h=nh), axis=AX.X), reads=[pfx + "sq"], writes=[pfx + "ss"])
        rstd_from_ss(ssq[:, 0:nh], ssq[:, 0:nh], 128, [pfx + "ss"], pfx + "ss")
        for h in range(nh):
            P.op("act", lambda e, h=h: e.activation(out=qn[:, h * 128:(h + 1) * 128], in_=psrc[:, h * 128:(h + 1) * 128], func=AF.Copy, scale=ssq[:, h:h + 1]),
                 reads=[pkey, pfx + "ss"], writes=[pfx + "qn"])
        q3 = qn.rearrange("p (h d) -> p h d", h=nh)
        P.op("dve", lambda e: e.tensor_tensor(out=q3, in0=q3, in1=g_t.unsqueeze(1).broadcast_to([128, nh, 128]), op=ALU.mult),
             reads=[pfx + "qn", "gq_t", "gk_t"], writes=[pfx + "qn"])
        P.op("dve", lambda e: e.tensor_tensor(out=t1.rearrange("p (h d) -> p h d", h=nh), in0=q3, in1=cos2.unsqueeze(1).broadcast_to([128, nh, 128]), op=ALU.mult),
             reads=[pfx + "qn"] + tkeys, writes=[pfx + "t1"])
        q5 = qn.rearrange("p (h a b j) -> p h a b j", h=nh, a=2, b=2)
        t5 = t2.rearrange("p (h a b j) -> p h a b j", h=nh, a=2, b=2)
        s4 = sins.rearrange("p (a b j) -> p a b j", a=2, b=2)
        for b in range(2):
            P.op("dve", lambda e, b=b: e.tensor_tensor(out=t5[:, :, :, b, :], in0=q5[:, :, :, 1 - b, :],
                                                      in1=s4[:, :, b, :].unsqueeze(1).broadcast_to([128, nh, 2, 32]), op=ALU.mult),
                 reads=[pfx + "qn"] + tkeys, writes=[pfx + "t2"])
        P.op("dve", lambda e: e.tensor_tensor(out=outb, in0=t1, in1=t2, op=ALU.add), reads=[pfx + "t1", pfx + "t2"], writes=[okey])

    def norm_transpose(xt, xkey, G, SHv, gkeys, junk, xh, xhkey, hT_dst, hkey, pfx, banks=(0, 1), only_col=None, mask_col=None, coltmp=None):
        P.op("act", lambda e: e.activation(out=junk, in_=xt, func=AF.Square, accum_out=st[:, 0:1]), reads=[xkey], writes=[pfx + "st"])
        rstd_from_ss(st[:, 0:1], st[:, 1:2], D, [pfx + "st"], pfx + "st1")
        P.op("act", lambda e: e.activation(out=xh, in_=xt, func=AF.Copy, scale=st[:, 1:2]), reads=[xkey, pfx + "st1"], writes=[xhkey])
        for g4 in range(4):
            bank = psb[banks[g4 % 2]]
            bk = "ps%d" % banks[g4 % 2]
            for j in range(8):
                kc = g4 * 8 + j
                P.op("pe", lambda e, bank=bank, j=j, kc=kc: e.transpose(out=bank[:, j * 128:(j + 1) * 128], in_=xh[:, kc * 128:(kc + 1) * 128], identity=ident),
                     reads=[xhkey, "ident"], writes=[bk])
            if only_col is None:
                for j in range(8):
                    kc = g4 * 8 + j
                    if g4 % 2 == 0:
                        P.op("act", lambda e, bank=bank, j=j, kc=kc: e.activation(out=hT_dst[:, kc, :], in_=bank[:, j * 128:(j + 1) * 128], func=AF.Identity,
                                                                                 scale=G[:, kc:kc + 1], bias=SHv[:, kc:kc + 1]),
                             reads=[bk] + gkeys, writes=[hkey + "_%d" % kc])
                    else:
                        P.op("dve", lambda e, bank=bank, j=j, kc=kc: e.tensor_scalar(out=hT_dst[:, kc, :], in0=bank[:, j * 128:(j + 1) * 128],
                                                                                    scalar1=G[:, kc:kc + 1], scalar2=SHv[:, kc:kc + 1], op0=ALU.mult, op1=ALU.add),
                             reads=[bk] + gkeys, writes=[hkey + "_%d" % kc])
            else:
                src = bank[:, 0:1024].rearrange("p (j t) -> p j t", j=8)[:, :, only_col]
                sl = slice(g4 * 8, g4 * 8 + 8)
                P.op("dve", lambda e, src=src, sl=sl: e.tensor_tensor(out=coltmp[:, sl], in0=src, in1=G[:, sl], op=ALU.mult), reads=[bk] + gkeys, writes=[pfx + "ct"])
                P.op("dve", lambda e, sl=sl: e.tensor_tensor(out=coltmp[:, sl], in0=coltmp[:, sl], in1=SHv[:, sl], op=ALU.add), reads=[pfx + "ct"] + gkeys, writes=[pfx + "ct"])
                P.op("dve", lambda e, sl=sl: e.tensor_scalar(out=hT_dst[:, sl], in0=coltmp[:, sl], scalar1=mask_col, scalar2=None, op0=ALU.mult),
                     reads=[pfx + "ct", "hmask"], writes=[hkey])

    mTab = sb.mark()
    COSo = sb.alloc([128, NOWN * 64], F32)
    SINo = sb.alloc([128, NOWN * 64], F32)
    mBig = sb.mark()
    COS = sb.alloc([128, NT * 64], F32)
    SIN = sb.alloc([128, NT * 64], F32)
    mR = sb.mark()
    angA = sb.alloc([128, NT * 64], F32)
    angK = sb.alloc([128, NT * 64], F32)
    angI = sb.alloc([128, NT * 64], mybir.dt.int32)
    P.op("dve", lambda e: e.tensor_tensor(out=angA.rearrange("p (m j) -> p m j", j=32), in0=pos.unsqueeze(2).broadcast_to([128, NT * 2, 32]),
                                          in1=freq.unsqueeze(1).broadcast_to([128, NT * 2, 32]), op=ALU.mult),
         reads=["pos", "freq"], writes=["angA"])
    for (dst, dkey, shift) in ((SIN, "SIN", 0.0), (COS, "COS", 0.5 * math.pi)):
        if shift:
            P.op("dve", lambda e, shift=shift: e.tensor_scalar(out=angA, in0=angA, scalar1=shift, scalar2=None, op0=ALU.add), reads=["angA"], writes=["angA"])
        P.op("dve", lambda e: e.tensor_scalar(out=angK, in0=angA, scalar1=1.0 / (2 * math.pi), scalar2=None, op0=ALU.mult), reads=["angA"], writes=["angK"])
        P.op("dve", lambda e: e.tensor_copy(out=angI, in_=angK), reads=["angK"], writes=["angI"])
        P.op("dve", lambda e: e.tensor_copy(out=angK, in_=angI), reads=["angI"], writes=["angK"])
        P.op("dve", lambda e: e.scalar_tensor_tensor(out=angK, in0=angK, scalar=-2.0 * math.pi, in1=angA, op0=ALU.mult, op1=ALU.add), reads=["angK", "angA"], writes=["angK"])
        P.op("dve", lambda e: e.tensor_scalar(out=angK, in0=angK, scalar1=3.1415925, scalar2=-3.1415925, op0=ALU.min, op1=ALU.max), reads=["angK"], writes=["angK"])
        P.op("act", lambda e, dst=dst: e.activation(out=dst, in_=angK, func=AF.Sin), reads=["angK"], writes=[dkey])
    P.op("dve", lambda e: e.tensor_copy(out=COSo, in_=COS[:, 0:NOWN * 64]), reads=["COS"], writes=["COSo"])
    P.op("dve", lambda e: e.tensor_copy(out=SINo, in_=SIN[:, 0:NOWN * 64]), reads=["SIN"], writes=["SINo"])
    P.barrier()
    sb.release(mR)

    def rope_apply(kn, nh, t, outb, t1, t2, rkeys, wkey, tkey, tabs=None):
        Ct, St, ck, sk = (COS, SIN, "COS", "SIN") if tabs is None else tabs
        cs = Ct[:, t * 64:(t + 1) * 64].rearrange("p (a j) -> p a j", a=2).unsqueeze(1).broadcast_to([128, nh, 2, 32])
        sn = St[:, t * 64:(t + 1) * 64].rearrange("p (a j) -> p a j", a=2).unsqueeze(1).broadcast_to([128, nh, 2, 32])
        q5 = kn.rearrange("p (h a b j) -> p h a b j", h=nh, a=2, b=2)
        a5 = t1.rearrange("p (h a b j) -> p h a b j", h=nh, a=2, b=2)
        b5 = t2.rearrange("p (h a b j) -> p h a b j", h=nh, a=2, b=2)
        o5 = outb.rearrange("p (h a b j) -> p h a b j", h=nh, a=2, b=2)
        for b_ in range(2):
            P.op("dve", lambda e, b_=b_: e.tensor_tensor(out=a5[:, :, :, b_, :], in0=q5[:, :, :, b_, :], in1=cs, op=ALU.mult), reads=rkeys + [ck], writes=[tkey + "t1"])
            P.op("dve", lambda e, b_=b_: e.tensor_tensor(out=b5[:, :, :, b_, :], in0=q5[:, :, :, 1 - b_, :], in1=sn, op=ALU.mult), reads=rkeys + [sk], writes=[tkey + "t2"])
        P.op("dve", lambda e: e.tensor_tensor(out=o5[:, :, :, 0, :], in0=a5[:, :, :, 0, :], in1=b5[:, :, :, 0, :], op=ALU.subtract), reads=[tkey + "t1", tkey + "t2"], writes=[wkey])
        P.op("dve", lambda e: e.tensor_tensor(out=o5[:, :, :, 1, :], in0=a5[:, :, :, 1, :], in1=b5[:, :, :, 1, :], op=ALU.add), reads=[tkey + "t1", tkey + "t2"], writes=[wkey])

    mB = sb.mark()
    xt = [sb.alloc([128, D], F32) for _ in range(2)]
    xh = [sb.alloc([128, D], BF16) for _ in range(2)]
    hT = [sb.alloc([128, 32, 128], BF16) for _ in range(2)]
    wkv = sb.alloc([128, 32, 1024], BF16)
    junk = sb.alloc([128, 512], BF16)
    wa2 = [sb.alloc([128, 16, 128], BF16) for _ in range(3)]
    kraw = [sb.alloc([128, 512], F32) for _ in range(2)]
    kn = sb.alloc([128, 512], F32)
    kt1 = sb.alloc([128, 512], F32)
    kt2 = sb.alloc([128, 512], F32)
    kr = [sb.alloc([128, 512], BF16) for _ in range(2)]
    kTs = [sb.alloc([128, 4, 128], BF16) for _ in range(2)]
    vaug = [sb.alloc([128, 4, VW], BF16) for _ in range(2)]
    ssx = sb.alloc([128, 4], F32)
    ssk = sb.alloc([128, 8], F32)
    for blk in range(4):
        P.op("pool", lambda e, blk=blk: e.dma_start(out=wkv[:, :, blk * 256:(blk + 1) * 256], in_=w_in_h[24 + blk].rearrange("p (k c) -> p k c", k=32)),
             writes=["wkv"], dma="L:wkv")
    for s_ in range(2):
        P.op("pool", lambda e, s_=s_: e.memset(vaug[s_][:, :, 128:VW], 1.0), writes=["vaug%d" % s_])

    def b_s0(t):
        s_ = t % 2
        src = xr[t * 128:(t + 1) * 128, :] if t < 64 else ctx[(t - 64) * 128:(t - 63) * 128, :]
        ld("sp", xt[s_], src, "xt%d" % s_)

    def b_s1(t):
        s_, c4 = t % 2, t % 4
        P.op("act", lambda e: e.activation(out=xh[s_], in_=xt[s_], func=AF.Square, accum_out=ssx[:, c4:c4 + 1]), reads=["xt%d" % s_], writes=["ssx%d" % c4, "xh%d" % s_])
        rstd_from_ss(ssx[:, c4:c4 + 1], ssx[:, c4:c4 + 1], D, ["ssx%d" % c4], "ssx%d" % c4)

    def b_s2(t):
        s_, c4 = t % 2, t % 4
        P.op("act", lambda e: e.activation(out=xh[s_], in_=xt[s_], func=AF.Copy, scale=ssx[:, c4:c4 + 1]), reads=["xt%d" % s_, "ssx%d" % c4], writes=["xh%d" % s_])

    def b_s3(t):
        s_ = t % 2
        G, SHv, gk = (Ga, SHa, ["Ga", "modc"]) if t < 64 else (Gc, SHc, ["Gc", "modx"])
        TB = (0, 1, 4, 5)
        for g4 in range(4):
            bank = psb[TB[g4]]
            bk = "ps%d" % TB[g4]
            for j in range(8):
                kc = g4 * 8 + j
                P.op("pe", lambda e, bank=bank, j=j, kc=kc: e.transpose(out=bank[:, j * 128:(j + 1) * 128], in_=xh[s_][:, kc * 128:(kc + 1) * 128], identity=ident),
                     reads=["xh%d" % s_, "ident"], writes=[bk])
        for g4 in range(4):
            bank = psb[TB[g4]]
            bk = "ps%d" % TB[g4]
            for j in range(8):
                kc = g4 * 8 + j
                if g4 % 2 == 0:
                    P.op("act", lambda e, bank=bank, j=j, kc=kc: e.activation(out=hT[s_][:, kc, :], in_=bank[:, j * 128:(j + 1) * 128], func=AF.Identity,
                                                                             scale=G[:, kc:kc + 1], bias=SHv[:, kc:kc + 1]),
                         reads=[bk] + gk, writes=["hT%d_%d" % (s_, kc)])
                else:
                    P.op("dve", lambda e, bank=bank, j=j, kc=kc: e.tensor_scalar(out=hT[s_][:, kc, :], in0=bank[:, j * 128:(j + 1) * 128],
                                                                                scalar1=G[:, kc:kc + 1], scalar2=SHv[:, kc:kc + 1], op0=ALU.mult, op1=ALU.add),
                         reads=[bk] + gk, writes=["hT%d_%d" % (s_, kc)])
        if t < NOWN:
            P.op("sp", lambda e: e.dma_start(out=hT_scr[t], in_=hT[s_].rearrange("p k c -> p (k c)")),
                 reads=["hT%d_%d" % (s_, kc_) for kc_ in range(32)], writes=["hT_scr%d" % t], dma="S:hT%d" % s_)

    def b_s4(t):
        s_ = t % 2
        for (pp, half) in ((2, 0), (3, 1)):
            for kc in range(32):
                P.op("pe", lambda e, pp=pp, kc=kc, half=half: e.matmul(ps[pp], lhsT=hT[s_][:, kc, :], rhs=wkv[:, kc, half * 512:(half + 1) * 512],
                                                                      start=(kc == 0), stop=(kc == 31)),
                     reads=["hT%d_%d" % (s_, kc), "wkv"], writes=["ps%d" % pp])

    def b_s5(t):
        s_ = t % 2
        pk, pv = ps[2], ps[3]
        pkk, pvk = "ps2", "ps3"
        P.op("act", lambda e: e.activation(out=kraw[s_], in_=pk, func=AF.Copy), reads=[pkk], writes=["kraw%d" % s_])
        P.op("act", lambda e: e.activation(out=vaug[s_][:, :, 0:128], in_=pv.rearrange("p (h d) -> p h d", h=4), func=AF.Copy), reads=[pvk], writes=["vaug%d" % s_])
        P.op("sp", lambda e: e.dma_start(out=V_scr.rearrange("h p (t c) -> p h t c", t=NT)[:, :, t, :], in_=vaug[s_]),
             reads=["vaug%d" % s_], writes=["V_scr"], dma="S:v%d" % s_)
        for h in range(4):
            P.op("act", lambda e, h=h: e.activation(out=junk[:, h * 128:(h + 1) * 128], in_=kraw[s_][:, h * 128:(h + 1) * 128], func=AF.Square,
                                                   accum_out=ssk[:, s_ * 4 + h:s_ * 4 + h + 1]),
                 reads=["kraw%d" % s_], writes=["rk%d" % s_])
        P.op("act", lambda e: e.activation(out=ssk[:, s_ * 4:s_ * 4 + 4], in_=ssk[:, s_ * 4:s_ * 4 + 4], func=AF.Sqrt, scale=1.0 / 128, bias=epsc),
             reads=["rk%d" % s_, "cst"], writes=["rk%d" % s_])

    def b_s6(t):
        s_ = t % 2
        rk = ssk[:, s_ * 4:s_ * 4 + 4]
        P.op("dve", lambda e: e.reciprocal(out=rk, in_=rk), reads=["rk%d" % s_], writes=["rk%d" % s_])
        k3 = kn.rearrange("p (h d) -> p h d", h=4)
        P.op("dve", lambda e: e.tensor_tensor(out=k3, in0=kraw[s_].rearrange("p (h d) -> p h d", h=4), in1=rk.unsqueeze(2).broadcast_to([128, 4, 128]), op=ALU.mult),
             reads=["kraw%d" % s_, "rk%d" % s_], writes=["kn"])
        P.op("dve", lambda e: e.tensor_tensor(out=k3, in0=k3, in1=gk_t.unsqueeze(1).broadcast_to([128, 4, 128]), op=ALU.mult), reads=["kn", "gk_t"], writes=["kn"])
        rope_apply(kn, 4, t, kr[s_], kt1, kt2, ["kn"], "kr%d" % s_, "B")

    def b_s7(t):
        s_ = t % 2
        for h in range(4):
            P.op("pe", lambda e, h=h: e.transpose(out=psb[6][:, h * 128:(h + 1) * 128], in_=kr[s_][:, h * 128:(h + 1) * 128], identity=ident),
                 reads=["kr%d" % s_, "ident"], writes=["ps6"])
        P.op("dve", lambda e: e.tensor_copy(out=kTs[s_], in_=psb[6][:, 0:512].rearrange("p (h t) -> p h t", h=4)), reads=["ps6"], writes=["kTs%d" % s_])
        P.op("sp", lambda e: e.dma_start(out=KT_scr.rearrange("h d n -> d h n")[:, :, t * 128:(t + 1) * 128], in_=kTs[s_]),
             reads=["kTs%d" % s_], writes=["KT_scr"], dma="S:k%d" % s_)

    NG2 = 256

    def a2_ld(g):
        ch, hh, sl = 64 + g // 2, g % 2, g % 3
        P.op("pool", lambda e: e.dma_start(out=wa2[sl].rearrange("p k c -> p (k c)"), in_=wchunk(ch)[:, hh * 2048:(hh + 1) * 2048]),
             writes=["wa2_%d" % sl], dma="L:wa2_%d" % sl)

    def a2_mmul(g):
        ch, hh, sl = 64 + g // 2, g % 2, g % 3
        c2 = 2 * (ch - 64)
        for k16 in range(16):
            kc = hh * 16 + k16
            P.op("pe", lambda e, k16=k16, kc=kc: e.matmul(ps[7][:, c2:c2 + 2], lhsT=wa2[sl][:, k16, :], rhs=svb[:, kc, :], start=(kc == 0), stop=(kc == 31)),
                 reads=["wa2_%d" % sl, "svb"], writes=["ps7"])

    def a2(t):
        if t >= 64:
            return
        for g in range(4 * t, 4 * t + 4):
            a2_ld(g)
            if g >= 2:
                a2_mmul(g - 2)

    pipeline(NT, [b_s0, b_s1, b_s2, b_s3, b_s4, b_s5, b_s6, b_s7, a2], [0, 1, 2, 3, 4, 5, 6, 7, 0], order=[5, 3, 7, 6, 4, 8, 2, 1, 0])
    a2_mmul(NG2 - 2)
    a2_mmul(NG2 - 1)
    pc7 = ps[7][:, 0:256].rearrange("p (j two) -> p j two", two=2)
    P.op("dve", lambda e: e.tensor_tensor(out=modc[:, 64:192], in0=pc7[:, :, 0], in1=modc[:, 64:192], op=ALU.add), reads=["ps7", "modc"], writes=["modc"])
    P.op("dve", lambda e: e.scalar_tensor_tensor(out=Gf, in0=modc[:, 128:160], scalar=1.0, in1=gcols[:, 32:64], op0=ALU.add, op1=ALU.mult),
         reads=["modc", "gcols"], writes=["Gf"])
    P.barrier()
    sb.release(mBig)

    if stop == "B":
        P.emit(final_waits=None)
        return nc
    mGG = sb.mark()
    identf = sb.alloc([128, 128], F32)
    dg = [sb.alloc([128, 128], F32) for _ in range(2)]
    grow = sb.alloc([128, 2048], F32)
    ggs = sb.alloc([128, 2048], F32)
    ld("sp", identf, ident_d, "identf")
    ndg = 0
    for which, c0 in ((0, 64), (1, 160)):
        for half in range(2):
            ld("sp", grow, bc((gpost_mix_row if which == 0 else gpost_ffn_row)[:, half * 2048:(half + 1) * 2048]), "grow")
            for q4 in range(4):
                bank = 1 + (q4 % 2)
                for jj in range(4):
                    j = c0 + half * 16 + q4 * 4 + jj
                    d_ = ndg % 2
                    ndg += 1
                    P.op("dve", lambda e, d_=d_, j=j: e.tensor_scalar(out=dg[d_], in0=identf, scalar1=modc[:, j:j + 1], scalar2=None, op0=ALU.mult),
                         reads=["identf", "modc"], writes=["dg%d" % d_])
                    P.op("pe", lambda e, d_=d_, jj=jj, bank=bank: e.matmul(ps[bank][:, jj * 128:(jj + 1) * 128], lhsT=onesf, rhs=dg[d_], start=True, stop=True),
                         reads=["dg%d" % d_, "onesf"], writes=["ps%d" % bank])
                P.op("dve", lambda e, q4=q4, bank=bank: e.tensor_tensor(out=ggs[:, q4 * 512:(q4 + 1) * 512], in0=ps[bank], in1=grow[:, q4 * 512:(q4 + 1) * 512], op=ALU.mult),
                     reads=["ps%d" % bank, "grow"], writes=["ggs"])
            P.op("sp", lambda e, which=which, half=half: e.dma_start(out=gg_scr[which, :, half * 2048:(half + 1) * 2048], in_=ggs),
                 reads=["ggs"], writes=["gg_scr"], dma="S:gg")
    P.barrier()
    sb.release(mGG)

    qT = sb.alloc([128, 16, NOWN * 128], BF16)
    mC = sb.mark()
    hTo = sb.alloc([128, NG, 32 * 128], BF16)
    wblk = [sb.alloc([128, 32, 256], BF16) for _ in range(2)]
    gv = sb.alloc([128, NG, 2048], BF16)
    lng = sb.alloc([128, 2048], F32)
    lnb = sb.alloc([128, 2048], F32)
    wsT = sb.alloc([128, 16, 128], BF16)
    gtmp = [sb.alloc([128, 256], F32) for _ in range(2)]
    oa = [sb.alloc([128, 256], F32) for _ in range(2)]
    oab = [sb.alloc([128, 256], BF16) for _ in range(2)]
    oTs = [sb.alloc([128, 2, 128], BF16) for _ in range(2)]
    qn = sb.alloc([128, 256], F32)
    qt1 = sb.alloc([128, 256], F32)
    qt2 = sb.alloc([128, 256], F32)
    ssq = [sb.alloc([128, 2], F32) for _ in range(2)]
    qr = [sb.alloc([128, 256], BF16) for _ in range(2)]
    vsum = sb.alloc([128, NOWN * 8], F32)
    vsq = sb.alloc([128, NOWN * 8], F32)
    vst = sb.alloc([128, NOWN * 4], F32)
    junkC = sb.alloc([128, 256], F32)
    ld("sp", lng, bc(lng_row), "lng")
    ld("sp", lnb, bc(lnb_row), "lnb")
    P.op("pool", lambda e: e.dma_start(out=wsT.rearrange("p g c -> p (g c)"), in_=wsT_d), writes=["wsT"], dma="L:wsT")

    nblk = [0]
    wslot = {}

    def load_wblk(blk, tag):
        if (blk, tag) in wslot:
            return
        s_ = nblk[0] % 2
        nblk[0] += 1
        wslot[(blk, tag)] = s_
        P.op("pool", lambda e: e.dma_start(out=wblk[s_].rearrange("p k c -> p (k c)"), in_=w_in_h[blk]), writes=["wblk%d" % s_], dma="L:wblk%d" % s_)

    def run_family(grp, blk0, stages_fn, delays):
        TL = list(range(grp * NG, grp * NG + NG))
        units = [(j, t) for j in range(8) for t in TL]

        def s0(u):
            j, t = units[u]
            load_wblk(blk0 + j, grp)
            if t == TL[1] and j + 1 < 8:
                load_wblk(blk0 + j + 1, grp)
            s_ = wslot[(blk0 + j, grp)]
            bank = u % 3
            for kc in range(32):
                P.op("pe", lambda e, kc=kc: e.matmul(ps[bank][:, 0:256], lhsT=hTo[:, t % NG, kc * 128:(kc + 1) * 128], rhs=wblk[s_][:, kc, :],
                                                     start=(kc == 0), stop=(kc == 31)),
                     reads=["hTo%d" % (t % NG), "wblk%d" % s_], writes=["ps%d" % bank])
        stages = [s0] + stages_fn(units)
        pipeline(len(units), stages, delays)

    def v_stages(units):
        def s1(u):
            j, t = units[u]
            bank, g_ = u % 3, u % 2
            P.op("act", lambda e: e.activation(out=gtmp[g_], in_=ps[bank][:, 0:256], func=AF.Gelu, accum_out=vsum[:, t * 8 + j:t * 8 + j + 1]),
                 reads=["ps%d" % bank], writes=["gtmp%d" % g_, "vsum%d" % (t * 8 + j)])
            P.op("act", lambda e: e.activation(out=junkC, in_=gtmp[g_], func=AF.Square, accum_out=vsq[:, t * 8 + j:t * 8 + j + 1]),
                 reads=["gtmp%d" % g_], writes=["vsq%d" % (t * 8 + j)])

        def s2(u):
            j, t = units[u]
            g_ = u % 2
            P.op("dve", lambda e: e.tensor_copy(out=gv[:, t % NG, j * 256:(j + 1) * 256], in_=gtmp[g_]), reads=["gtmp%d" % g_], writes=["gv%d" % (t % NG)])
        return [s1, s2]

    def u_stages(units):
        def s0b(u):
            j, t = units[u]
            mb = 3 + u % 3
            for gi in range(2):
                g = 2 * j + gi
                P.op("pe", lambda e, gi=gi, g=g: e.matmul(ps[mb][:, gi * 128:(gi + 1) * 128], lhsT=wsT[:, g, :], rhs=gv[:, t % NG, g * 128:(g + 1) * 128], start=True, stop=True),
                     reads=["wsT", "gv%d" % (t % NG)], writes=["ps%d" % mb])

        def s1(u):
            bank, g_ = u % 3, u % 2
            P.op("act", lambda e: e.activation(out=gtmp[g_], in_=ps[bank][:, 0:256], func=AF.Gelu), reads=["ps%d" % bank], writes=["gtmp%d" % g_])

        def s2(u):
            j, t = units[u]
            mb, g_ = 3 + u % 3, u % 2
            for gi in range(2):
                g = 2 * j + gi
                P.op("dve", lambda e, gi=gi, g=g: e.scalar_tensor_tensor(out=oa[g_][:, gi * 128:(gi + 1) * 128], in0=ps[mb][:, gi * 128:(gi + 1) * 128],
                                                                        scalar=bs_col[:, g:g + 1], in1=gtmp[g_][:, gi * 128:(gi + 1) * 128], op0=ALU.add, op1=ALU.mult),
                     reads=["ps%d" % mb, "bs_col", "gtmp%d" % g_], writes=["oa%d" % g_])
            P.op("dve", lambda e: e.tensor_copy(out=oab[g_], in_=oa[g_]), reads=["oa%d" % g_], writes=["oab%d" % g_])

        def s3(u):
            j, t = units[u]
            g_ = u % 2
            tb = 6 + u % 2
            for gi in range(2):
                P.op("pe", lambda e, gi=gi: e.transpose(out=psb[tb][:, gi * 128:(gi + 1) * 128], in_=oab[g_][:, gi * 128:(gi + 1) * 128], identity=ident),
                     reads=["oab%d" % g_, "ident"], writes=["ps%d" % tb])
            col = t * 8 + j
            P.op("act", lambda e: e.activation(out=junkC, in_=oa[g_], func=AF.Square, accum_out=ssA[:, col:col + 1]), reads=["oa%d" % g_], writes=["ssA%d" % col])

        def s4(u):
            j, t = units[u]
            g_ = u % 2
            tb = 6 + u % 2
            for gi in range(2):
                kc = 2 * j + gi
                P.op("act", lambda e, gi=gi, kc=kc: e.activation(out=oTs[g_][:, gi, :], in_=psb[tb][:, gi * 128:(gi + 1) * 128], func=AF.Copy, scale=goutc[:, kc:kc + 1]),
                     reads=["ps%d" % tb, "goutc"], writes=["oTs%d" % g_])
            P.op("sp", lambda e: e.dma_start(out=oT_scr[2 * j:2 * j + 2].rearrange("k p n -> p k n")[:, :, t * 128:(t + 1) * 128], in_=oTs[g_]),
                 reads=["oTs%d" % g_], writes=["oT_scr"], dma="S:oT%d" % g_)
        return [s0b, s1, s2, s3, s4]

    def q_stages(units):
        def s1(u):
            bank, g_ = u % 3, u % 2
            for h in range(2):
                P.op("act", lambda e, h=h: e.activation(out=junkC[:, h * 128:(h + 1) * 128], in_=ps[bank][:, h * 128:(h + 1) * 128], func=AF.Square, accum_out=ssq[g_][:, h:h + 1]),
                     reads=["ps%d" % bank], writes=["ssq%d" % g_])
            P.op("act", lambda e: e.activation(out=ssq[g_], in_=ssq[g_], func=AF.Sqrt, scale=1.0 / 128, bias=epsc), reads=["ssq%d" % g_, "cst"], writes=["ssq%d" % g_])

        def s2(u):
            j, t = units[u]
            bank, g_ = u % 3, u % 2
            P.op("dve", lambda e: e.reciprocal(out=ssq[g_], in_=ssq[g_]), reads=["ssq%d" % g_], writes=["ssq%d" % g_])
            q3 = qn.rearrange("p (h d) -> p h d", h=2)
            P.op("dve", lambda e: e.tensor_tensor(out=q3, in0=ps[bank][:, 0:256].rearrange("p (h d) -> p h d", h=2), in1=ssq[g_].unsqueeze(2).broadcast_to([128, 2, 128]), op=ALU.mult),
                 reads=["ps%d" % bank, "ssq%d" % g_], writes=["Cqn"])
            P.op("dve", lambda e: e.tensor_tensor(out=q3, in0=q3, in1=gq_t.unsqueeze(1).broadcast_to([128, 2, 128]), op=ALU.mult), reads=["Cqn", "gq_t"], writes=["Cqn"])
            rope_apply(qn, 2, t, qr[g_], qt1, qt2, ["Cqn"], "qr%d" % g_, "Cq", tabs=(COSo, SINo, "COSo", "SINo"))

        def s3(u):
            g_ = u % 2
            tb = 6 + u % 2
            for gi in range(2):
                P.op("pe", lambda e, gi=gi: e.transpose(out=psb[tb][:, gi * 128:(gi + 1) * 128], in_=qr[g_][:, gi * 128:(gi + 1) * 128], identity=ident),
                     reads=["qr%d" % g_, "ident"], writes=["ps%d" % tb])

        def s4(u):
            j, t = units[u]
            tb = 6 + u % 2
            P.op("act", lambda e: e.activation(out=qT[:, 2 * j:2 * j + 2, t * 128:(t + 1) * 128], in_=psb[tb][:, 0:256].rearrange("p (h n) -> p h n", h=2), func=AF.Copy),
                 reads=["ps%d" % tb], writes=["qT"])
        return [s1, s2, s3, s4]

    mean = vst[:, 2 * NOWN:3 * NOWN]
    var = vst[:, 3 * NOWN:4 * NOWN]
    for grp in range(NOWN // NG):
        TL = list(range(grp * NG, grp * NG + NG))
        for t in TL:
            ld("sp", hTo[:, t % NG, :], hT_scr[t], "hTo%d" % (t % NG), reads=["hT_scr%d" % t])
        run_family(grp, 8, v_stages, [0, 1, 2])
        allv = ["vsum%d" % c_ for c_ in range(NOWN * 8)] + ["vsq%d" % c_ for c_ in range(NOWN * 8)]
        P.op("dve", lambda e: e.reduce_sum(out=vst[:, 0:NOWN], in_=vsum.rearrange("p (t j) -> p t j", j=8), axis=AX.X), reads=allv, writes=["vst"])
        P.op("dve", lambda e: e.reduce_sum(out=vst[:, NOWN:2 * NOWN], in_=vsq.rearrange("p (t j) -> p t j", j=8), axis=AX.X), reads=allv, writes=["vst"])
        P.op("dve", lambda e: e.tensor_scalar(out=mean, in0=vst[:, 0:NOWN], scalar1=1.0 / 2048, scalar2=None, op0=ALU.mult), reads=["vst"], writes=["vmean"])
        P.op("dve", lambda e: e.tensor_tensor(out=var, in0=mean, in1=mean, op=ALU.mult), reads=["vmean"], writes=["vvar"])
        P.op("dve", lambda e: e.scalar_tensor_tensor(out=var, in0=vst[:, NOWN:2 * NOWN], scalar=1.0 / 2048, in1=var, op0=ALU.mult, op1=ALU.subtract),
             reads=["vst", "vvar"], writes=["vvar"])
        P.op("act", lambda e: e.activation(out=var, in_=var, func=AF.Sqrt, bias=epsc), reads=["vvar", "cst"], writes=["vvar"])
        P.op("dve", lambda e: e.reciprocal(out=var, in_=var), reads=["vvar"], writes=["vvar"])
        for t in TL:
            P.op("dve", lambda e, t=t: e.tensor_scalar(out=gv[:, t % NG, :], in0=gv[:, t % NG, :], scalar1=mean[:, t:t + 1], scalar2=var[:, t:t + 1], op0=ALU.subtract, op1=ALU.mult),
                 reads=["gv%d" % (t % NG), "vmean", "vvar"], writes=["gv%d" % (t % NG)])
            P.op("dve", lambda e, t=t: e.tensor_tensor(out=gv[:, t % NG, :], in0=gv[:, t % NG, :], in1=lng, op=ALU.mult), reads=["gv%d" % (t % NG), "lng"], writes=["gv%d" % (t % NG)])
            P.op("dve", lambda e, t=t: e.tensor_tensor(out=gv[:, t % NG, :], in0=gv[:, t % NG, :], in1=lnb, op=ALU.add), reads=["gv%d" % (t % NG), "lnb"], writes=["gv%d" % (t % NG)])
        run_family(grp, 0, u_stages, [0, 0, 1, 2, 3, 4])
        run_family(grp, 16, q_stages, [0, 1, 2, 3, 4])
    allssA = ["ssA%d" % c_ for c_ in range(NOWN * 8)]
    P.barrier()
    sb.release(mC)

    if stop == "C":
        dbgC = nc.dram_tensor("dbgC", [128, 16 * NOWN * 128], BF16, kind="ExternalOutput").ap()
        P.op("sp", lambda e: e.dma_start(out=dbgC, in_=qT.rearrange("p h n -> p (h n)")), reads=["qT"], dma="dbg")
        dbgC2 = nc.dram_tensor("dbgC2", [128, NOWN * 8], F32, kind="ExternalOutput").ap()
        P.op("sp", lambda e: e.dma_start(out=dbgC2, in_=ssA), reads=allssA, dma="dbg")
        P.emit(final_waits=None)
        return nc
    mD = sb.mark()
    KTh = [sb.alloc([128, NKEY], BF16) for _ in range(2)]
    Vh = [sb.alloc([128, NT, VW], BF16) for _ in range(2)]
    NPT = 3
    PT = [sb.alloc([128, 512], BF16) for _ in range(NPT)]
    ob = [[sb.alloc([128, 128], F32) for _ in range(4)] for _ in range(2)]
    obb = [sb.alloc([128, 128], BF16) for _ in range(4)]
    rden = sb.alloc([128, 4], F32)
    obT = [sb.alloc([128, 512], BF16) for _ in range(2)]
    junkD = sb.alloc([128, 128], F32)
    SCALE = 1.0 / math.sqrt(128.0)
    QB = [(0, 512), (512, 512), (1024, 256)]
    blocks = []
    for kvh in range(4):
        for hh in range(4):
            for (q0, nq) in QB:
                blocks.append((kvh, kvh * 4 + hh, q0, nq))
    units = [(bi, kt) for bi in range(len(blocks)) for kt in range(NT)]
    loaded = set()

    def load_kv(kvh):
        if kvh in loaded or kvh >= 4:
            return
        loaded.add(kvh)
        s_ = kvh % 2
        ld("sp", KTh[s_], KT_scr[kvh], "KTh%d" % s_, reads=["KT_scr"])
        ld("sp", Vh[s_].rearrange("p t c -> p (t c)"), V_scr[kvh], "Vh%d" % s_, reads=["V_scr"])

    def st_S(u):
        bi, kt = units[u]
        kvh, head, q0, nq = blocks[bi]
        load_kv(kvh)
        s_ = kvh % 2
        sbk = u % 3
        P.op("pe", lambda e: e.matmul(ps[sbk][:, 0:nq], lhsT=KTh[s_][:, kt * 128:(kt + 1) * 128], rhs=qT[:, head, q0:q0 + nq], start=True, stop=True),
             reads=["KTh%d" % s_, "qT"], writes=["ps%d" % sbk])

    def st_exp(u):
        bi, kt = units[u]
        kvh, head, q0, nq = blocks[bi]
        sbk = u % 3
        pk = u % NPT
        P.op("act", lambda e: e.activation(out=PT[pk][:, 0:nq], in_=ps[sbk][:, 0:nq], func=AF.Exp, scale=SCALE),
             reads=["ps%d" % sbk], writes=["PT%d" % pk])

    def epi_dve(bi):
        kvh, head, q0, nq = blocks[bi]
        nsub = nq // 128
        so = bi % 2
        for qs in range(nsub):
            pb, pbk = ps[3 + qs], "ps%d" % (3 + qs)
            P.op("dve", lambda e, pb=pb, qs=qs: e.reciprocal(out=rden[:, qs:qs + 1], in_=pb[:, 128:129]), reads=[pbk], writes=["rden%d" % qs])
            P.op("dve", lambda e, pb=pb, qs=qs: e.tensor_scalar(out=ob[so][qs], in0=pb[:, 0:128], scalar1=rden[:, qs:qs + 1], scalar2=None, op0=ALU.mult),
                 reads=[pbk, "rden%d" % qs], writes=["ob%d_%d" % (so, qs)])
        for qs in range(nsub):
            P.op("dve", lambda e, qs=qs: e.tensor_copy(out=obb[qs], in_=ob[so][qs]), reads=["ob%d_%d" % (so, qs)], writes=["obb%d" % qs])
        for qs in range(nsub):
            P.op("pe", lambda e, qs=qs: e.transpose(out=psb[7][:, qs * 128:(qs + 1) * 128], in_=obb[qs], identity=ident), reads=["obb%d" % qs, "ident"], writes=["ps7"])
        P.op("dve", lambda e: e.tensor_scalar(out=obT[so][:, 0:nq], in0=psb[7][:, 0:nq], scalar1=goutc[:, 16 + head:17 + head], scalar2=None, op0=ALU.mult),
             reads=["ps7", "goutc"], writes=["obT%d" % so])
        P.op("sp", lambda e: e.dma_start(out=oT_scr[16 + head, :, q0:q0 + nq], in_=obT[so][:, 0:nq]), reads=["obT%d" % so], writes=["oT_scr"], dma="S:obT%d" % so)

    def epi_act(bi):
        kvh, head, q0, nq = blocks[bi]
        so = bi % 2
        for qs in range(nq // 128):
            t = q0 // 128 + qs
            col = t * 16 + head
            P.op("act", lambda e, qs=qs, col=col: e.activation(out=junkD, in_=ob[so][qs], func=AF.Square, accum_out=ssB[:, col:col + 1]),
                 reads=["ob%d_%d" % (so, qs)], writes=["ssB%d" % col])

    def st_PV(u):
        bi, kt = units[u]
        kvh, head, q0, nq = blocks[bi]
        s_ = kvh % 2
        pk = u % NPT
        for qs in range(nq // 128):
            P.op("pe", lambda e, qs=qs: e.matmul(ps[3 + qs][:, 0:129], lhsT=PT[pk][:, qs * 128:(qs + 1) * 128], rhs=Vh[s_][:, kt, 0:129],
                                                 start=(kt == 0), stop=(kt == NT - 1)),
                 reads=["PT%d" % pk, "Vh%d" % s_], writes=["ps%d" % (3 + qs)])
        if kt == NT - 1:
            epi_dve(bi)
        if kt == 8 and bi > 0:
            epi_act(bi - 1)
        if kt == 0 and bi % 12 == 1:
            load_kv(kvh + 1)

    pipeline(len(units), [st_S, st_exp, st_PV], [0, 1, 3])
    epi_act(len(blocks) - 1)
    allssB = ["ssB%d" % c_ for c_ in range(NOWN * 16)]
    P.barrier()
    sb.release(mD)
    sb.release(mC)
    sb.release(mTab)

    if stop == "D":
        dbgD = nc.dram_tensor("dbgD", [128, NOWN * 16], F32, kind="ExternalOutput").ap()
        P.op("sp", lambda e: e.dma_start(out=dbgD, in_=ssB), reads=allssB, dma="dbg")
        P.emit(final_waits=None)
        return nc
    mE = sb.mark()
    oT = sb.alloc([128, 32, NOWN * 128], BF16)
    wo = [sb.alloc([128, 32, 512], BF16) for _ in range(2)]
    osb = [sb.alloc([128, 512], F32) for _ in range(2)]
    otmp = sb.alloc([128, 512], F32)
    junkE = sb.alloc([128, 512], F32)
    for kc in range(32):
        ld("sp", oT[:, kc, :], oT_scr[kc], "oT", reads=["oT_scr"], writes=["oT"])
    P.op("dve", lambda e: e.reduce_sum(out=rA, in_=ssA.rearrange("p (t j) -> p t j", j=8), axis=AX.X), reads=allssA, writes=["rA"])
    P.op("dve", lambda e: e.reduce_sum(out=rB, in_=ssB.rearrange("p (t j) -> p t j", j=16), axis=AX.X), reads=allssB, writes=["rB"])
    rstd_from_ss(rA, rA, 2048, ["rA"], "rA")
    rstd_from_ss(rB, rB, 2048, ["rB"], "rB")
    nE = 0
    for cbk in range(8):
        s_ = cbk % 2
        P.op("pool", lambda e, s_=s_, cbk=cbk: e.dma_start(out=wo[s_].rearrange("p k c -> p (k c)"), in_=w_out_h[cbk]), writes=["wo%d" % s_], dma="L:wo%d" % s_)
        for t in range(NOWN):
            pa, pbn = nE % 2, 2 + (nE % 2)
            so = nE % 2
            nE += 1
            for kc in range(16):
                P.op("pe", lambda e, pa=pa, kc=kc, t=t, s_=s_: e.matmul(ps[pa], lhsT=oT[:, kc, t * 128:(t + 1) * 128], rhs=wo[s_][:, kc, :], start=(kc == 0), stop=(kc == 15)),
                     reads=["oT", "wo%d" % s_], writes=["ps%d" % pa])
            for kc in range(16, 32):
                P.op("pe", lambda e, pbn=pbn, kc=kc, t=t, s_=s_: e.matmul(ps[pbn], lhsT=oT[:, kc, t * 128:(t + 1) * 128], rhs=wo[s_][:, kc, :], start=(kc == 16), stop=(kc == 31)),
                     reads=["oT", "wo%d" % s_], writes=["ps%d" % pbn])
            P.op("act", lambda e, pa=pa, t=t: e.activation(out=otmp, in_=ps[pa], func=AF.Copy, scale=rA[:, t:t + 1]), reads=["ps%d" % pa, "rA"], writes=["otmp"])
            P.op("dve", lambda e, pbn=pbn, t=t, so=so: e.scalar_tensor_tensor(out=osb[so], in0=ps[pbn], scalar=rB[:, t:t + 1], in1=otmp, op0=ALU.mult, op1=ALU.add),
                 reads=["ps%d" % pbn, "rB", "otmp"], writes=["osb%d" % so])
            P.op("act", lambda e, so=so, t=t, cbk=cbk: e.activation(out=junkE, in_=osb[so], func=AF.Square, accum_out=ssO[:, t * 8 + cbk:t * 8 + cbk + 1]),
                 reads=["osb%d" % so], writes=["ssO"])
            P.op("sp", lambda e, so=so, t=t, cbk=cbk: e.dma_start(out=o_scr[t * 128:(t + 1) * 128, cbk * 512:(cbk + 1) * 512], in_=osb[so]),
                 reads=["osb%d" % so], writes=["o_scr"], dma="S:osb%d" % so)
    P.op("dve", lambda e: e.reduce_sum(out=rO, in_=ssO.rearrange("p (t j) -> p t j", j=8), axis=AX.X), reads=["ssO"], writes=["rO"])
    rstd_from_ss(rO, rO, D, ["rO"], "rO")
    P.barrier()
    sb.release(mE)

    if stop == "E":
        dbgE = nc.dram_tensor("dbgE", [128, NOWN], F32, kind="ExternalOutput").ap()
        P.op("sp", lambda e: e.dma_start(out=dbgE, in_=rO), reads=["rO"], dma="dbg")
        P.emit(final_waits=None)
        return nc
    out_ops = []
    for blk in range(2):
        mF = sb.mark()
        HTF_BYTES = 32 * 516 * 2
        HTF_OFF = (SB.LIMIT - HTF_BYTES) // 64 * 64
        hTf = nc.alloc_sbuf_tensor_at("hTf%d" % blk, [128, 32, 516], BF16, offset=HTF_OFF).ap()
        mF2 = sb.mark()
        orow = [sb.alloc([128, D], F32) for _ in range(2)]
        xrow = [sb.alloc([128, D], F32) for _ in range(2)]
        xm = [sb.alloc([128, D], F32) for _ in range(2)]
        ggrow = sb.alloc([128, D], F32)
        junkF = sb.alloc([128, D], BF16)
        xhF = [sb.alloc([128, D], BF16) for _ in range(2)]
        coltmp = sb.alloc([128, 32], F32)
        stF = sb.alloc([128, 2], F32)
        ld("sp", ggrow, gg_scr[0], "ggrow", reads=["gg_scr"])

        one_c = cst[:, 3:4]
        if blk == 0:
            PASSES = [(0, "h", (127, 512, hmask[:, 0:1])), (1, "o", 0), (2, "o", 1), (3, "o", 2), (4, "o", 3),
                      (5, "h", (0, 513, one_c)), (9, "h", (0, 515, hmask[:, 1:2]))]
        else:
            PASSES = [(5, "o", 0), (6, "o", 1), (7, "o", 2), (8, "o", 3)]

        def f_s0(ti):
            t, s_ = PASSES[ti][0], ti % 2
            ld("sp", orow[s_], o_scr[t * 128:(t + 1) * 128, :], "orow%d" % s_, reads=["o_scr"])
            ld("sp", xrow[s_], xr[t * 128:(t + 1) * 128, :], "xrow%d" % s_)

        def f_s1(ti):
            t, s_ = PASSES[ti][0], ti % 2
            P.op("dve", lambda e: e.scalar_tensor_tensor(out=xm[s_], in0=orow[s_], scalar=rO[:, t:t + 1], in1=ggrow, op0=ALU.mult, op1=ALU.mult),
                 reads=["orow%d" % s_, "rO", "ggrow"], writes=["xm%d" % s_])
            P.op("dve", lambda e: e.tensor_tensor(out=xm[s_], in0=xm[s_], in1=xrow[s_], op=ALU.add), reads=["xm%d" % s_, "xrow%d" % s_], writes=["xm%d" % s_])
            if PASSES[ti][1] == "o":
                own = t - 1
                P.op("sp", lambda e: e.dma_start(out=xm_scr[own * 128:(own + 1) * 128, :], in_=xm[s_]), reads=["xm%d" % s_], writes=["xm_scr"], dma="S:xm%d" % s_)

        def f_s2(ti):
            s_ = ti % 2
            P.op("act", lambda e: e.activation(out=junkF, in_=xm[s_], func=AF.Square, accum_out=stF[:, s_:s_ + 1]), reads=["xm%d" % s_], writes=["stF%d" % s_])
            rstd_from_ss(stF[:, s_:s_ + 1], stF[:, s_:s_ + 1], D, ["stF%d" % s_], "stF%d" % s_)

        def f_s3(ti):
            s_ = ti % 2
            P.op("act", lambda e: e.activation(out=xhF[s_], in_=xm[s_], func=AF.Copy, scale=stF[:, s_:s_ + 1]), reads=["xm%d" % s_, "stF%d" % s_], writes=["xhF%d" % s_])

        def f_s4(ti):
            s_ = ti % 2
            TB = (0, 1, 2, 3)
            for g4 in range(4):
                for j in range(8):
                    kc = g4 * 8 + j
                    P.op("pe", lambda e, g4=g4, j=j, kc=kc: e.transpose(out=psb[TB[g4]][:, j * 128:(j + 1) * 128], in_=xhF[s_][:, kc * 128:(kc + 1) * 128], identity=ident),
                         reads=["xhF%d" % s_, "ident"], writes=["ps%d" % TB[g4]])
            if PASSES[ti][1] == "o":
                oi = PASSES[ti][2]
                for g4 in range(4):
                    bank, bk = psb[TB[g4]], "ps%d" % TB[g4]
                    for j in range(8):
                        kc = g4 * 8 + j
                        dst = hTf[:, kc, oi * 128:(oi + 1) * 128]
                        if g4 % 2 == 0:
                            P.op("act", lambda e, bank=bank, j=j, kc=kc, dst=dst: e.activation(out=dst, in_=bank[:, j * 128:(j + 1) * 128], func=AF.Identity,
                                                                                              scale=Gf[:, kc:kc + 1], bias=SHf[:, kc:kc + 1]),
                                 reads=[bk, "Gf", "modc"], writes=["hTf_%d_%d" % (oi, kc)])
                        else:
                            P.op("dve", lambda e, bank=bank, j=j, kc=kc, dst=dst: e.tensor_scalar(out=dst, in0=bank[:, j * 128:(j + 1) * 128],
                                                                                                 scalar1=Gf[:, kc:kc + 1], scalar2=SHf[:, kc:kc + 1], op0=ALU.mult, op1=ALU.add),
                                 reads=[bk, "Gf", "modc"], writes=["hTf_%d_%d" % (oi, kc)])
            else:
                col, dstc, mcol = PASSES[ti][2]
                for g4 in range(4):
                    bank, bk = psb[TB[g4]], "ps%d" % TB[g4]
                    src = bank[:, 0:1024].rearrange("p (j t) -> p j t", j=8)[:, :, col]
                    sl = slice(g4 * 8, g4 * 8 + 8)
                    P.op("dve", lambda e, src=src, sl=sl: e.tensor_tensor(out=coltmp[:, sl], in0=src, in1=Gf[:, sl], op=ALU.mult), reads=[bk, "Gf"], writes=["Fct"])
                    P.op("dve", lambda e, sl=sl: e.tensor_tensor(out=coltmp[:, sl], in0=coltmp[:, sl], in1=SHf[:, sl], op=ALU.add), reads=["Fct", "modc"], writes=["Fct"])
                    P.op("dve", lambda e, sl=sl, dstc=dstc, mcol=mcol: e.tensor_scalar(out=hTf[:, sl, dstc], in0=coltmp[:, sl], scalar1=mcol, scalar2=None, op0=ALU.mult),
                         reads=["Fct", "hmask", "cst"], writes=["hTf_h%d_%d" % (ti, g4)])

        pipeline(len(PASSES), [f_s0, f_s1, f_s2, f_s3, f_s4], [0, 1, 2, 3, 4])
        assert sb.off <= HTF_OFF, sb.off
        if blk == 0:
            P.op("dve", lambda e: e.tensor_copy(out=hTf[:, :, 514], in_=hTf[:, :, 511]), reads=["hTf_3_%d" % kc_ for kc_ in range(32)], writes=["hTf_L1"])
        P.barrier()
        sb.release(mF2)
        act = sb.alloc([128, NFC, 512], BF16)
        mG = sb.mark()
        wg = [sb.alloc([128, 32, 128], BF16) for _ in range(2)]
        wu = [sb.alloc([128, 32, 128], BF16) for _ in range(2)]
        ag = sb.alloc([128, 514], F32)
        au = sb.alloc([128, 514], F32)
        cg = sb.alloc([128, 512], F32)
        cu = sb.alloc([128, 512], F32)
        sg = sb.alloc([128, 512], F32)
        cp4 = convp.rearrange("p (c f) -> p c f", f=4)
        for i in range(NFC):
            s_ = i % 2
            P.op("pool", lambda e, s_=s_, i=i: e.dma_start(out=wg[s_].rearrange("p k c -> p (k c)"), in_=w_up_g[i]), writes=["wg%d" % s_], dma="L:wg%d" % s_)
            P.op("pool", lambda e, s_=s_, i=i: e.dma_start(out=wu[s_].rearrange("p k c -> p (k c)"), in_=w_up_u[i]), writes=["wu%d" % s_], dma="L:wu%d" % s_)
            for (wsrc, wkey, pm, ph, abuf, akey, cbuf, ckey, ci) in ((wg[s_], "wg%d" % s_, s_, 4, ag, "ag", cg, "cg", i),
                                                                   (wu[s_], "wu%d" % s_, 2 + s_, 5, au, "au", cu, "cu", NFC + i)):
                for kc in range(32):
                    P.op("pe", lambda e, pm=pm, kc=kc, wsrc=wsrc: e.matmul(ps[pm], lhsT=wsrc[:, kc, :], rhs=hTf[:, kc, 0:512], start=(kc == 0), stop=(kc == 31)),
                         reads=[wkey], writes=["ps%d" % pm])
                if blk == 0:
                    for kc in range(32):
                        P.op("pe", lambda e, ph=ph, kc=kc, wsrc=wsrc: e.matmul(ps[ph][:, 0:4], lhsT=wsrc[:, kc, :], rhs=hTf[:, kc, 512:516], start=(kc == 0), stop=(kc == 31)),
                             reads=[wkey], writes=["ps%d" % ph])
                P.op("act", lambda e, pm=pm, abuf=abuf: e.activation(out=abuf[:, 1:513], in_=ps[pm], func=AF.Copy), reads=["ps%d" % pm], writes=[akey])
                if blk == 0:
                    P.op("dve", lambda e, ph=ph, abuf=abuf: e.tensor_copy(out=abuf[:, 0:1], in_=ps[ph][:, 0:1]), reads=["ps%d" % ph], writes=[akey])
                    P.op("dve", lambda e, ph=ph, abuf=abuf: e.tensor_copy(out=abuf[:, 513:514], in_=ps[ph][:, 1:2]), reads=["ps%d" % ph], writes=[akey])
                    P.op("dve", lambda e, ph=ph, ci=ci: e.tensor_copy(out=ahalo[:, ci, :], in_=ps[ph][:, 2:4]), reads=["ps%d" % ph], writes=["ahalo%d" % ci])
                else:
                    P.op("dve", lambda e, abuf=abuf, ci=ci: e.tensor_copy(out=abuf[:, 0:1], in_=ahalo[:, ci, 0:1]), reads=["ahalo%d" % ci], writes=[akey])
                    P.op("dve", lambda e, abuf=abuf, ci=ci: e.tensor_copy(out=abuf[:, 513:514], in_=ahalo[:, ci, 1:2]), reads=["ahalo%d" % ci], writes=[akey])
                P.op("dve", lambda e, abuf=abuf, cbuf=cbuf, ci=ci: e.tensor_scalar(out=cbuf, in0=abuf[:, 0:512], scalar1=cp4[:, ci, 0:1], scalar2=cp4[:, ci, 3:4], op0=ALU.mult, op1=ALU.add),
                     reads=[akey, "convp"], writes=[ckey])
                P.op("dve", lambda e, abuf=abuf, cbuf=cbuf, ci=ci: e.scalar_tensor_tensor(out=cbuf, in0=abuf[:, 1:513], scalar=cp4[:, ci, 1:2], in1=cbuf, op0=ALU.mult, op1=ALU.add),
                     reads=[akey, "convp", ckey], writes=[ckey])
                P.op("dve", lambda e, abuf=abuf, cbuf=cbuf, ci=ci: e.scalar_tensor_tensor(out=cbuf, in0=abuf[:, 2:514], scalar=cp4[:, ci, 2:3], in1=cbuf, op0=ALU.mult, op1=ALU.add),
                     reads=[akey, "convp", ckey], writes=[ckey])
            P.op("act", lambda e: e.activation(out=sg, in_=cg, func=AF.Silu), reads=["cg"], writes=["sg"])
            P.op("dve", lambda e, i=i: e.tensor_tensor(out=act[:, i, :], in0=sg, in1=cu, op=ALU.mult), reads=["sg", "cu"], writes=["act"])
        assert sb.off <= HTF_OFF, sb.off
        P.barrier()
        sb.release(mG)
        KG = 8
        groups = [(k0, min(k0 + KG, NFC)) for k0 in range(0, NFC, KG)]
        wd = [sb.alloc([128, KG, 512], BF16) for _ in range(3)]
        fsb = [sb.alloc([128, 512], F32) for _ in range(2)]
        junkG = sb.alloc([128, 512], F32)
        nwd = 0
        nf = 0
        for cbk in range(8):
            base = 4 * (cbk % 2)
            for (k0, k1) in groups:
                s_ = nwd % 3
                nwd += 1
                P.op("pool", lambda e, s_=s_, cbk=cbk, k0=k0, k1=k1: e.dma_start(out=wd[s_][:, 0:k1 - k0, :].rearrange("p k c -> p (k c)"),
                                                                               in_=w_down_h[cbk][:, k0 * 512:k1 * 512]),
                     writes=["wd%d" % s_], dma="L:wd%d" % s_)
                for kc in range(k0, k1):
                    for q in range(4):
                        P.op("pe", lambda e, base=base, q=q, kc=kc, k0=k0, s_=s_: e.matmul(ps[base + q], lhsT=act[:, kc, q * 128:(q + 1) * 128], rhs=wd[s_][:, kc - k0, :],
                                                                                       start=(kc == 0), stop=(kc == NFC - 1)),
                             reads=["act", "wd%d" % s_], writes=["ps%d" % (base + q)])
            for q in range(4):
                so = nf % 2
                nf += 1
                own = 4 * blk + q
                P.op("act", lambda e, base=base, q=q, so=so: e.activation(out=fsb[so], in_=ps[base + q], func=AF.Copy), reads=["ps%d" % (base + q)], writes=["fsb%d" % so])
                P.op("act", lambda e, so=so, q=q, cbk=cbk: e.activation(out=junkG, in_=fsb[so], func=AF.Square, accum_out=ssF[:, q * 8 + cbk:q * 8 + cbk + 1]),
                     reads=["fsb%d" % so], writes=["ssF"])
                P.op("sp", lambda e, so=so, own=own, cbk=cbk: e.dma_start(out=f_scr[own * 128:(own + 1) * 128, cbk * 512:(cbk + 1) * 512], in_=fsb[so]),
                     reads=["fsb%d" % so], writes=["f_scr"], dma="S:fsb%d" % so)
        P.op("dve", lambda e: e.reduce_sum(out=rF[:, 0:4], in_=ssF[:, 0:32].rearrange("p (t j) -> p t j", j=8), axis=AX.X), reads=["ssF"], writes=["rF"])
        rstd_from_ss(rF[:, 0:4], rF[:, 0:4], D, ["rF"], "rF")
        frow = [sb.alloc([128, D], F32) for _ in range(2)]
        xmrow = [sb.alloc([128, D], F32) for _ in range(2)]
        ggf = sb.alloc([128, D], F32)
        ld("sp", ggf, gg_scr[1], "ggf", reads=["gg_scr"])

        def z_s0(q):
            own, s_ = 4 * blk + q, q % 2
            ld("sp", frow[s_], f_scr[own * 128:(own + 1) * 128, :], "frow%d" % s_, reads=["f_scr"])
            ld("sp", xmrow[s_], xm_scr[own * 128:(own + 1) * 128, :], "xmrow%d" % s_, reads=["xm_scr"])

        def z_s1(q):
            own, s_ = 4 * blk + q, q % 2
            P.op("dve", lambda e: e.scalar_tensor_tensor(out=frow[s_], in0=frow[s_], scalar=rF[:, q:q + 1], in1=ggf, op0=ALU.mult, op1=ALU.mult),
                 reads=["frow%d" % s_, "rF", "ggf"], writes=["frow%d" % s_])
            P.op("dve", lambda e: e.tensor_tensor(out=frow[s_], in0=frow[s_], in1=xmrow[s_], op=ALU.add), reads=["frow%d" % s_, "xmrow%d" % s_], writes=["frow%d" % s_])
            o = P.op("sp", lambda e: e.dma_start(out=out[own * 128:(own + 1) * 128, :], in_=frow[s_]), reads=["frow%d" % s_], dma="S:out%d" % s_)
            out_ops.append(o.idx)

        pipeline(4, [z_s0, z_s1], [0, 1])
        P.barrier()
        sb.release(mF)

    P.emit(final_waits=out_ops)
    return nc


_CACHE = {}


def _host_layouts(inp):
    f = np.float32
    A = {}

    def col(v):
        v = np.asarray(v, f).reshape(-1, 128)
        return np.ascontiguousarray(v.T)

    A["ctx"] = np.ascontiguousarray(inp["ctx"][0], dtype=f)
    A["jidx"] = np.ascontiguousarray(np.broadcast_to(np.arange(32, dtype=f)[None, :], (128, 32)))
    A["cvec"] = np.concatenate([col(inp["c"][0]), col(inp["c_ctx"])], axis=1)
    w_ada_h = np.ascontiguousarray(np.asarray(inp["w_ada"][0], f).reshape(32, 128, 192, 128).transpose(2, 1, 0, 3)).reshape(192, 128, 32 * 128)
    A["w_ada_a"] = np.ascontiguousarray(w_ada_h[:96])
    A["w_ada_b"] = np.ascontiguousarray(w_ada_h[96:])
    A["b_ada_col"] = col(inp["b_ada"][0])
    A["b_ada_row"] = np.ascontiguousarray(inp["b_ada"][0][None, :], dtype=f)
    A["gcols"] = np.concatenate([col(inp["g_pre_mix"][0]), col(inp["g_pre_ffn"][0])], axis=1)
    A["goutc"] = col(np.concatenate([inp["g_out_a"][0], inp["g_out_b"][0]]))
    A["gpost_mix_row"] = np.ascontiguousarray(inp["g_post_mix"][0][None, :], dtype=f)
    A["gpost_ffn_row"] = np.ascontiguousarray(inp["g_post_ffn"][0][None, :], dtype=f)
    A["lng_row"] = np.ascontiguousarray(inp["ln_v_g"][0][None, :], dtype=f)
    A["lnb_row"] = np.ascontiguousarray(inp["ln_v_b"][0][None, :], dtype=f)
    A["gq_row"] = np.ascontiguousarray(inp["g_q"][0][None, :], dtype=f)
    A["gk_row"] = np.ascontiguousarray(inp["g_k"][0][None, :], dtype=f)
    w_in = np.asarray(inp["w_in"][0], f)
    A["w_in_h"] = np.ascontiguousarray(w_in.reshape(32, 128, 28, 256).transpose(2, 1, 0, 3)).reshape(28, 128, 32 * 256)
    ws = np.asarray(inp["w_s"][0], f)
    A["wsT"] = np.ascontiguousarray(ws.transpose(2, 0, 1)).reshape(128, 16 * 128)
    A["bs_col"] = np.ascontiguousarray(np.asarray(inp["b_s"][0], f).T)
    w_out = np.asarray(inp["w_out"][0], f)
    A["w_out_h"] = np.ascontiguousarray(w_out.reshape(32, 128, 8, 512).transpose(2, 1, 0, 3)).reshape(8, 128, 32 * 512)
    w_up = np.asarray(inp["w_up"][0], f)
    w_up_h = np.ascontiguousarray(w_up.reshape(32, 128, 2 * NFC, 128).transpose(2, 1, 0, 3)).reshape(2 * NFC, 128, 32 * 128)
    A["w_up_g"] = np.ascontiguousarray(w_up_h[:NFC])
    A["w_up_u"] = np.ascontiguousarray(w_up_h[NFC:])
    cw = np.asarray(inp["conv_w"][0], f)
    cb = np.asarray(inp["conv_b"][0], f)
    cp = np.concatenate([cw, cb[None, :]], axis=0)
    A["convp"] = np.ascontiguousarray(cp.reshape(4, 2 * NFC, 128).transpose(2, 1, 0)).reshape(128, 2 * NFC * 4)
    w_down = np.asarray(inp["w_down"][0], f)
    A["w_down_h"] = np.ascontiguousarray(w_down.reshape(NFC, 128, 8, 512).transpose(2, 1, 0, 3)).reshape(8, 128, NFC * 512)
    A["ident"] = np.eye(128, dtype=f)
    return A


def _run(inputs, stop=None, debug=False, cores=NCORE):
    nc = build_program(stop=stop, debug=debug)
    in_maps = _in_maps(inputs)[:cores]
    return run_bass_kernel_spmd(nc, in_maps, core_ids=list(range(cores)))


def _in_maps(inputs):
    shared = _host_layouts(inputs)
    x = np.asarray(inputs["x"][0], np.float32)
    tok = np.arange(S)
    in_maps = []
    for i in range(NCORE):
        shift = (TOK * i - 128) % S
        order = (tok + shift) % S
        m = dict(shared)
        m["xr"] = np.ascontiguousarray(x[order])
        rowi = (order // 64).astype(np.float32)
        coli = (order % 64).astype(np.float32)
        pos = np.zeros((128, NT, 2), np.float32)
        pos[:, :64, 0] = rowi.reshape(64, 128).T
        pos[:, :64, 1] = coli.reshape(64, 128).T
        m["pos"] = pos.reshape(128, NT * 2)
        hm = np.ones((128, 2), np.float32)
        if i == 0:
            hm[:, 0] = 0.0
        if i == NCORE - 1:
            hm[:, 1] = 0.0
        m["hmask"] = hm
        in_maps.append(m)
    return in_maps


def kernel(**inputs):
    if "nc" not in _CACHE:
        _CACHE["nc"] = build_program()
    nc = _CACHE["nc"]
    in_maps = _in_maps(inputs)
    res = run_bass_kernel_spmd(nc, in_maps, core_ids=list(range(NCORE)))
    outs = [np.asarray(r["out"], np.float32) for r in res.results]
    return np.concatenate(outs, axis=0)[None, :, :]
```
